# Optimizing a Trainium2 kernel written in Bass

```python
import jax, jax.numpy as jnp
from jax import lax
import numpy as np

D_MODEL = 2048
BATCH = 4
SEQ = 2048
DEPTH = 1
DEC_BATCH = 128
DEC_SEQ = 8
PAST_LEN = 16384
PAGE_SIZE = 128

N_META = 16
MIX_W = D_MODEL
POOL_W = MIX_W // 2
MLSTM_W = MIX_W - POOL_W
POOL_WINDOWS = (2, 4, 8, 16)
N_POOL_GROUPS = len(POOL_WINDOWS)
POOL_GW = POOL_W // N_POOL_GROUPS
POOL_HIST = max(POOL_WINDOWS) - 1
N_HEADS = 4
HEAD_DIM = MLSTM_W // N_HEADS
CHUNK = 64
D_FF = -(-8 * D_MODEL // (3 * 256)) * 256
IN_W = POOL_W + 4 * MLSTM_W + 2 * N_HEADS
EPS = 1e-6

kernel_name = 'hymba_pool_mlstm_decoder_step'


def rmsnorm(x, w):
    xf = x.astype(jnp.float32)
    y = xf * lax.rsqrt(jnp.mean(xf * xf, axis=-1, keepdims=True) + EPS)
    return (y * w.astype(jnp.float32)).astype(x.dtype)


def pool_mix(u_prev, u_new, pos0, w_pool, scale):
    B, T, _ = u_new.shape
    ext = jnp.concatenate([u_prev.astype(u_new.dtype), u_new], axis=1)
    cs = jnp.pad(jnp.cumsum(ext.astype(jnp.float32), axis=1), ((0, 0), (1, 0), (0, 0)))
    end = cs[:, POOL_HIST + 1:]
    pos = pos0 + jnp.arange(T, dtype=jnp.int32)
    u32 = u_new.astype(jnp.float32)
    outs = []
    for g, w in enumerate(POOL_WINDOWS):
        sl = slice(g * POOL_GW, (g + 1) * POOL_GW)
        start = cs[:, POOL_HIST + 1 - w:POOL_HIST + 1 - w + T, sl]
        cnt = jnp.minimum(w, pos + 1).astype(jnp.float32)[None, :, None]
        outs.append((end[..., sl] - start) / cnt - u32[..., sl])
    d = jnp.stack(outs, axis=2)
    y = jnp.einsum('btgc,gcd->btgd', d, w_pool.astype(jnp.float32)).reshape(B, T, POOL_W)
    y = y * scale.astype(jnp.float32)
    return y.astype(u_new.dtype), ext[:, -POOL_HIST:]


def _mlstm_chunk(carry, inp):
    C, n, m = carry
    q, k, v, ig, lf = inp
    L = q.shape[-2]
    b = jnp.cumsum(lf, axis=-1)
    a = ig - b
    m_t = jnp.maximum(m[..., None] + b, lax.cummax(a, axis=a.ndim - 1) + b)
    inter = jnp.exp(m[..., None] + b - m_t)
    causal = jnp.tril(jnp.ones((L, L), dtype=bool))
    log_d = a[..., None, :] + (b - m_t)[..., :, None]
    dmat = jnp.exp(jnp.where(causal, log_d, -jnp.inf))
    s = jnp.einsum('bhtd,bhsd->bhts', q, k) * dmat
    num = jnp.einsum('bhts,bhsv->bhtv', s, v) + inter[..., None] * jnp.einsum('bhtd,bhdv->bhtv', q, C)
    qn = s.sum(-1) + inter * jnp.einsum('bhtd,bhd->bht', q, n)
    h = num / jnp.maximum(jnp.abs(qn), jnp.exp(-m_t))[..., None]
    m_new = m_t[..., -1]
    decay = jnp.exp(m + b[..., -1] - m_new)
    w_s = jnp.exp(a + (b[..., -1] - m_new)[..., None])
    C_new = decay[..., None, None] * C + jnp.einsum('bhs,bhsd,bhsv->bhdv', w_s, k, v)
    n_new = decay[..., None] * n + jnp.einsum('bhs,bhsd->bhd', w_s, k)
    return (C_new, n_new, m_new), h


def mlstm_scan(q, k, v, ig, lf, C, n, m, chunk):
    B, H, T, Dh = q.shape
    nc = T // chunk

    def to_chunks(a):
        return jnp.moveaxis(a.reshape(B, H, nc, chunk, *a.shape[3:]), 2, 0)

    (C, n, m), h = lax.scan(_mlstm_chunk, (C, n, m), (to_chunks(q), to_chunks(k), to_chunks(v), to_chunks(ig), to_chunks(lf)))
    h = jnp.moveaxis(h, 0, 2).reshape(B, H, T, Dh)
    return h, C, n, m


def mlstm_mixer(q, k, v, o, gi, gf, C0, n0, m0, segments, b_i, b_f, g_norm):
    B, T, _ = q.shape

    def heads(a):
        return a.reshape(B, T, N_HEADS, HEAD_DIM).transpose(0, 2, 1, 3).astype(jnp.float32)

    qh, kh, vh = heads(q), heads(k) * (HEAD_DIM ** -0.5), heads(v)
    ig = (gi.astype(jnp.float32) + b_i.astype(jnp.float32)).transpose(0, 2, 1)
    lf = jax.nn.log_sigmoid(gf.astype(jnp.float32) + b_f.astype(jnp.float32)).transpose(0, 2, 1)
    C, n, m = C0.astype(jnp.float32), n0.astype(jnp.float32), m0.astype(jnp.float32)
    hs = []
    start = 0
    for length, chunk in segments:
        sl = slice(start, start + length)
        h, C, n, m = mlstm_scan(qh[:, :, sl], kh[:, :, sl], vh[:, :, sl], ig[:, :, sl], lf[:, :, sl], C, n, m, chunk)
        hs.append(h)
        start += length
    h = jnp.concatenate(hs, axis=2)
    h = h * lax.rsqrt(jnp.mean(h * h, axis=-1, keepdims=True) + EPS) * g_norm.astype(jnp.float32).reshape(N_HEADS, 1, HEAD_DIM)
    h = h.transpose(0, 2, 1, 3).reshape(B, T, MLSTM_W)
    y = jax.nn.sigmoid(o.astype(jnp.float32)) * h
    return y.astype(q.dtype), C, n, m


def block(x, pool_prev, C0, n0, m0, pos0, segments, norm_mix_w, w_in, b_i, b_f, w_pool, pool_scale, g_norm, w_out, norm_ffn_w, w_gate, w_up, w_down):
    xn = rmsnorm(x, norm_mix_w)
    p = xn @ w_in.astype(x.dtype)
    u = p[..., :POOL_W]
    off = POOL_W
    q = p[..., off:off + MLSTM_W]
    k = p[..., off + MLSTM_W:off + 2 * MLSTM_W]
    v = p[..., off + 2 * MLSTM_W:off + 3 * MLSTM_W]
    o = p[..., off + 3 * MLSTM_W:off + 4 * MLSTM_W]
    gi = p[..., off + 4 * MLSTM_W:off + 4 * MLSTM_W + N_HEADS]
    gf = p[..., off + 4 * MLSTM_W + N_HEADS:]
    y_pool, pool_new = pool_mix(pool_prev, u, pos0, w_pool, pool_scale)
    y_ml, C, n, m = mlstm_mixer(q, k, v, o, gi, gf, C0, n0, m0, segments, b_i, b_f, g_norm)
    x = x + jnp.concatenate([y_pool, y_ml], axis=-1) @ w_out.astype(x.dtype)
    xn2 = rmsnorm(x, norm_ffn_w)
    x = x + (jax.nn.silu(xn2 @ w_gate.astype(x.dtype)) * (xn2 @ w_up.astype(x.dtype))) @ w_down.astype(x.dtype)
    return x, pool_new, C, n, m


def setup_inputs(seed: int = 0) -> dict:
    key = jax.random.key(seed)
    ks = jax.random.split(key, 20)
    f32 = jnp.float32
    nrm = lambda k, s: jax.random.normal(k, s, dtype=f32)
    return {
        'x_prompt': nrm(ks[0], (BATCH, SEQ, D_MODEL)),
        'x_sample': nrm(ks[1], (DEC_BATCH, DEC_SEQ, D_MODEL)),
        'state_pool': nrm(ks[2], (DEPTH, DEC_BATCH, POOL_HIST, POOL_W)),
        'state_mlstm_C': nrm(ks[3], (DEPTH, DEC_BATCH, N_HEADS, HEAD_DIM, HEAD_DIM)) * HEAD_DIM ** -0.5,
        'state_mlstm_n': nrm(ks[4], (DEPTH, DEC_BATCH, N_HEADS, HEAD_DIM)) * HEAD_DIM ** -0.5,
        'state_mlstm_m': nrm(ks[5], (DEPTH, DEC_BATCH, N_HEADS)) * 0.5,
        'meta_tokens': nrm(ks[6], (N_META, D_MODEL)),
        'norm_mix_w': 1.0 + 0.02 * nrm(ks[7], (DEPTH, D_MODEL)),
        'w_in': nrm(ks[8], (DEPTH, D_MODEL, IN_W)) * D_MODEL ** -0.5,
        'b_igate': 0.1 * nrm(ks[9], (DEPTH, N_HEADS)),
        'b_fgate': jnp.linspace(3.0, 6.0, N_HEADS, dtype=f32)[None, :] + 0.01 * nrm(ks[10], (DEPTH, N_HEADS)),
        'w_pool': nrm(ks[11], (DEPTH, N_POOL_GROUPS, POOL_GW, POOL_GW)) * POOL_GW ** -0.5,
        'pool_scale': 1.0 + 0.1 * nrm(ks[12], (DEPTH, POOL_W)),
        'mlstm_norm_w': 1.0 + 0.02 * nrm(ks[13], (DEPTH, MLSTM_W)),
        'w_out': nrm(ks[14], (DEPTH, MIX_W, D_MODEL)) * MIX_W ** -0.5,
        'norm_ffn_w': 1.0 + 0.02 * nrm(ks[15], (DEPTH, D_MODEL)),
        'w_gate': nrm(ks[16], (DEPTH, D_MODEL, D_FF)) * D_MODEL ** -0.5,
        'w_up': nrm(ks[17], (DEPTH, D_MODEL, D_FF)) * D_MODEL ** -0.5,
        'w_down': nrm(ks[18], (DEPTH, D_FF, D_MODEL)) * D_FF ** -0.5,
        'norm_final_w': 1.0 + 0.02 * nrm(ks[19], (D_MODEL,)),
    }


def reference(x_prompt, x_sample, state_pool, state_mlstm_C, state_mlstm_n, state_mlstm_m, meta_tokens, norm_mix_w, w_in, b_igate, b_fgate, w_pool, pool_scale, mlstm_norm_w, w_out, norm_ffn_w, w_gate, w_up, w_down, norm_final_w):
    B, T_p, _ = x_prompt.shape
    T_s = x_sample.shape[1]
    dt = x_prompt.dtype
    hp = jnp.concatenate([jnp.broadcast_to(meta_tokens.astype(dt)[None], (B, N_META, D_MODEL)), x_prompt], axis=1)
    hs = x_sample
    seg_p = ((N_META, N_META), (T_p, CHUNK))
    seg_s = ((T_s, T_s),)
    pool_p, C_p, n_p, m_p = [], [], [], []
    pool_s, C_s, n_s, m_s = [], [], [], []
    for l in range(DEPTH):
        wl = (norm_mix_w[l], w_in[l], b_igate[l], b_fgate[l], w_pool[l], pool_scale[l], mlstm_norm_w[l], w_out[l], norm_ffn_w[l], w_gate[l], w_up[l], w_down[l])
        zp = jnp.zeros((B, POOL_HIST, POOL_W), dt)
        zC = jnp.zeros((B, N_HEADS, HEAD_DIM, HEAD_DIM), jnp.float32)
        zn = jnp.zeros((B, N_HEADS, HEAD_DIM), jnp.float32)
        zm = jnp.zeros((B, N_HEADS), jnp.float32)
        hp, sp, Cp, np_, mp = block(hp, zp, zC, zn, zm, 0, seg_p, *wl)
        hs, ss, Cs, ns, ms = block(hs, state_pool[l], state_mlstm_C[l], state_mlstm_n[l], state_mlstm_m[l], PAST_LEN, seg_s, *wl)
        pool_p.append(sp); C_p.append(Cp); n_p.append(np_); m_p.append(mp)
        pool_s.append(ss); C_s.append(Cs); n_s.append(ns); m_s.append(ms)
    y_prompt = rmsnorm(hp, norm_final_w)[:, N_META:]
    y_sample = rmsnorm(hs, norm_final_w)
    return (y_prompt, y_sample, jnp.stack(pool_p), jnp.stack(C_p), jnp.stack(n_p), jnp.stack(m_p), jnp.stack(pool_s), jnp.stack(C_s), jnp.stack(n_s), jnp.stack(m_s))
```

```python
import math
import numpy as np
import concourse.bass as bass
import concourse.mybir as mybir
from concourse.bass_utils import run_bass_kernel_spmd

F32 = mybir.dt.float32
BF16 = mybir.dt.bfloat16
AF = mybir.ActivationFunctionType
ALU = mybir.AluOpType
AX = mybir.AxisListType

D = 2048
KC = 16
NH = 4
HD = 256
HDE = 257
DFF = 5632
FCH = 44
EPS = 1e-6
LN16 = math.log(16.0)
BIG = 1.0e30
ENGS = ("pe", "act", "dve", "pool", "sp")
STOP = None


class Buf:
    __slots__ = ("ap", "w", "r", "excl")

    def __init__(self, ap, excl=False):
        self.ap = ap
        self.w = None
        self.r = []
        self.excl = excl


class Prog:
    def __init__(self):
        self.streams = {e: [] for e in ENGS}
        self.cnt = {e: 0 for e in ENGS}
        self.seen = {e: {} for e in ENGS}
        self.ndma = {"sp": 20, "pool": 12}
        self.dma_cnt = {q: [0] * n for q, n in self.ndma.items()}
        self.dma_rr = {q: 0 for q in self.ndma}
        self.dead = False
        self.stop = STOP

    def checkpoint(self, k):
        if self.stop is not None and k >= self.stop - 1e-9:
            self.dead = True

    def _waits(self, eng, deps):
        out = []
        for d in deps:
            if d is None:
                continue
            k, c = d
            if self.seen[eng].get(k, 0) >= c:
                continue
            self.seen[eng][k] = c
            out.append((k, c))
        return out

    def _deps(self, eng, reads, writes):
        deps = []
        for b in reads:
            if b.w is not None and not (eng == "pe" and b.w[0] == "pe"):
                deps.append(b.w)
            if b.excl:
                for t in b.r:
                    if t is not None and t[0] != eng:
                        deps.append(t)
        for b in writes:
            for t in [b.w] + b.r:
                if t is not None and t[0] != eng:
                    deps.append(t)
        return deps

    def op(self, eng, fn, reads=(), writes=(), signal=True):
        if self.dead:
            return None
        waits = self._waits(eng, self._deps(eng, reads, writes))
        if signal:
            self.cnt[eng] += 1
            tok = (eng, self.cnt[eng])
        else:
            tok = (eng, self.cnt[eng] + 1)
        self.streams[eng].append((waits, fn, "inc" if signal else None))
        for b in reads:
            b.r.append(tok)
        for b in writes:
            b.w = tok
            b.r = []
        return tok

    def dma(self, q, out_ap, in_ap, reads=(), writes=(), slow=False):
        if self.dead:
            return None
        i = self.dma_rr[q]
        self.dma_rr[q] = (i + 1) % self.ndma[q]
        key = (q, i)
        deps = self._deps("dmaq", reads, writes)
        prev = self.dma_cnt[q][i]
        if prev:
            deps.append((key, prev))
        waits = self._waits(q, deps)
        self.dma_cnt[q][i] = prev + 16
        tok = (key, prev + 16)
        if slow:
            fn = lambda e, o=out_ap, a=in_ap: e.dma_start(out=o, in_=a, allow_slow_non_contiguous=True)
        else:
            fn = lambda e, o=out_ap, a=in_ap: e.dma_start(out=o, in_=a)
        self.streams[q].append((waits, fn, key))
        for b in reads:
            b.r.append(tok)
        for b in writes:
            b.w = tok
            b.r = []
        return tok

    def replay(self, eng, e, sems):
        for waits, fn, sig in self.streams[eng]:
            for k, c in waits:
                e.wait_ge(sems[k], c)
            ins = fn(e)
            if sig == "inc":
                ins.then_inc(sems[eng], 1)
            elif sig is not None:
                ins.then_inc(sems[sig], 16)

    def final_waits(self, eng, e, sems):
        for i, c in enumerate(self.dma_cnt.get(eng, [])):
            if c:
                e.wait_ge(sems[(eng, i)], c)


def build_nc():
    nc = bass.Bass("TRN2", target_bir_lowering=False)
    P = Prog()

    def din(name, shape, dt=F32):
        return nc.dram_tensor(name, list(shape), dt, kind="ExternalInput").ap()

    def dout(name, shape, dt=F32):
        return nc.dram_tensor(name, list(shape), dt, kind="ExternalOutput").ap()

    x_meta = din("x_meta", [16, D])
    x_pre = din("x_pre", [1024, D])
    x_own = din("x_own", [1024, D])
    x_halo = din("x_halo", [16, D])
    x_smp = din("x_smp", [128, D])
    flag_d = din("flag", [128, 1])
    st_pool = din("st_pool", [16, 15, 1024])
    st_C = din("st_C", [16, NH, HD, HD])
    st_n = din("st_n", [64, HD])
    st_m = din("st_m", [16, NH])
    norm_mix_w = din("norm_mix_w", [D])
    w_in = din("w_in", [D, 5128])
    b_ig = din("b_igate", [NH])
    b_fg = din("b_fgate", [NH])
    w_pool = din("w_pool", [4, 256, 256])
    pool_scale = din("pool_scale", [1024])
    mnorm_w = din("mlstm_norm_w", [1024])
    w_out = din("w_out", [D, D])
    norm_ffn_w = din("norm_ffn_w", [D])
    w_gate = din("w_gate", [D, DFF])
    w_up = din("w_up", [D, DFF])
    w_down = din("w_down", [DFF, D])
    norm_final_w = din("norm_final_w", [D])

    y_own = dout("y_own", [1024, D])
    y_smp = dout("y_smp", [128, D])
    pool_p = dout("pool_p", [15, 1024])
    C_p = dout("C_p", [NH, HD, HD])
    n_p = dout("n_p", [NH, HD])
    m_p = dout("m_p", [1, NH])
    pool_s = dout("pool_s", [16, 15, 1024])
    C_s = dout("C_s", [16, NH, HD, HD])
    n_s = dout("n_s", [128, 128])
    m_s = dout("m_s", [16, NH])

    w_in_v = w_in.rearrange("(kc p) n -> p kc n", p=128)
    w_out_v = w_out.rearrange("(kc p) n -> p kc n", p=128)
    w_gate_v = w_gate.rearrange("(kc p) n -> p kc n", p=128)
    w_up_v = w_up.rearrange("(kc p) n -> p kc n", p=128)
    w_down_v = w_down.rearrange("(fc p) n -> p fc n", p=128)

    from contextlib import ExitStack
    es = ExitStack()

    def sb(name, shape, dt):
        return es.enter_context(nc.sbuf_tensor(name, list(shape), dt))

    def ps(name, shape, dt):
        return es.enter_context(nc.psum_tensor(name, list(shape), dt))

    with es:
        Z1 = sb("z1", [128, KC * 1168 // 2], F32)
        Z2 = sb("z2", [128, KC * 1152 // 2], F32)
        Z3 = sb("z3", [128, 8192], F32)
        Z4 = sb("z4", [128, 9 * D], F32)
        Z1b = Z1[:, :].bitcast(BF16)
        Z2b = Z2[:, :].bitcast(BF16)
        Z3b = Z3[:, :].bitcast(BF16)
        Z4BUFS = []
        tl = [9600]

        def zt(nfl, dt=F32, shape=None):
            a0 = tl[0]
            tl[0] += nfl
            assert tl[0] <= 9 * D, tl[0]
            ap = Z4[:, a0:a0 + nfl]
            if dt == BF16:
                ap = ap.bitcast(BF16)
            if shape is not None:
                ap = ap.rearrange(shape[0], **shape[1])
            bb = Buf(ap)
            Z4BUFS.append(bb)
            return bb

        ones32 = Buf(sb("ones32", [128, 128], F32)[:, :])
        ident32 = Buf(sb("ident32", [128, 128], F32)[:, :])
        identb = Buf(sb("identb", [128, 128], BF16)[:, :])
        tri32 = Buf(sb("tri32", [128, 128], F32)[:, :])
        tril32 = Buf(sb("tril32", [128, 128], F32)[:, :])
        maskneg = Buf(sb("maskneg", [128, 128], F32)[:, :])
        maskposb = Buf(sb("maskposb", [128, 128], BF16)[:, :])
        triB32 = Buf(sb("triB32", [128, 128], F32)[:, :])
        trilB32 = Buf(sb("trilB32", [128, 128], F32)[:, :])
        masknegB = Buf(sb("masknegB", [128, 128], F32)[:, :])
        maskposbB = Buf(sb("maskposbB", [128, 128], BF16)[:, :])
        SLm = Buf(sb("SLm", [128, 128], F32)[:, :])
        sel32 = Buf(sb("sel32", [16, 128], F32)[:, :])
        selA = Buf(sb("selA", [16, 128], F32)[:, :])
        sellastT = Buf(sb("sellastT", [16, 128], F32)[:, :])
        blockmask = Buf(sb("blockmask", [128, 16], F32)[:, :])
        blockA = Buf(sb("blockA", [128, 16], F32)[:, :])
        sellast = Buf(sb("sellast", [128, 16], F32)[:, :])
        bias8 = Buf(sb("bias8", [128, 8], F32)[:, :])
        gcol = Buf(sb("gcol", [128, 8], F32)[:, :])
        pscol = Buf(sb("pscol", [128, 8], F32)[:, :])
        flagc = Buf(sb("flagc", [128, 1], F32)[:, :])
        fsc = Buf(sb("fsc", [128, 1], F32)[:, :])
        negBc = Buf(sb("negBc", [128, 4], F32)[:, :])
        Mc = Buf(sb("Mc", [128, 4], F32)[:, :])
        negBm = Buf(sb("negBm", [128, 4], F32)[:, :])
        Mm = Buf(sb("Mm", [128, 4], F32)[:, :])
        C32t = sb("C32", [128, NH, 2, HDE], F32)
        C32 = [Buf(C32t[:, h, :, :]) for h in range(NH)]
        Cbft = sb("Cbf", [128, NH, 2, HDE], BF16)
        Cbf = [Buf(Cbft[:, h, :, :]) for h in range(NH)]
        sst = sb("ss", [128, 2, 8], F32)
        SS = [Buf(sst[:, i, :]) for i in range(2)]
        xnb1 = Buf(sb("xnb", [128, D], BF16)[:, :])
        xnb = [xnb1, xnb1]
        sq = xnb1
        wpl = Buf(sb("wpl", [128, 4, 2, 256], BF16)[:, :, :, :])

        selrepA = zt(1024, BF16, ("p (b t) -> p b t", dict(b=16)))
        wg8 = zt(64, BF16, ("p (k n) -> p k n", dict(k=KC)))
        gwsb = zt(10 * 19 * 4)
        GWl = [Buf(gwsb.ap[:, j * 76:(j + 1) * 76].rearrange("p (k h) -> p k h", k=19)) for j in range(10)]
        Z4BUFS.extend(GWl)
        GW = GWl[0:9] + GWl[0:10]
        dg = zt(512, F32, ("p (h s) -> p h s", dict(h=4)))
        tmpm = zt(512, F32, ("p (h s) -> p h s", dict(h=4)))
        dgM = [zt(128) for i in range(2)]
        DT = [zt(128) for i in range(2)]
        interb = [zt(128) for i in range(2)]
        STb = [zt(64, BF16) for i in range(2)]
        qkT = [zt(256, BF16, ("p (a t) -> p a t", dict(a=4))) for i in range(2)]
        qtl = [zt(128, BF16, ("p (a t) -> p a t", dict(a=2))) for i in range(2)]
        kwb = [zt(128, BF16) for i in range(2)]
        colsb = zt(24)
        COLS = [Buf(colsb.ap[:, i * 12:(i + 1) * 12]) for i in range(2)]
        Z4BUFS.extend(COLS)
        ytm = [zt(128, BF16) for i in range(2)]
        sqs = zt(128, BF16)
        dsel = zt(64, F32, ("p (b h) -> p b h", dict(b=16)))
        decs = zt(64, F32, ("p (b h) -> p b h", dict(b=16)))
        nT = zt(128, F32, ("p (b h c) -> p b h c", dict(b=16, h=4)))
        nS = zt(128, F32, ("p (b h c) -> p b h c", dict(b=16, h=4)))
        nrow = zt(128)
        nin = zt(256)
        msm = zt(4)
        msout = zt(4)
        mpos = zt(4)
        mvec = zt(4)
        qt32 = zt(256, F32, ("p (a t) -> p a t", dict(a=2)))
        qm = [zt(512, F32, ("p (b t) -> p b t", dict(b=4))) for i in range(2)]
        Cout = [zt(260) for i in range(2)]
        sgt = zt(256)

        o4 = [0]

        def z4(nfl):
            a = o4[0]
            o4[0] += nfl
            assert o4[0] <= 9600, o4[0]
            return Z4[:, a:a + nfl]
        xt = [Buf(z4(D)) for _ in range(2)]
        o4[0] = 0
        utm = Buf(z4(1024))
        uT = Buf(z4(8 * 143).rearrange("p (c t) -> p c t", c=8))
        uTs = Buf(z4(8 * 16 * 23).rearrange("p (c b t) -> p c b t", c=8, b=16))
        sw = [Buf(z4(1472)) for _ in range(2)]
        dTb = Buf(z4(512).bitcast(BF16).rearrange("p (c t) -> p c t", c=8))
        sptm = Buf(z4(1024))
        SET_PL = [utm, uT, uTs, sw[0], sw[1], dTb, sptm]
        o4[0] = 0
        q_tm = Buf(z4(9 * 128).bitcast(BF16).rearrange("p (i f) -> p i f", i=9))
        k_tm = Buf(z4(9 * 128).bitcast(BF16).rearrange("p (i f) -> p i f", i=9))
        sigo = Buf(z4(9 * 128).bitcast(BF16).rearrange("p (i f) -> p i f", i=9))
        v_ext = Buf(z4(9 * 130).bitcast(BF16).rearrange("p (i f) -> p i f", i=9))
        Vmask = Buf(z4(16 * 130).bitcast(BF16).rearrange("p (b f) -> p b f", b=16))
        C32s = [Buf(z4(2 * 2 * HDE).rearrange("p (b c v) -> p b c v", b=2, c=2)) for _ in range(2)]
        SET_HD = [q_tm, k_tm, sigo, v_ext, Vmask, C32s[0], C32s[1]]
        Z4BUFS.extend(xt + SET_PL + SET_HD)
        x_new = [Buf(Z4[:, i * D:(i + 1) * D]) for i in range(9)]
        C32mt = Z2[:, 2400:2400 + NH * 2 * HDE].rearrange("p (h c v) -> p h c v", h=NH, c=2)
        C32m = [Buf(C32mt[:, h, :, :]) for h in range(NH)]
        wb = Buf(Z2[:, 4600:4600 + D])
        silt = [Buf(Z2[:, 6400 + i * 512:6400 + (i + 1) * 512]) for i in range(2)]
        W2 = [Buf(Z3b[:, i * 8192:(i + 1) * 8192].rearrange("p (k n) -> p k n", k=KC)) for i in range(2)]

        def fence(bufs):
            P.op("dve", lambda e: e.memset(fsc.ap, 0.0), writes=[fsc] + list(bufs))

        PJ = [Buf(ps("pj%d" % i, [128, 512], F32)[:, :], excl=True) for i in range(2)]
        PT = [Buf(ps("pt%d" % i, [128, 1024], BF16)[:, :], excl=True) for i in range(2)]
        PS = [Buf(ps("ps%d" % i, [128, 512], F32)[:, :], excl=True) for i in range(2)]
        PC = [Buf(ps("pc%d" % i, [128, 512], F32)[:, :], excl=True) for i in range(2)]

        def act(out_b, in_b, func, bias=0.0, scale=1.0, accum=None, extra_r=(), o=None, i=None):
            oa = out_b.ap if o is None else o
            ia = in_b.ap if i is None else i
            kw = {}
            if accum is not None:
                kw["accum_out"] = accum[1]
            w = [out_b] + ([accum[0]] if accum is not None else [])
            return P.op("act", lambda e: e.activation(out=oa, in_=ia, func=func, bias=bias, scale=scale, **kw),
                        reads=[in_b] + list(extra_r), writes=w)

        def tt(eng, out_b, a_b, b_b, op, o=None, a=None, b=None):
            oa = out_b.ap if o is None else o
            aa = a_b.ap if a is None else a
            ba = b_b.ap if b is None else b
            return P.op(eng, lambda e: e.tensor_tensor(out=oa, in0=aa, in1=ba, op=op), reads=[a_b, b_b], writes=[out_b])

        def ts(eng, out_b, a_b, s1, s2, op0, op1=None, o=None, a=None, extra_r=()):
            oa = out_b.ap if o is None else o
            aa = a_b.ap if a is None else a
            if op1 is None:
                f = lambda e: e.tensor_scalar(out=oa, in0=aa, scalar1=s1, scalar2=None, op0=op0)
            else:
                f = lambda e: e.tensor_scalar(out=oa, in0=aa, scalar1=s1, scalar2=s2, op0=op0, op1=op1)
            return P.op(eng, f, reads=[a_b] + list(extra_r), writes=[out_b])

        def stt(out_b, a_b, sc, b_b, op0, op1, o=None, a=None, b=None, extra_r=()):
            oa = out_b.ap if o is None else o
            aa = a_b.ap if a is None else a
            ba = b_b.ap if b is None else b
            return P.op("dve", lambda e: e.scalar_tensor_tensor(out=oa, in0=aa, scalar=sc, in1=ba, op0=op0, op1=op1),
                        reads=[a_b, b_b] + list(extra_r), writes=[out_b])

        def red(out_b, in_b, o, i):
            return P.op("dve", lambda e: e.tensor_reduce(out=o, in_=i, axis=AX.X, op=ALU.max), reads=[in_b], writes=[out_b])

        def cp(eng, out_b, in_b, o=None, i=None):
            oa = out_b.ap if o is None else o
            ia = in_b.ap if i is None else i
            if eng == "act":
                return P.op("act", lambda e: e.copy(out=oa, in_=ia), reads=[in_b], writes=[out_b])
            return P.op(eng, lambda e: e.tensor_copy(out=oa, in_=ia), reads=[in_b], writes=[out_b])

        def mm(out_b, o, l_b, l, r_b, r, start, stop, signal=None):
            if signal is None:
                signal = stop
            return P.op("pe", lambda e: e.matmul(out=o, lhsT=l, rhs=r, start=start, stop=stop),
                        reads=[l_b, r_b], writes=[out_b], signal=signal)

        def tr(out_b, o, in_b, i, id_b, idap, signal):
            return P.op("pe", lambda e: e.transpose(out=o, in_=i, identity=idap), reads=[in_b, id_b], writes=[out_b], signal=signal)

        def memset(eng, b, val, ap=None):
            a = b.ap if ap is None else ap
            return P.op(eng, lambda e: e.memset(a, val), writes=[b])

        def asel(out_b, in_b, pattern, cmp, base, cm, o=None, i=None):
            oa = out_b.ap if o is None else o
            ia = in_b.ap if i is None else i
            return P.op("pool", lambda e: e.affine_select(out=oa, in_=ia, pattern=pattern, compare_op=cmp, fill=0.0,
                                                          base=base, channel_multiplier=cm), reads=[in_b], writes=[out_b])

        memset("pool", ones32, 1.0)
        asel(ident32, ones32, [[1, 128]], ALU.is_equal, 0, -1)
        asel(tri32, ones32, [[1, 128]], ALU.is_ge, 0, -1)
        asel(tril32, ones32, [[-1, 128]], ALU.is_ge, 0, 1)
        cp("pool", identb, ident32)
        ts("pool", maskneg, tril32, BIG, -BIG, ALU.mult, ALU.add)
        ts("pool", maskposb, tri32, -BIG, BIG, ALU.mult, ALU.add)
        asel(selA, ones32, [[1, 128]], ALU.is_ge, 0, -8, i=ones32.ap[0:16, :])
        asel(sel32, selA, [[-1, 128]], ALU.is_ge, 7, 8)
        asel(sellastT, ones32, [[1, 128]], ALU.is_equal, -7, -8, i=ones32.ap[0:16, :])
        asel(blockA, ones32, [[-8, 16]], ALU.is_ge, 0, 1, i=ones32.ap[:, 0:16])
        asel(blockmask, blockA, [[8, 16]], ALU.is_ge, 7, -1)
        asel(sellast, ones32, [[-8, 16]], ALU.is_equal, -7, 1, i=ones32.ap[:, 0:16])
        memset("pool", selrepA, 1.0)
        asel(selrepA, selrepA, [[-8, 16], [1, 128]], ALU.is_ge, 0, 0)
        asel(selrepA, selrepA, [[8, 16], [-1, 128]], ALU.is_ge, 7, 0)
        SELREP = selrepA
        mm(PS[0], PS[0].ap[:, 0:128], sel32, sel32.ap, sel32, sel32.ap, True, True)
        tt("dve", triB32, tri32, PS[0], ALU.mult, b=PS[0].ap[:, 0:128])
        tt("dve", trilB32, tril32, PS[0], ALU.mult, b=PS[0].ap[:, 0:128])
        ts("dve", masknegB, trilB32, BIG, -BIG, ALU.mult, ALU.add)
        ts("dve", maskposbB, triB32, -BIG, BIG, ALU.mult, ALU.add)
        mm(PS[1], PS[1].ap[:, 0:128], sellastT, sellastT.ap, sel32, sel32.ap, True, True)
        cp("dve", SLm, PS[1], i=PS[1].ap[:, 0:128])

        P.dma("sp", bias8.ap[:, 0:4], b_ig.partition_broadcast(128), writes=[bias8])
        P.dma("sp", bias8.ap[:, 4:8], b_fg.partition_broadcast(128), writes=[bias8])
        P.dma("sp", gcol.ap, mnorm_w.rearrange("(c p) -> p c", p=128), writes=[gcol], slow=True)
        P.dma("sp", pscol.ap, pool_scale.rearrange("(c p) -> p c", p=128), writes=[pscol], slow=True)
        P.dma("sp", flagc.ap, flag_d, writes=[flagc])
        P.dma("sp", wb.ap, norm_mix_w.partition_broadcast(128), writes=[wb])
        P.dma("pool", wg8.ap, w_in_v[:, :, 5120:5128], writes=[wg8])
        P.dma("pool", wpl.ap, w_pool.rearrange("g (c p) e -> p g c e", p=128), writes=[wpl])
        for h in range(NH):
            memset("dve", C32[h], 0.0)
            memset("dve", Cbf[h], 0.0)
        memset("dve", negBc, 0.0)
        memset("dve", Mc, 0.0)

        P.checkpoint(1)
        rr = [0]

        def norm_tile(src_ap, L, dstT, tok0, x_in=None, wbuf=None, wap=None):
            i = rr[0] % 2
            rr[0] += 1
            if x_in is None:
                xb = xt[i]
                P.dma("sp", xb.ap[:L, :], src_ap, writes=[xb])
            else:
                xb = x_in
            ssb = SS[i]
            act(sq, xb, AF.Square, accum=(ssb, ssb.ap[:L, 0:1]), o=sq.ap[:L, :], i=xb.ap[:L, :])
            act(ssb, ssb, AF.Ln, bias=EPS, scale=1.0 / D, o=ssb.ap[:L, 1:2], i=ssb.ap[:L, 0:1])
            act(ssb, ssb, AF.Exp, scale=-0.5, o=ssb.ap[:L, 2:3], i=ssb.ap[:L, 1:2])
            if wbuf is None:
                wbuf, wap = wb, wb.ap
            stt(xnb[i], xb, ssb.ap[:L, 2:3], wbuf, ALU.mult, ALU.mult, o=xnb[i].ap[:L, :], a=xb.ap[:L, :], b=wap[:L, :],
                extra_r=[ssb])
            for half in range(2):
                pt = PT[half]
                for j in range(8):
                    kc = half * 8 + j
                    tr(pt, pt.ap[:, j * 128:j * 128 + L], xnb[i], xnb[i].ap[:L, kc * 128:(kc + 1) * 128],
                       identb, identb.ap[:L, :L], signal=(j == 7))
                src = pt.ap.rearrange("p (j t) -> p j t", j=8)[:, :, 0:L]
                cp("act" if half == 0 else "dve", dstT, pt, o=dstT.ap[:, half * 8:half * 8 + 8, tok0:tok0 + L], i=src)
            return xb

        pjr = [0]

        def proj(xT, tok0, L, Wb, Wap, ncols):
            pj = PJ[pjr[0] % 2]
            pjr[0] += 1
            for kc in range(KC):
                mm(pj, pj.ap[:L, 0:ncols], xT, xT.ap[:, kc, tok0:tok0 + L], Wb, Wap[:, kc, 0:ncols], kc == 0, kc == KC - 1)
            return pj

        K_IG, K_Z, K_E, K_SP, K_NEGB, K_A, K_M, K_NEGM, K_EM, K_MPREV, K_MEND, K_W, K_DEC, K_T1, K_CMX, K_T2, K_AADJ, K_CME, K_INTER = range(19)

        def gate_prep(xT, tok0, L, g, mode):
            G = GW[g]

            def k(kind, rows=L):
                return G.ap[:rows, kind, :]
            pj = proj(xT, tok0, L, wg8, wg8.ap, 8)
            tt("dve", G, pj, bias8, ALU.add, o=G.ap[:L, 0:2, :].rearrange("p a h -> p (a h)"), a=pj.ap[:L, 0:8], b=bias8.ap[:L, :])
            act(G, G, AF.Exp, scale=-1.0, o=k(K_E), i=k(K_Z))
            act(G, G, AF.Ln, bias=1.0, o=k(K_SP), i=k(K_E))
            TRI = tri32 if mode == "p" else triB32
            MN = maskneg if mode == "p" else masknegB
            p0 = PS[0]
            mm(p0, p0.ap[:L, 0:4], TRI, TRI.ap[:L, :L], G, k(K_SP), True, True)
            if mode == "p":
                mm(p0, p0.ap[:, 4:8], ones32, ones32.ap[:L, :], G, k(K_SP), True, True)
                tt("dve", G, p0, negBc, ALU.add, o=k(K_NEGB), a=p0.ap[:L, 0:4], b=negBc.ap[:L, :])
            else:
                cp("dve", G, p0, o=k(K_NEGB), i=p0.ap[:L, 0:4])
                mm(p0, p0.ap[:, 8:12], sel32, sel32.ap, msm, msm.ap[0:16, :], True, True)
                cp("dve", G, p0, o=k(K_MPREV, 128), i=p0.ap[:, 8:12])
            tt("dve", G, G, G, ALU.add, o=k(K_A), a=k(K_IG), b=k(K_NEGB))
            ts("dve", G, G, -LN16, None, ALU.add, o=k(K_AADJ), a=k(K_A))
            tt("dve", dg, ident32, G, ALU.mult, o=dg.ap[:L, :, :L],
               a=ident32.ap[:L, :L].unsqueeze(1).to_broadcast([L, 4, L]),
               b=k(K_A).unsqueeze(2).to_broadcast([L, 4, L]))
            p1 = PS[1]
            for h in range(NH):
                mm(p1, p1.ap[:, h * 128:h * 128 + L], ones32, ones32.ap[:L, :], dg, dg.ap[:L, h, :L], True, True, signal=(h == NH - 1))
            Ab = p1.ap.rearrange("p (h s) -> p h s", h=4)
            tt("dve", tmpm, p1, MN, ALU.add, o=tmpm.ap[:L, :, :L], a=Ab[:L, :, :L],
               b=MN.ap[:L, :L].unsqueeze(1).to_broadcast([L, 4, L]))
            red(G, tmpm, k(K_CMX), tmpm.ap[:L, :, :L])
            if mode == "p":
                cp("dve", G, Mc, o=k(K_MPREV, 128), i=Mc.ap)
                tt("dve", G, G, Mc, ALU.max, o=k(K_M), a=k(K_CMX), b=Mc.ap[:L, :])
                red(G, p1, k(K_CME, 128), Ab[:, :, :L])
                tt("dve", G, G, Mc, ALU.max, o=k(K_MEND, 128), a=k(K_CME, 128), b=Mc.ap)
            else:
                tt("dve", G, G, G, ALU.max, o=k(K_M), a=k(K_CMX), b=k(K_MPREV))
                mm(p0, p0.ap[:, 12:16], SLm, SLm.ap, G, k(K_M), True, True)
                cp("dve", G, p0, o=k(K_MEND, 128), i=p0.ap[:, 12:16])
            tt("dve", G, G, G, ALU.subtract, o=k(K_NEGM), a=k(K_NEGB), b=k(K_M))
            tt("dve", G, G, G, ALU.subtract, o=k(K_INTER), a=k(K_MPREV), b=k(K_M))
            act(G, G, AF.Exp, o=k(K_INTER), i=k(K_INTER))
            act(G, G, AF.Exp, o=k(K_EM), i=k(K_NEGM))
            tt("dve", G, G, G, ALU.subtract, o=k(K_T1), a=k(K_AADJ), b=k(K_MEND))
            act(G, G, AF.Exp, o=k(K_W), i=k(K_T1))
            tt("dve", G, G, G, ALU.subtract, o=k(K_T2, 128), a=k(K_MPREV, 128), b=k(K_MEND, 128))
            act(G, G, AF.Exp, o=k(K_DEC, 128), i=k(K_T2, 128))
            if mode == "p":
                cp("dve", Mc, G, i=k(K_MEND, 128))
                tt("dve", negBc, negBc, p0, ALU.add, b=p0.ap[:, 4:8])
            else:
                tt("dve", dsel, sellast, G, ALU.mult, a=sellast.ap.unsqueeze(2).to_broadcast([128, 16, 4]),
                   b=k(K_T2, 128).unsqueeze(1).to_broadcast([128, 16, 4]))
                mm(p1, p1.ap[:, 0:64], ones32, ones32.ap, dsel, dsel.ap.rearrange("p b h -> p (b h)"), True, True)
                act(decs, p1, AF.Exp, o=decs.ap.rearrange("p b h -> p (b h)"), i=p1.ap[:, 0:64])
                ts("dve", mpos, G, -1.0, None, ALU.mult, a=k(K_NEGM))
                mm(p0, p0.ap[0:16, 16:20], sellast, sellast.ap, mpos, mpos.ap, True, True)
                cp("dve", msout, p0, o=msout.ap[0:16, :], i=p0.ap[0:16, 16:20])
                P.dma("sp", m_s, msout.ap[0:16, :], reads=[msout])

        cr = [0]

        def state_update(h, g, L, ktm_ap, vext_ap, k_b, v_b):
            G = GW[g]
            i = cr[0] % 2
            kw = kwb[i]
            ts("dve", kw, k_b, G.ap[:L, K_W, h:h + 1], None, ALU.mult, o=kw.ap[:L, :], a=ktm_ap, extra_r=[G])
            for dc in range(2):
                mm(PC[dc], PC[dc].ap[:, 0:HDE], kw, kw.ap[:L, dc * 128:(dc + 1) * 128], v_b, vext_ap, True, True)
            for dc in range(2):
                stt(C32[h], C32[h], G.ap[:, K_DEC, h:h + 1], PC[dc], ALU.mult, ALU.add,
                    o=C32[h].ap[:, dc, :], a=C32[h].ap[:, dc, :], b=PC[dc].ap[:, 0:HDE], extra_r=[G])
            cp("act", Cbf[h], C32[h])

        xnTp = Buf(Z1b[:, 0:KC * 1040].rearrange("p (k t) -> p k t", k=KC))
        kvp = Buf(Z2b[:, 0:9 * 516].rearrange("p (i f) -> p i f", i=9))
        pre_tiles = [(x_meta, 16, 1024)] + [(x_pre[i * 128:(i + 1) * 128, :], 128, i * 128) for i in range(8)]
        for (src, L, tok0) in pre_tiles:
            norm_tile(src, L, xnTp, tok0)
        P.checkpoint(2)
        for gi, (src, L, tok0) in enumerate(pre_tiles):
            gate_prep(xnTp, tok0, L, gi, "p")
            if gi == 0:
                cp("dve", negBm, negBc)
                cp("dve", Mm, Mc)
        P.checkpoint(3)
        memset("dve", kvp, 1.0)
        WA = W2
        for h in range(NH):
            W = WA[h % 2]
            P.dma("pool", W.ap[:, :, 0:256], w_in_v[:, :, 2048 + h * 256:2048 + (h + 1) * 256], writes=[W])
            P.dma("pool", W.ap[:, :, 256:512], w_in_v[:, :, 3072 + h * 256:3072 + (h + 1) * 256], writes=[W])
            for gi, (src, L, tok0) in enumerate(pre_tiles):
                pj = proj(xnTp, tok0, L, W, W.ap, 512)
                cp("act", kvp, pj, o=kvp.ap[:L, gi, 0:512].rearrange("p (a f) -> p a f", a=2)[:, :, 0:256]
                   if False else kvp.ap[:L, gi, 0:256], i=pj.ap[:L, 0:256])
                cp("dve", kvp, pj, o=kvp.ap[:L, gi, 256:512], i=pj.ap[:L, 256:512])
            for gi, (src, L, tok0) in enumerate(pre_tiles):
                state_update(h, gi, L, kvp.ap[:L, gi, 0:256], kvp.ap[:L, gi, 256:513], kvp, kvp)
                if gi == 0:
                    cp("dve", C32m[h], C32[h])
        for h in range(NH):
            tt("dve", C32[h], C32[h], C32m[h], ALU.subtract)
            stt(C32[h], C32[h], flagc.ap[:, 0:1], C32m[h], ALU.mult, ALU.add, extra_r=[flagc])
            cp("act", Cbf[h], C32[h])
        for (cur, sav) in ((negBc, negBm), (Mc, Mm)):
            tt("dve", cur, cur, sav, ALU.subtract)
            stt(cur, cur, flagc.ap[:, 0:1], sav, ALU.mult, ALU.add, extra_r=[flagc])

        P.checkpoint(4)
        xnT = Buf(Z1b[:, 0:KC * 1168].rearrange("p (k t) -> p k t", k=KC))
        fence([xnTp, xnT])
        yT = Buf(Z2b[:, 0:KC * 1152].rearrange("p (k t) -> p k t", k=KC))
        main_tiles = [(x_own[i * 128:(i + 1) * 128, :], 128, i * 128) for i in range(8)] + [(x_smp, 128, 1024)]
        for (src, L, tok0) in main_tiles + [(x_halo, 16, 1152)]:
            norm_tile(src, L, xnT, tok0)
        P.dma("sp", msm.ap[0:16, :], st_m, writes=[msm])
        P.dma("sp", nin.ap[0:64, :], st_n, writes=[nin])
        for dc in range(2):
            tr(PS[0], PS[0].ap[:, dc * 64:(dc + 1) * 64], nin, nin.ap[0:64, dc * 128:(dc + 1) * 128], ident32, ident32.ap[0:64, 0:64], signal=(dc == 1))
        cp("dve", nT, PS[0], o=nT.ap.rearrange("p b h c -> p c (b h)"), i=PS[0].ap[:, 0:128].rearrange("p (c x) -> p c x", c=2))
        for gi, (src, L, tok0) in enumerate(main_tiles):
            gate_prep(xnT, tok0, L, 9 + gi, "p" if gi < 8 else "s")
        tt("dve", mvec, Mc, negBc, ALU.subtract)
        P.dma("sp", m_p, mvec.ap[0:1, :], reads=[mvec])

        P.checkpoint(5)
        WU = W2
        fence([kvp, wb, yT] + C32m + xt + SET_PL)
        for half in range(2):
            P.dma("pool", WU[half].ap, w_in_v[:, :, half * 512:(half + 1) * 512], writes=[WU[half]])
        sp_rows = st_pool.rearrange("b j c -> (b j) c")
        for blk, (r0, nr) in enumerate(((0, 128), (128, 112))):
            P.dma("sp", sptm.ap[0:nr, :], sp_rows[r0:r0 + nr, :], writes=[sptm])
            for cc in range(8):
                p = PS[cc % 2]
                tr(p, p.ap[:, 0:nr], sptm, sptm.ap[0:nr, cc * 128:(cc + 1) * 128], ident32, ident32.ap[0:nr, 0:nr], signal=True)
                eng = "dve" if cc % 2 else "act"
                if blk == 0:
                    cp(eng, uTs, p, o=uTs.ap[:, cc, 0:8, 0:15], i=p.ap[:, 0:120].rearrange("p (b j) -> p b j", b=8))
                    cp(eng, uTs, p, o=uTs.ap[:, cc, 8, 0:8], i=p.ap[:, 120:128])
                else:
                    cp(eng, uTs, p, o=uTs.ap[:, cc, 8, 8:15], i=p.ap[:, 0:7])
                    cp(eng, uTs, p, o=uTs.ap[:, cc, 9:16, 0:15], i=p.ap[:, 7:112].rearrange("p (b j) -> p b j", b=7))

        def pool_group(U, shp, g, o_ap):
            nd = len(shp)
            T = shp[-1]

            def v(b, t0, t1):
                if nd == 1:
                    return b[:, :, t0:t1]
                return b[:, :, :, t0:t1]
            if nd == 1:
                sv = [sw[i].ap[:, 0:2 * T].rearrange("p (c t) -> p c t", c=2) for i in range(2)]
            else:
                sv = [sw[i].ap[:, 0:2 * shp[0] * T].rearrange("p (c b t) -> p c b t", c=2, b=shp[0]) for i in range(2)]
            cur_b, cur = U, (U.ap[:, 2 * g:2 * g + 2, :] if nd == 1 else U.ap[:, 2 * g:2 * g + 2, :, :])
            for k in range(g + 1):
                sh = 1 << k
                lo = 2 * sh - 1
                nb, nv = sw[k % 2], sv[k % 2]
                tt("dve", nb, cur_b, cur_b, ALU.add, o=v(nv, lo, T), a=v(cur, lo, T), b=v(cur, lo - sh, T - sh))
                cur_b, cur = nb, nv
            wdw = 2 << g
            uv = U.ap[:, 2 * g:2 * g + 2, :] if nd == 1 else U.ap[:, 2 * g:2 * g + 2, :, :]
            stt(dTb, cur_b, 1.0 / wdw, U, ALU.mult, ALU.subtract, o=o_ap, a=v(cur, 15, T), b=v(uv, 15, T))

        def pool_tile(tok0, L, ytok0, hist_mode):
            for half in range(2):
                pj = proj(xnT, tok0, L, WU[half], WU[half].ap, 512)
                cp("act" if half == 0 else "dve", utm, pj, o=utm.ap[:L, half * 512:(half + 1) * 512], i=pj.ap[:L, :])
            if hist_mode == "smp":
                for b in range(16):
                    P.dma("sp", pool_s[b, 7:15, :], utm.ap[b * 8:(b + 1) * 8, :], reads=[utm])
                    P.dma("sp", pool_s[b, 0:7, :], st_pool[b, 8:15, :])
            if hist_mode == "own" and tok0 == 7 * 128:
                P.dma("sp", pool_p, utm.ap[113:128, :], reads=[utm])
            for cc in range(8):
                p = PS[cc % 2]
                tr(p, p.ap[:, 0:L], utm, utm.ap[:L, cc * 128:(cc + 1) * 128], ident32, ident32.ap[:L, :L], signal=True)
                if hist_mode == "halo":
                    cp("dve" if cc % 2 else "act", uT, p, o=uT.ap[:, cc, 0:15], i=p.ap[:, 1:16])
                elif hist_mode == "own":
                    cp("dve" if cc % 2 else "act", uT, p, o=uT.ap[:, cc, 15:143], i=p.ap[:, 0:128])
                else:
                    cp("dve" if cc % 2 else "act", uTs, p, o=uTs.ap[:, cc, :, 15:23], i=p.ap[:, 0:128].rearrange("p (b t) -> p b t", b=16))
            if hist_mode == "halo":
                return
            if hist_mode == "own":
                U, shp, hcol = uT, (143,), 15
            else:
                U, shp, hcol = uTs, (16, 23), 15
            for g in range(4):
                if hist_mode == "own":
                    o_ap = dTb.ap[:, 2 * g:2 * g + 2, :]
                else:
                    o_ap = dTb.ap[:, 2 * g:2 * g + 2, :].rearrange("p c (b t) -> p c b t", b=16)
                pool_group(U, shp, g, o_ap)
            for g in range(4):
                for ec in range(2):
                    p = PC[ec]
                    for c in range(2):
                        mm(p, p.ap[:, 0:128], wpl, wpl.ap[:, g, c, ec * 128:(ec + 1) * 128], dTb, dTb.ap[:, 2 * g + c, :], c == 0, c == 1)
                    ch = 2 * g + ec
                    ts("dve", yT, p, pscol.ap[:, ch:ch + 1], None, ALU.mult, o=yT.ap[:, ch, ytok0:ytok0 + 128], a=p.ap[:, 0:128], extra_r=[pscol])
            if hist_mode == "own":
                cp("dve", sw[0], uT, o=sw[0].ap[:, 0:120].rearrange("p (c t) -> p c t", c=8), i=uT.ap[:, :, 128:143])
                cp("dve", uT, sw[0], o=uT.ap[:, :, 0:15], i=sw[0].ap[:, 0:120].rearrange("p (c t) -> p c t", c=8))

        pool_tile(1152, 16, 0, "halo")
        for i in range(8):
            pool_tile(i * 128, 128, i * 128, "own")
        pool_tile(1024, 128, 1024, "smp")

        P.checkpoint(6)
        chn = [0]

        def chunk(h, ti, g, mode):
            G = GW[g]
            par = chn[0] % 2
            chn[0] += 1
            L = 128
            tok0 = ti * 128
            MP = maskposb if mode == "p" else maskposbB
            pt = PT[par]
            for dc in range(2):
                tr(pt, pt.ap[:, dc * 128:(dc + 1) * 128], k_tm, k_tm.ap[:, ti, dc * 128:(dc + 1) * 128], identb, identb.ap, signal=False)
            for dc in range(2):
                tr(pt, pt.ap[:, (2 + dc) * 128:(3 + dc) * 128], q_tm, q_tm.ap[:, ti, dc * 128:(dc + 1) * 128], identb, identb.ap, signal=(dc == 1))
            cp("act", qkT[par], pt, o=qkT[par].ap.rearrange("p a t -> p (a t)"), i=pt.ap[:, 0:512])
            ts("dve", dgM[par], ident32, G.ap[:, K_M, h:h + 1], None, ALU.mult, extra_r=[G])
            p = PS[par]
            mm(p, p.ap[:, 0:128], ones32, ones32.ap, dgM[par], dgM[par].ap, True, True)
            mm(p, p.ap[:, 128:256], ones32, ones32.ap, dgM[par], dgM[par].ap, True, False, signal=False)
            mm(p, p.ap[:, 128:256], identb, identb.ap, MP, MP.ap, False, True)
            act(DT[par], p, AF.Exp, bias=G.ap[:, K_AADJ, h:h + 1], scale=-1.0, i=p.ap[:, 128:256], extra_r=[G])
            if mode == "p":
                act(interb[par], p, AF.Exp, bias=G.ap[:, K_MPREV, h:h + 1], scale=-1.0, i=p.ap[:, 0:128], extra_r=[G])
            else:
                ts("dve", dgM[par], ident32, G.ap[:, K_INTER, h:h + 1], None, ALU.mult, extra_r=[G])
                mm(p, p.ap[:, 384:512], ones32, ones32.ap, dgM[par], dgM[par].ap, True, True)
                cp("act", interb[par], p, i=p.ap[:, 384:512])
            for dc in range(2):
                mm(p, p.ap[:, 256:384], qkT[par], qkT[par].ap[:, dc, :], qkT[par], qkT[par].ap[:, 2 + dc, :], dc == 0, dc == 1)
            tt("dve", STb[par], p, DT[par], ALU.mult, a=p.ap[:, 256:384])
            pn = PJ[par]
            if mode == "p":
                tt("dve", qtl[par], qkT[par], interb[par], ALU.mult, a=qkT[par].ap[:, 2:4, :],
                   b=interb[par].ap.unsqueeze(1).to_broadcast([128, 2, 128]))
                mm(pn, pn.ap[:, 0:HDE], STb[par], STb[par].ap, v_ext, v_ext.ap[:, ti, 0:HDE], True, False, signal=False)
                for dc in range(2):
                    mm(pn, pn.ap[:, 0:HDE], qtl[par], qtl[par].ap[:, dc, :], Cbf[h], Cbf[h].ap[:, dc, :], False, dc == 1)
            else:
                tt("dve", qt32, qkT[par], interb[par], ALU.mult, a=qkT[par].ap[:, 2:4, :],
                   b=interb[par].ap.unsqueeze(1).to_broadcast([128, 2, 128]))
                ts("dve", kwb[par], k_tm, G.ap[:, K_W, h:h + 1], None, ALU.mult, a=k_tm.ap[:, ti, :], extra_r=[G])
                tt("dve", Vmask, v_ext, blockmask, ALU.mult, o=Vmask.ap[:, :, 0:HDE],
                   a=v_ext.ap[:, ti, 0:HDE].unsqueeze(1).to_broadcast([128, 16, HDE]),
                   b=blockmask.ap.unsqueeze(2).to_broadcast([128, 16, HDE]))
                mm(pn, pn.ap[:, 0:HDE], STb[par], STb[par].ap, v_ext, v_ext.ap[:, ti, 0:HDE], True, False, signal=False)
                for grp in range(8):
                    cs_ = C32s[grp % 2]
                    for bl in range(2):
                        P.dma("sp", cs_.ap[:, bl, :, 0:HD], st_C[grp * 2 + bl, h].rearrange("(c p) v -> p c v", p=128), writes=[cs_])
                    cp("dve", cs_, nT, o=cs_.ap[:, :, :, HD], i=nT.ap[:, grp * 2:(grp + 1) * 2, h, :])
                    qmb = qm[grp % 2]
                    for dc in range(2):
                        tt("dve", qmb, qt32, SELREP, ALU.mult, o=qmb.ap[:, dc * 2:dc * 2 + 2, :],
                           a=qt32.ap[:, dc, :].unsqueeze(1).to_broadcast([128, 2, 128]), b=SELREP.ap[:, grp * 2:(grp + 1) * 2, :])
                    for bl in range(2):
                        for dc in range(2):
                            last = (grp == 7 and bl == 1 and dc == 1)
                            mm(pn, pn.ap[:, 0:HDE], qmb, qmb.ap[:, dc * 2 + bl, :], cs_, cs_.ap[:, bl, dc, :], False, last,
                               signal=(last or (bl == 1 and dc == 1)))
                    for bl in range(2):
                        b = grp * 2 + bl
                        for dc in range(2):
                            pc = PC[dc]
                            co = Cout[dc]
                            mm(pc, pc.ap[:, 0:HDE], kwb[par], kwb[par].ap[:, dc * 128:(dc + 1) * 128], Vmask, Vmask.ap[:, b, 0:HDE], True, True)
                            stt(co, cs_, decs.ap[:, b, h:h + 1], pc, ALU.mult, ALU.add, o=co.ap[:, 0:HDE], a=cs_.ap[:, bl, dc, :], b=pc.ap[:, 0:HDE], extra_r=[decs])
                            P.dma("sp", C_s[b, h, dc * 128:(dc + 1) * 128, :], co.ap[:, 0:HD], reads=[co])
                            cp("act", nS, co, o=nS.ap[:, b, h, dc:dc + 1], i=co.ap[:, HD:HDE])
            cl = COLS[par]
            act(sqs, pn, AF.Square, accum=(cl, cl.ap[:, 0:1]), o=sqs.ap[:, 0:HD], i=pn.ap[:, 0:HD])
            ts("dve", cl, pn, -1.0, None, ALU.mult, o=cl.ap[:, 8:9], a=pn.ap[:, HD:HDE])
            tt("dve", cl, pn, cl, ALU.max, o=cl.ap[:, 9:10], a=pn.ap[:, HD:HDE], b=cl.ap[:, 8:9])
            tt("dve", cl, cl, G, ALU.max, o=cl.ap[:, 1:2], a=cl.ap[:, 9:10], b=G.ap[:, K_EM, h:h + 1])
            P.op("dve", lambda e, o=cl.ap[:, 2:3], i=cl.ap[:, 1:2]: e.reciprocal(out=o, in_=i), reads=[cl], writes=[cl])
            tt("dve", cl, cl, cl, ALU.mult, o=cl.ap[:, 3:4], a=cl.ap[:, 2:3], b=cl.ap[:, 2:3])
            tt("dve", cl, cl, cl, ALU.mult, o=cl.ap[:, 4:5], a=cl.ap[:, 3:4], b=cl.ap[:, 0:1])
            act(cl, cl, AF.Ln, bias=EPS, scale=1.0 / HD, o=cl.ap[:, 5:6], i=cl.ap[:, 4:5])
            act(cl, cl, AF.Exp, scale=-0.5, o=cl.ap[:, 6:7], i=cl.ap[:, 5:6])
            tt("dve", cl, cl, cl, ALU.mult, o=cl.ap[:, 7:8], a=cl.ap[:, 6:7], b=cl.ap[:, 2:3])
            yb = ytm[par]
            stt(yb, pn, cl.ap[:, 7:8], sigo, ALU.mult, ALU.mult, o=yb.ap[:, 0:HD], a=pn.ap[:, 0:HD],
                b=sigo.ap[:, ti, :], extra_r=[cl])
            pt2 = PT[par]
            for c in range(2):
                tr(pt2, pt2.ap[:, 512 + c * 128:512 + (c + 1) * 128], yb, yb.ap[:, c * 128:(c + 1) * 128], identb, identb.ap, signal=(c == 1))
            for c in range(2):
                ch = 2 * h + c
                ts("dve", yT, pt2, gcol.ap[:, ch:ch + 1], None, ALU.mult, o=yT.ap[:, 8 + ch, ti * 128:(ti + 1) * 128],
                   a=pt2.ap[:, 512 + c * 128:512 + (c + 1) * 128], extra_r=[gcol])
            if mode == "p":
                state_update(h, g, L, k_tm.ap[:, ti, :], v_ext.ap[:, ti, 0:HDE], k_tm, v_ext)

        WH = W2
        fence(SET_PL + SET_HD)
        P.checkpoint(6.01)
        memset("dve", v_ext, 1.0)
        memset("dve", Vmask, 0.0)
        P.checkpoint(6.02)
        for h in range(NH):
            for half in range(2):
                for j in range(2):
                    c0 = 1024 + (half * 2 + j) * 1024 + h * 256
                    P.dma("pool", WH[half].ap[:, :, j * 256:(j + 1) * 256], w_in_v[:, :, c0:c0 + 256], writes=[WH[half]])
            if h == 0:
                P.checkpoint(6.03)
            for ti, (src, L, tok0) in enumerate(main_tiles):
                if h == 0 and ti == 1:
                    P.checkpoint(6.04)
                pj = proj(xnT, tok0, L, WH[0], WH[0].ap, 512)
                if h == 0 and ti == 0:
                    P.checkpoint(6.031)
                cp("act", q_tm, pj, o=q_tm.ap[:, ti, :], i=pj.ap[:, 0:256])
                if h == 0 and ti == 0:
                    P.checkpoint(6.032)
                cp("dve", k_tm, pj, o=k_tm.ap[:, ti, :], i=pj.ap[:, 256:512])
                if h == 0 and ti == 0:
                    P.checkpoint(6.033)
                pj = proj(xnT, tok0, L, WH[1], WH[1].ap, 512)
                cp("dve", v_ext, pj, o=v_ext.ap[:, ti, 0:HD], i=pj.ap[:, 0:256])
                if h == 0 and ti == 0:
                    P.checkpoint(6.035)
                act(sgt, pj, AF.Exp, scale=-1.0, i=pj.ap[:, 256:512])
                ts("dve", sgt, sgt, 1.0, None, ALU.add)
                P.op("dve", lambda e, o=sgt.ap, i=sgt.ap: e.reciprocal(out=o, in_=i), reads=[sgt], writes=[sgt])
                cp("dve", sigo, sgt, o=sigo.ap[:, ti, :])
            if h == 0:
                P.checkpoint(6.1)
            for ti in range(9):
                chunk(h, ti, 9 + ti, "p" if ti < 8 else "s")
                if h == 0 and ti == 0:
                    P.checkpoint(6.2)
                if h == 0 and ti == 7:
                    P.checkpoint(6.3)
                if h == 0 and ti == 8:
                    P.checkpoint(6.4)
        P.checkpoint(7)
        for h in range(NH):
            for dc in range(2):
                P.dma("sp", C_p[h, dc * 128:(dc + 1) * 128, :], C32[h].ap[:, dc, 0:HD], reads=[C32[h]])
                P.dma("sp", n_p[h:h + 1, dc * 128:(dc + 1) * 128].rearrange("o p -> p o"), C32[h].ap[:, dc, HD:HDE], reads=[C32[h]], slow=True)
        P.checkpoint(7.5)
        tr(PS[0], PS[0].ap[:, 0:128], nS, nS.ap.rearrange("p b h c -> p (b h c)"), ident32, ident32.ap, signal=True)
        cp("dve", nrow, PS[0], i=PS[0].ap[:, 0:128])
        P.dma("sp", n_s, nrow.ap, reads=[nrow])

        P.checkpoint(8)
        all_src = [x_own[i * 128:(i + 1) * 128, :] for i in range(8)] + [x_smp]
        z4_users = Z4BUFS
        for i in range(9):
            P.dma("sp", x_new[i].ap, all_src[i], reads=[], writes=[x_new[i]] + z4_users)
        WO = [W2[0], W2[0]]
        for cg in range(4):
            W = WO[cg % 2]
            P.dma("pool", W.ap, w_out_v[:, :, cg * 512:(cg + 1) * 512], writes=[W])
            for i in range(9):
                pj = proj(yT, i * 128, 128, W, W.ap, 512)
                tt("dve", x_new[i], x_new[i], pj, ALU.add, o=x_new[i].ap[:, cg * 512:(cg + 1) * 512],
                   a=x_new[i].ap[:, cg * 512:(cg + 1) * 512], b=pj.ap[:, 0:512])
        xn2T = Buf(Z1b[:, 0:KC * 1152].rearrange("p (k t) -> p k t", k=KC))
        fence([xnT, xn2T])
        P.dma("sp", Z3[:, 4096:4096 + D], norm_ffn_w.partition_broadcast(128), writes=[W2[1]])
        for i in range(9):
            norm_tile(None, 128, xn2T, i * 128, x_in=x_new[i], wbuf=W2[1], wap=Z3[:, 4096:4096 + D])

        P.checkpoint(9)
        hT = Buf(Z2b[:, 0:11 * 1152].rearrange("p (f t) -> p f t", f=11))
        WG = [Buf(Z3b[:, i * 2048:(i + 1) * 2048].rearrange("p (k n) -> p k n", k=KC)) for i in range(2)]
        WUp = [Buf(Z3b[:, 4096 + i * 2048:4096 + (i + 1) * 2048].rearrange("p (k n) -> p k n", k=KC)) for i in range(2)]
        WD = [Buf(Z3b[:, 8192 + i * 2816:8192 + (i + 1) * 2816].rearrange("p (f n) -> p f n", f=11)) for i in range(2)]
        fence([W2[0], W2[1], yT, hT] + WG + WUp + WD + silt)
        tgs = [(0, 512), (512, 512), (1024, 128)]
        fr = [0]
        for qd in range(4):
            for fl in range(11):
                fc = qd * 11 + fl
                wgb = WG[fr[0] % 2]
                wub = WUp[fr[0] % 2]
                fr[0] += 1
                P.dma("pool", wgb.ap, w_gate_v[:, :, fc * 128:(fc + 1) * 128], writes=[wgb])
                P.dma("pool", wub.ap, w_up_v[:, :, fc * 128:(fc + 1) * 128], writes=[wub])
                for ti, (t0, tn) in enumerate(tgs):
                    pg = PJ[ti % 2]
                    pu = PS[ti % 2]
                    for kc in range(KC):
                        mm(pg, pg.ap[:, 0:tn], wgb, wgb.ap[:, kc, :], xn2T, xn2T.ap[:, kc, t0:t0 + tn], kc == 0, kc == KC - 1)
                    for kc in range(KC):
                        mm(pu, pu.ap[:, 0:tn], wub, wub.ap[:, kc, :], xn2T, xn2T.ap[:, kc, t0:t0 + tn], kc == 0, kc == KC - 1)
                    st_ = silt[ti % 2]
                    act(st_, pg, AF.Silu, o=st_.ap[:, 0:tn], i=pg.ap[:, 0:tn])
                    tt("dve", hT, st_, pu, ALU.mult, o=hT.ap[:, fl, t0:t0 + tn], a=st_.ap[:, 0:tn], b=pu.ap[:, 0:tn])
            for cg in range(8):
                W = WD[cg % 2]
                P.dma("pool", W.ap, w_down_v[:, qd * 11:(qd + 1) * 11, cg * 256:(cg + 1) * 256], writes=[W])
                for i in range(9):
                    pc = PC[i % 2]
                    for fl in range(11):
                        mm(pc, pc.ap[:, 0:256], hT, hT.ap[:, fl, i * 128:(i + 1) * 128], W, W.ap[:, fl, :], fl == 0, fl == 10)
                    tt("dve", x_new[i], x_new[i], pc, ALU.add, o=x_new[i].ap[:, cg * 256:(cg + 1) * 256],
                       a=x_new[i].ap[:, cg * 256:(cg + 1) * 256], b=pc.ap[:, 0:256])

        P.checkpoint(10)
        wbE = Buf(Z1[:, 0:D])
        fence([xn2T, wbE])
        P.dma("sp", wbE.ap, norm_final_w.partition_broadcast(128), writes=[wbE])
        for i in range(9):
            ssb = SS[i % 2]
            xb = x_new[i]
            act(sq, xb, AF.Square, accum=(ssb, ssb.ap[:, 0:1]))
            act(ssb, ssb, AF.Ln, bias=EPS, scale=1.0 / D, o=ssb.ap[:, 1:2], i=ssb.ap[:, 0:1])
            act(ssb, ssb, AF.Exp, scale=-0.5, o=ssb.ap[:, 2:3], i=ssb.ap[:, 1:2])
            stt(xb, xb, ssb.ap[:, 2:3], wbE, ALU.mult, ALU.mult, extra_r=[ssb])
            dst = y_own[i * 128:(i + 1) * 128, :] if i < 8 else y_smp
            P.dma("sp", dst, xb.ap, reads=[xb])

        sem_es = ExitStack()
        with sem_es:
            sems = {}
            for e in ENGS:
                sems[e] = sem_es.enter_context(nc.semaphore("s_" + e))
            for q, n in P.ndma.items():
                for i in range(n):
                    sems[(q, i)] = sem_es.enter_context(nc.semaphore("d_%s%d" % (q, i)))
            with nc.Block() as block:
                @block.tensor
                def _(e):
                    P.replay("pe", e, sems)

                @block.scalar
                def _(e):
                    P.replay("act", e, sems)

                @block.vector
                def _(e):
                    P.replay("dve", e, sems)

                @block.gpsimd
                def _(e):
                    P.replay("pool", e, sems)
                    P.final_waits("pool", e, sems)

                @block.sync
                def _(e):
                    P.replay("sp", e, sems)
                    P.final_waits("sp", e, sems)
    return nc


_NC = None


def _get_nc():
    global _NC
    if _NC is None:
        _NC = build_nc()
    return _NC


def kernel(x_prompt, x_sample, state_pool, state_mlstm_C, state_mlstm_n, state_mlstm_m, meta_tokens, norm_mix_w, w_in,
           b_igate, b_fgate, w_pool, pool_scale, mlstm_norm_w, w_out, norm_ffn_w, w_gate, w_up, w_down, norm_final_w):
    f = lambda a: np.ascontiguousarray(np.asarray(a, dtype=np.float32))
    xp, xs = f(x_prompt), f(x_sample)
    meta = f(meta_tokens)
    shared = {
        "norm_mix_w": f(norm_mix_w)[0], "w_in": f(w_in)[0], "b_igate": f(b_igate)[0], "b_fgate": f(b_fgate)[0],
        "w_pool": f(w_pool)[0], "pool_scale": f(pool_scale)[0], "mlstm_norm_w": f(mlstm_norm_w)[0], "w_out": f(w_out)[0],
        "norm_ffn_w": f(norm_ffn_w)[0], "w_gate": f(w_gate)[0], "w_up": f(w_up)[0], "w_down": f(w_down)[0],
        "norm_final_w": f(norm_final_w),
    }
    sp, sC, sn, sm = f(state_pool)[0], f(state_mlstm_C)[0], f(state_mlstm_n)[0], f(state_mlstm_m)[0]
    in_maps = []
    for c in range(8):
        s, h = c // 2, c % 2
        m = dict(shared)
        m["x_meta"] = meta
        m["x_pre"] = xp[s, 0:1024] if h == 1 else np.zeros((1024, D), np.float32)
        m["x_own"] = np.ascontiguousarray(xp[s, 1024 * h:1024 * (h + 1)])
        m["x_halo"] = meta if h == 0 else np.ascontiguousarray(xp[s, 1008:1024])
        m["x_smp"] = np.ascontiguousarray(xs[16 * c:16 * (c + 1)].reshape(128, D))
        m["flag"] = np.full((128, 1), float(h), np.float32)
        m["st_pool"] = np.ascontiguousarray(sp[16 * c:16 * (c + 1)])
        m["st_C"] = np.ascontiguousarray(sC[16 * c:16 * (c + 1)])
        m["st_n"] = np.ascontiguousarray(sn[16 * c:16 * (c + 1)].reshape(64, HD))
        m["st_m"] = np.ascontiguousarray(sm[16 * c:16 * (c + 1)])
        in_maps.append(m)
    nc = _get_nc()
    res = run_bass_kernel_spmd(nc, in_maps, core_ids=list(range(8))).results
    y_prompt = np.stack([np.concatenate([res[2 * s]["y_own"], res[2 * s + 1]["y_own"]], axis=0) for s in range(4)])
    y_sample = np.concatenate([r["y_smp"] for r in res], axis=0).reshape(128, 8, D)
    pool_pp = np.stack([res[2 * s + 1]["pool_p"] for s in range(4)])[None]
    C_pp = np.stack([res[2 * s + 1]["C_p"] for s in range(4)])[None]
    n_pp = np.stack([res[2 * s + 1]["n_p"] for s in range(4)])[None]
    m_pp = np.stack([res[2 * s + 1]["m_p"].reshape(NH) for s in range(4)])[None]
    pool_ss = np.concatenate([r["pool_s"] for r in res], axis=0)[None]
    C_ss = np.concatenate([r["C_s"] for r in res], axis=0)[None]
    n_ss = np.concatenate([r["n_s"].reshape(16, NH, HD) for r in res], axis=0)[None]
    m_ss = np.concatenate([r["m_s"] for r in res], axis=0)[None]
    outs = (y_prompt, y_sample, pool_pp, C_pp, n_pp, m_pp, pool_ss, C_ss, n_ss, m_ss)
    return tuple(np.ascontiguousarray(o, dtype=np.float32) for o in outs)
```

```python
import math
import numpy as np
import concourse.bass as bass
import concourse.mybir as mybir
from concourse.bass_utils import run_bass_kernel_spmd

F32 = mybir.dt.float32
BF16 = mybir.dt.bfloat16
AF = mybir.ActivationFunctionType
ALU = mybir.AluOpType
AX = mybir.AxisListType

D = 2048
KC = 16
NH = 4
HD = 256
HDE = 257
DFF = 5632
FCH = 44
EPS = 1e-6
LN16 = math.log(16.0)
BIG = 1.0e30
ENGS = ("pe", "act", "dve", "pool", "sp")
STOP = None


class Buf:
    __slots__ = ("ap", "w", "r", "excl")

    def __init__(self, ap, excl=False):
        self.ap = ap
        self.w = None
        self.r = []
        self.excl = excl


class Prog:
    def __init__(self):
        self.streams = {e: [] for e in ENGS}
        self.cnt = {e: 0 for e in ENGS}
        self.seen = {e: {} for e in ENGS}
        self.ndma = {"sp": 20, "pool": 12}
        self.dma_cnt = {q: [0] * n for q, n in self.ndma.items()}
        self.dma_rr = {q: 0 for q in self.ndma}
        self.dead = False
        self.stop = STOP

    def checkpoint(self, k):
        if self.stop is not None and k >= self.stop - 1e-9:
            self.dead = True

    def _waits(self, eng, deps):
        out = []
        for d in deps:
            if d is None:
                continue
            k, c = d
            if self.seen[eng].get(k, 0) >= c:
                continue
            self.seen[eng][k] = c
            out.append((k, c))
        return out

    def _deps(self, eng, reads, writes):
        deps = []
        for b in reads:
            if b.w is not None and not (eng == "pe" and b.w[0] == "pe"):
                deps.append(b.w)
            if b.excl:
                for t in b.r:
                    if t is not None and t[0] != eng:
                        deps.append(t)
        for b in writes:
            for t in [b.w] + b.r:
                if t is not None and t[0] != eng:
                    deps.append(t)
        return deps

    def op(self, eng, fn, reads=(), writes=(), signal=True):
        if self.dead:
            return None
        waits = self._waits(eng, self._deps(eng, reads, writes))
        if signal:
            self.cnt[eng] += 1
            tok = (eng, self.cnt[eng])
        else:
            tok = (eng, self.cnt[eng] + 1)
        self.streams[eng].append((waits, fn, "inc" if signal else None))
        for b in reads:
            b.r.append(tok)
        for b in writes:
            b.w = tok
            b.r = []
        return tok

    def dma(self, q, out_ap, in_ap, reads=(), writes=(), slow=False):
        if self.dead:
            return None
        i = self.dma_rr[q]
        self.dma_rr[q] = (i + 1) % self.ndma[q]
        key = (q, i)
        deps = self._deps("dmaq", reads, writes)
        prev = self.dma_cnt[q][i]
        if prev:
            deps.append((key, prev))
        waits = self._waits(q, deps)
        self.dma_cnt[q][i] = prev + 16
        tok = (key, prev + 16)
        if slow:
            fn = lambda e, o=out_ap, a=in_ap: e.dma_start(out=o, in_=a, allow_slow_non_contiguous=True)
        else:
            fn = lambda e, o=out_ap, a=in_ap: e.dma_start(out=o, in_=a)
        self.streams[q].append((waits, fn, key))
        for b in reads:
            b.r.append(tok)
        for b in writes:
            b.w = tok
            b.r = []
        return tok

    def replay(self, eng, e, sems):
        for waits, fn, sig in self.streams[eng]:
            for k, c in waits:
                e.wait_ge(sems[k], c)
            ins = fn(e)
            if sig == "inc":
                ins.then_inc(sems[eng], 1)
            elif sig is not None:
                ins.then_inc(sems[sig], 16)

    def final_waits(self, eng, e, sems):
        for i, c in enumerate(self.dma_cnt.get(eng, [])):
            if c:
                e.wait_ge(sems[(eng, i)], c)


def build_nc():
    nc = bass.Bass("TRN2", target_bir_lowering=False)
    P = Prog()

    def din(name, shape, dt=F32):
        return nc.dram_tensor(name, list(shape), dt, kind="ExternalInput").ap()

    def dout(name, shape, dt=F32):
        return nc.dram_tensor(name, list(shape), dt, kind="ExternalOutput").ap()

    x_meta = din("x_meta", [16, D])
    x_pre = din("x_pre", [1024, D])
    x_own = din("x_own", [1024, D])
    x_halo = din("x_halo", [16, D])
    x_smp = din("x_smp", [128, D])
    flag_d = din("flag", [128, 1])
    st_pool = din("st_pool", [16, 15, 1024])
    st_C = din("st_C", [16, NH, HD, HD])
    st_n = din("st_n", [64, HD])
    st_m = din("st_m", [16, NH])
    norm_mix_w = din("norm_mix_w", [D])
    w_in = din("w_in", [D, 5128])
    b_ig = din("b_igate", [NH])
    b_fg = din("b_fgate", [NH])
    w_pool = din("w_pool", [4, 256, 256])
    pool_scale = din("pool_scale", [1024])
    mnorm_w = din("mlstm_norm_w", [1024])
    w_out = din("w_out", [D, D])
    norm_ffn_w = din("norm_ffn_w", [D])
    w_gate = din("w_gate", [D, DFF])
    w_up = din("w_up", [D, DFF])
    w_down = din("w_down", [DFF, D])
    norm_final_w = din("norm_final_w", [D])

    y_own = dout("y_own", [1024, D])
    y_smp = dout("y_smp", [128, D])
    pool_p = dout("pool_p", [15, 1024])
    C_p = dout("C_p", [NH, HD, HD])
    n_p = dout("n_p", [NH, HD])
    m_p = dout("m_p", [1, NH])
    pool_s = dout("pool_s", [16, 15, 1024])
    C_s = dout("C_s", [16, NH, HD, HD])
    n_s = dout("n_s", [128, 128])
    m_s = dout("m_s", [16, NH])

    w_in_v = w_in.rearrange("(kc p) n -> p kc n", p=128)
    w_out_v = w_out.rearrange("(kc p) n -> p kc n", p=128)
    w_gate_v = w_gate.rearrange("(kc p) n -> p kc n", p=128)
    w_up_v = w_up.rearrange("(kc p) n -> p kc n", p=128)
    w_down_v = w_down.rearrange("(fc p) n -> p fc n", p=128)

    from contextlib import ExitStack
    es = ExitStack()

    def sb(name, shape, dt):
        return es.enter_context(nc.sbuf_tensor(name, list(shape), dt))

    def ps(name, shape, dt):
        return es.enter_context(nc.psum_tensor(name, list(shape), dt))

    with es:
        Z1 = sb("z1", [128, KC * 1168 // 2], F32)
        Z2 = sb("z2", [128, KC * 1152 // 2], F32)
        Z3 = sb("z3", [128, 8192], F32)
        Z4 = sb("z4", [128, 9 * D], F32)
        Z1b = Z1[:, :].bitcast(BF16)
        Z2b = Z2[:, :].bitcast(BF16)
        Z3b = Z3[:, :].bitcast(BF16)
        Z4BUFS = []
        tl = [9600]

        def zt(nfl, dt=F32, shape=None):
            a0 = tl[0]
            tl[0] += nfl
            assert tl[0] <= 9 * D, tl[0]
            ap = Z4[:, a0:a0 + nfl]
            if dt == BF16:
                ap = ap.bitcast(BF16)
            if shape is not None:
                ap = ap.rearrange(shape[0], **shape[1])
            bb = Buf(ap)
            Z4BUFS.append(bb)
            return bb

        ones32 = Buf(sb("ones32", [128, 128], F32)[:, :])
        ident32 = Buf(sb("ident32", [128, 128], F32)[:, :])
        identb = Buf(sb("identb", [128, 128], BF16)[:, :])
        tri32 = Buf(sb("tri32", [128, 128], F32)[:, :])
        tril32 = Buf(sb("tril32", [128, 128], F32)[:, :])
        maskneg = Buf(sb("maskneg", [128, 128], F32)[:, :])
        maskposb = Buf(sb("maskposb", [128, 128], BF16)[:, :])
        triB32 = Buf(sb("triB32", [128, 128], F32)[:, :])
        trilB32 = Buf(sb("trilB32", [128, 128], F32)[:, :])
        masknegB = Buf(sb("masknegB", [128, 128], F32)[:, :])
        maskposbB = Buf(sb("maskposbB", [128, 128], BF16)[:, :])
        SLm = Buf(sb("SLm", [128, 128], F32)[:, :])
        sel32 = Buf(sb("sel32", [16, 128], F32)[:, :])
        selA = Buf(sb("selA", [16, 128], F32)[:, :])
        sellastT = Buf(sb("sellastT", [16, 128], F32)[:, :])
        blockmask = Buf(sb("blockmask", [128, 16], F32)[:, :])
        blockA = Buf(sb("blockA", [128, 16], F32)[:, :])
        sellast = Buf(sb("sellast", [128, 16], F32)[:, :])
        bias8 = Buf(sb("bias8", [128, 8], F32)[:, :])
        gcol = Buf(sb("gcol", [128, 8], F32)[:, :])
        pscol = Buf(sb("pscol", [128, 8], F32)[:, :])
        flagc = Buf(sb("flagc", [128, 1], F32)[:, :])
        fsc = Buf(sb("fsc", [128, 1], F32)[:, :])
        negBc = Buf(sb("negBc", [128, 4], F32)[:, :])
        Mc = Buf(sb("Mc", [128, 4], F32)[:, :])
        negBm = Buf(sb("negBm", [128, 4], F32)[:, :])
        Mm = Buf(sb("Mm", [128, 4], F32)[:, :])
        C32t = sb("C32", [128, NH, 2, HDE], F32)
        C32 = [Buf(C32t[:, h, :, :]) for h in range(NH)]
        Cbft = sb("Cbf", [128, NH, 2, HDE], BF16)
        Cbf = [Buf(Cbft[:, h, :, :]) for h in range(NH)]
        sst = sb("ss", [128, 2, 8], F32)
        SS = [Buf(sst[:, i, :]) for i in range(2)]
        xnb1 = Buf(sb("xnb", [128, D], BF16)[:, :])
        xnb = [xnb1, xnb1]
        sq = xnb1
        wpl = Buf(sb("wpl", [128, 4, 2, 256], BF16)[:, :, :, :])

        selrepA = zt(1024, BF16, ("p (b t) -> p b t", dict(b=16)))
        wg8 = zt(64, BF16, ("p (k n) -> p k n", dict(k=KC)))
        gwsb = zt(10 * 19 * 4)
        GWl = [Buf(gwsb.ap[:, j * 76:(j + 1) * 76].rearrange("p (k h) -> p k h", k=19)) for j in range(10)]
        Z4BUFS.extend(GWl)
        GW = GWl[0:9] + GWl[0:10]
        dg = zt(512, F32, ("p (h s) -> p h s", dict(h=4)))
        tmpm = zt(512, F32, ("p (h s) -> p h s", dict(h=4)))
        dgM = [zt(128) for i in range(2)]
        DT = [zt(128) for i in range(2)]
        interb = [zt(128) for i in range(2)]
        STb = [zt(64, BF16) for i in range(2)]
        qkT = [zt(256, BF16, ("p (a t) -> p a t", dict(a=4))) for i in range(2)]
        qtl = [zt(128, BF16, ("p (a t) -> p a t", dict(a=2))) for i in range(2)]
        kwb = [zt(128, BF16) for i in range(2)]
        colsb = zt(24)
        COLS = [Buf(colsb.ap[:, i * 12:(i + 1) * 12]) for i in range(2)]
        Z4BUFS.extend(COLS)
        ytm = [zt(128, BF16) for i in range(2)]
        sqs = zt(128, BF16)
        dsel = zt(64, F32, ("p (b h) -> p b h", dict(b=16)))
        decs = zt(64, F32, ("p (b h) -> p b h", dict(b=16)))
        nT = zt(128, F32, ("p (b h c) -> p b h c", dict(b=16, h=4)))
        nS = zt(128, F32, ("p (b h c) -> p b h c", dict(b=16, h=4)))
        nrow = zt(128)
        nin = zt(256)
        msm = zt(4)
        msout = zt(4)
        mpos = zt(4)
        mvec = zt(4)
        qt32 = zt(256, F32, ("p (a t) -> p a t", dict(a=2)))
        qm = [zt(512, F32, ("p (b t) -> p b t", dict(b=4))) for i in range(2)]
        Cout = [zt(260) for i in range(2)]
        sgt = zt(256)

        o4 = [0]

        def z4(nfl):
            a = o4[0]
            o4[0] += nfl
            assert o4[0] <= 9600, o4[0]
            return Z4[:, a:a + nfl]
        xt = [Buf(z4(D)) for _ in range(2)]
        o4[0] = 0
        utm = Buf(z4(1024))
        uT = Buf(z4(8 * 143).rearrange("p (c t) -> p c t", c=8))
        uTs = Buf(z4(8 * 16 * 23).rearrange("p (c b t) -> p c b t", c=8, b=16))
        sw = [Buf(z4(1472)) for _ in range(2)]
        dTb = Buf(z4(512).bitcast(BF16).rearrange("p (c t) -> p c t", c=8))
        sptm = Buf(z4(1024))
        SET_PL = [utm, uT, uTs, sw[0], sw[1], dTb, sptm]
        o4[0] = 0
        q_tm = Buf(z4(9 * 128).bitcast(BF16).rearrange("p (i f) -> p i f", i=9))
        k_tm = Buf(z4(9 * 128).bitcast(BF16).rearrange("p (i f) -> p i f", i=9))
        sigo = Buf(z4(9 * 128).bitcast(BF16).rearrange("p (i f) -> p i f", i=9))
        v_ext = Buf(z4(9 * 130).bitcast(BF16).rearrange("p (i f) -> p i f", i=9))
        Vmask = Buf(z4(16 * 130).bitcast(BF16).rearrange("p (b f) -> p b f", b=16))
        C32s = [Buf(z4(2 * 2 * HDE).rearrange("p (b c v) -> p b c v", b=2, c=2)) for _ in range(2)]
        SET_HD = [q_tm, k_tm, sigo, v_ext, Vmask, C32s[0], C32s[1]]
        Z4BUFS.extend(xt + SET_PL + SET_HD)
        x_new = [Buf(Z4[:, i * D:(i + 1) * D]) for i in range(9)]
        C32mt = Z2[:, 2400:2400 + NH * 2 * HDE].rearrange("p (h c v) -> p h c v", h=NH, c=2)
        C32m = [Buf(C32mt[:, h, :, :]) for h in range(NH)]
        wb = Buf(Z2[:, 4600:4600 + D])
        silt = [Buf(Z2[:, 6400 + i * 512:6400 + (i + 1) * 512]) for i in range(2)]
        W2 = [Buf(Z3b[:, i * 8192:(i + 1) * 8192].rearrange("p (k n) -> p k n", k=KC)) for i in range(2)]

        def fence(bufs):
            P.op("dve", lambda e: e.memset(fsc.ap, 0.0), writes=[fsc] + list(bufs))

        PJ = [Buf(ps("pj%d" % i, [128, 512], F32)[:, :], excl=True) for i in range(2)]
        PT = [Buf(ps("pt%d" % i, [128, 1024], BF16)[:, :], excl=True) for i in range(2)]
        PS = [Buf(ps("ps%d" % i, [128, 512], F32)[:, :], excl=True) for i in range(2)]
        PC = [Buf(ps("pc%d" % i, [128, 512], F32)[:, :], excl=True) for i in range(2)]

        def act(out_b, in_b, func, bias=0.0, scale=1.0, accum=None, extra_r=(), o=None, i=None):
            oa = out_b.ap if o is None else o
            ia = in_b.ap if i is None else i
            kw = {}
            if accum is not None:
                kw["accum_out"] = accum[1]
            w = [out_b] + ([accum[0]] if accum is not None else [])
            return P.op("act", lambda e: e.activation(out=oa, in_=ia, func=func, bias=bias, scale=scale, **kw),
                        reads=[in_b] + list(extra_r), writes=w)

        def tt(eng, out_b, a_b, b_b, op, o=None, a=None, b=None):
            oa = out_b.ap if o is None else o
            aa = a_b.ap if a is None else a
            ba = b_b.ap if b is None else b
            return P.op(eng, lambda e: e.tensor_tensor(out=oa, in0=aa, in1=ba, op=op), reads=[a_b, b_b], writes=[out_b])

        def ts(eng, out_b, a_b, s1, s2, op0, op1=None, o=None, a=None, extra_r=()):
            oa = out_b.ap if o is None else o
            aa = a_b.ap if a is None else a
            if op1 is None:
                f = lambda e: e.tensor_scalar(out=oa, in0=aa, scalar1=s1, scalar2=None, op0=op0)
            else:
                f = lambda e: e.tensor_scalar(out=oa, in0=aa, scalar1=s1, scalar2=s2, op0=op0, op1=op1)
            return P.op(eng, f, reads=[a_b] + list(extra_r), writes=[out_b])

        def stt(out_b, a_b, sc, b_b, op0, op1, o=None, a=None, b=None, extra_r=()):
            oa = out_b.ap if o is None else o
            aa = a_b.ap if a is None else a
            ba = b_b.ap if b is None else b
            return P.op("dve", lambda e: e.scalar_tensor_tensor(out=oa, in0=aa, scalar=sc, in1=ba, op0=op0, op1=op1),
                        reads=[a_b, b_b] + list(extra_r), writes=[out_b])

        def red(out_b, in_b, o, i):
            return P.op("dve", lambda e: e.tensor_reduce(out=o, in_=i, axis=AX.X, op=ALU.max), reads=[in_b], writes=[out_b])

        def cp(eng, out_b, in_b, o=None, i=None):
            oa = out_b.ap if o is None else o
            ia = in_b.ap if i is None else i
            if eng == "act":
                return P.op("act", lambda e: e.copy(out=oa, in_=ia), reads=[in_b], writes=[out_b])
            return P.op(eng, lambda e: e.tensor_copy(out=oa, in_=ia), reads=[in_b], writes=[out_b])

        def mm(out_b, o, l_b, l, r_b, r, start, stop, signal=None):
            if signal is None:
                signal = stop
            return P.op("pe", lambda e: e.matmul(out=o, lhsT=l, rhs=r, start=start, stop=stop),
                        reads=[l_b, r_b], writes=[out_b], signal=signal)

        def tr(out_b, o, in_b, i, id_b, idap, signal):
            return P.op("pe", lambda e: e.transpose(out=o, in_=i, identity=idap), reads=[in_b, id_b], writes=[out_b], signal=signal)

        def memset(eng, b, val, ap=None):
            a = b.ap if ap is None else ap
            return P.op(eng, lambda e: e.memset(a, val), writes=[b])

        def asel(out_b, in_b, pattern, cmp, base, cm, o=None, i=None):
            oa = out_b.ap if o is None else o
            ia = in_b.ap if i is None else i
            return P.op("pool", lambda e: e.affine_select(out=oa, in_=ia, pattern=pattern, compare_op=cmp, fill=0.0,
                                                          base=base, channel_multiplier=cm), reads=[in_b], writes=[out_b])

        memset("pool", ones32, 1.0)
        asel(ident32, ones32, [[1, 128]], ALU.is_equal, 0, -1)
        asel(tri32, ones32, [[1, 128]], ALU.is_ge, 0, -1)
        asel(tril32, ones32, [[-1, 128]], ALU.is_ge, 0, 1)
        cp("pool", identb, ident32)
        ts("pool", maskneg, tril32, BIG, -BIG, ALU.mult, ALU.add)
        ts("pool", maskposb, tri32, -BIG, BIG, ALU.mult, ALU.add)
        asel(selA, ones32, [[1, 128]], ALU.is_ge, 0, -8, i=ones32.ap[0:16, :])
        asel(sel32, selA, [[-1, 128]], ALU.is_ge, 7, 8)
        asel(sellastT, ones32, [[1, 128]], ALU.is_equal, -7, -8, i=ones32.ap[0:16, :])
        asel(blockA, ones32, [[-8, 16]], ALU.is_ge, 0, 1, i=ones32.ap[:, 0:16])
        asel(blockmask, blockA, [[8, 16]], ALU.is_ge, 7, -1)
        asel(sellast, ones32, [[-8, 16]], ALU.is_equal, -7, 1, i=ones32.ap[:, 0:16])
        memset("pool", selrepA, 1.0)
        asel(selrepA, selrepA, [[-8, 16], [1, 128]], ALU.is_ge, 0, 0)
        asel(selrepA, selrepA, [[8, 16], [-1, 128]], ALU.is_ge, 7, 0)
        SELREP = selrepA
        mm(PS[0], PS[0].ap[:, 0:128], sel32, sel32.ap, sel32, sel32.ap, True, True)
        tt("dve", triB32, tri32, PS[0], ALU.mult, b=PS[0].ap[:, 0:128])
        tt("dve", trilB32, tril32, PS[0], ALU.mult, b=PS[0].ap[:, 0:128])
        ts("dve", masknegB, trilB32, BIG, -BIG, ALU.mult, ALU.add)
        ts("dve", maskposbB, triB32, -BIG, BIG, ALU.mult, ALU.add)
        mm(PS[1], PS[1].ap[:, 0:128], sellastT, sellastT.ap, sel32, sel32.ap, True, True)
        cp("dve", SLm, PS[1], i=PS[1].ap[:, 0:128])

        P.dma("sp", bias8.ap[:, 0:4], b_ig.partition_broadcast(128), writes=[bias8])
        P.dma("sp", bias8.ap[:, 4:8], b_fg.partition_broadcast(128), writes=[bias8])
        P.dma("sp", gcol.ap, mnorm_w.rearrange("(c p) -> p c", p=128), writes=[gcol], slow=True)
        P.dma("sp", pscol.ap, pool_scale.rearrange("(c p) -> p c", p=128), writes=[pscol], slow=True)
        P.dma("sp", flagc.ap, flag_d, writes=[flagc])
        P.dma("sp", wb.ap, norm_mix_w.partition_broadcast(128), writes=[wb])
        P.dma("pool", wg8.ap, w_in_v[:, :, 5120:5128], writes=[wg8])
        P.dma("pool", wpl.ap, w_pool.rearrange("g (c p) e -> p g c e", p=128), writes=[wpl])
        for h in range(NH):
            memset("dve", C32[h], 0.0)
            memset("dve", Cbf[h], 0.0)
        memset("dve", negBc, 0.0)
        memset("dve", Mc, 0.0)

        P.checkpoint(1)
        rr = [0]

        def norm_tile(src_ap, L, dstT, tok0, x_in=None, wbuf=None, wap=None):
            i = rr[0] % 2
            rr[0] += 1
            if x_in is None:
                xb = xt[i]
                P.dma("sp", xb.ap[:L, :], src_ap, writes=[xb])
            else:
                xb = x_in
            ssb = SS[i]
            act(sq, xb, AF.Square, accum=(ssb, ssb.ap[:L, 0:1]), o=sq.ap[:L, :], i=xb.ap[:L, :])
            act(ssb, ssb, AF.Ln, bias=EPS, scale=1.0 / D, o=ssb.ap[:L, 1:2], i=ssb.ap[:L, 0:1])
            act(ssb, ssb, AF.Exp, scale=-0.5, o=ssb.ap[:L, 2:3], i=ssb.ap[:L, 1:2])
            if wbuf is None:
                wbuf, wap = wb, wb.ap
            stt(xnb[i], xb, ssb.ap[:L, 2:3], wbuf, ALU.mult, ALU.mult, o=xnb[i].ap[:L, :], a=xb.ap[:L, :], b=wap[:L, :],
                extra_r=[ssb])
            for half in range(2):
                pt = PT[half]
                for j in range(8):
                    kc = half * 8 + j
                    tr(pt, pt.ap[:, j * 128:j * 128 + L], xnb[i], xnb[i].ap[:L, kc * 128:(kc + 1) * 128],
                       identb, identb.ap[:L, :L], signal=(j == 7))
                src = pt.ap.rearrange("p (j t) -> p j t", j=8)[:, :, 0:L]
                cp("act" if half == 0 else "dve", dstT, pt, o=dstT.ap[:, half * 8:half * 8 + 8, tok0:tok0 + L], i=src)
            return xb

        pjr = [0]

        def proj(xT, tok0, L, Wb, Wap, ncols):
            pj = PJ[pjr[0] % 2]
            pjr[0] += 1
            for kc in range(KC):
                mm(pj, pj.ap[:L, 0:ncols], xT, xT.ap[:, kc, tok0:tok0 + L], Wb, Wap[:, kc, 0:ncols], kc == 0, kc == KC - 1)
            return pj

        K_IG, K_Z, K_E, K_SP, K_NEGB, K_A, K_M, K_NEGM, K_EM, K_MPREV, K_MEND, K_W, K_DEC, K_T1, K_CMX, K_T2, K_AADJ, K_CME, K_INTER = range(19)

        def gate_prep(xT, tok0, L, g, mode):
            G = GW[g]

            def k(kind, rows=L):
                return G.ap[:rows, kind, :]
            pj = proj(xT, tok0, L, wg8, wg8.ap, 8)
            tt("dve", G, pj, bias8, ALU.add, o=G.ap[:L, 0:2, :].rearrange("p a h -> p (a h)"), a=pj.ap[:L, 0:8], b=bias8.ap[:L, :])
            act(G, G, AF.Exp, scale=-1.0, o=k(K_E), i=k(K_Z))
            act(G, G, AF.Ln, bias=1.0, o=k(K_SP), i=k(K_E))
            TRI = tri32 if mode == "p" else triB32
            MN = maskneg if mode == "p" else masknegB
            p0 = PS[0]
            mm(p0, p0.ap[:L, 0:4], TRI, TRI.ap[:L, :L], G, k(K_SP), True, True)
            if mode == "p":
                mm(p0, p0.ap[:, 4:8], ones32, ones32.ap[:L, :], G, k(K_SP), True, True)
                tt("dve", G, p0, negBc, ALU.add, o=k(K_NEGB), a=p0.ap[:L, 0:4], b=negBc.ap[:L, :])
            else:
                cp("dve", G, p0, o=k(K_NEGB), i=p0.ap[:L, 0:4])
                mm(p0, p0.ap[:, 8:12], sel32, sel32.ap, msm, msm.ap[0:16, :], True, True)
                cp("dve", G, p0, o=k(K_MPREV, 128), i=p0.ap[:, 8:12])
            tt("dve", G, G, G, ALU.add, o=k(K_A), a=k(K_IG), b=k(K_NEGB))
            ts("dve", G, G, -LN16, None, ALU.add, o=k(K_AADJ), a=k(K_A))
            tt("dve", dg, ident32, G, ALU.mult, o=dg.ap[:L, :, :L],
               a=ident32.ap[:L, :L].unsqueeze(1).to_broadcast([L, 4, L]),
               b=k(K_A).unsqueeze(2).to_broadcast([L, 4, L]))
            p1 = PS[1]
            for h in range(NH):
                mm(p1, p1.ap[:, h * 128:h * 128 + L], ones32, ones32.ap[:L, :], dg, dg.ap[:L, h, :L], True, True, signal=(h == NH - 1))
            Ab = p1.ap.rearrange("p (h s) -> p h s", h=4)
            tt("dve", tmpm, p1, MN, ALU.add, o=tmpm.ap[:L, :, :L], a=Ab[:L, :, :L],
               b=MN.ap[:L, :L].unsqueeze(1).to_broadcast([L, 4, L]))
            red(G, tmpm, k(K_CMX), tmpm.ap[:L, :, :L])
            if mode == "p":
                cp("dve", G, Mc, o=k(K_MPREV, 128), i=Mc.ap)
                tt("dve", G, G, Mc, ALU.max, o=k(K_M), a=k(K_CMX), b=Mc.ap[:L, :])
                red(G, p1, k(K_CME, 128), Ab[:, :, :L])
                tt("dve", G, G, Mc, ALU.max, o=k(K_MEND, 128), a=k(K_CME, 128), b=Mc.ap)
            else:
                tt("dve", G, G, G, ALU.max, o=k(K_M), a=k(K_CMX), b=k(K_MPREV))
                mm(p0, p0.ap[:, 12:16], SLm, SLm.ap, G, k(K_M), True, True)
                cp("dve", G, p0, o=k(K_MEND, 128), i=p0.ap[:, 12:16])
            tt("dve", G, G, G, ALU.subtract, o=k(K_NEGM), a=k(K_NEGB), b=k(K_M))
            tt("dve", G, G, G, ALU.subtract, o=k(K_INTER), a=k(K_MPREV), b=k(K_M))
            act(G, G, AF.Exp, o=k(K_INTER), i=k(K_INTER))
            act(G, G, AF.Exp, o=k(K_EM), i=k(K_NEGM))
            tt("dve", G, G, G, ALU.subtract, o=k(K_T1), a=k(K_AADJ), b=k(K_MEND))
            act(G, G, AF.Exp, o=k(K_W), i=k(K_T1))
            tt("dve", G, G, G, ALU.subtract, o=k(K_T2, 128), a=k(K_MPREV, 128), b=k(K_MEND, 128))
            act(G, G, AF.Exp, o=k(K_DEC, 128), i=k(K_T2, 128))
            if mode == "p":
                cp("dve", Mc, G, i=k(K_MEND, 128))
                tt("dve", negBc, negBc, p0, ALU.add, b=p0.ap[:, 4:8])
            else:
                tt("dve", dsel, sellast, G, ALU.mult, a=sellast.ap.unsqueeze(2).to_broadcast([128, 16, 4]),
                   b=k(K_T2, 128).unsqueeze(1).to_broadcast([128, 16, 4]))
                mm(p1, p1.ap[:, 0:64], ones32, ones32.ap, dsel, dsel.ap.rearrange("p b h -> p (b h)"), True, True)
                act(decs, p1, AF.Exp, o=decs.ap.rearrange("p b h -> p (b h)"), i=p1.ap[:, 0:64])
                ts("dve", mpos, G, -1.0, None, ALU.mult, a=k(K_NEGM))
                mm(p0, p0.ap[0:16, 16:20], sellast, sellast.ap, mpos, mpos.ap, True, True)
                cp("dve", msout, p0, o=msout.ap[0:16, :], i=p0.ap[0:16, 16:20])
                P.dma("sp", m_s, msout.ap[0:16, :], reads=[msout])

        cr = [0]

        def state_update(h, g, L, ktm_ap, vext_ap, k_b, v_b):
            G = GW[g]
            i = cr[0] % 2
            kw = kwb[i]
            ts("dve", kw, k_b, G.ap[:L, K_W, h:h + 1], None, ALU.mult, o=kw.ap[:L, :], a=ktm_ap, extra_r=[G])
            for dc in range(2):
                mm(PC[dc], PC[dc].ap[:, 0:HDE], kw, kw.ap[:L, dc * 128:(dc + 1) * 128], v_b, vext_ap, True, True)
            for dc in range(2):
                stt(C32[h], C32[h], G.ap[:, K_DEC, h:h + 1], PC[dc], ALU.mult, ALU.add,
                    o=C32[h].ap[:, dc, :], a=C32[h].ap[:, dc, :], b=PC[dc].ap[:, 0:HDE], extra_r=[G])
            cp("act", Cbf[h], C32[h])

        xnTp = Buf(Z1b[:, 0:KC * 1040].rearrange("p (k t) -> p k t", k=KC))
        kvp = Buf(Z2b[:, 0:9 * 516].rearrange("p (i f) -> p i f", i=9))
        pre_tiles = [(x_meta, 16, 1024)] + [(x_pre[i * 128:(i + 1) * 128, :], 128, i * 128) for i in range(8)]
        for (src, L, tok0) in pre_tiles:
            norm_tile(src, L, xnTp, tok0)
        P.checkpoint(2)
        for gi, (src, L, tok0) in enumerate(pre_tiles):
            gate_prep(xnTp, tok0, L, gi, "p")
            if gi == 0:
                cp("dve", negBm, negBc)
                cp("dve", Mm, Mc)
        P.checkpoint(3)
        memset("dve", kvp, 1.0)
        WA = W2
        for h in range(NH):
            W = WA[h % 2]
            P.dma("pool", W.ap[:, :, 0:256], w_in_v[:, :, 2048 + h * 256:2048 + (h + 1) * 256], writes=[W])
            P.dma("pool", W.ap[:, :, 256:512], w_in_v[:, :, 3072 + h * 256:3072 + (h + 1) * 256], writes=[W])
            for gi, (src, L, tok0) in enumerate(pre_tiles):
                pj = proj(xnTp, tok0, L, W, W.ap, 512)
                cp("act", kvp, pj, o=kvp.ap[:L, gi, 0:512].rearrange("p (a f) -> p a f", a=2)[:, :, 0:256]
                   if False else kvp.ap[:L, gi, 0:256], i=pj.ap[:L, 0:256])
                cp("dve", kvp, pj, o=kvp.ap[:L, gi, 256:512], i=pj.ap[:L, 256:512])
            for gi, (src, L, tok0) in enumerate(pre_tiles):
                state_update(h, gi, L, kvp.ap[:L, gi, 0:256], kvp.ap[:L, gi, 256:513], kvp, kvp)
                if gi == 0:
                    cp("dve", C32m[h], C32[h])
        for h in range(NH):
            tt("dve", C32[h], C32[h], C32m[h], ALU.subtract)
            stt(C32[h], C32[h], flagc.ap[:, 0:1], C32m[h], ALU.mult, ALU.add, extra_r=[flagc])
            cp("act", Cbf[h], C32[h])
        for (cur, sav) in ((negBc, negBm), (Mc, Mm)):
            tt("dve", cur, cur, sav, ALU.subtract)
            stt(cur, cur, flagc.ap[:, 0:1], sav, ALU.mult, ALU.add, extra_r=[flagc])

        P.checkpoint(4)
        xnT = Buf(Z1b[:, 0:KC * 1168].rearrange("p (k t) -> p k t", k=KC))
        fence([xnTp, xnT])
        yT = Buf(Z2b[:, 0:KC * 1152].rearrange("p (k t) -> p k t", k=KC))
        main_tiles = [(x_own[i * 128:(i + 1) * 128, :], 128, i * 128) for i in range(8)] + [(x_smp, 128, 1024)]
        for (src, L, tok0) in main_tiles + [(x_halo, 16, 1152)]:
            norm_tile(src, L, xnT, tok0)
        P.dma("sp", msm.ap[0:16, :], st_m, writes=[msm])
        P.dma("sp", nin.ap[0:64, :], st_n, writes=[nin])
        for dc in range(2):
            tr(PS[0], PS[0].ap[:, dc * 64:(dc + 1) * 64], nin, nin.ap[0:64, dc * 128:(dc + 1) * 128], ident32, ident32.ap[0:64, 0:64], signal=(dc == 1))
        cp("dve", nT, PS[0], o=nT.ap.rearrange("p b h c -> p c (b h)"), i=PS[0].ap[:, 0:128].rearrange("p (c x) -> p c x", c=2))
        for gi, (src, L, tok0) in enumerate(main_tiles):
            gate_prep(xnT, tok0, L, 9 + gi, "p" if gi < 8 else "s")
        tt("dve", mvec, Mc, negBc, ALU.subtract)
        P.dma("sp", m_p, mvec.ap[0:1, :], reads=[mvec])

        P.checkpoint(5)
        WU = W2
        fence([kvp, wb, yT] + C32m + xt + SET_PL)
        for half in range(2):
            P.dma("pool", WU[half].ap, w_in_v[:, :, half * 512:(half + 1) * 512], writes=[WU[half]])
        sp_rows = st_pool.rearrange("b j c -> (b j) c")
        for blk, (r0, nr) in enumerate(((0, 128), (128, 112))):
            P.dma("sp", sptm.ap[0:nr, :], sp_rows[r0:r0 + nr, :], writes=[sptm])
            for cc in range(8):
                p = PS[cc % 2]
                tr(p, p.ap[:, 0:nr], sptm, sptm.ap[0:nr, cc * 128:(cc + 1) * 128], ident32, ident32.ap[0:nr, 0:nr], signal=True)
                eng = "dve" if cc % 2 else "act"
                if blk == 0:
                    cp(eng, uTs, p, o=uTs.ap[:, cc, 0:8, 0:15], i=p.ap[:, 0:120].rearrange("p (b j) -> p b j", b=8))
                    cp(eng, uTs, p, o=uTs.ap[:, cc, 8, 0:8], i=p.ap[:, 120:128])
                else:
                    cp(eng, uTs, p, o=uTs.ap[:, cc, 8, 8:15], i=p.ap[:, 0:7])
                    cp(eng, uTs, p, o=uTs.ap[:, cc, 9:16, 0:15], i=p.ap[:, 7:112].rearrange("p (b j) -> p b j", b=7))

        def pool_group(U, shp, g, o_ap):
            nd = len(shp)
            T = shp[-1]

            def v(b, t0, t1):
                if nd == 1:
                    return b[:, :, t0:t1]
                return b[:, :, :, t0:t1]
            if nd == 1:
                sv = [sw[i].ap[:, 0:2 * T].rearrange("p (c t) -> p c t", c=2) for i in range(2)]
            else:
                sv = [sw[i].ap[:, 0:2 * shp[0] * T].rearrange("p (c b t) -> p c b t", c=2, b=shp[0]) for i in range(2)]
            cur_b, cur = U, (U.ap[:, 2 * g:2 * g + 2, :] if nd == 1 else U.ap[:, 2 * g:2 * g + 2, :, :])
            for k in range(g + 1):
                sh = 1 << k
                lo = 2 * sh - 1
                nb, nv = sw[k % 2], sv[k % 2]
                tt("dve", nb, cur_b, cur_b, ALU.add, o=v(nv, lo, T), a=v(cur, lo, T), b=v(cur, lo - sh, T - sh))
                cur_b, cur = nb, nv
            wdw = 2 << g
            uv = U.ap[:, 2 * g:2 * g + 2, :] if nd == 1 else U.ap[:, 2 * g:2 * g + 2, :, :]
            stt(dTb, cur_b, 1.0 / wdw, U, ALU.mult, ALU.subtract, o=o_ap, a=v(cur, 15, T), b=v(uv, 15, T))

        def pool_tile(tok0, L, ytok0, hist_mode):
            for half in range(2):
                pj = proj(xnT, tok0, L, WU[half], WU[half].ap, 512)
                cp("act" if half == 0 else "dve", utm, pj, o=utm.ap[:L, half * 512:(half + 1) * 512], i=pj.ap[:L, :])
            if hist_mode == "smp":
                for b in range(16):
                    P.dma("sp", pool_s[b, 7:15, :], utm.ap[b * 8:(b + 1) * 8, :], reads=[utm])
                    P.dma("sp", pool_s[b, 0:7, :], st_pool[b, 8:15, :])
            if hist_mode == "own" and tok0 == 7 * 128:
                P.dma("sp", pool_p, utm.ap[113:128, :], reads=[utm])
            for cc in range(8):
                p = PS[cc % 2]
                tr(p, p.ap[:, 0:L], utm, utm.ap[:L, cc * 128:(cc + 1) * 128], ident32, ident32.ap[:L, :L], signal=True)
                if hist_mode == "halo":
                    cp("dve" if cc % 2 else "act", uT, p, o=uT.ap[:, cc, 0:15], i=p.ap[:, 1:16])
                elif hist_mode == "own":
                    cp("dve" if cc % 2 else "act", uT, p, o=uT.ap[:, cc, 15:143], i=p.ap[:, 0:128])
                else:
                    cp("dve" if cc % 2 else "act", uTs, p, o=uTs.ap[:, cc, :, 15:23], i=p.ap[:, 0:128].rearrange("p (b t) -> p b t", b=16))
            if hist_mode == "halo":
                return
            if hist_mode == "own":
                U, shp, hcol = uT, (143,), 15
            else:
                U, shp, hcol = uTs, (16, 23), 15
            for g in range(4):
                if hist_mode == "own":
                    o_ap = dTb.ap[:, 2 * g:2 * g + 2, :]
                else:
                    o_ap = dTb.ap[:, 2 * g:2 * g + 2, :].rearrange("p c (b t) -> p c b t", b=16)
                pool_group(U, shp, g, o_ap)
            for g in range(4):
                for ec in range(2):
                    p = PC[ec]
                    for c in range(2):
                        mm(p, p.ap[:, 0:128], wpl, wpl.ap[:, g, c, ec * 128:(ec + 1) * 128], dTb, dTb.ap[:, 2 * g + c, :], c == 0, c == 1)
                    ch = 2 * g + ec
                    ts("dve", yT, p, pscol.ap[:, ch:ch + 1], None, ALU.mult, o=yT.ap[:, ch, ytok0:ytok0 + 128], a=p.ap[:, 0:128], extra_r=[pscol])
            if hist_mode == "own":
                cp("dve", sw[0], uT, o=sw[0].ap[:, 0:120].rearrange("p (c t) -> p c t", c=8), i=uT.ap[:, :, 128:143])
                cp("dve", uT, sw[0], o=uT.ap[:, :, 0:15], i=sw[0].ap[:, 0:120].rearrange("p (c t) -> p c t", c=8))

        pool_tile(1152, 16, 0, "halo")
        for i in range(8):
            pool_tile(i * 128, 128, i * 128, "own")
        pool_tile(1024, 128, 1024, "smp")

        P.checkpoint(6)
        q_t = [Buf(q_tm.ap[:, i, :]) for i in range(9)]
        k_t = [Buf(k_tm.ap[:, i, :]) for i in range(9)]
        s_t = [Buf(sigo.ap[:, i, :]) for i in range(9)]
        v_t = [Buf(v_ext.ap[:, i, :]) for i in range(9)]
        Z4BUFS.extend(q_t + k_t + s_t + v_t)

        def front(h, ti, g, mode):
            G = GW[g]
            par = ti % 2
            MP = maskposb if mode == "p" else maskposbB
            pt = PT[par]
            for dc in range(2):
                tr(pt, pt.ap[:, dc * 128:(dc + 1) * 128], k_t[ti], k_t[ti].ap[:, dc * 128:(dc + 1) * 128], identb, identb.ap, signal=False)
            for dc in range(2):
                tr(pt, pt.ap[:, (2 + dc) * 128:(3 + dc) * 128], q_t[ti], q_t[ti].ap[:, dc * 128:(dc + 1) * 128], identb, identb.ap, signal=(dc == 1))
            yield
            cp("act", qkT[par], pt, o=qkT[par].ap.rearrange("p a t -> p (a t)"), i=pt.ap[:, 0:512])
            ts("dve", dgM[par], ident32, G.ap[:, K_M, h:h + 1], None, ALU.mult, extra_r=[G])
            yield
            p = PS[par]
            mm(p, p.ap[:, 0:128], ones32, ones32.ap, dgM[par], dgM[par].ap, True, True)
            mm(p, p.ap[:, 128:256], ones32, ones32.ap, dgM[par], dgM[par].ap, True, False, signal=False)
            mm(p, p.ap[:, 128:256], identb, identb.ap, MP, MP.ap, False, True)
            yield
            act(DT[par], p, AF.Exp, bias=G.ap[:, K_AADJ, h:h + 1], scale=-1.0, i=p.ap[:, 128:256], extra_r=[G])
            yield
            if mode == "p":
                act(interb[par], p, AF.Exp, bias=G.ap[:, K_MPREV, h:h + 1], scale=-1.0, i=p.ap[:, 0:128], extra_r=[G])
            else:
                ts("dve", dgM[par], ident32, G.ap[:, K_INTER, h:h + 1], None, ALU.mult, extra_r=[G])
                mm(p, p.ap[:, 384:512], ones32, ones32.ap, dgM[par], dgM[par].ap, True, True)
                cp("act", interb[par], p, i=p.ap[:, 384:512])
            yield
            for dc in range(2):
                mm(p, p.ap[:, 256:384], qkT[par], qkT[par].ap[:, dc, :], qkT[par], qkT[par].ap[:, 2 + dc, :], dc == 0, dc == 1)
            yield
            tt("dve", STb[par], p, DT[par], ALU.mult, a=p.ap[:, 256:384])
            yield
            ts("dve", kwb[par], k_t[ti], G.ap[:, K_W, h:h + 1], None, ALU.mult, extra_r=[G])
            yield
            if mode == "p":
                tt("dve", qtl[par], qkT[par], interb[par], ALU.mult, a=qkT[par].ap[:, 2:4, :],
                   b=interb[par].ap.unsqueeze(1).to_broadcast([128, 2, 128]))
            else:
                tt("dve", qt32, qkT[par], interb[par], ALU.mult, a=qkT[par].ap[:, 2:4, :],
                   b=interb[par].ap.unsqueeze(1).to_broadcast([128, 2, 128]))
                yield
                tt("dve", Vmask, v_t[ti], blockmask, ALU.mult, o=Vmask.ap[:, :, 0:HDE],
                   a=v_t[ti].ap[:, 0:HDE].unsqueeze(1).to_broadcast([128, 16, HDE]),
                   b=blockmask.ap.unsqueeze(2).to_broadcast([128, 16, HDE]))
            yield

        def back(h, ti, g, mode):
            G = GW[g]
            par = ti % 2
            pn = PN[par]
            vb = v_t[ti]
            if mode == "p":
                mm(pn, pn.ap[:, 0:HDE], STb[par], STb[par].ap, vb, vb.ap[:, 0:HDE], True, False, signal=False)
                for dc in range(2):
                    mm(pn, pn.ap[:, 0:HDE], qtl[par], qtl[par].ap[:, dc, :], Cbf[h], Cbf[h].ap[:, dc, :], False, dc == 1)
                yield
                for dc in range(2):
                    mm(PC[dc], PC[dc].ap[:, 0:HDE], kwb[par], kwb[par].ap[:, dc * 128:(dc + 1) * 128], vb, vb.ap[:, 0:HDE], True, True)
                yield
                for dc in range(2):
                    stt(C32[h], C32[h], G.ap[:, K_DEC, h:h + 1], PC[dc], ALU.mult, ALU.add,
                        o=C32[h].ap[:, dc, :], a=C32[h].ap[:, dc, :], b=PC[dc].ap[:, 0:HDE], extra_r=[G])
                    yield
                cp("act", Cbf[h], C32[h])
                yield
            else:
                mm(pn, pn.ap[:, 0:HDE], STb[par], STb[par].ap, vb, vb.ap[:, 0:HDE], True, False, signal=False)
                for grp in range(8):
                    cs_ = C32s[grp % 2]
                    for bl in range(2):
                        P.dma("sp", cs_.ap[:, bl, :, 0:HD], st_C[grp * 2 + bl, h].rearrange("(c p) v -> p c v", p=128), writes=[cs_])
                    cp("dve", cs_, nT, o=cs_.ap[:, :, :, HD], i=nT.ap[:, grp * 2:(grp + 1) * 2, h, :])
                    qmb = qm[grp % 2]
                    for dc in range(2):
                        tt("dve", qmb, qt32, SELREP, ALU.mult, o=qmb.ap[:, dc * 2:dc * 2 + 2, :],
                           a=qt32.ap[:, dc, :].unsqueeze(1).to_broadcast([128, 2, 128]), b=SELREP.ap[:, grp * 2:(grp + 1) * 2, :])
                    yield
                    for bl in range(2):
                        for dc in range(2):
                            last = (grp == 7 and bl == 1 and dc == 1)
                            mm(pn, pn.ap[:, 0:HDE], qmb, qmb.ap[:, dc * 2 + bl, :], cs_, cs_.ap[:, bl, dc, :], False, last,
                               signal=(last or (bl == 1 and dc == 1)))
                    yield
                    for bl in range(2):
                        b = grp * 2 + bl
                        for dc in range(2):
                            pc = PC[dc]
                            co = Cout[dc]
                            mm(pc, pc.ap[:, 0:HDE], kwb[par], kwb[par].ap[:, dc * 128:(dc + 1) * 128], Vmask, Vmask.ap[:, b, 0:HDE], True, True)
                            stt(co, cs_, decs.ap[:, b, h:h + 1], pc, ALU.mult, ALU.add, o=co.ap[:, 0:HDE], a=cs_.ap[:, bl, dc, :], b=pc.ap[:, 0:HDE], extra_r=[decs])
                            P.dma("sp", C_s[b, h, dc * 128:(dc + 1) * 128, :], co.ap[:, 0:HD], reads=[co])
                            cp("act", nS, co, o=nS.ap[:, b, h, dc:dc + 1], i=co.ap[:, HD:HDE])
                            yield
            cl = COLS[par]
            act(sqs, pn, AF.Square, accum=(cl, cl.ap[:, 0:1]), o=sqs.ap[:, 0:HD], i=pn.ap[:, 0:HD])
            yield
            ts("dve", cl, pn, -1.0, None, ALU.mult, o=cl.ap[:, 8:9], a=pn.ap[:, HD:HDE])
            tt("dve", cl, pn, cl, ALU.max, o=cl.ap[:, 9:10], a=pn.ap[:, HD:HDE], b=cl.ap[:, 8:9])
            yield
            tt("dve", cl, cl, G, ALU.max, o=cl.ap[:, 1:2], a=cl.ap[:, 9:10], b=G.ap[:, K_EM, h:h + 1])
            P.op("dve", lambda e, o=cl.ap[:, 2:3], i=cl.ap[:, 1:2]: e.reciprocal(out=o, in_=i), reads=[cl], writes=[cl])
            yield
            tt("dve", cl, cl, cl, ALU.mult, o=cl.ap[:, 3:4], a=cl.ap[:, 2:3], b=cl.ap[:, 2:3])
            tt("dve", cl, cl, cl, ALU.mult, o=cl.ap[:, 4:5], a=cl.ap[:, 3:4], b=cl.ap[:, 0:1])
            yield
            act(cl, cl, AF.Ln, bias=EPS, scale=1.0 / HD, o=cl.ap[:, 5:6], i=cl.ap[:, 4:5])
            act(cl, cl, AF.Exp, scale=-0.5, o=cl.ap[:, 6:7], i=cl.ap[:, 5:6])
            yield
            tt("dve", cl, cl, cl, ALU.mult, o=cl.ap[:, 7:8], a=cl.ap[:, 6:7], b=cl.ap[:, 2:3])
            yb = ytm[par]
            stt(yb, pn, cl.ap[:, 7:8], s_t[ti], ALU.mult, ALU.mult, o=yb.ap[:, 0:HD], a=pn.ap[:, 0:HD],
                b=s_t[ti].ap, extra_r=[cl])
            yield
            pt2 = PT[par]
            for c in range(2):
                tr(pt2, pt2.ap[:, 512 + c * 128:512 + (c + 1) * 128], yb, yb.ap[:, c * 128:(c + 1) * 128], identb, identb.ap, signal=(c == 1))
            yield
            for c in range(2):
                ch = 2 * h + c
                ts("dve", yT, pt2, gcol.ap[:, ch:ch + 1], None, ALU.mult, o=yT.ap[:, 8 + ch, ti * 128:(ti + 1) * 128],
                   a=pt2.ap[:, 512 + c * 128:512 + (c + 1) * 128], extra_r=[gcol])
                yield

        def load_head_w(h):
            for half in range(2):
                for j in range(2):
                    c0 = 1024 + (half * 2 + j) * 1024 + h * 256
                    P.dma("pool", W2[half].ap[:, :, j * 256:(j + 1) * 256], w_in_v[:, :, c0:c0 + 256], writes=[W2[half]])

        def projgen(h, ti):
            tok0 = main_tiles[ti][2]
            pj = PJ[0]
            for kc in range(KC):
                mm(pj, pj.ap[:, 0:512], xnT, xnT.ap[:, kc, tok0:tok0 + 128], W2[0], W2[0].ap[:, kc, :], kc == 0, kc == KC - 1)
                if kc % 2 == 1:
                    yield
            cp("act", q_t[ti], pj, i=pj.ap[:, 0:256])
            cp("dve", k_t[ti], pj, i=pj.ap[:, 256:512])
            yield
            pj = PJ[1]
            for kc in range(KC):
                mm(pj, pj.ap[:, 0:512], xnT, xnT.ap[:, kc, tok0:tok0 + 128], W2[1], W2[1].ap[:, kc, :], kc == 0, kc == KC - 1)
                if kc % 2 == 1:
                    yield
            cp("dve", v_t[ti], pj, o=v_t[ti].ap[:, 0:HD], i=pj.ap[:, 0:256])
            act(sgt, pj, AF.Exp, scale=-1.0, i=pj.ap[:, 256:512])
            yield
            ts("dve", sgt, sgt, 1.0, None, ALU.add)
            P.op("dve", lambda e, o=sgt.ap, i=sgt.ap: e.reciprocal(out=o, in_=i), reads=[sgt], writes=[sgt])
            cp("dve", s_t[ti], sgt)
            yield

        def interleave(gens):
            gens = list(gens)
            while gens:
                for g_ in list(gens):
                    try:
                        next(g_)
                    except StopIteration:
                        gens.remove(g_)

        fence(SET_PL + SET_HD + q_t + k_t + s_t + v_t)
        memset("dve", v_ext, 1.0)
        for vt_ in v_t:
            vt_.w = v_ext.w
        memset("dve", Vmask, 0.0)
        PN = PS
        load_head_w(0)
        for ti in range(9):
            interleave([projgen(0, ti)])
        for h in range(NH):
            if h + 1 < NH:
                load_head_w(h + 1)
            for s_ in range(11):
                tasks = []
                if s_ <= 8:
                    tasks.append(front(h, s_, 9 + s_, "p" if s_ < 8 else "s"))
                if 1 <= s_ <= 9:
                    tasks.append(back(h, s_ - 1, 9 + s_ - 1, "p" if s_ - 1 < 8 else "s"))
                if h + 1 < NH and 2 <= s_ <= 10:
                    tasks.append(projgen(h + 1, s_ - 2))
                interleave(tasks)
        P.checkpoint(7)
        for h in range(NH):
            for dc in range(2):
                P.dma("sp", C_p[h, dc * 128:(dc + 1) * 128, :], C32[h].ap[:, dc, 0:HD], reads=[C32[h]])
                P.dma("sp", n_p[h:h + 1, dc * 128:(dc + 1) * 128].rearrange("o p -> p o"), C32[h].ap[:, dc, HD:HDE], reads=[C32[h]], slow=True)
        P.checkpoint(7.5)
        tr(PS[0], PS[0].ap[:, 0:128], nS, nS.ap.rearrange("p b h c -> p (b h c)"), ident32, ident32.ap, signal=True)
        cp("dve", nrow, PS[0], i=PS[0].ap[:, 0:128])
        P.dma("sp", n_s, nrow.ap, reads=[nrow])

        P.checkpoint(8)
        all_src = [x_own[i * 128:(i + 1) * 128, :] for i in range(8)] + [x_smp]
        z4_users = Z4BUFS
        for i in range(9):
            P.dma("sp", x_new[i].ap, all_src[i], reads=[], writes=[x_new[i]] + z4_users)
        WO = [W2[0], W2[0]]
        for cg in range(4):
            W = WO[cg % 2]
            P.dma("pool", W.ap, w_out_v[:, :, cg * 512:(cg + 1) * 512], writes=[W])
            for i in range(9):
                pj = proj(yT, i * 128, 128, W, W.ap, 512)
                tt("dve", x_new[i], x_new[i], pj, ALU.add, o=x_new[i].ap[:, cg * 512:(cg + 1) * 512],
                   a=x_new[i].ap[:, cg * 512:(cg + 1) * 512], b=pj.ap[:, 0:512])
        xn2T = Buf(Z1b[:, 0:KC * 1152].rearrange("p (k t) -> p k t", k=KC))
        fence([xnT, xn2T])
        P.dma("sp", Z3[:, 4096:4096 + D], norm_ffn_w.partition_broadcast(128), writes=[W2[1]])
        for i in range(9):
            norm_tile(None, 128, xn2T, i * 128, x_in=x_new[i], wbuf=W2[1], wap=Z3[:, 4096:4096 + D])

        P.checkpoint(9)
        hT = Buf(Z2b[:, 0:11 * 1152].rearrange("p (f t) -> p f t", f=11))
        WG = [Buf(Z3b[:, i * 2048:(i + 1) * 2048].rearrange("p (k n) -> p k n", k=KC)) for i in range(2)]
        WUp = [Buf(Z3b[:, 4096 + i * 2048:4096 + (i + 1) * 2048].rearrange("p (k n) -> p k n", k=KC)) for i in range(2)]
        WD = [Buf(Z3b[:, 8192 + i * 2816:8192 + (i + 1) * 2816].rearrange("p (f n) -> p f n", f=11)) for i in range(2)]
        fence([W2[0], W2[1], yT, hT] + WG + WUp + WD + silt)
        tgs = [(0, 512), (512, 512), (1024, 128)]
        fr = [0]
        for qd in range(4):
            for fl in range(11):
                fc = qd * 11 + fl
                wgb = WG[fr[0] % 2]
                wub = WUp[fr[0] % 2]
                fr[0] += 1
                P.dma("pool", wgb.ap, w_gate_v[:, :, fc * 128:(fc + 1) * 128], writes=[wgb])
                P.dma("pool", wub.ap, w_up_v[:, :, fc * 128:(fc + 1) * 128], writes=[wub])
                for ti, (t0, tn) in enumerate(tgs):
                    pg = PJ[ti % 2]
                    pu = PS[ti % 2]
                    for kc in range(KC):
                        mm(pg, pg.ap[:, 0:tn], wgb, wgb.ap[:, kc, :], xn2T, xn2T.ap[:, kc, t0:t0 + tn], kc == 0, kc == KC - 1)
                    for kc in range(KC):
                        mm(pu, pu.ap[:, 0:tn], wub, wub.ap[:, kc, :], xn2T, xn2T.ap[:, kc, t0:t0 + tn], kc == 0, kc == KC - 1)
                    st_ = silt[ti % 2]
                    act(st_, pg, AF.Silu, o=st_.ap[:, 0:tn], i=pg.ap[:, 0:tn])
                    tt("dve", hT, st_, pu, ALU.mult, o=hT.ap[:, fl, t0:t0 + tn], a=st_.ap[:, 0:tn], b=pu.ap[:, 0:tn])
            for cg in range(8):
                W = WD[cg % 2]
                P.dma("pool", W.ap, w_down_v[:, qd * 11:(qd + 1) * 11, cg * 256:(cg + 1) * 256], writes=[W])
                for i in range(9):
                    pc = PC[i % 2]
                    for fl in range(11):
                        mm(pc, pc.ap[:, 0:256], hT, hT.ap[:, fl, i * 128:(i + 1) * 128], W, W.ap[:, fl, :], fl == 0, fl == 10)
                    tt("dve", x_new[i], x_new[i], pc, ALU.add, o=x_new[i].ap[:, cg * 256:(cg + 1) * 256],
                       a=x_new[i].ap[:, cg * 256:(cg + 1) * 256], b=pc.ap[:, 0:256])

        P.checkpoint(10)
        wbE = Buf(Z1[:, 0:D])
        fence([xn2T, wbE])
        P.dma("sp", wbE.ap, norm_final_w.partition_broadcast(128), writes=[wbE])
        for i in range(9):
            ssb = SS[i % 2]
            xb = x_new[i]
            act(sq, xb, AF.Square, accum=(ssb, ssb.ap[:, 0:1]))
            act(ssb, ssb, AF.Ln, bias=EPS, scale=1.0 / D, o=ssb.ap[:, 1:2], i=ssb.ap[:, 0:1])
            act(ssb, ssb, AF.Exp, scale=-0.5, o=ssb.ap[:, 2:3], i=ssb.ap[:, 1:2])
            stt(xb, xb, ssb.ap[:, 2:3], wbE, ALU.mult, ALU.mult, extra_r=[ssb])
            dst = y_own[i * 128:(i + 1) * 128, :] if i < 8 else y_smp
            P.dma("sp", dst, xb.ap, reads=[xb])

        sem_es = ExitStack()
        with sem_es:
            sems = {}
            for e in ENGS:
                sems[e] = sem_es.enter_context(nc.semaphore("s_" + e))
            for q, n in P.ndma.items():
                for i in range(n):
                    sems[(q, i)] = sem_es.enter_context(nc.semaphore("d_%s%d" % (q, i)))
            with nc.Block() as block:
                @block.tensor
                def _(e):
                    P.replay("pe", e, sems)

                @block.scalar
                def _(e):
                    P.replay("act", e, sems)

                @block.vector
                def _(e):
                    P.replay("dve", e, sems)

                @block.gpsimd
                def _(e):
                    P.replay("pool", e, sems)
                    P.final_waits("pool", e, sems)

                @block.sync
                def _(e):
                    P.replay("sp", e, sems)
                    P.final_waits("sp", e, sems)
    return nc


_NC = None


def _get_nc():
    global _NC
    if _NC is None:
        _NC = build_nc()
    return _NC


def kernel(x_prompt, x_sample, state_pool, state_mlstm_C, state_mlstm_n, state_mlstm_m, meta_tokens, norm_mix_w, w_in,
           b_igate, b_fgate, w_pool, pool_scale, mlstm_norm_w, w_out, norm_ffn_w, w_gate, w_up, w_down, norm_final_w):
    f = lambda a: np.ascontiguousarray(np.asarray(a, dtype=np.float32))
    xp, xs = f(x_prompt), f(x_sample)
    meta = f(meta_tokens)
    shared = {
        "norm_mix_w": f(norm_mix_w)[0], "w_in": f(w_in)[0], "b_igate": f(b_igate)[0], "b_fgate": f(b_fgate)[0],
        "w_pool": f(w_pool)[0], "pool_scale": f(pool_scale)[0], "mlstm_norm_w": f(mlstm_norm_w)[0], "w_out": f(w_out)[0],
        "norm_ffn_w": f(norm_ffn_w)[0], "w_gate": f(w_gate)[0], "w_up": f(w_up)[0], "w_down": f(w_down)[0],
        "norm_final_w": f(norm_final_w),
    }
    sp, sC, sn, sm = f(state_pool)[0], f(state_mlstm_C)[0], f(state_mlstm_n)[0], f(state_mlstm_m)[0]
    in_maps = []
    for c in range(8):
        s, h = c // 2, c % 2
        m = dict(shared)
        m["x_meta"] = meta
        m["x_pre"] = xp[s, 0:1024] if h == 1 else np.zeros((1024, D), np.float32)
        m["x_own"] = np.ascontiguousarray(xp[s, 1024 * h:1024 * (h + 1)])
        m["x_halo"] = meta if h == 0 else np.ascontiguousarray(xp[s, 1008:1024])
        m["x_smp"] = np.ascontiguousarray(xs[16 * c:16 * (c + 1)].reshape(128, D))
        m["flag"] = np.full((128, 1), float(h), np.float32)
        m["st_pool"] = np.ascontiguousarray(sp[16 * c:16 * (c + 1)])
        m["st_C"] = np.ascontiguousarray(sC[16 * c:16 * (c + 1)])
        m["st_n"] = np.ascontiguousarray(sn[16 * c:16 * (c + 1)].reshape(64, HD))
        m["st_m"] = np.ascontiguousarray(sm[16 * c:16 * (c + 1)])
        in_maps.append(m)
    nc = _get_nc()
    res = run_bass_kernel_spmd(nc, in_maps, core_ids=list(range(8))).results
    y_prompt = np.stack([np.concatenate([res[2 * s]["y_own"], res[2 * s + 1]["y_own"]], axis=0) for s in range(4)])
    y_sample = np.concatenate([r["y_smp"] for r in res], axis=0).reshape(128, 8, D)
    pool_pp = np.stack([res[2 * s + 1]["pool_p"] for s in range(4)])[None]
    C_pp = np.stack([res[2 * s + 1]["C_p"] for s in range(4)])[None]
    n_pp = np.stack([res[2 * s + 1]["n_p"] for s in range(4)])[None]
    m_pp = np.stack([res[2 * s + 1]["m_p"].reshape(NH) for s in range(4)])[None]
    pool_ss = np.concatenate([r["pool_s"] for r in res], axis=0)[None]
    C_ss = np.concatenate([r["C_s"] for r in res], axis=0)[None]
    n_ss = np.concatenate([r["n_s"].reshape(16, NH, HD) for r in res], axis=0)[None]
    m_ss = np.concatenate([r["m_s"] for r in res], axis=0)[None]
    outs = (y_prompt, y_sample, pool_pp, C_pp, n_pp, m_pp, pool_ss, C_ss, n_ss, m_ss)
    return tuple(np.ascontiguousarray(o, dtype=np.float32) for o in outs)
```

```python
import math
import numpy as np
import concourse.bass as bass
import concourse.mybir as mybir
from concourse.bass_utils import run_bass_kernel_spmd

F32 = mybir.dt.float32
BF16 = mybir.dt.bfloat16
AF = mybir.ActivationFunctionType
ALU = mybir.AluOpType
AX = mybir.AxisListType

D = 2048
KC = 16
NH = 4
HD = 256
HDE = 257
DFF = 5632
FCH = 44
EPS = 1e-6
LN16 = math.log(16.0)
BIG = 1.0e30
ENGS = ("pe", "act", "dve", "pool", "sp")
STOP = None


class Buf:
    __slots__ = ("ap", "w", "r", "excl")

    def __init__(self, ap, excl=False):
        self.ap = ap
        self.w = None
        self.r = []
        self.excl = excl


class Prog:
    def __init__(self):
        self.streams = {e: [] for e in ENGS}
        self.cnt = {e: 0 for e in ENGS}
        self.seen = {e: {} for e in ENGS}
        self.ndma = {"sp": 20, "pool": 12}
        self.dma_cnt = {q: [0] * n for q, n in self.ndma.items()}
        self.dma_rr = {q: 0 for q in self.ndma}
        self.dead = False
        self.stop = STOP

    def checkpoint(self, k):
        if self.stop is not None and k >= self.stop - 1e-9:
            self.dead = True

    def _waits(self, eng, deps):
        out = []
        for d in deps:
            if d is None:
                continue
            k, c = d
            if self.seen[eng].get(k, 0) >= c:
                continue
            self.seen[eng][k] = c
            out.append((k, c))
        return out

    def _deps(self, eng, reads, writes):
        deps = []
        for b in reads:
            if b.w is not None and not (eng == "pe" and b.w[0] == "pe"):
                deps.append(b.w)
            if b.excl:
                for t in b.r:
                    if t is not None and t[0] != eng:
                        deps.append(t)
        for b in writes:
            for t in [b.w] + b.r:
                if t is not None and not (eng == "pe" and t[0] == "pe"):
                    deps.append(t)
        return deps

    def op(self, eng, fn, reads=(), writes=(), signal=True):
        if self.dead:
            return None
        waits = self._waits(eng, self._deps(eng, reads, writes))
        if signal:
            self.cnt[eng] += 1
            tok = (eng, self.cnt[eng])
        else:
            tok = (eng, self.cnt[eng] + 1)
        self.streams[eng].append((waits, fn, "inc" if signal else None))
        for b in reads:
            b.r.append(tok)
        for b in writes:
            b.w = tok
            b.r = []
        return tok

    def dma(self, q, out_ap, in_ap, reads=(), writes=(), slow=False):
        if self.dead:
            return None
        i = self.dma_rr[q]
        self.dma_rr[q] = (i + 1) % self.ndma[q]
        key = (q, i)
        deps = self._deps("dmaq", reads, writes)
        prev = self.dma_cnt[q][i]
        if prev:
            deps.append((key, prev))
        waits = self._waits(q, deps)
        self.dma_cnt[q][i] = prev + 16
        tok = (key, prev + 16)
        if slow:
            fn = lambda e, o=out_ap, a=in_ap: e.dma_start(out=o, in_=a, allow_slow_non_contiguous=True)
        else:
            fn = lambda e, o=out_ap, a=in_ap: e.dma_start(out=o, in_=a)
        self.streams[q].append((waits, fn, key))
        for b in reads:
            b.r.append(tok)
        for b in writes:
            b.w = tok
            b.r = []
        return tok

    def replay(self, eng, e, sems):
        for waits, fn, sig in self.streams[eng]:
            for k, c in waits:
                e.wait_ge(sems[k], c)
            ins = fn(e)
            if sig == "inc":
                ins.then_inc(sems[eng], 1)
            elif sig is not None:
                ins.then_inc(sems[sig], 16)

    def final_waits(self, eng, e, sems):
        for i, c in enumerate(self.dma_cnt.get(eng, [])):
            if c:
                e.wait_ge(sems[(eng, i)], c)


def build_nc():
    nc = bass.Bass("TRN2", target_bir_lowering=False)
    P = Prog()

    def din(name, shape, dt=F32):
        return nc.dram_tensor(name, list(shape), dt, kind="ExternalInput").ap()

    def dout(name, shape, dt=F32):
        return nc.dram_tensor(name, list(shape), dt, kind="ExternalOutput").ap()

    x_meta = din("x_meta", [16, D])
    x_pre = din("x_pre", [1024, D])
    x_own = din("x_own", [1024, D])
    x_halo = din("x_halo", [16, D])
    x_smp = din("x_smp", [128, D])
    flag_d = din("flag", [128, 1])
    st_pool = din("st_pool", [16, 15, 1024])
    st_C = din("st_C", [16, NH, HD, HD])
    st_n = din("st_n", [64, HD])
    st_m = din("st_m", [16, NH])
    norm_mix_w = din("norm_mix_w", [D])
    w_in = din("w_in", [D, 5128])
    b_ig = din("b_igate", [NH])
    b_fg = din("b_fgate", [NH])
    w_pool = din("w_pool", [4, 256, 256])
    pool_scale = din("pool_scale", [1024])
    mnorm_w = din("mlstm_norm_w", [1024])
    w_out = din("w_out", [D, D])
    norm_ffn_w = din("norm_ffn_w", [D])
    w_gate = din("w_gate", [D, DFF])
    w_up = din("w_up", [D, DFF])
    w_down = din("w_down", [DFF, D])
    norm_final_w = din("norm_final_w", [D])

    y_own = dout("y_own", [1024, D])
    y_smp = dout("y_smp", [128, D])
    pool_p = dout("pool_p", [15, 1024])
    C_p = dout("C_p", [NH, HD, HD])
    n_p = dout("n_p", [NH, HD])
    m_p = dout("m_p", [1, NH])
    pool_s = dout("pool_s", [16, 15, 1024])
    C_s = dout("C_s", [16, NH, HD, HD])
    n_s = dout("n_s", [128, 128])
    m_s = dout("m_s", [16, NH])

    w_in_v = w_in.rearrange("(kc p) n -> p kc n", p=128)
    w_out_v = w_out.rearrange("(kc p) n -> p kc n", p=128)
    w_gate_v = w_gate.rearrange("(kc p) n -> p kc n", p=128)
    w_up_v = w_up.rearrange("(kc p) n -> p kc n", p=128)
    w_down_v = w_down.rearrange("(fc p) n -> p fc n", p=128)

    from contextlib import ExitStack
    es = ExitStack()

    def sb(name, shape, dt):
        return es.enter_context(nc.sbuf_tensor(name, list(shape), dt))

    def ps(name, shape, dt):
        return es.enter_context(nc.psum_tensor(name, list(shape), dt))

    with es:
        Z1 = sb("z1", [128, KC * 1168 // 2], F32)
        Z2 = sb("z2", [128, KC * 1152 // 2], F32)
        Z3 = sb("z3", [128, 8192], F32)
        Z4 = sb("z4", [128, 9 * D], F32)
        Z1b = Z1[:, :].bitcast(BF16)
        Z2b = Z2[:, :].bitcast(BF16)
        Z3b = Z3[:, :].bitcast(BF16)
        Z4BUFS = []
        tl = [9600]

        def zt(nfl, dt=F32, shape=None):
            a0 = tl[0]
            tl[0] += nfl
            assert tl[0] <= 9 * D, tl[0]
            ap = Z4[:, a0:a0 + nfl]
            if dt == BF16:
                ap = ap.bitcast(BF16)
            if shape is not None:
                ap = ap.rearrange(shape[0], **shape[1])
            bb = Buf(ap)
            Z4BUFS.append(bb)
            return bb

        ones32 = Buf(sb("ones32", [128, 128], F32)[:, :])
        ident32 = Buf(sb("ident32", [128, 128], F32)[:, :])
        identb = Buf(sb("identb", [128, 128], BF16)[:, :])
        tri32 = Buf(sb("tri32", [128, 128], F32)[:, :])
        tril32 = Buf(sb("tril32", [128, 128], F32)[:, :])
        maskneg = Buf(sb("maskneg", [128, 128], F32)[:, :])
        maskposb = Buf(sb("maskposb", [128, 128], BF16)[:, :])
        triB32 = Buf(sb("triB32", [128, 128], F32)[:, :])
        trilB32 = Buf(sb("trilB32", [128, 128], F32)[:, :])
        masknegB = Buf(sb("masknegB", [128, 128], F32)[:, :])
        maskposbB = Buf(sb("maskposbB", [128, 128], BF16)[:, :])
        SLm = Buf(sb("SLm", [128, 128], F32)[:, :])
        sel32 = Buf(sb("sel32", [16, 128], F32)[:, :])
        selA = Buf(sb("selA", [16, 128], F32)[:, :])
        sellastT = Buf(sb("sellastT", [16, 128], F32)[:, :])
        blockmask = Buf(sb("blockmask", [128, 16], F32)[:, :])
        blockA = Buf(sb("blockA", [128, 16], F32)[:, :])
        sellast = Buf(sb("sellast", [128, 16], F32)[:, :])
        bias8 = Buf(sb("bias8", [128, 8], F32)[:, :])
        gcol = Buf(sb("gcol", [128, 8], F32)[:, :])
        pscol = Buf(sb("pscol", [128, 8], F32)[:, :])
        flagc = Buf(sb("flagc", [128, 1], F32)[:, :])
        fsc = Buf(sb("fsc", [128, 1], F32)[:, :])
        negBc = Buf(sb("negBc", [128, 4], F32)[:, :])
        Mc = Buf(sb("Mc", [128, 4], F32)[:, :])
        negBm = Buf(sb("negBm", [128, 4], F32)[:, :])
        Mm = Buf(sb("Mm", [128, 4], F32)[:, :])
        C32t = sb("C32", [128, NH, 2, HDE], F32)
        C32 = [Buf(C32t[:, h, :, :]) for h in range(NH)]
        Cbft = sb("Cbf", [128, NH, 2, HDE], BF16)
        Cbf = [Buf(Cbft[:, h, :, :]) for h in range(NH)]
        sst = sb("ss", [128, 2, 8], F32)
        SS = [Buf(sst[:, i, :]) for i in range(2)]
        xnb1 = Buf(sb("xnb", [128, D], BF16)[:, :])
        xnb = [xnb1, xnb1]
        sq = xnb1
        wpl = Buf(sb("wpl", [128, 4, 2, 256], BF16)[:, :, :, :])

        selrepA = zt(1024, BF16, ("p (b t) -> p b t", dict(b=16)))
        wg8 = zt(64, BF16, ("p (k n) -> p k n", dict(k=KC)))
        gwsb = zt(10 * 20 * 4)
        GWl = [Buf(gwsb.ap[:, j * 80:(j + 1) * 80].rearrange("p (k h) -> p k h", k=20)) for j in range(10)]
        Z4BUFS.extend(GWl)
        GW = GWl[0:9] + GWl[0:10]
        dg = zt(512, F32, ("p (h s) -> p h s", dict(h=4)))
        tmpm = zt(512, F32, ("p (h s) -> p h s", dict(h=4)))
        dgM = [zt(128) for i in range(2)]
        DT = [zt(128) for i in range(2)]
        interb = [zt(128) for i in range(2)]
        STb = [zt(64, BF16) for i in range(2)]
        qkT = [zt(256, BF16, ("p (a t) -> p a t", dict(a=4))) for i in range(2)]
        qtl = [zt(128, BF16, ("p (a t) -> p a t", dict(a=2))) for i in range(2)]
        kwb = [zt(128, BF16) for i in range(2)]
        colsb = zt(24)
        COLS = [Buf(colsb.ap[:, i * 12:(i + 1) * 12]) for i in range(2)]
        Z4BUFS.extend(COLS)
        ytm = [zt(128, BF16) for i in range(2)]
        sqs = zt(128, BF16)
        dsel = zt(64, F32, ("p (b h) -> p b h", dict(b=16)))
        decs = zt(64, F32, ("p (b h) -> p b h", dict(b=16)))
        nT = zt(128, F32, ("p (b h c) -> p b h c", dict(b=16, h=4)))
        nS = zt(128, F32, ("p (b h c) -> p b h c", dict(b=16, h=4)))
        nrow = zt(128)
        nin = zt(256)
        msm = zt(4)
        msout = zt(4)
        mpos = zt(4)
        mvec = zt(4)
        qt32 = zt(256, F32, ("p (a t) -> p a t", dict(a=2)))
        qm = [zt(512, F32, ("p (b t) -> p b t", dict(b=4))) for i in range(2)]
        Cout = [zt(260) for i in range(2)]
        sgt = zt(256)

        o4 = [0]

        def z4(nfl):
            a = o4[0]
            o4[0] += nfl
            assert o4[0] <= 9600, o4[0]
            return Z4[:, a:a + nfl]
        xt = [Buf(z4(D)) for _ in range(2)]
        o4[0] = 0
        utm = Buf(z4(1024))
        uT = Buf(z4(8 * 143).rearrange("p (c t) -> p c t", c=8))
        uTs = Buf(z4(8 * 16 * 23).rearrange("p (c b t) -> p c b t", c=8, b=16))
        sw = [Buf(z4(1472)) for _ in range(2)]
        dTb = Buf(z4(512).bitcast(BF16).rearrange("p (c t) -> p c t", c=8))
        sptm = Buf(z4(1024))
        SET_PL = [utm, uT, uTs, sw[0], sw[1], dTb, sptm]
        o4[0] = 0
        q_tm = Buf(z4(9 * 128).bitcast(BF16).rearrange("p (i f) -> p i f", i=9))
        k_tm = Buf(z4(9 * 128).bitcast(BF16).rearrange("p (i f) -> p i f", i=9))
        sigo = Buf(z4(9 * 128).bitcast(BF16).rearrange("p (i f) -> p i f", i=9))
        v_ext = Buf(z4(9 * 130).bitcast(BF16).rearrange("p (i f) -> p i f", i=9))
        Vmask = Buf(z4(16 * 130).bitcast(BF16).rearrange("p (b f) -> p b f", b=16))
        C32s = [Buf(z4(2 * 2 * HDE).rearrange("p (b c v) -> p b c v", b=2, c=2)) for _ in range(2)]
        SET_HD = [q_tm, k_tm, sigo, v_ext, Vmask, C32s[0], C32s[1]]
        Z4BUFS.extend(xt + SET_PL + SET_HD)
        x_new = [Buf(Z4[:, i * D:(i + 1) * D]) for i in range(9)]
        C32mt = Z2[:, 2400:2400 + NH * 2 * HDE].rearrange("p (h c v) -> p h c v", h=NH, c=2)
        C32m = [Buf(C32mt[:, h, :, :]) for h in range(NH)]
        wb = Buf(Z2[:, 4600:4600 + D])
        silt = [Buf(Z2[:, 6400 + i * 512:6400 + (i + 1) * 512]) for i in range(2)]
        W2 = [Buf(Z3b[:, i * 8192:(i + 1) * 8192].rearrange("p (k n) -> p k n", k=KC)) for i in range(2)]

        def fence(bufs):
            P.op("dve", lambda e: e.memset(fsc.ap, 0.0), writes=[fsc] + list(bufs))

        PJ = [Buf(ps("pj%d" % i, [128, 512], F32)[:, :], excl=True) for i in range(2)]
        PT = [Buf(ps("pt%d" % i, [128, 1024], BF16)[:, :], excl=True) for i in range(2)]
        PS = [Buf(ps("ps%d" % i, [128, 512], F32)[:, :], excl=True) for i in range(2)]
        PC = [Buf(ps("pc%d" % i, [128, 512], F32)[:, :], excl=True) for i in range(2)]

        def act(out_b, in_b, func, bias=0.0, scale=1.0, accum=None, extra_r=(), o=None, i=None):
            oa = out_b.ap if o is None else o
            ia = in_b.ap if i is None else i
            kw = {}
            if accum is not None:
                kw["accum_out"] = accum[1]
            w = [out_b] + ([accum[0]] if accum is not None else [])
            return P.op("act", lambda e: e.activation(out=oa, in_=ia, func=func, bias=bias, scale=scale, **kw),
                        reads=[in_b] + list(extra_r), writes=w)

        def tt(eng, out_b, a_b, b_b, op, o=None, a=None, b=None):
            oa = out_b.ap if o is None else o
            aa = a_b.ap if a is None else a
            ba = b_b.ap if b is None else b
            return P.op(eng, lambda e: e.tensor_tensor(out=oa, in0=aa, in1=ba, op=op), reads=[a_b, b_b], writes=[out_b])

        def ts(eng, out_b, a_b, s1, s2, op0, op1=None, o=None, a=None, extra_r=()):
            oa = out_b.ap if o is None else o
            aa = a_b.ap if a is None else a
            if op1 is None:
                f = lambda e: e.tensor_scalar(out=oa, in0=aa, scalar1=s1, scalar2=None, op0=op0)
            else:
                f = lambda e: e.tensor_scalar(out=oa, in0=aa, scalar1=s1, scalar2=s2, op0=op0, op1=op1)
            return P.op(eng, f, reads=[a_b] + list(extra_r), writes=[out_b])

        def stt(out_b, a_b, sc, b_b, op0, op1, o=None, a=None, b=None, extra_r=()):
            oa = out_b.ap if o is None else o
            aa = a_b.ap if a is None else a
            ba = b_b.ap if b is None else b
            return P.op("dve", lambda e: e.scalar_tensor_tensor(out=oa, in0=aa, scalar=sc, in1=ba, op0=op0, op1=op1),
                        reads=[a_b, b_b] + list(extra_r), writes=[out_b])

        def red(out_b, in_b, o, i):
            return P.op("dve", lambda e: e.tensor_reduce(out=o, in_=i, axis=AX.X, op=ALU.max), reads=[in_b], writes=[out_b])

        def cp(eng, out_b, in_b, o=None, i=None):
            oa = out_b.ap if o is None else o
            ia = in_b.ap if i is None else i
            if eng == "act":
                return P.op("act", lambda e: e.copy(out=oa, in_=ia), reads=[in_b], writes=[out_b])
            return P.op(eng, lambda e: e.tensor_copy(out=oa, in_=ia), reads=[in_b], writes=[out_b])

        def mm(out_b, o, l_b, l, r_b, r, start, stop, signal=None):
            if signal is None:
                signal = stop
            return P.op("pe", lambda e: e.matmul(out=o, lhsT=l, rhs=r, start=start, stop=stop),
                        reads=[l_b, r_b], writes=[out_b], signal=signal)

        def tr(out_b, o, in_b, i, id_b, idap, signal):
            return P.op("pe", lambda e: e.transpose(out=o, in_=i, identity=idap), reads=[in_b, id_b], writes=[out_b], signal=signal)

        def memset(eng, b, val, ap=None):
            a = b.ap if ap is None else ap
            return P.op(eng, lambda e: e.memset(a, val), writes=[b])

        def asel(out_b, in_b, pattern, cmp, base, cm, o=None, i=None):
            oa = out_b.ap if o is None else o
            ia = in_b.ap if i is None else i
            return P.op("pool", lambda e: e.affine_select(out=oa, in_=ia, pattern=pattern, compare_op=cmp, fill=0.0,
                                                          base=base, channel_multiplier=cm), reads=[in_b], writes=[out_b])

        memset("pool", ones32, 1.0)
        asel(ident32, ones32, [[1, 128]], ALU.is_equal, 0, -1)
        asel(tri32, ones32, [[1, 128]], ALU.is_ge, 0, -1)
        asel(tril32, ones32, [[-1, 128]], ALU.is_ge, 0, 1)
        cp("pool", identb, ident32)
        ts("pool", maskneg, tril32, BIG, -BIG, ALU.mult, ALU.add)
        ts("pool", maskposb, tri32, -BIG, BIG, ALU.mult, ALU.add)
        asel(selA, ones32, [[1, 128]], ALU.is_ge, 0, -8, i=ones32.ap[0:16, :])
        asel(sel32, selA, [[-1, 128]], ALU.is_ge, 7, 8)
        asel(sellastT, ones32, [[1, 128]], ALU.is_equal, -7, -8, i=ones32.ap[0:16, :])
        asel(blockA, ones32, [[-8, 16]], ALU.is_ge, 0, 1, i=ones32.ap[:, 0:16])
        asel(blockmask, blockA, [[8, 16]], ALU.is_ge, 7, -1)
        asel(sellast, ones32, [[-8, 16]], ALU.is_equal, -7, 1, i=ones32.ap[:, 0:16])
        memset("pool", selrepA, 1.0)
        asel(selrepA, selrepA, [[-8, 16], [1, 128]], ALU.is_ge, 0, 0)
        asel(selrepA, selrepA, [[8, 16], [-1, 128]], ALU.is_ge, 7, 0)
        SELREP = selrepA
        mm(PS[0], PS[0].ap[:, 0:128], sel32, sel32.ap, sel32, sel32.ap, True, True)
        tt("dve", triB32, tri32, PS[0], ALU.mult, b=PS[0].ap[:, 0:128])
        tt("dve", trilB32, tril32, PS[0], ALU.mult, b=PS[0].ap[:, 0:128])
        ts("dve", masknegB, trilB32, BIG, -BIG, ALU.mult, ALU.add)
        ts("dve", maskposbB, triB32, -BIG, BIG, ALU.mult, ALU.add)
        mm(PS[1], PS[1].ap[:, 0:128], sellastT, sellastT.ap, sel32, sel32.ap, True, True)
        cp("dve", SLm, PS[1], i=PS[1].ap[:, 0:128])

        P.dma("sp", bias8.ap[:, 0:4], b_ig.partition_broadcast(128), writes=[bias8])
        P.dma("sp", bias8.ap[:, 4:8], b_fg.partition_broadcast(128), writes=[bias8])
        P.dma("sp", gcol.ap, mnorm_w.rearrange("(c p) -> p c", p=128), writes=[gcol], slow=True)
        P.dma("sp", pscol.ap, pool_scale.rearrange("(c p) -> p c", p=128), writes=[pscol], slow=True)
        P.dma("sp", flagc.ap, flag_d, writes=[flagc])
        P.dma("sp", wb.ap, norm_mix_w.partition_broadcast(128), writes=[wb])
        P.dma("pool", wg8.ap, w_in_v[:, :, 5120:5128], writes=[wg8])
        P.dma("pool", wpl.ap, w_pool.rearrange("g (c p) e -> p g c e", p=128), writes=[wpl])
        for h in range(NH):
            memset("dve", C32[h], 0.0)
            memset("dve", Cbf[h], 0.0)
        memset("dve", negBc, 0.0)
        memset("dve", Mc, 0.0)

        P.checkpoint(1)
        rr = [0]

        def norm_tile(src_ap, L, dstT, tok0, x_in=None, wbuf=None, wap=None):
            i = rr[0] % 2
            rr[0] += 1
            if x_in is None:
                xb = xt[i]
                P.dma("sp", xb.ap[:L, :], src_ap, writes=[xb])
            else:
                xb = x_in
            ssb = SS[i]
            act(sq, xb, AF.Square, accum=(ssb, ssb.ap[:L, 0:1]), o=sq.ap[:L, :], i=xb.ap[:L, :])
            act(ssb, ssb, AF.Ln, bias=EPS, scale=1.0 / D, o=ssb.ap[:L, 1:2], i=ssb.ap[:L, 0:1])
            act(ssb, ssb, AF.Exp, scale=-0.5, o=ssb.ap[:L, 2:3], i=ssb.ap[:L, 1:2])
            if wbuf is None:
                wbuf, wap = wb, wb.ap
            stt(xnb[i], xb, ssb.ap[:L, 2:3], wbuf, ALU.mult, ALU.mult, o=xnb[i].ap[:L, :], a=xb.ap[:L, :], b=wap[:L, :],
                extra_r=[ssb])
            for half in range(2):
                pt = PT[half]
                for j in range(8):
                    kc = half * 8 + j
                    tr(pt, pt.ap[:, j * 128:j * 128 + L], xnb[i], xnb[i].ap[:L, kc * 128:(kc + 1) * 128],
                       identb, identb.ap[:L, :L], signal=(j == 7))
                src = pt.ap.rearrange("p (j t) -> p j t", j=8)[:, :, 0:L]
                cp("act" if half == 0 else "dve", dstT, pt, o=dstT.ap[:, half * 8:half * 8 + 8, tok0:tok0 + L], i=src)
            return xb

        pjr = [0]

        def proj(xT, tok0, L, Wb, Wap, ncols):
            pj = PJ[pjr[0] % 2]
            pjr[0] += 1
            for kc in range(KC):
                mm(pj, pj.ap[:L, 0:ncols], xT, xT.ap[:, kc, tok0:tok0 + L], Wb, Wap[:, kc, 0:ncols], kc == 0, kc == KC - 1)
            return pj

        K_IG, K_Z, K_E, K_SP, K_NEGB, K_A, K_M, K_NEGM, K_EM, K_MPREV, K_MEND, K_W, K_DEC, K_T1, K_CMX, K_T2, K_AADJ, K_CME, K_INTER, K_EM2 = range(20)

        def gate_prep(xT, tok0, L, g, mode):
            G = GW[g]

            def k(kind, rows=L):
                return G.ap[:rows, kind, :]
            pj = PC[1]
            for kc in range(KC):
                mm(pj, pj.ap[:L, 0:8], xT, xT.ap[:, kc, tok0:tok0 + L], wg8, wg8.ap[:, kc, 0:8], kc == 0, kc == KC - 1)
            tt("dve", G, pj, bias8, ALU.add, o=G.ap[:L, 0:2, :].rearrange("p a h -> p (a h)"), a=pj.ap[:L, 0:8], b=bias8.ap[:L, :])
            act(G, G, AF.Exp, scale=-1.0, o=k(K_E), i=k(K_Z))
            act(G, G, AF.Ln, bias=1.0, o=k(K_SP), i=k(K_E))
            TRI = tri32 if mode == "p" else triB32
            MN = maskneg if mode == "p" else masknegB
            p0 = PS[0]
            mm(p0, p0.ap[:L, 0:4], TRI, TRI.ap[:L, :L], G, k(K_SP), True, True)
            if mode == "p":
                mm(p0, p0.ap[:, 4:8], ones32, ones32.ap[:L, :], G, k(K_SP), True, True)
                tt("dve", G, p0, negBc, ALU.add, o=k(K_NEGB), a=p0.ap[:L, 0:4], b=negBc.ap[:L, :])
            else:
                cp("dve", G, p0, o=k(K_NEGB), i=p0.ap[:L, 0:4])
                mm(p0, p0.ap[:, 8:12], sel32, sel32.ap, msm, msm.ap[0:16, :], True, True)
                cp("dve", G, p0, o=k(K_MPREV, 128), i=p0.ap[:, 8:12])
            tt("dve", G, G, G, ALU.add, o=k(K_A), a=k(K_IG), b=k(K_NEGB))
            ts("dve", G, G, -LN16, None, ALU.add, o=k(K_AADJ), a=k(K_A))
            tt("dve", dg, ident32, G, ALU.mult, o=dg.ap[:L, :, :L],
               a=ident32.ap[:L, :L].unsqueeze(1).to_broadcast([L, 4, L]),
               b=k(K_A).unsqueeze(2).to_broadcast([L, 4, L]))
            p1 = PS[1]
            for h in range(NH):
                mm(p1, p1.ap[:, h * 128:h * 128 + L], ones32, ones32.ap[:L, :], dg, dg.ap[:L, h, :L], True, True, signal=(h == NH - 1))
            Ab = p1.ap.rearrange("p (h s) -> p h s", h=4)
            tt("dve", tmpm, p1, MN, ALU.add, o=tmpm.ap[:L, :, :L], a=Ab[:L, :, :L],
               b=MN.ap[:L, :L].unsqueeze(1).to_broadcast([L, 4, L]))
            red(G, tmpm, k(K_CMX), tmpm.ap[:L, :, :L])
            if mode == "p":
                cp("dve", G, Mc, o=k(K_MPREV, 128), i=Mc.ap)
                tt("dve", G, G, Mc, ALU.max, o=k(K_M), a=k(K_CMX), b=Mc.ap[:L, :])
                red(G, p1, k(K_CME, 128), Ab[:, :, :L])
                tt("dve", G, G, Mc, ALU.max, o=k(K_MEND, 128), a=k(K_CME, 128), b=Mc.ap)
            else:
                tt("dve", G, G, G, ALU.max, o=k(K_M), a=k(K_CMX), b=k(K_MPREV))
                mm(p0, p0.ap[:, 12:16], SLm, SLm.ap, G, k(K_M), True, True)
                cp("dve", G, p0, o=k(K_MEND, 128), i=p0.ap[:, 12:16])
            tt("dve", G, G, G, ALU.subtract, o=k(K_NEGM), a=k(K_NEGB), b=k(K_M))
            tt("dve", G, G, G, ALU.subtract, o=k(K_INTER), a=k(K_MPREV), b=k(K_M))
            act(G, G, AF.Exp, o=k(K_INTER), i=k(K_INTER))
            act(G, G, AF.Exp, o=k(K_EM), i=k(K_NEGM))
            act(G, G, AF.Exp, scale=2.0, o=k(K_EM2), i=k(K_NEGM))
            tt("dve", G, G, G, ALU.subtract, o=k(K_T1), a=k(K_AADJ), b=k(K_MEND))
            act(G, G, AF.Exp, o=k(K_W), i=k(K_T1))
            tt("dve", G, G, G, ALU.subtract, o=k(K_T2, 128), a=k(K_MPREV, 128), b=k(K_MEND, 128))
            act(G, G, AF.Exp, o=k(K_DEC, 128), i=k(K_T2, 128))
            if mode == "p":
                cp("dve", Mc, G, i=k(K_MEND, 128))
                tt("dve", negBc, negBc, p0, ALU.add, b=p0.ap[:, 4:8])
            else:
                tt("dve", dsel, sellast, G, ALU.mult, a=sellast.ap.unsqueeze(2).to_broadcast([128, 16, 4]),
                   b=k(K_T2, 128).unsqueeze(1).to_broadcast([128, 16, 4]))
                mm(p1, p1.ap[:, 0:64], ones32, ones32.ap, dsel, dsel.ap.rearrange("p b h -> p (b h)"), True, True)
                act(decs, p1, AF.Exp, o=decs.ap.rearrange("p b h -> p (b h)"), i=p1.ap[:, 0:64])
                ts("dve", mpos, G, -1.0, None, ALU.mult, a=k(K_NEGM))
                mm(p0, p0.ap[0:16, 16:20], sellast, sellast.ap, mpos, mpos.ap, True, True)
                cp("dve", msout, p0, o=msout.ap[0:16, :], i=p0.ap[0:16, 16:20])
                P.dma("sp", m_s, msout.ap[0:16, :], reads=[msout])

        cr = [0]

        def state_update(h, g, L, ktm_ap, vext_ap, k_b, v_b):
            G = GW[g]
            i = cr[0] % 2
            kw = kwb[i]
            ts("dve", kw, k_b, G.ap[:L, K_W, h:h + 1], None, ALU.mult, o=kw.ap[:L, :], a=ktm_ap, extra_r=[G])
            for dc in range(2):
                mm(PC[dc], PC[dc].ap[:, 0:HDE], kw, kw.ap[:L, dc * 128:(dc + 1) * 128], v_b, vext_ap, True, True)
            for dc in range(2):
                stt(C32[h], C32[h], G.ap[:, K_DEC, h:h + 1], PC[dc], ALU.mult, ALU.add,
                    o=C32[h].ap[:, dc, :], a=C32[h].ap[:, dc, :], b=PC[dc].ap[:, 0:HDE], extra_r=[G])
            cp("act", Cbf[h], C32[h])

        xnTp = Buf(Z1b[:, 0:KC * 1040].rearrange("p (k t) -> p k t", k=KC))
        kvp = Buf(Z2b[:, 0:9 * 516].rearrange("p (i f) -> p i f", i=9))
        pre_tiles = [(x_meta, 16, 1024)] + [(x_pre[i * 128:(i + 1) * 128, :], 128, i * 128) for i in range(8)]
        for (src, L, tok0) in pre_tiles:
            norm_tile(src, L, xnTp, tok0)
        P.checkpoint(2)
        def gatesA():
            for gi, (src, L, tok0) in enumerate(pre_tiles):
                gate_prep(xnTp, tok0, L, gi, "p")
                if gi == 0:
                    cp("dve", negBm, negBc)
                    cp("dve", Mm, Mc)
                yield

        kvp_t = [[Buf(kvp.ap[:, gi, :]) for gi in range(9)] for _ in range(1)][0]

        def loadA(h):
            W = W2[h % 2]
            P.dma("pool", W.ap[:, :, 0:256], w_in_v[:, :, 2048 + h * 256:2048 + (h + 1) * 256], writes=[W])
            P.dma("pool", W.ap[:, :, 256:512], w_in_v[:, :, 3072 + h * 256:3072 + (h + 1) * 256], writes=[W])

        def projA(h, dst):
            W = W2[h % 2]
            for gi, (src, L, tok0) in enumerate(pre_tiles):
                pj = PJ[gi % 2]
                for kc in range(KC):
                    mm(pj, pj.ap[:L, 0:512], xnTp, xnTp.ap[:, kc, tok0:tok0 + L], W, W.ap[:, kc, :], kc == 0, kc == KC - 1)
                    if kc % 4 == 3:
                        yield
                cp("act", dst[gi], pj, o=dst[gi].ap[:L, 0:512], i=pj.ap[:L, 0:512])
                yield

        def updA(h, srcb):
            for gi, (src, L, tok0) in enumerate(pre_tiles):
                G = GW[gi]
                kw = kwb[gi % 2]
                kb = srcb[gi]
                ts("dve", kw, kb, G.ap[:L, K_W, h:h + 1], None, ALU.mult, o=kw.ap[:L, :], a=kb.ap[:L, 0:256], extra_r=[G])
                yield
                for dc in range(2):
                    mm(PC[dc], PC[dc].ap[:, 0:HDE], kw, kw.ap[:L, dc * 128:(dc + 1) * 128], kb, kb.ap[:L, 256:513], True, True)
                yield
                for dc in range(2):
                    stt(C32[h], C32[h], G.ap[:, K_DEC, h:h + 1], PC[dc], ALU.mult, ALU.add,
                        o=C32[h].ap[:, dc, :], a=C32[h].ap[:, dc, :], b=PC[dc].ap[:, 0:HDE], extra_r=[G])
                yield
                if gi == 0:
                    cp("dve", C32m[h], C32[h])

        kvp2 = Buf(Z2b[:, 13312:13312 + 9 * 516].rearrange("p (i f) -> p i f", i=9))
        memset("dve", kvp, 1.0)
        memset("dve", kvp2, 1.0)
        kvA = [[Buf(kvp.ap[:, gi, :]) for gi in range(9)], [Buf(kvp2.ap[:, gi, :]) for gi in range(9)]]
        for lst, par_b in ((kvA[0], kvp), (kvA[1], kvp2)):
            for b_ in lst:
                b_.w = par_b.w
        loadA(0)
        loadA(1)
        interleave0 = None
        gens = [gatesA(), projA(0, kvA[0])]
        while gens:
            for g_ in list(gens):
                try:
                    next(g_)
                except StopIteration:
                    gens.remove(g_)
        for h in range(NH):
            gens = [updA(h, kvA[h % 2])]
            if h + 1 < NH:
                gens.append(projA(h + 1, kvA[(h + 1) % 2]))
            if h + 2 < NH:
                pass
            while gens:
                for g_ in list(gens):
                    try:
                        next(g_)
                    except StopIteration:
                        gens.remove(g_)
            if h + 2 < NH:
                loadA(h + 2)
        for h in range(NH):
            tt("dve", C32[h], C32[h], C32m[h], ALU.subtract)
            stt(C32[h], C32[h], flagc.ap[:, 0:1], C32m[h], ALU.mult, ALU.add, extra_r=[flagc])
            cp("act", Cbf[h], C32[h])
        for (cur, sav) in ((negBc, negBm), (Mc, Mm)):
            tt("dve", cur, cur, sav, ALU.subtract)
            stt(cur, cur, flagc.ap[:, 0:1], sav, ALU.mult, ALU.add, extra_r=[flagc])

        P.checkpoint(4)
        xnT = Buf(Z1b[:, 0:KC * 1168].rearrange("p (k t) -> p k t", k=KC))
        fence([xnTp, xnT])
        yT = Buf(Z2b[:, 0:KC * 1152].rearrange("p (k t) -> p k t", k=KC))
        main_tiles = [(x_own[i * 128:(i + 1) * 128, :], 128, i * 128) for i in range(8)] + [(x_smp, 128, 1024)]
        for (src, L, tok0) in main_tiles + [(x_halo, 16, 1152)]:
            norm_tile(src, L, xnT, tok0)
        P.dma("sp", msm.ap[0:16, :], st_m, writes=[msm])
        P.dma("sp", nin.ap[0:64, :], st_n, writes=[nin])
        for dc in range(2):
            tr(PS[0], PS[0].ap[:, dc * 64:(dc + 1) * 64], nin, nin.ap[0:64, dc * 128:(dc + 1) * 128], ident32, ident32.ap[0:64, 0:64], signal=(dc == 1))
        cp("dve", nT, PS[0], o=nT.ap.rearrange("p b h c -> p c (b h)"), i=PS[0].ap[:, 0:128].rearrange("p (c x) -> p c x", c=2))
        for gi, (src, L, tok0) in enumerate(main_tiles):
            gate_prep(xnT, tok0, L, 9 + gi, "p" if gi < 8 else "s")
        tt("dve", mvec, Mc, negBc, ALU.subtract)
        P.dma("sp", m_p, mvec.ap[0:1, :], reads=[mvec])

        P.checkpoint(5)
        WU = W2
        fence([kvp, kvp2, wb, yT] + C32m + xt + SET_PL + kvA[0] + kvA[1])
        for half in range(2):
            P.dma("pool", WU[half].ap, w_in_v[:, :, half * 512:(half + 1) * 512], writes=[WU[half]])
        sp_rows = st_pool.rearrange("b j c -> (b j) c")
        for blk, (r0, nr) in enumerate(((0, 128), (128, 112))):
            P.dma("sp", sptm.ap[0:nr, :], sp_rows[r0:r0 + nr, :], writes=[sptm])
            for cc in range(8):
                p = PS[cc % 2]
                tr(p, p.ap[:, 0:nr], sptm, sptm.ap[0:nr, cc * 128:(cc + 1) * 128], ident32, ident32.ap[0:nr, 0:nr], signal=True)
                eng = "dve" if cc % 2 else "act"
                if blk == 0:
                    cp(eng, uTs, p, o=uTs.ap[:, cc, 0:8, 0:15], i=p.ap[:, 0:120].rearrange("p (b j) -> p b j", b=8))
                    cp(eng, uTs, p, o=uTs.ap[:, cc, 8, 0:8], i=p.ap[:, 120:128])
                else:
                    cp(eng, uTs, p, o=uTs.ap[:, cc, 8, 8:15], i=p.ap[:, 0:7])
                    cp(eng, uTs, p, o=uTs.ap[:, cc, 9:16, 0:15], i=p.ap[:, 7:112].rearrange("p (b j) -> p b j", b=7))

        def pool_group(U, shp, g, o_ap):
            nd = len(shp)
            T = shp[-1]

            def v(b, t0, t1):
                if nd == 1:
                    return b[:, :, t0:t1]
                return b[:, :, :, t0:t1]
            if nd == 1:
                sv = [sw[i].ap[:, 0:2 * T].rearrange("p (c t) -> p c t", c=2) for i in range(2)]
            else:
                sv = [sw[i].ap[:, 0:2 * shp[0] * T].rearrange("p (c b t) -> p c b t", c=2, b=shp[0]) for i in range(2)]
            cur_b, cur = U, (U.ap[:, 2 * g:2 * g + 2, :] if nd == 1 else U.ap[:, 2 * g:2 * g + 2, :, :])
            for k in range(g + 1):
                sh = 1 << k
                lo = 2 * sh - 1
                nb, nv = sw[k % 2], sv[k % 2]
                tt("dve", nb, cur_b, cur_b, ALU.add, o=v(nv, lo, T), a=v(cur, lo, T), b=v(cur, lo - sh, T - sh))
                cur_b, cur = nb, nv
            wdw = 2 << g
            uv = U.ap[:, 2 * g:2 * g + 2, :] if nd == 1 else U.ap[:, 2 * g:2 * g + 2, :, :]
            stt(dTb, cur_b, 1.0 / wdw, U, ALU.mult, ALU.subtract, o=o_ap, a=v(cur, 15, T), b=v(uv, 15, T))

        def pool_tile(tok0, L, ytok0, hist_mode):
            for half in range(2):
                pj = proj(xnT, tok0, L, WU[half], WU[half].ap, 512)
                cp("act" if half == 0 else "dve", utm, pj, o=utm.ap[:L, half * 512:(half + 1) * 512], i=pj.ap[:L, :])
            if hist_mode == "smp":
                for b in range(16):
                    P.dma("sp", pool_s[b, 7:15, :], utm.ap[b * 8:(b + 1) * 8, :], reads=[utm])
                    P.dma("sp", pool_s[b, 0:7, :], st_pool[b, 8:15, :])
            if hist_mode == "own" and tok0 == 7 * 128:
                P.dma("sp", pool_p, utm.ap[113:128, :], reads=[utm])
            for cc in range(8):
                p = PS[cc % 2]
                tr(p, p.ap[:, 0:L], utm, utm.ap[:L, cc * 128:(cc + 1) * 128], ident32, ident32.ap[:L, :L], signal=True)
                if hist_mode == "halo":
                    cp("dve" if cc % 2 else "act", uT, p, o=uT.ap[:, cc, 0:15], i=p.ap[:, 1:16])
                elif hist_mode == "own":
                    cp("dve" if cc % 2 else "act", uT, p, o=uT.ap[:, cc, 15:143], i=p.ap[:, 0:128])
                else:
                    cp("dve" if cc % 2 else "act", uTs, p, o=uTs.ap[:, cc, :, 15:23], i=p.ap[:, 0:128].rearrange("p (b t) -> p b t", b=16))
            if hist_mode == "halo":
                return
            if hist_mode == "own":
                U, shp, hcol = uT, (143,), 15
            else:
                U, shp, hcol = uTs, (16, 23), 15
            for g in range(4):
                if hist_mode == "own":
                    o_ap = dTb.ap[:, 2 * g:2 * g + 2, :]
                else:
                    o_ap = dTb.ap[:, 2 * g:2 * g + 2, :].rearrange("p c (b t) -> p c b t", b=16)
                pool_group(U, shp, g, o_ap)
            for g in range(4):
                for ec in range(2):
                    p = PC[ec]
                    for c in range(2):
                        mm(p, p.ap[:, 0:128], wpl, wpl.ap[:, g, c, ec * 128:(ec + 1) * 128], dTb, dTb.ap[:, 2 * g + c, :], c == 0, c == 1)
                    ch = 2 * g + ec
                    ts("dve", yT, p, pscol.ap[:, ch:ch + 1], None, ALU.mult, o=yT.ap[:, ch, ytok0:ytok0 + 128], a=p.ap[:, 0:128], extra_r=[pscol])
            if hist_mode == "own":
                cp("dve", sw[0], uT, o=sw[0].ap[:, 0:120].rearrange("p (c t) -> p c t", c=8), i=uT.ap[:, :, 128:143])
                cp("dve", uT, sw[0], o=uT.ap[:, :, 0:15], i=sw[0].ap[:, 0:120].rearrange("p (c t) -> p c t", c=8))

        pool_tile(1152, 16, 0, "halo")
        for i in range(8):
            pool_tile(i * 128, 128, i * 128, "own")
        pool_tile(1024, 128, 1024, "smp")

        P.checkpoint(6)
        q_t = [Buf(q_tm.ap[:, i, :]) for i in range(9)]
        k_t = [Buf(k_tm.ap[:, i, :]) for i in range(9)]
        s_t = [Buf(sigo.ap[:, i, :]) for i in range(9)]
        v_t = [Buf(v_ext.ap[:, i, :]) for i in range(9)]
        Z4BUFS.extend(q_t + k_t + s_t + v_t)

        def front(h, ti, g, mode):
            G = GW[g]
            par = ti % 2
            MP = maskposb if mode == "p" else maskposbB
            pt = PT[par]
            for dc in range(2):
                tr(pt, pt.ap[:, dc * 128:(dc + 1) * 128], k_t[ti], k_t[ti].ap[:, dc * 128:(dc + 1) * 128], identb, identb.ap, signal=False)
            for dc in range(2):
                tr(pt, pt.ap[:, (2 + dc) * 128:(3 + dc) * 128], q_t[ti], q_t[ti].ap[:, dc * 128:(dc + 1) * 128], identb, identb.ap, signal=(dc == 1))
            yield
            cp("act", qkT[par], pt, o=qkT[par].ap.rearrange("p a t -> p (a t)"), i=pt.ap[:, 0:512])
            ts("dve", dgM[par], ident32, G.ap[:, K_M, h:h + 1], None, ALU.mult, extra_r=[G])
            yield
            p = PS[par]
            mm(p, p.ap[:, 0:128], ones32, ones32.ap, dgM[par], dgM[par].ap, True, True)
            mm(p, p.ap[:, 128:256], ones32, ones32.ap, dgM[par], dgM[par].ap, True, False, signal=False)
            mm(p, p.ap[:, 128:256], identb, identb.ap, MP, MP.ap, False, True)
            yield
            act(DT[par], p, AF.Exp, bias=G.ap[:, K_AADJ, h:h + 1], scale=-1.0, i=p.ap[:, 128:256], extra_r=[G])
            yield
            if mode == "p":
                act(interb[par], p, AF.Exp, bias=G.ap[:, K_MPREV, h:h + 1], scale=-1.0, i=p.ap[:, 0:128], extra_r=[G])
            else:
                ts("dve", dgM[par], ident32, G.ap[:, K_INTER, h:h + 1], None, ALU.mult, extra_r=[G])
                mm(p, p.ap[:, 384:512], ones32, ones32.ap, dgM[par], dgM[par].ap, True, True)
                cp("act", interb[par], p, i=p.ap[:, 384:512])
            yield
            for dc in range(2):
                mm(p, p.ap[:, 256:384], qkT[par], qkT[par].ap[:, dc, :], qkT[par], qkT[par].ap[:, 2 + dc, :], dc == 0, dc == 1)
            yield
            tt("dve", STb[par], p, DT[par], ALU.mult, a=p.ap[:, 256:384])
            yield
            ts("dve", kwb[par], k_t[ti], G.ap[:, K_W, h:h + 1], None, ALU.mult, extra_r=[G])
            yield
            if mode == "p":
                tt("dve", qtl[par], qkT[par], interb[par], ALU.mult, a=qkT[par].ap[:, 2:4, :],
                   b=interb[par].ap.unsqueeze(1).to_broadcast([128, 2, 128]))
            else:
                tt("dve", qt32, qkT[par], interb[par], ALU.mult, a=qkT[par].ap[:, 2:4, :],
                   b=interb[par].ap.unsqueeze(1).to_broadcast([128, 2, 128]))
                yield
                tt("dve", Vmask, v_t[ti], blockmask, ALU.mult, o=Vmask.ap[:, :, 0:HDE],
                   a=v_t[ti].ap[:, 0:HDE].unsqueeze(1).to_broadcast([128, 16, HDE]),
                   b=blockmask.ap.unsqueeze(2).to_broadcast([128, 16, HDE]))
            yield

        def back(h, ti, g, mode):
            G = GW[g]
            par = ti % 2
            pn = PN[par]
            vb = v_t[ti]
            if mode == "p":
                mm(pn, pn.ap[:, 0:HDE], STb[par], STb[par].ap, vb, vb.ap[:, 0:HDE], True, False, signal=False)
                for dc in range(2):
                    mm(pn, pn.ap[:, 0:HDE], qtl[par], qtl[par].ap[:, dc, :], Cbf[h], Cbf[h].ap[:, dc, :], False, dc == 1)
                yield
                for dc in range(2):
                    mm(PC[dc], PC[dc].ap[:, 0:HDE], kwb[par], kwb[par].ap[:, dc * 128:(dc + 1) * 128], vb, vb.ap[:, 0:HDE], True, True)
                yield
                for dc in range(2):
                    stt(C32[h], C32[h], G.ap[:, K_DEC, h:h + 1], PC[dc], ALU.mult, ALU.add,
                        o=C32[h].ap[:, dc, :], a=C32[h].ap[:, dc, :], b=PC[dc].ap[:, 0:HDE], extra_r=[G])
                    yield
                cp("act", Cbf[h], C32[h])
                yield
            else:
                mm(pn, pn.ap[:, 0:HDE], STb[par], STb[par].ap, vb, vb.ap[:, 0:HDE], True, False, signal=False)
                for grp in range(8):
                    cs_ = C32s[grp % 2]
                    for bl in range(2):
                        P.dma("sp", cs_.ap[:, bl, :, 0:HD], st_C[grp * 2 + bl, h].rearrange("(c p) v -> p c v", p=128), writes=[cs_])
                    cp("dve", cs_, nT, o=cs_.ap[:, :, :, HD], i=nT.ap[:, grp * 2:(grp + 1) * 2, h, :])
                    qmb = qm[grp % 2]
                    for dc in range(2):
                        tt("dve", qmb, qt32, SELREP, ALU.mult, o=qmb.ap[:, dc * 2:dc * 2 + 2, :],
                           a=qt32.ap[:, dc, :].unsqueeze(1).to_broadcast([128, 2, 128]), b=SELREP.ap[:, grp * 2:(grp + 1) * 2, :])
                    yield
                    for bl in range(2):
                        for dc in range(2):
                            last = (grp == 7 and bl == 1 and dc == 1)
                            mm(pn, pn.ap[:, 0:HDE], qmb, qmb.ap[:, dc * 2 + bl, :], cs_, cs_.ap[:, bl, dc, :], False, last,
                               signal=(last or (bl == 1 and dc == 1)))
                    yield
                    for bl in range(2):
                        b = grp * 2 + bl
                        for dc in range(2):
                            pc = PC[dc]
                            co = Cout[dc]
                            mm(pc, pc.ap[:, 0:HDE], kwb[par], kwb[par].ap[:, dc * 128:(dc + 1) * 128], Vmask, Vmask.ap[:, b, 0:HDE], True, True)
                            stt(co, cs_, decs.ap[:, b, h:h + 1], pc, ALU.mult, ALU.add, o=co.ap[:, 0:HDE], a=cs_.ap[:, bl, dc, :], b=pc.ap[:, 0:HDE], extra_r=[decs])
                            P.dma("sp", C_s[b, h, dc * 128:(dc + 1) * 128, :], co.ap[:, 0:HD], reads=[co])
                            cp("act", nS, co, o=nS.ap[:, b, h, dc:dc + 1], i=co.ap[:, HD:HDE])
                            yield
            cl = COLS[par]
            act(sqs, pn, AF.Square, accum=(cl, cl.ap[:, 0:1]), o=sqs.ap[:, 0:HD], i=pn.ap[:, 0:HD])
            P.op("dve", lambda e, o=cl.ap[:, 1:2], i=pn.ap[:, HD:HDE], s2=G.ap[:, K_EM2, h:h + 1]:
                 e.tensor_scalar(out=o, in0=i, scalar1=i, scalar2=s2, op0=ALU.mult, op1=ALU.max), reads=[pn, G], writes=[cl])
            yield
            stt(cl, cl, EPS * HD, cl, ALU.mult, ALU.add, o=cl.ap[:, 2:3], a=cl.ap[:, 1:2], b=cl.ap[:, 0:1])
            yield
            act(cl, cl, AF.Ln, scale=1.0 / HD, o=cl.ap[:, 5:6], i=cl.ap[:, 2:3])
            act(cl, cl, AF.Exp, scale=-0.5, o=cl.ap[:, 7:8], i=cl.ap[:, 5:6])
            yield
            yb = ytm[par]
            stt(yb, pn, cl.ap[:, 7:8], s_t[ti], ALU.mult, ALU.mult, o=yb.ap[:, 0:HD], a=pn.ap[:, 0:HD],
                b=s_t[ti].ap, extra_r=[cl])
            yield
            pt2 = PT[par]
            for c in range(2):
                tr(pt2, pt2.ap[:, 512 + c * 128:512 + (c + 1) * 128], yb, yb.ap[:, c * 128:(c + 1) * 128], identb, identb.ap, signal=(c == 1))
            yield
            for c in range(2):
                ch = 2 * h + c
                ts("dve", yT, pt2, gcol.ap[:, ch:ch + 1], None, ALU.mult, o=yT.ap[:, 8 + ch, ti * 128:(ti + 1) * 128],
                   a=pt2.ap[:, 512 + c * 128:512 + (c + 1) * 128], extra_r=[gcol])
                yield

        def load_head_w(h):
            for half in range(2):
                for j in range(2):
                    c0 = 1024 + (half * 2 + j) * 1024 + h * 256
                    P.dma("pool", W2[half].ap[:, :, j * 256:(j + 1) * 256], w_in_v[:, :, c0:c0 + 256], writes=[W2[half]])

        def projgen(h, ti):
            tok0 = main_tiles[ti][2]
            pj = PJ[0]
            for kc in range(KC):
                mm(pj, pj.ap[:, 0:512], xnT, xnT.ap[:, kc, tok0:tok0 + 128], W2[0], W2[0].ap[:, kc, :], kc == 0, kc == KC - 1)
                if kc % 2 == 1:
                    yield
            cp("act", q_t[ti], pj, i=pj.ap[:, 0:256])
            cp("dve", k_t[ti], pj, i=pj.ap[:, 256:512])
            yield
            pj = PJ[1]
            for kc in range(KC):
                mm(pj, pj.ap[:, 0:512], xnT, xnT.ap[:, kc, tok0:tok0 + 128], W2[1], W2[1].ap[:, kc, :], kc == 0, kc == KC - 1)
                if kc % 2 == 1:
                    yield
            cp("dve", v_t[ti], pj, o=v_t[ti].ap[:, 0:HD], i=pj.ap[:, 0:256])
            act(sgt, pj, AF.Exp, scale=-1.0, i=pj.ap[:, 256:512])
            yield
            ts("dve", sgt, sgt, 1.0, None, ALU.add)
            P.op("dve", lambda e, o=sgt.ap, i=sgt.ap: e.reciprocal(out=o, in_=i), reads=[sgt], writes=[sgt])
            cp("dve", s_t[ti], sgt)
            yield

        def interleave(gens):
            gens = list(gens)
            while gens:
                for g_ in list(gens):
                    try:
                        next(g_)
                    except StopIteration:
                        gens.remove(g_)

        fence(SET_PL + SET_HD + q_t + k_t + s_t + v_t)
        memset("dve", v_ext, 1.0)
        for vt_ in v_t:
            vt_.w = v_ext.w
        memset("dve", Vmask, 0.0)
        PN = PS
        load_head_w(0)
        for ti in range(9):
            interleave([projgen(0, ti)])
        for h in range(NH):
            if h + 1 < NH:
                load_head_w(h + 1)
            for s_ in range(11):
                tasks = []
                if s_ <= 8:
                    tasks.append(front(h, s_, 9 + s_, "p" if s_ < 8 else "s"))
                if 1 <= s_ <= 9:
                    tasks.append(back(h, s_ - 1, 9 + s_ - 1, "p" if s_ - 1 < 8 else "s"))
                if h + 1 < NH and 2 <= s_ <= 10:
                    tasks.append(projgen(h + 1, s_ - 2))
                interleave(tasks)
        P.checkpoint(7)
        for h in range(NH):
            for dc in range(2):
                P.dma("sp", C_p[h, dc * 128:(dc + 1) * 128, :], C32[h].ap[:, dc, 0:HD], reads=[C32[h]])
                P.dma("sp", n_p[h:h + 1, dc * 128:(dc + 1) * 128].rearrange("o p -> p o"), C32[h].ap[:, dc, HD:HDE], reads=[C32[h]], slow=True)
        P.checkpoint(7.5)
        tr(PS[0], PS[0].ap[:, 0:128], nS, nS.ap.rearrange("p b h c -> p (b h c)"), ident32, ident32.ap, signal=True)
        cp("dve", nrow, PS[0], i=PS[0].ap[:, 0:128])
        P.dma("sp", n_s, nrow.ap, reads=[nrow])

        P.checkpoint(8)
        all_src = [x_own[i * 128:(i + 1) * 128, :] for i in range(8)] + [x_smp]
        z4_users = Z4BUFS
        for i in range(9):
            P.dma("sp", x_new[i].ap, all_src[i], reads=[], writes=[x_new[i]] + z4_users)
        WO = [W2[0], W2[1]]
        for cg in range(4):
            W = WO[cg % 2]
            P.dma("pool", W.ap, w_out_v[:, :, cg * 512:(cg + 1) * 512], writes=[W])
            for i in range(9):
                pj = proj(yT, i * 128, 128, W, W.ap, 512)
                tt("dve", x_new[i], x_new[i], pj, ALU.add, o=x_new[i].ap[:, cg * 512:(cg + 1) * 512],
                   a=x_new[i].ap[:, cg * 512:(cg + 1) * 512], b=pj.ap[:, 0:512])
        P.dma("sp", Z3[:, 4096:4096 + D], norm_ffn_w.partition_broadcast(128), writes=[W2[1]])
        xn2T = Buf(Z1b[:, 0:KC * 1152].rearrange("p (k t) -> p k t", k=KC))
        fence([xnT, xn2T])
        for i in range(9):
            norm_tile(None, 128, xn2T, i * 128, x_in=x_new[i], wbuf=W2[1], wap=Z3[:, 4096:4096 + D])

        P.checkpoint(9)
        hT = Buf(Z2b[:, 0:11 * 1152].rearrange("p (f t) -> p f t", f=11))
        WG = [Buf(Z3b[:, i * 2048:(i + 1) * 2048].rearrange("p (k n) -> p k n", k=KC)) for i in range(2)]
        WUp = [Buf(Z3b[:, 4096 + i * 2048:4096 + (i + 1) * 2048].rearrange("p (k n) -> p k n", k=KC)) for i in range(2)]
        WD = [Buf(Z3b[:, 8192 + i * 2816:8192 + (i + 1) * 2816].rearrange("p (f n) -> p f n", f=11)) for i in range(2)]
        fence([W2[0], W2[1], yT, hT] + WG + WUp + WD + silt)
        tgs = [(0, 512), (512, 512), (1024, 128)]
        fr = [0]
        for qd in range(4):
            for fl in range(11):
                fc = qd * 11 + fl
                wgb = WG[fr[0] % 2]
                wub = WUp[fr[0] % 2]
                fr[0] += 1
                P.dma("pool", wgb.ap, w_gate_v[:, :, fc * 128:(fc + 1) * 128], writes=[wgb])
                P.dma("pool", wub.ap, w_up_v[:, :, fc * 128:(fc + 1) * 128], writes=[wub])
                for ti, (t0, tn) in enumerate(tgs):
                    pg = PJ[ti % 2]
                    pu = PS[ti % 2]
                    for kc in range(KC):
                        mm(pg, pg.ap[:, 0:tn], wgb, wgb.ap[:, kc, :], xn2T, xn2T.ap[:, kc, t0:t0 + tn], kc == 0, kc == KC - 1)
                    for kc in range(KC):
                        mm(pu, pu.ap[:, 0:tn], wub, wub.ap[:, kc, :], xn2T, xn2T.ap[:, kc, t0:t0 + tn], kc == 0, kc == KC - 1)
                    st_ = silt[ti % 2]
                    act(st_, pg, AF.Silu, o=st_.ap[:, 0:tn], i=pg.ap[:, 0:tn])
                    tt("dve", hT, st_, pu, ALU.mult, o=hT.ap[:, fl, t0:t0 + tn], a=st_.ap[:, 0:tn], b=pu.ap[:, 0:tn])
            for cg in range(8):
                W = WD[cg % 2]
                P.dma("pool", W.ap, w_down_v[:, qd * 11:(qd + 1) * 11, cg * 256:(cg + 1) * 256], writes=[W])
                for i in range(9):
                    pc = PC[i % 2]
                    for fl in range(11):
                        mm(pc, pc.ap[:, 0:256], hT, hT.ap[:, fl, i * 128:(i + 1) * 128], W, W.ap[:, fl, :], fl == 0, fl == 10)
                    tt("dve", x_new[i], x_new[i], pc, ALU.add, o=x_new[i].ap[:, cg * 256:(cg + 1) * 256],
                       a=x_new[i].ap[:, cg * 256:(cg + 1) * 256], b=pc.ap[:, 0:256])

        P.checkpoint(10)
        wbE = Buf(Z1[:, 0:D])
        fence([xn2T, wbE])
        P.dma("sp", wbE.ap, norm_final_w.partition_broadcast(128), writes=[wbE])
        for i in range(9):
            ssb = SS[i % 2]
            xb = x_new[i]
            act(sq, xb, AF.Square, accum=(ssb, ssb.ap[:, 0:1]))
            act(ssb, ssb, AF.Ln, bias=EPS, scale=1.0 / D, o=ssb.ap[:, 1:2], i=ssb.ap[:, 0:1])
            act(ssb, ssb, AF.Exp, scale=-0.5, o=ssb.ap[:, 2:3], i=ssb.ap[:, 1:2])
            stt(xb, xb, ssb.ap[:, 2:3], wbE, ALU.mult, ALU.mult, extra_r=[ssb])
            dst = y_own[i * 128:(i + 1) * 128, :] if i < 8 else y_smp
            P.dma("sp", dst, xb.ap, reads=[xb])

        sem_es = ExitStack()
        with sem_es:
            sems = {}
            for e in ENGS:
                sems[e] = sem_es.enter_context(nc.semaphore("s_" + e))
            for q, n in P.ndma.items():
                for i in range(n):
                    sems[(q, i)] = sem_es.enter_context(nc.semaphore("d_%s%d" % (q, i)))
            with nc.Block() as block:
                @block.tensor
                def _(e):
                    P.replay("pe", e, sems)

                @block.scalar
                def _(e):
                    P.replay("act", e, sems)

                @block.vector
                def _(e):
                    P.replay("dve", e, sems)

                @block.gpsimd
                def _(e):
                    P.replay("pool", e, sems)
                    P.final_waits("pool", e, sems)

                @block.sync
                def _(e):
                    P.replay("sp", e, sems)
                    P.final_waits("sp", e, sems)
    return nc


_NC = None


def _get_nc():
    global _NC
    if _NC is None:
        _NC = build_nc()
    return _NC


def kernel(x_prompt, x_sample, state_pool, state_mlstm_C, state_mlstm_n, state_mlstm_m, meta_tokens, norm_mix_w, w_in,
           b_igate, b_fgate, w_pool, pool_scale, mlstm_norm_w, w_out, norm_ffn_w, w_gate, w_up, w_down, norm_final_w):
    f = lambda a: np.ascontiguousarray(np.asarray(a, dtype=np.float32))
    xp, xs = f(x_prompt), f(x_sample)
    meta = f(meta_tokens)
    shared = {
        "norm_mix_w": f(norm_mix_w)[0], "w_in": f(w_in)[0], "b_igate": f(b_igate)[0], "b_fgate": f(b_fgate)[0],
        "w_pool": f(w_pool)[0], "pool_scale": f(pool_scale)[0], "mlstm_norm_w": f(mlstm_norm_w)[0], "w_out": f(w_out)[0],
        "norm_ffn_w": f(norm_ffn_w)[0], "w_gate": f(w_gate)[0], "w_up": f(w_up)[0], "w_down": f(w_down)[0],
        "norm_final_w": f(norm_final_w),
    }
    sp, sC, sn, sm = f(state_pool)[0], f(state_mlstm_C)[0], f(state_mlstm_n)[0], f(state_mlstm_m)[0]
    in_maps = []
    for c in range(8):
        s, h = c // 2, c % 2
        m = dict(shared)
        m["x_meta"] = meta
        m["x_pre"] = xp[s, 0:1024] if h == 1 else np.zeros((1024, D), np.float32)
        m["x_own"] = np.ascontiguousarray(xp[s, 1024 * h:1024 * (h + 1)])
        m["x_halo"] = meta if h == 0 else np.ascontiguousarray(xp[s, 1008:1024])
        m["x_smp"] = np.ascontiguousarray(xs[16 * c:16 * (c + 1)].reshape(128, D))
        m["flag"] = np.full((128, 1), float(h), np.float32)
        m["st_pool"] = np.ascontiguousarray(sp[16 * c:16 * (c + 1)])
        m["st_C"] = np.ascontiguousarray(sC[16 * c:16 * (c + 1)])
        m["st_n"] = np.ascontiguousarray(sn[16 * c:16 * (c + 1)].reshape(64, HD))
        m["st_m"] = np.ascontiguousarray(sm[16 * c:16 * (c + 1)])
        in_maps.append(m)
    nc = _get_nc()
    res = run_bass_kernel_spmd(nc, in_maps, core_ids=list(range(8))).results
    y_prompt = np.stack([np.concatenate([res[2 * s]["y_own"], res[2 * s + 1]["y_own"]], axis=0) for s in range(4)])
    y_sample = np.concatenate([r["y_smp"] for r in res], axis=0).reshape(128, 8, D)
    pool_pp = np.stack([res[2 * s + 1]["pool_p"] for s in range(4)])[None]
    C_pp = np.stack([res[2 * s + 1]["C_p"] for s in range(4)])[None]
    n_pp = np.stack([res[2 * s + 1]["n_p"] for s in range(4)])[None]
    m_pp = np.stack([res[2 * s + 1]["m_p"].reshape(NH) for s in range(4)])[None]
    pool_ss = np.concatenate([r["pool_s"] for r in res], axis=0)[None]
    C_ss = np.concatenate([r["C_s"] for r in res], axis=0)[None]
    n_ss = np.concatenate([r["n_s"].reshape(16, NH, HD) for r in res], axis=0)[None]
    m_ss = np.concatenate([r["m_s"] for r in res], axis=0)[None]
    outs = (y_prompt, y_sample, pool_pp, C_pp, n_pp, m_pp, pool_ss, C_ss, n_ss, m_ss)
    return tuple(np.ascontiguousarray(o, dtype=np.float32) for o in outs)
```

```python
import math
import numpy as np
import concourse.bass as bass
import concourse.mybir as mybir
from concourse.bass_utils import run_bass_kernel_spmd

F32 = mybir.dt.float32
BF16 = mybir.dt.bfloat16
AF = mybir.ActivationFunctionType
ALU = mybir.AluOpType
AX = mybir.AxisListType

D = 2048
KC = 16
NH = 4
HD = 256
HDE = 257
DFF = 5632
FCH = 44
EPS = 1e-6
LN16 = math.log(16.0)
BIG = 1.0e30
ENGS = ("pe", "act", "dve", "pool", "sp")
STOP = None


class Buf:
    __slots__ = ("ap", "w", "r", "excl")

    def __init__(self, ap, excl=False):
        self.ap = ap
        self.w = None
        self.r = []
        self.excl = excl


class Prog:
    def __init__(self):
        self.streams = {e: [] for e in ENGS}
        self.cnt = {e: 0 for e in ENGS}
        self.seen = {e: {} for e in ENGS}
        self.ndma = {"sp": 20, "pool": 12}
        self.dma_cnt = {q: [0] * n for q, n in self.ndma.items()}
        self.dma_rr = {q: 0 for q in self.ndma}
        self.dead = False
        self.stop = STOP

    def checkpoint(self, k):
        if self.stop is not None and k >= self.stop - 1e-9:
            self.dead = True

    def _waits(self, eng, deps):
        out = []
        for d in deps:
            if d is None:
                continue
            k, c = d
            if self.seen[eng].get(k, 0) >= c:
                continue
            self.seen[eng][k] = c
            out.append((k, c))
        return out

    def _deps(self, eng, reads, writes):
        deps = []
        for b in reads:
            if b.w is not None and not (eng == "pe" and b.w[0] == "pe"):
                deps.append(b.w)
            if b.excl:
                for t in b.r:
                    if t is not None and t[0] != eng:
                        deps.append(t)
        for b in writes:
            for t in [b.w] + b.r:
                if t is not None and not (eng == "pe" and t[0] == "pe"):
                    deps.append(t)
        return deps

    def op(self, eng, fn, reads=(), writes=(), signal=True):
        if self.dead:
            return None
        waits = self._waits(eng, self._deps(eng, reads, writes))
        if signal:
            self.cnt[eng] += 1
            tok = (eng, self.cnt[eng])
        else:
            tok = (eng, self.cnt[eng] + 1)
        self.streams[eng].append((waits, fn, "inc" if signal else None))
        for b in reads:
            b.r.append(tok)
        for b in writes:
            b.w = tok
            b.r = []
        return tok

    def dma(self, q, out_ap, in_ap, reads=(), writes=(), slow=False):
        if self.dead:
            return None
        i = self.dma_rr[q]
        self.dma_rr[q] = (i + 1) % self.ndma[q]
        key = (q, i)
        deps = self._deps("dmaq", reads, writes)
        prev = self.dma_cnt[q][i]
        if prev:
            deps.append((key, prev))
        waits = self._waits(q, deps)
        self.dma_cnt[q][i] = prev + 16
        tok = (key, prev + 16)
        if slow:
            fn = lambda e, o=out_ap, a=in_ap: e.dma_start(out=o, in_=a, allow_slow_non_contiguous=True)
        else:
            fn = lambda e, o=out_ap, a=in_ap: e.dma_start(out=o, in_=a)
        self.streams[q].append((waits, fn, key))
        for b in reads:
            b.r.append(tok)
        for b in writes:
            b.w = tok
            b.r = []
        return tok

    def replay(self, eng, e, sems):
        for waits, fn, sig in self.streams[eng]:
            for k, c in waits:
                e.wait_ge(sems[k], c)
            ins = fn(e)
            if sig == "inc":
                ins.then_inc(sems[eng], 1)
            elif sig is not None:
                ins.then_inc(sems[sig], 16)

    def final_waits(self, eng, e, sems):
        for i, c in enumerate(self.dma_cnt.get(eng, [])):
            if c:
                e.wait_ge(sems[(eng, i)], c)


def build_nc():
    nc = bass.Bass("TRN2", target_bir_lowering=False)
    P = Prog()

    def din(name, shape, dt=F32):
        return nc.dram_tensor(name, list(shape), dt, kind="ExternalInput").ap()

    def dout(name, shape, dt=F32):
        return nc.dram_tensor(name, list(shape), dt, kind="ExternalOutput").ap()

    x_meta = din("x_meta", [16, D])
    x_pre = din("x_pre", [1024, D])
    x_own = din("x_own", [1024, D])
    x_halo = din("x_halo", [16, D])
    x_smp = din("x_smp", [128, D])
    flag_d = din("flag", [128, 1])
    st_pool = din("st_pool", [16, 15, 1024])
    st_C = din("st_C", [16, NH, HD, HD])
    st_n = din("st_n", [64, HD])
    st_m = din("st_m", [16, NH])
    norm_mix_w = din("norm_mix_w", [D])
    w_in = din("w_in", [D, 5128])
    b_ig = din("b_igate", [NH])
    b_fg = din("b_fgate", [NH])
    w_pool = din("w_pool", [4, 256, 256])
    pool_scale = din("pool_scale", [1024])
    mnorm_w = din("mlstm_norm_w", [1024])
    w_out = din("w_out", [D, D])
    norm_ffn_w = din("norm_ffn_w", [D])
    w_gate = din("w_gate", [D, DFF])
    w_up = din("w_up", [D, DFF])
    w_down = din("w_down", [DFF, D])
    norm_final_w = din("norm_final_w", [D])

    y_own = dout("y_own", [1024, D])
    y_smp = dout("y_smp", [128, D])
    pool_p = dout("pool_p", [15, 1024])
    C_p = dout("C_p", [NH, HD, HD])
    n_p = dout("n_p", [NH, HD])
    m_p = dout("m_p", [1, NH])
    pool_s = dout("pool_s", [16, 15, 1024])
    C_s = dout("C_s", [16, NH, HD, HD])
    n_s = dout("n_s", [128, 128])
    m_s = dout("m_s", [16, NH])

    w_in_v = w_in.rearrange("(kc p) n -> p kc n", p=128)
    w_out_v = w_out.rearrange("(kc p) n -> p kc n", p=128)
    w_gate_v = w_gate.rearrange("(kc p) n -> p kc n", p=128)
    w_up_v = w_up.rearrange("(kc p) n -> p kc n", p=128)
    w_down_v = w_down.rearrange("(fc p) n -> p fc n", p=128)

    from contextlib import ExitStack
    es = ExitStack()

    def sb(name, shape, dt):
        return es.enter_context(nc.sbuf_tensor(name, list(shape), dt))

    def ps(name, shape, dt):
        return es.enter_context(nc.psum_tensor(name, list(shape), dt))

    with es:
        Z1 = sb("z1", [128, KC * 1168 // 2], F32)
        Z2 = sb("z2", [128, KC * 1152 // 2], F32)
        Z3 = sb("z3", [128, 8192], F32)
        Z4 = sb("z4", [128, 9 * D], F32)
        Z1b = Z1[:, :].bitcast(BF16)
        Z2b = Z2[:, :].bitcast(BF16)
        Z3b = Z3[:, :].bitcast(BF16)
        Z4BUFS = []
        tl = [9600]

        def zt(nfl, dt=F32, shape=None):
            a0 = tl[0]
            tl[0] += nfl
            assert tl[0] <= 9 * D, tl[0]
            ap = Z4[:, a0:a0 + nfl]
            if dt == BF16:
                ap = ap.bitcast(BF16)
            if shape is not None:
                ap = ap.rearrange(shape[0], **shape[1])
            bb = Buf(ap)
            Z4BUFS.append(bb)
            return bb

        ones32 = Buf(sb("ones32", [128, 128], F32)[:, :])
        ident32 = Buf(sb("ident32", [128, 128], F32)[:, :])
        identb = Buf(sb("identb", [128, 128], BF16)[:, :])
        tri32 = Buf(sb("tri32", [128, 128], F32)[:, :])
        tril32 = Buf(sb("tril32", [128, 128], F32)[:, :])
        maskneg = Buf(sb("maskneg", [128, 128], F32)[:, :])
        maskposb = Buf(sb("maskposb", [128, 128], BF16)[:, :])
        triB32 = Buf(sb("triB32", [128, 128], F32)[:, :])
        trilB32 = Buf(sb("trilB32", [128, 128], F32)[:, :])
        masknegB = Buf(sb("masknegB", [128, 128], F32)[:, :])
        maskposbB = Buf(sb("maskposbB", [128, 128], BF16)[:, :])
        SLm = Buf(sb("SLm", [128, 128], F32)[:, :])
        sel32 = Buf(sb("sel32", [16, 128], F32)[:, :])
        selA = Buf(sb("selA", [16, 128], F32)[:, :])
        sellastT = Buf(sb("sellastT", [16, 128], F32)[:, :])
        blockmask = Buf(sb("blockmask", [128, 16], F32)[:, :])
        blockA = Buf(sb("blockA", [128, 16], F32)[:, :])
        sellast = Buf(sb("sellast", [128, 16], F32)[:, :])
        bias8 = Buf(sb("bias8", [128, 8], F32)[:, :])
        gcol = Buf(sb("gcol", [128, 8], F32)[:, :])
        pscol = Buf(sb("pscol", [128, 8], F32)[:, :])
        flagc = Buf(sb("flagc", [128, 1], F32)[:, :])
        fsc = Buf(sb("fsc", [128, 1], F32)[:, :])
        negBc = Buf(sb("negBc", [128, 4], F32)[:, :])
        Mc = Buf(sb("Mc", [128, 4], F32)[:, :])
        negBm = Buf(sb("negBm", [128, 4], F32)[:, :])
        Mm = Buf(sb("Mm", [128, 4], F32)[:, :])
        C32t = sb("C32", [128, NH, 2, HDE], F32)
        C32 = [Buf(C32t[:, h, :, :]) for h in range(NH)]
        Cbft = sb("Cbf", [128, NH, 2, HDE], BF16)
        Cbf = [Buf(Cbft[:, h, :, :]) for h in range(NH)]
        sst = sb("ss", [128, 2, 8], F32)
        SS = [Buf(sst[:, i, :]) for i in range(2)]
        xnb1 = Buf(sb("xnb", [128, D], BF16)[:, :])
        xnb = [xnb1, xnb1]
        sq = xnb1
        wpl = Buf(sb("wpl", [128, 4, 2, 256], BF16)[:, :, :, :])

        selrepA = zt(1024, BF16, ("p (b t) -> p b t", dict(b=16)))
        wg8 = zt(64, BF16, ("p (k n) -> p k n", dict(k=KC)))
        gwsb = zt(10 * 20 * 4)
        GWl = [Buf(gwsb.ap[:, j * 80:(j + 1) * 80].rearrange("p (k h) -> p k h", k=20)) for j in range(10)]
        Z4BUFS.extend(GWl)
        GW = GWl[0:9] + GWl[0:10]
        dg = zt(512, F32, ("p (h s) -> p h s", dict(h=4)))
        tmpm = zt(512, F32, ("p (h s) -> p h s", dict(h=4)))
        dgM = [zt(128) for i in range(2)]
        DT = [zt(128) for i in range(2)]
        interb = [zt(128) for i in range(2)]
        STb = [zt(64, BF16) for i in range(2)]
        qkT = [zt(256, BF16, ("p (a t) -> p a t", dict(a=4))) for i in range(2)]
        qtl = [zt(128, BF16, ("p (a t) -> p a t", dict(a=2))) for i in range(2)]
        kwb = [zt(128, BF16) for i in range(2)]
        colsb = zt(24)
        COLS = [Buf(colsb.ap[:, i * 12:(i + 1) * 12]) for i in range(2)]
        Z4BUFS.extend(COLS)
        ytm = [zt(128, BF16) for i in range(2)]
        sqs = zt(128, BF16)
        dsel = zt(64, F32, ("p (b h) -> p b h", dict(b=16)))
        decs = zt(64, F32, ("p (b h) -> p b h", dict(b=16)))
        nT = zt(128, F32, ("p (b h c) -> p b h c", dict(b=16, h=4)))
        nS = zt(128, F32, ("p (b h c) -> p b h c", dict(b=16, h=4)))
        nrow = zt(128)
        nin = zt(256)
        msm = zt(4)
        msout = zt(4)
        mpos = zt(4)
        mvec = zt(4)
        qt32 = zt(256, F32, ("p (a t) -> p a t", dict(a=2)))
        qm = [zt(512, F32, ("p (b t) -> p b t", dict(b=4))) for i in range(2)]
        C32s3 = zt(2 * 2 * HDE, F32, ("p (b c v) -> p b c v", dict(b=2, c=2)))
        sgt = zt(256)

        o4 = [0]

        def z4(nfl):
            a = o4[0]
            o4[0] += nfl
            assert o4[0] <= 9600, o4[0]
            return Z4[:, a:a + nfl]
        xt = [Buf(z4(D)) for _ in range(2)]
        o4[0] = 0
        utm = Buf(z4(1024))
        uT = Buf(z4(8 * 143).rearrange("p (c t) -> p c t", c=8))
        uTs = Buf(z4(8 * 16 * 23).rearrange("p (c b t) -> p c b t", c=8, b=16))
        sw = [Buf(z4(1472)) for _ in range(2)]
        dTb = Buf(z4(512).bitcast(BF16).rearrange("p (c t) -> p c t", c=8))
        sptm = Buf(z4(1024))
        SET_PL = [utm, uT, uTs, sw[0], sw[1], dTb, sptm]
        o4[0] = 0
        q_tm = Buf(z4(9 * 128).bitcast(BF16).rearrange("p (i f) -> p i f", i=9))
        k_tm = Buf(z4(9 * 128).bitcast(BF16).rearrange("p (i f) -> p i f", i=9))
        sigo = Buf(z4(9 * 128).bitcast(BF16).rearrange("p (i f) -> p i f", i=9))
        v_ext = Buf(z4(9 * 130).bitcast(BF16).rearrange("p (i f) -> p i f", i=9))
        Vmask = Buf(z4(16 * 130).bitcast(BF16).rearrange("p (b f) -> p b f", b=16))
        C32s = [Buf(z4(2 * 2 * HDE).rearrange("p (b c v) -> p b c v", b=2, c=2)) for _ in range(2)]
        C32s.append(C32s3)
        SET_HD = [q_tm, k_tm, sigo, v_ext, Vmask, C32s[0], C32s[1]]
        Z4BUFS.extend(xt + SET_PL + SET_HD)
        x_new = [Buf(Z4[:, i * D:(i + 1) * D]) for i in range(9)]
        C32mt = Z2[:, 2400:2400 + NH * 2 * HDE].rearrange("p (h c v) -> p h c v", h=NH, c=2)
        C32m = [Buf(C32mt[:, h, :, :]) for h in range(NH)]
        wb = Buf(Z2[:, 4600:4600 + D])
        silt = [Buf(Z2[:, 6400 + i * 512:6400 + (i + 1) * 512]) for i in range(2)]
        W2 = [Buf(Z3b[:, i * 8192:(i + 1) * 8192].rearrange("p (k n) -> p k n", k=KC)) for i in range(2)]

        def fence(bufs):
            P.op("dve", lambda e: e.memset(fsc.ap, 0.0), writes=[fsc] + list(bufs))

        PJ = [Buf(ps("pj%d" % i, [128, 512], F32)[:, :], excl=True) for i in range(2)]
        PT = [Buf(ps("pt%d" % i, [128, 1024], BF16)[:, :], excl=True) for i in range(2)]
        PS = [Buf(ps("ps%d" % i, [128, 512], F32)[:, :], excl=True) for i in range(2)]
        PC = [Buf(ps("pc%d" % i, [128, 512], F32)[:, :], excl=True) for i in range(2)]

        def act(out_b, in_b, func, bias=0.0, scale=1.0, accum=None, extra_r=(), o=None, i=None):
            oa = out_b.ap if o is None else o
            ia = in_b.ap if i is None else i
            kw = {}
            if accum is not None:
                kw["accum_out"] = accum[1]
            w = [out_b] + ([accum[0]] if accum is not None else [])
            return P.op("act", lambda e: e.activation(out=oa, in_=ia, func=func, bias=bias, scale=scale, **kw),
                        reads=[in_b] + list(extra_r), writes=w)

        def tt(eng, out_b, a_b, b_b, op, o=None, a=None, b=None):
            oa = out_b.ap if o is None else o
            aa = a_b.ap if a is None else a
            ba = b_b.ap if b is None else b
            return P.op(eng, lambda e: e.tensor_tensor(out=oa, in0=aa, in1=ba, op=op), reads=[a_b, b_b], writes=[out_b])

        def ts(eng, out_b, a_b, s1, s2, op0, op1=None, o=None, a=None, extra_r=()):
            oa = out_b.ap if o is None else o
            aa = a_b.ap if a is None else a
            if op1 is None:
                f = lambda e: e.tensor_scalar(out=oa, in0=aa, scalar1=s1, scalar2=None, op0=op0)
            else:
                f = lambda e: e.tensor_scalar(out=oa, in0=aa, scalar1=s1, scalar2=s2, op0=op0, op1=op1)
            return P.op(eng, f, reads=[a_b] + list(extra_r), writes=[out_b])

        def stt(out_b, a_b, sc, b_b, op0, op1, o=None, a=None, b=None, extra_r=()):
            oa = out_b.ap if o is None else o
            aa = a_b.ap if a is None else a
            ba = b_b.ap if b is None else b
            return P.op("dve", lambda e: e.scalar_tensor_tensor(out=oa, in0=aa, scalar=sc, in1=ba, op0=op0, op1=op1),
                        reads=[a_b, b_b] + list(extra_r), writes=[out_b])

        def red(out_b, in_b, o, i):
            return P.op("dve", lambda e: e.tensor_reduce(out=o, in_=i, axis=AX.X, op=ALU.max), reads=[in_b], writes=[out_b])

        def cp(eng, out_b, in_b, o=None, i=None):
            oa = out_b.ap if o is None else o
            ia = in_b.ap if i is None else i
            if eng == "act":
                return P.op("act", lambda e: e.copy(out=oa, in_=ia), reads=[in_b], writes=[out_b])
            return P.op(eng, lambda e: e.tensor_copy(out=oa, in_=ia), reads=[in_b], writes=[out_b])

        def mm(out_b, o, l_b, l, r_b, r, start, stop, signal=None):
            if signal is None:
                signal = stop
            return P.op("pe", lambda e: e.matmul(out=o, lhsT=l, rhs=r, start=start, stop=stop),
                        reads=[l_b, r_b], writes=[out_b], signal=signal)

        def tr(out_b, o, in_b, i, id_b, idap, signal):
            return P.op("pe", lambda e: e.transpose(out=o, in_=i, identity=idap), reads=[in_b, id_b], writes=[out_b], signal=signal)

        def memset(eng, b, val, ap=None):
            a = b.ap if ap is None else ap
            return P.op(eng, lambda e: e.memset(a, val), writes=[b])

        def asel(out_b, in_b, pattern, cmp, base, cm, o=None, i=None):
            oa = out_b.ap if o is None else o
            ia = in_b.ap if i is None else i
            return P.op("pool", lambda e: e.affine_select(out=oa, in_=ia, pattern=pattern, compare_op=cmp, fill=0.0,
                                                          base=base, channel_multiplier=cm), reads=[in_b], writes=[out_b])

        memset("pool", ones32, 1.0)
        asel(ident32, ones32, [[1, 128]], ALU.is_equal, 0, -1)
        asel(tri32, ones32, [[1, 128]], ALU.is_ge, 0, -1)
        asel(tril32, ones32, [[-1, 128]], ALU.is_ge, 0, 1)
        cp("pool", identb, ident32)
        ts("pool", maskneg, tril32, BIG, -BIG, ALU.mult, ALU.add)
        ts("pool", maskposb, tri32, -BIG, BIG, ALU.mult, ALU.add)
        asel(selA, ones32, [[1, 128]], ALU.is_ge, 0, -8, i=ones32.ap[0:16, :])
        asel(sel32, selA, [[-1, 128]], ALU.is_ge, 7, 8)
        asel(sellastT, ones32, [[1, 128]], ALU.is_equal, -7, -8, i=ones32.ap[0:16, :])
        asel(blockA, ones32, [[-8, 16]], ALU.is_ge, 0, 1, i=ones32.ap[:, 0:16])
        asel(blockmask, blockA, [[8, 16]], ALU.is_ge, 7, -1)
        asel(sellast, ones32, [[-8, 16]], ALU.is_equal, -7, 1, i=ones32.ap[:, 0:16])
        memset("pool", selrepA, 1.0)
        asel(selrepA, selrepA, [[-8, 16], [1, 128]], ALU.is_ge, 0, 0)
        asel(selrepA, selrepA, [[8, 16], [-1, 128]], ALU.is_ge, 7, 0)
        SELREP = selrepA
        mm(PS[0], PS[0].ap[:, 0:128], sel32, sel32.ap, sel32, sel32.ap, True, True)
        tt("dve", triB32, tri32, PS[0], ALU.mult, b=PS[0].ap[:, 0:128])
        tt("dve", trilB32, tril32, PS[0], ALU.mult, b=PS[0].ap[:, 0:128])
        ts("dve", masknegB, trilB32, BIG, -BIG, ALU.mult, ALU.add)
        ts("dve", maskposbB, triB32, -BIG, BIG, ALU.mult, ALU.add)
        mm(PS[1], PS[1].ap[:, 0:128], sellastT, sellastT.ap, sel32, sel32.ap, True, True)
        cp("dve", SLm, PS[1], i=PS[1].ap[:, 0:128])

        P.dma("sp", bias8.ap[:, 0:4], b_ig.partition_broadcast(128), writes=[bias8])
        P.dma("sp", bias8.ap[:, 4:8], b_fg.partition_broadcast(128), writes=[bias8])
        P.dma("sp", gcol.ap, mnorm_w.rearrange("(c p) -> p c", p=128), writes=[gcol], slow=True)
        P.dma("sp", pscol.ap, pool_scale.rearrange("(c p) -> p c", p=128), writes=[pscol], slow=True)
        P.dma("sp", flagc.ap, flag_d, writes=[flagc])
        P.dma("sp", wb.ap, norm_mix_w.partition_broadcast(128), writes=[wb])
        P.dma("pool", wg8.ap, w_in_v[:, :, 5120:5128], writes=[wg8])
        P.dma("pool", wpl.ap, w_pool.rearrange("g (c p) e -> p g c e", p=128), writes=[wpl])
        for h in range(NH):
            memset("dve", C32[h], 0.0)
            memset("dve", Cbf[h], 0.0)
        memset("dve", negBc, 0.0)
        memset("dve", Mc, 0.0)

        P.checkpoint(1)
        rr = [0]

        def norm_tile(src_ap, L, dstT, tok0, x_in=None, wbuf=None, wap=None):
            i = rr[0] % 2
            rr[0] += 1
            if x_in is None:
                xb = xt[i]
                P.dma("sp", xb.ap[:L, :], src_ap, writes=[xb])
            else:
                xb = x_in
            ssb = SS[i]
            act(sq, xb, AF.Square, accum=(ssb, ssb.ap[:L, 0:1]), o=sq.ap[:L, :], i=xb.ap[:L, :])
            act(ssb, ssb, AF.Ln, bias=EPS, scale=1.0 / D, o=ssb.ap[:L, 1:2], i=ssb.ap[:L, 0:1])
            act(ssb, ssb, AF.Exp, scale=-0.5, o=ssb.ap[:L, 2:3], i=ssb.ap[:L, 1:2])
            if wbuf is None:
                wbuf, wap = wb, wb.ap
            stt(xnb[i], xb, ssb.ap[:L, 2:3], wbuf, ALU.mult, ALU.mult, o=xnb[i].ap[:L, :], a=xb.ap[:L, :], b=wap[:L, :],
                extra_r=[ssb])
            for half in range(2):
                pt = PT[half]
                for j in range(8):
                    kc = half * 8 + j
                    tr(pt, pt.ap[:, j * 128:j * 128 + L], xnb[i], xnb[i].ap[:L, kc * 128:(kc + 1) * 128],
                       identb, identb.ap[:L, :L], signal=(j == 7))
                src = pt.ap.rearrange("p (j t) -> p j t", j=8)[:, :, 0:L]
                cp("act" if half == 0 else "dve", dstT, pt, o=dstT.ap[:, half * 8:half * 8 + 8, tok0:tok0 + L], i=src)
            return xb

        pjr = [0]

        def proj(xT, tok0, L, Wb, Wap, ncols):
            pj = PJ[pjr[0] % 2]
            pjr[0] += 1
            for kc in range(KC):
                mm(pj, pj.ap[:L, 0:ncols], xT, xT.ap[:, kc, tok0:tok0 + L], Wb, Wap[:, kc, 0:ncols], kc == 0, kc == KC - 1)
            return pj

        K_IG, K_Z, K_E, K_SP, K_NEGB, K_A, K_M, K_NEGM, K_EM, K_MPREV, K_MEND, K_W, K_DEC, K_T1, K_CMX, K_T2, K_AADJ, K_CME, K_INTER, K_EM2 = range(20)

        def gate_prep(xT, tok0, L, g, mode):
            G = GW[g]

            def k(kind, rows=L):
                return G.ap[:rows, kind, :]
            pj = PC[1]
            for kc in range(KC):
                mm(pj, pj.ap[:L, 0:8], xT, xT.ap[:, kc, tok0:tok0 + L], wg8, wg8.ap[:, kc, 0:8], kc == 0, kc == KC - 1)
            tt("dve", G, pj, bias8, ALU.add, o=G.ap[:L, 0:2, :].rearrange("p a h -> p (a h)"), a=pj.ap[:L, 0:8], b=bias8.ap[:L, :])
            act(G, G, AF.Exp, scale=-1.0, o=k(K_E), i=k(K_Z))
            act(G, G, AF.Ln, bias=1.0, o=k(K_SP), i=k(K_E))
            TRI = tri32 if mode == "p" else triB32
            MN = maskneg if mode == "p" else masknegB
            p0 = PS[0]
            mm(p0, p0.ap[:L, 0:4], TRI, TRI.ap[:L, :L], G, k(K_SP), True, True)
            if mode == "p":
                mm(p0, p0.ap[:, 4:8], ones32, ones32.ap[:L, :], G, k(K_SP), True, True)
                tt("dve", G, p0, negBc, ALU.add, o=k(K_NEGB), a=p0.ap[:L, 0:4], b=negBc.ap[:L, :])
            else:
                cp("dve", G, p0, o=k(K_NEGB), i=p0.ap[:L, 0:4])
                mm(p0, p0.ap[:, 8:12], sel32, sel32.ap, msm, msm.ap[0:16, :], True, True)
                cp("dve", G, p0, o=k(K_MPREV, 128), i=p0.ap[:, 8:12])
            tt("dve", G, G, G, ALU.add, o=k(K_A), a=k(K_IG), b=k(K_NEGB))
            ts("dve", G, G, -LN16, None, ALU.add, o=k(K_AADJ), a=k(K_A))
            tt("dve", dg, ident32, G, ALU.mult, o=dg.ap[:L, :, :L],
               a=ident32.ap[:L, :L].unsqueeze(1).to_broadcast([L, 4, L]),
               b=k(K_A).unsqueeze(2).to_broadcast([L, 4, L]))
            p1 = PS[1]
            for h in range(NH):
                mm(p1, p1.ap[:, h * 128:h * 128 + L], ones32, ones32.ap[:L, :], dg, dg.ap[:L, h, :L], True, True, signal=(h == NH - 1))
            Ab = p1.ap.rearrange("p (h s) -> p h s", h=4)
            tt("dve", tmpm, p1, MN, ALU.add, o=tmpm.ap[:L, :, :L], a=Ab[:L, :, :L],
               b=MN.ap[:L, :L].unsqueeze(1).to_broadcast([L, 4, L]))
            red(G, tmpm, k(K_CMX), tmpm.ap[:L, :, :L])
            if mode == "p":
                cp("dve", G, Mc, o=k(K_MPREV, 128), i=Mc.ap)
                tt("dve", G, G, Mc, ALU.max, o=k(K_M), a=k(K_CMX), b=Mc.ap[:L, :])
                red(G, p1, k(K_CME, 128), Ab[:, :, :L])
                tt("dve", G, G, Mc, ALU.max, o=k(K_MEND, 128), a=k(K_CME, 128), b=Mc.ap)
            else:
                tt("dve", G, G, G, ALU.max, o=k(K_M), a=k(K_CMX), b=k(K_MPREV))
                mm(p0, p0.ap[:, 12:16], SLm, SLm.ap, G, k(K_M), True, True)
                cp("dve", G, p0, o=k(K_MEND, 128), i=p0.ap[:, 12:16])
            tt("dve", G, G, G, ALU.subtract, o=k(K_NEGM), a=k(K_NEGB), b=k(K_M))
            tt("dve", G, G, G, ALU.subtract, o=k(K_INTER), a=k(K_MPREV), b=k(K_M))
            act(G, G, AF.Exp, o=k(K_INTER), i=k(K_INTER))
            act(G, G, AF.Exp, o=k(K_EM), i=k(K_NEGM))
            act(G, G, AF.Exp, scale=2.0, o=k(K_EM2), i=k(K_NEGM))
            tt("dve", G, G, G, ALU.subtract, o=k(K_T1), a=k(K_AADJ), b=k(K_MEND))
            act(G, G, AF.Exp, o=k(K_W), i=k(K_T1))
            tt("dve", G, G, G, ALU.subtract, o=k(K_T2, 128), a=k(K_MPREV, 128), b=k(K_MEND, 128))
            act(G, G, AF.Exp, o=k(K_DEC, 128), i=k(K_T2, 128))
            if mode == "p":
                cp("dve", Mc, G, i=k(K_MEND, 128))
                tt("dve", negBc, negBc, p0, ALU.add, b=p0.ap[:, 4:8])
            else:
                tt("dve", dsel, sellast, G, ALU.mult, a=sellast.ap.unsqueeze(2).to_broadcast([128, 16, 4]),
                   b=k(K_T2, 128).unsqueeze(1).to_broadcast([128, 16, 4]))
                mm(p1, p1.ap[:, 0:64], ones32, ones32.ap, dsel, dsel.ap.rearrange("p b h -> p (b h)"), True, True)
                act(decs, p1, AF.Exp, o=decs.ap.rearrange("p b h -> p (b h)"), i=p1.ap[:, 0:64])
                ts("dve", mpos, G, -1.0, None, ALU.mult, a=k(K_NEGM))
                mm(p0, p0.ap[0:16, 16:20], sellast, sellast.ap, mpos, mpos.ap, True, True)
                cp("dve", msout, p0, o=msout.ap[0:16, :], i=p0.ap[0:16, 16:20])
                P.dma("sp", m_s, msout.ap[0:16, :], reads=[msout])

        cr = [0]

        def state_update(h, g, L, ktm_ap, vext_ap, k_b, v_b):
            G = GW[g]
            i = cr[0] % 2
            kw = kwb[i]
            ts("dve", kw, k_b, G.ap[:L, K_W, h:h + 1], None, ALU.mult, o=kw.ap[:L, :], a=ktm_ap, extra_r=[G])
            for dc in range(2):
                mm(PC[dc], PC[dc].ap[:, 0:HDE], kw, kw.ap[:L, dc * 128:(dc + 1) * 128], v_b, vext_ap, True, True)
            for dc in range(2):
                stt(C32[h], C32[h], G.ap[:, K_DEC, h:h + 1], PC[dc], ALU.mult, ALU.add,
                    o=C32[h].ap[:, dc, :], a=C32[h].ap[:, dc, :], b=PC[dc].ap[:, 0:HDE], extra_r=[G])
            cp("act", Cbf[h], C32[h])

        xnTp = Buf(Z1b[:, 0:KC * 1040].rearrange("p (k t) -> p k t", k=KC))
        kvp = Buf(Z2b[:, 0:9 * 516].rearrange("p (i f) -> p i f", i=9))
        pre_tiles = [(x_meta, 16, 1024)] + [(x_pre[i * 128:(i + 1) * 128, :], 128, i * 128) for i in range(8)]
        for (src, L, tok0) in pre_tiles:
            norm_tile(src, L, xnTp, tok0)
        P.checkpoint(2)
        def gatesA():
            for gi, (src, L, tok0) in enumerate(pre_tiles):
                gate_prep(xnTp, tok0, L, gi, "p")
                if gi == 0:
                    cp("dve", negBm, negBc)
                    cp("dve", Mm, Mc)
                yield

        kvp_t = [[Buf(kvp.ap[:, gi, :]) for gi in range(9)] for _ in range(1)][0]

        def loadA(h):
            W = W2[h % 2]
            P.dma("pool", W.ap[:, :, 0:256], w_in_v[:, :, 2048 + h * 256:2048 + (h + 1) * 256], writes=[W])
            P.dma("pool", W.ap[:, :, 256:512], w_in_v[:, :, 3072 + h * 256:3072 + (h + 1) * 256], writes=[W])

        def projA(h, dst):
            W = W2[h % 2]
            for gi, (src, L, tok0) in enumerate(pre_tiles):
                pj = PJ[gi % 2]
                for kc in range(KC):
                    mm(pj, pj.ap[:L, 0:512], xnTp, xnTp.ap[:, kc, tok0:tok0 + L], W, W.ap[:, kc, :], kc == 0, kc == KC - 1)
                    if kc % 4 == 3:
                        yield
                cp("act", dst[gi], pj, o=dst[gi].ap[:L, 0:512], i=pj.ap[:L, 0:512])
                yield

        def updA(h, srcb):
            for gi, (src, L, tok0) in enumerate(pre_tiles):
                G = GW[gi]
                kw = kwb[gi % 2]
                kb = srcb[gi]
                ts("dve", kw, kb, G.ap[:L, K_W, h:h + 1], None, ALU.mult, o=kw.ap[:L, :], a=kb.ap[:L, 0:256], extra_r=[G])
                yield
                for dc in range(2):
                    mm(PC[dc], PC[dc].ap[:, 0:HDE], kw, kw.ap[:L, dc * 128:(dc + 1) * 128], kb, kb.ap[:L, 256:513], True, True)
                yield
                for dc in range(2):
                    stt(C32[h], C32[h], G.ap[:, K_DEC, h:h + 1], PC[dc], ALU.mult, ALU.add,
                        o=C32[h].ap[:, dc, :], a=C32[h].ap[:, dc, :], b=PC[dc].ap[:, 0:HDE], extra_r=[G])
                yield
                if gi == 0:
                    cp("dve", C32m[h], C32[h])

        kvp2 = Buf(Z2b[:, 13312:13312 + 9 * 516].rearrange("p (i f) -> p i f", i=9))
        memset("dve", kvp, 1.0)
        memset("dve", kvp2, 1.0)
        kvA = [[Buf(kvp.ap[:, gi, :]) for gi in range(9)], [Buf(kvp2.ap[:, gi, :]) for gi in range(9)]]
        for lst, par_b in ((kvA[0], kvp), (kvA[1], kvp2)):
            for b_ in lst:
                b_.w = par_b.w
        loadA(0)
        loadA(1)
        interleave0 = None
        gens = [gatesA(), projA(0, kvA[0])]
        while gens:
            for g_ in list(gens):
                try:
                    next(g_)
                except StopIteration:
                    gens.remove(g_)
        for h in range(NH):
            gens = [updA(h, kvA[h % 2])]
            if h + 1 < NH:
                gens.append(projA(h + 1, kvA[(h + 1) % 2]))
            if h + 2 < NH:
                pass
            while gens:
                for g_ in list(gens):
                    try:
                        next(g_)
                    except StopIteration:
                        gens.remove(g_)
            if h + 2 < NH:
                loadA(h + 2)
        for h in range(NH):
            tt("dve", C32[h], C32[h], C32m[h], ALU.subtract)
            stt(C32[h], C32[h], flagc.ap[:, 0:1], C32m[h], ALU.mult, ALU.add, extra_r=[flagc])
            cp("act", Cbf[h], C32[h])
        for (cur, sav) in ((negBc, negBm), (Mc, Mm)):
            tt("dve", cur, cur, sav, ALU.subtract)
            stt(cur, cur, flagc.ap[:, 0:1], sav, ALU.mult, ALU.add, extra_r=[flagc])

        P.checkpoint(4)
        xnT = Buf(Z1b[:, 0:KC * 1168].rearrange("p (k t) -> p k t", k=KC))
        fence([xnTp, xnT])
        yT = Buf(Z2b[:, 0:KC * 1152].rearrange("p (k t) -> p k t", k=KC))
        main_tiles = [(x_own[i * 128:(i + 1) * 128, :], 128, i * 128) for i in range(8)] + [(x_smp, 128, 1024)]
        for (src, L, tok0) in main_tiles + [(x_halo, 16, 1152)]:
            norm_tile(src, L, xnT, tok0)
        P.dma("sp", msm.ap[0:16, :], st_m, writes=[msm])
        P.dma("sp", nin.ap[0:64, :], st_n, writes=[nin])
        for dc in range(2):
            tr(PS[0], PS[0].ap[:, dc * 64:(dc + 1) * 64], nin, nin.ap[0:64, dc * 128:(dc + 1) * 128], ident32, ident32.ap[0:64, 0:64], signal=(dc == 1))
        cp("dve", nT, PS[0], o=nT.ap.rearrange("p b h c -> p c (b h)"), i=PS[0].ap[:, 0:128].rearrange("p (c x) -> p c x", c=2))
        for gi, (src, L, tok0) in enumerate(main_tiles):
            gate_prep(xnT, tok0, L, 9 + gi, "p" if gi < 8 else "s")
        tt("dve", mvec, Mc, negBc, ALU.subtract)
        P.dma("sp", m_p, mvec.ap[0:1, :], reads=[mvec])

        P.checkpoint(5)
        WU = W2
        fence([kvp, kvp2, wb, yT] + C32m + xt + SET_PL + kvA[0] + kvA[1])
        for half in range(2):
            P.dma("pool", WU[half].ap, w_in_v[:, :, half * 512:(half + 1) * 512], writes=[WU[half]])
        sp_rows = st_pool.rearrange("b j c -> (b j) c")
        for blk, (r0, nr) in enumerate(((0, 128), (128, 112))):
            P.dma("sp", sptm.ap[0:nr, :], sp_rows[r0:r0 + nr, :], writes=[sptm])
            for cc in range(8):
                p = PS[cc % 2]
                tr(p, p.ap[:, 0:nr], sptm, sptm.ap[0:nr, cc * 128:(cc + 1) * 128], ident32, ident32.ap[0:nr, 0:nr], signal=True)
                eng = "dve" if cc % 2 else "act"
                if blk == 0:
                    cp(eng, uTs, p, o=uTs.ap[:, cc, 0:8, 0:15], i=p.ap[:, 0:120].rearrange("p (b j) -> p b j", b=8))
                    cp(eng, uTs, p, o=uTs.ap[:, cc, 8, 0:8], i=p.ap[:, 120:128])
                else:
                    cp(eng, uTs, p, o=uTs.ap[:, cc, 8, 8:15], i=p.ap[:, 0:7])
                    cp(eng, uTs, p, o=uTs.ap[:, cc, 9:16, 0:15], i=p.ap[:, 7:112].rearrange("p (b j) -> p b j", b=7))

        def pool_group(U, shp, g, o_ap):
            nd = len(shp)
            T = shp[-1]

            def v(b, t0, t1):
                if nd == 1:
                    return b[:, :, t0:t1]
                return b[:, :, :, t0:t1]
            if nd == 1:
                sv = [sw[i].ap[:, 0:2 * T].rearrange("p (c t) -> p c t", c=2) for i in range(2)]
            else:
                sv = [sw[i].ap[:, 0:2 * shp[0] * T].rearrange("p (c b t) -> p c b t", c=2, b=shp[0]) for i in range(2)]
            cur_b, cur = U, (U.ap[:, 2 * g:2 * g + 2, :] if nd == 1 else U.ap[:, 2 * g:2 * g + 2, :, :])
            for k in range(g + 1):
                sh = 1 << k
                lo = 2 * sh - 1
                nb, nv = sw[k % 2], sv[k % 2]
                tt("dve", nb, cur_b, cur_b, ALU.add, o=v(nv, lo, T), a=v(cur, lo, T), b=v(cur, lo - sh, T - sh))
                cur_b, cur = nb, nv
            wdw = 2 << g
            uv = U.ap[:, 2 * g:2 * g + 2, :] if nd == 1 else U.ap[:, 2 * g:2 * g + 2, :, :]
            stt(dTb, cur_b, 1.0 / wdw, U, ALU.mult, ALU.subtract, o=o_ap, a=v(cur, 15, T), b=v(uv, 15, T))

        def pool_tile(tok0, L, ytok0, hist_mode):
            for half in range(2):
                pj = proj(xnT, tok0, L, WU[half], WU[half].ap, 512)
                cp("act" if half == 0 else "dve", utm, pj, o=utm.ap[:L, half * 512:(half + 1) * 512], i=pj.ap[:L, :])
            if hist_mode == "smp":
                for b in range(16):
                    P.dma("sp", pool_s[b, 7:15, :], utm.ap[b * 8:(b + 1) * 8, :], reads=[utm])
                    P.dma("sp", pool_s[b, 0:7, :], st_pool[b, 8:15, :])
            if hist_mode == "own" and tok0 == 7 * 128:
                P.dma("sp", pool_p, utm.ap[113:128, :], reads=[utm])
            for cc in range(8):
                p = PS[cc % 2]
                tr(p, p.ap[:, 0:L], utm, utm.ap[:L, cc * 128:(cc + 1) * 128], ident32, ident32.ap[:L, :L], signal=True)
                if hist_mode == "halo":
                    cp("dve" if cc % 2 else "act", uT, p, o=uT.ap[:, cc, 0:15], i=p.ap[:, 1:16])
                elif hist_mode == "own":
                    cp("dve" if cc % 2 else "act", uT, p, o=uT.ap[:, cc, 15:143], i=p.ap[:, 0:128])
                else:
                    cp("dve" if cc % 2 else "act", uTs, p, o=uTs.ap[:, cc, :, 15:23], i=p.ap[:, 0:128].rearrange("p (b t) -> p b t", b=16))
            if hist_mode == "halo":
                return
            if hist_mode == "own":
                U, shp, hcol = uT, (143,), 15
            else:
                U, shp, hcol = uTs, (16, 23), 15
            for g in range(4):
                if hist_mode == "own":
                    o_ap = dTb.ap[:, 2 * g:2 * g + 2, :]
                else:
                    o_ap = dTb.ap[:, 2 * g:2 * g + 2, :].rearrange("p c (b t) -> p c b t", b=16)
                pool_group(U, shp, g, o_ap)
            for g in range(4):
                for ec in range(2):
                    p = PC[ec]
                    for c in range(2):
                        mm(p, p.ap[:, 0:128], wpl, wpl.ap[:, g, c, ec * 128:(ec + 1) * 128], dTb, dTb.ap[:, 2 * g + c, :], c == 0, c == 1)
                    ch = 2 * g + ec
                    ts("dve", yT, p, pscol.ap[:, ch:ch + 1], None, ALU.mult, o=yT.ap[:, ch, ytok0:ytok0 + 128], a=p.ap[:, 0:128], extra_r=[pscol])
            if hist_mode == "own":
                cp("dve", sw[0], uT, o=sw[0].ap[:, 0:120].rearrange("p (c t) -> p c t", c=8), i=uT.ap[:, :, 128:143])
                cp("dve", uT, sw[0], o=uT.ap[:, :, 0:15], i=sw[0].ap[:, 0:120].rearrange("p (c t) -> p c t", c=8))

        pool_tile(1152, 16, 0, "halo")
        for i in range(8):
            pool_tile(i * 128, 128, i * 128, "own")
        pool_tile(1024, 128, 1024, "smp")

        P.checkpoint(6)
        q_t = [Buf(q_tm.ap[:, i, :]) for i in range(9)]
        k_t = [Buf(k_tm.ap[:, i, :]) for i in range(9)]
        s_t = [Buf(sigo.ap[:, i, :]) for i in range(9)]
        v_t = [Buf(v_ext.ap[:, i, :]) for i in range(9)]
        Z4BUFS.extend(q_t + k_t + s_t + v_t)

        def front(h, ti, g, mode):
            G = GW[g]
            par = ti % 2
            MP = maskposb if mode == "p" else maskposbB
            pt = PT[par]
            for dc in range(2):
                tr(pt, pt.ap[:, dc * 128:(dc + 1) * 128], k_t[ti], k_t[ti].ap[:, dc * 128:(dc + 1) * 128], identb, identb.ap, signal=False)
            for dc in range(2):
                tr(pt, pt.ap[:, (2 + dc) * 128:(3 + dc) * 128], q_t[ti], q_t[ti].ap[:, dc * 128:(dc + 1) * 128], identb, identb.ap, signal=(dc == 1))
            yield
            cp("act", qkT[par], pt, o=qkT[par].ap.rearrange("p a t -> p (a t)"), i=pt.ap[:, 0:512])
            ts("dve", dgM[par], ident32, G.ap[:, K_M, h:h + 1], None, ALU.mult, extra_r=[G])
            yield
            p = PS[par]
            mm(p, p.ap[:, 0:128], ones32, ones32.ap, dgM[par], dgM[par].ap, True, True)
            mm(p, p.ap[:, 128:256], ones32, ones32.ap, dgM[par], dgM[par].ap, True, False, signal=False)
            mm(p, p.ap[:, 128:256], identb, identb.ap, MP, MP.ap, False, True)
            yield
            act(DT[par], p, AF.Exp, bias=G.ap[:, K_AADJ, h:h + 1], scale=-1.0, i=p.ap[:, 128:256], extra_r=[G])
            yield
            if mode == "p":
                act(interb[par], p, AF.Exp, bias=G.ap[:, K_MPREV, h:h + 1], scale=-1.0, i=p.ap[:, 0:128], extra_r=[G])
            else:
                ts("dve", dgM[par], ident32, G.ap[:, K_INTER, h:h + 1], None, ALU.mult, extra_r=[G])
                mm(p, p.ap[:, 384:512], ones32, ones32.ap, dgM[par], dgM[par].ap, True, True)
                cp("act", interb[par], p, i=p.ap[:, 384:512])
            yield
            for dc in range(2):
                mm(p, p.ap[:, 256:384], qkT[par], qkT[par].ap[:, dc, :], qkT[par], qkT[par].ap[:, 2 + dc, :], dc == 0, dc == 1)
            yield
            tt("dve", STb[par], p, DT[par], ALU.mult, a=p.ap[:, 256:384])
            yield
            ts("dve", kwb[par], k_t[ti], G.ap[:, K_W, h:h + 1], None, ALU.mult, extra_r=[G])
            yield
            if mode == "p":
                tt("dve", qtl[par], qkT[par], interb[par], ALU.mult, a=qkT[par].ap[:, 2:4, :],
                   b=interb[par].ap.unsqueeze(1).to_broadcast([128, 2, 128]))
            else:
                tt("dve", qt32, qkT[par], interb[par], ALU.mult, a=qkT[par].ap[:, 2:4, :],
                   b=interb[par].ap.unsqueeze(1).to_broadcast([128, 2, 128]))
                yield
                tt("dve", Vmask, v_t[ti], blockmask, ALU.mult, o=Vmask.ap[:, :, 0:HDE],
                   a=v_t[ti].ap[:, 0:HDE].unsqueeze(1).to_broadcast([128, 16, HDE]),
                   b=blockmask.ap.unsqueeze(2).to_broadcast([128, 16, HDE]))
            yield

        def back(h, ti, g, mode):
            G = GW[g]
            par = ti % 2
            pn = PN[par]
            vb = v_t[ti]
            if mode == "p":
                mm(pn, pn.ap[:, 0:HDE], STb[par], STb[par].ap, vb, vb.ap[:, 0:HDE], True, False, signal=False)
                for dc in range(2):
                    mm(pn, pn.ap[:, 0:HDE], qtl[par], qtl[par].ap[:, dc, :], Cbf[h], Cbf[h].ap[:, dc, :], False, dc == 1)
                yield
                for dc in range(2):
                    mm(PC[dc], PC[dc].ap[:, 0:HDE], kwb[par], kwb[par].ap[:, dc * 128:(dc + 1) * 128], vb, vb.ap[:, 0:HDE], True, True)
                yield
                for dc in range(2):
                    stt(C32[h], C32[h], G.ap[:, K_DEC, h:h + 1], PC[dc], ALU.mult, ALU.add,
                        o=C32[h].ap[:, dc, :], a=C32[h].ap[:, dc, :], b=PC[dc].ap[:, 0:HDE], extra_r=[G])
                    yield
                cp("act", Cbf[h], C32[h])
                yield
            else:
                mm(pn, pn.ap[:, 0:HDE], STb[par], STb[par].ap, vb, vb.ap[:, 0:HDE], True, False, signal=False)
                for grp in range(8):
                    cs_ = C32s[grp % 3]
                    for bl in range(2):
                        P.dma("sp", cs_.ap[:, bl, :, 0:HD], st_C[grp * 2 + bl, h].rearrange("(c p) v -> p c v", p=128), writes=[cs_])
                    cp("dve", cs_, nT, o=cs_.ap[:, :, :, HD], i=nT.ap[:, grp * 2:(grp + 1) * 2, h, :])
                    qmb = qm[grp % 2]
                    for dc in range(2):
                        tt("dve", qmb, qt32, SELREP, ALU.mult, o=qmb.ap[:, dc * 2:dc * 2 + 2, :],
                           a=qt32.ap[:, dc, :].unsqueeze(1).to_broadcast([128, 2, 128]), b=SELREP.ap[:, grp * 2:(grp + 1) * 2, :])
                    yield
                    for bl in range(2):
                        for dc in range(2):
                            last = (grp == 7 and bl == 1 and dc == 1)
                            mm(pn, pn.ap[:, 0:HDE], qmb, qmb.ap[:, dc * 2 + bl, :], cs_, cs_.ap[:, bl, dc, :], False, last,
                               signal=(last or (bl == 1 and dc == 1)))
                    yield
                    for bl in range(2):
                        b = grp * 2 + bl
                        for dc in range(2):
                            pc = PC[dc]
                            mm(pc, pc.ap[:, 0:HDE], kwb[par], kwb[par].ap[:, dc * 128:(dc + 1) * 128], Vmask, Vmask.ap[:, b, 0:HDE], True, True)
                            stt(cs_, cs_, decs.ap[:, b, h:h + 1], pc, ALU.mult, ALU.add, o=cs_.ap[:, bl, dc, :], a=cs_.ap[:, bl, dc, :], b=pc.ap[:, 0:HDE], extra_r=[decs])
                            yield
                    cp("act", nS, cs_, o=nS.ap[:, grp * 2:(grp + 1) * 2, h, :], i=cs_.ap[:, :, :, HD])
                    for bl in range(2):
                        b = grp * 2 + bl
                        P.dma("sp", C_s[b, h].rearrange("(c p) v -> p c v", p=128), cs_.ap[:, bl, :, 0:HD], reads=[cs_])
                    yield
            cl = COLS[par]
            act(sqs, pn, AF.Square, accum=(cl, cl.ap[:, 0:1]), o=sqs.ap[:, 0:HD], i=pn.ap[:, 0:HD])
            P.op("dve", lambda e, o=cl.ap[:, 1:2], i=pn.ap[:, HD:HDE], s2=G.ap[:, K_EM2, h:h + 1]:
                 e.tensor_scalar(out=o, in0=i, scalar1=i, scalar2=s2, op0=ALU.mult, op1=ALU.max), reads=[pn, G], writes=[cl])
            yield
            stt(cl, cl, EPS * HD, cl, ALU.mult, ALU.add, o=cl.ap[:, 2:3], a=cl.ap[:, 1:2], b=cl.ap[:, 0:1])
            yield
            act(cl, cl, AF.Ln, scale=1.0 / HD, o=cl.ap[:, 5:6], i=cl.ap[:, 2:3])
            act(cl, cl, AF.Exp, scale=-0.5, o=cl.ap[:, 7:8], i=cl.ap[:, 5:6])
            yield
            yb = ytm[par]
            stt(yb, pn, cl.ap[:, 7:8], s_t[ti], ALU.mult, ALU.mult, o=yb.ap[:, 0:HD], a=pn.ap[:, 0:HD],
                b=s_t[ti].ap, extra_r=[cl])
            yield
            pt2 = PT[par]
            for c in range(2):
                tr(pt2, pt2.ap[:, 512 + c * 128:512 + (c + 1) * 128], yb, yb.ap[:, c * 128:(c + 1) * 128], identb, identb.ap, signal=(c == 1))
            yield
            for c in range(2):
                ch = 2 * h + c
                ts("dve", yT, pt2, gcol.ap[:, ch:ch + 1], None, ALU.mult, o=yT.ap[:, 8 + ch, ti * 128:(ti + 1) * 128],
                   a=pt2.ap[:, 512 + c * 128:512 + (c + 1) * 128], extra_r=[gcol])
                yield

        def load_head_w(h):
            for half in range(2):
                for j in range(2):
                    c0 = 1024 + (half * 2 + j) * 1024 + h * 256
                    P.dma("pool", W2[half].ap[:, :, j * 256:(j + 1) * 256], w_in_v[:, :, c0:c0 + 256], writes=[W2[half]])

        def projgen(h, ti):
            tok0 = main_tiles[ti][2]
            pj = PJ[0]
            for kc in range(KC):
                mm(pj, pj.ap[:, 0:512], xnT, xnT.ap[:, kc, tok0:tok0 + 128], W2[0], W2[0].ap[:, kc, :], kc == 0, kc == KC - 1)
                if kc % 2 == 1:
                    yield
            cp("act", q_t[ti], pj, i=pj.ap[:, 0:256])
            cp("dve", k_t[ti], pj, i=pj.ap[:, 256:512])
            yield
            pj = PJ[1]
            for kc in range(KC):
                mm(pj, pj.ap[:, 0:512], xnT, xnT.ap[:, kc, tok0:tok0 + 128], W2[1], W2[1].ap[:, kc, :], kc == 0, kc == KC - 1)
                if kc % 2 == 1:
                    yield
            cp("dve", v_t[ti], pj, o=v_t[ti].ap[:, 0:HD], i=pj.ap[:, 0:256])
            act(sgt, pj, AF.Exp, scale=-1.0, i=pj.ap[:, 256:512])
            yield
            act(sgt, sgt, AF.Ln, bias=1.0)
            yield
            act(s_t[ti], sgt, AF.Exp, scale=-1.0)
            yield

        def interleave(gens):
            gens = list(gens)
            while gens:
                for g_ in list(gens):
                    try:
                        next(g_)
                    except StopIteration:
                        gens.remove(g_)

        fence(SET_PL + SET_HD + q_t + k_t + s_t + v_t)
        memset("dve", v_ext, 1.0)
        for vt_ in v_t:
            vt_.w = v_ext.w
        memset("dve", Vmask, 0.0)
        PN = PS
        load_head_w(0)
        for ti in range(9):
            interleave([projgen(0, ti)])
        for h in range(NH):
            if h + 1 < NH:
                load_head_w(h + 1)
            for s_ in range(11):
                tasks = []
                if s_ <= 8:
                    tasks.append(front(h, s_, 9 + s_, "p" if s_ < 8 else "s"))
                if 1 <= s_ <= 9:
                    tasks.append(back(h, s_ - 1, 9 + s_ - 1, "p" if s_ - 1 < 8 else "s"))
                if h + 1 < NH and 2 <= s_ <= 10:
                    tasks.append(projgen(h + 1, s_ - 2))
                interleave(tasks)
        P.checkpoint(7)
        for h in range(NH):
            for dc in range(2):
                P.dma("sp", C_p[h, dc * 128:(dc + 1) * 128, :], C32[h].ap[:, dc, 0:HD], reads=[C32[h]])
                P.dma("sp", n_p[h:h + 1, dc * 128:(dc + 1) * 128].rearrange("o p -> p o"), C32[h].ap[:, dc, HD:HDE], reads=[C32[h]], slow=True)
        P.checkpoint(7.5)
        tr(PS[0], PS[0].ap[:, 0:128], nS, nS.ap.rearrange("p b h c -> p (b h c)"), ident32, ident32.ap, signal=True)
        cp("dve", nrow, PS[0], i=PS[0].ap[:, 0:128])
        P.dma("sp", n_s, nrow.ap, reads=[nrow])

        P.checkpoint(8)
        all_src = [x_own[i * 128:(i + 1) * 128, :] for i in range(8)] + [x_smp]
        z4_users = Z4BUFS
        for i in range(9):
            P.dma("sp", x_new[i].ap, all_src[i], reads=[], writes=[x_new[i]] + z4_users)
        WO = [W2[0], W2[1]]
        for cg in range(4):
            W = WO[cg % 2]
            P.dma("pool", W.ap, w_out_v[:, :, cg * 512:(cg + 1) * 512], writes=[W])
            for i in range(9):
                pj = proj(yT, i * 128, 128, W, W.ap, 512)
                tt("dve", x_new[i], x_new[i], pj, ALU.add, o=x_new[i].ap[:, cg * 512:(cg + 1) * 512],
                   a=x_new[i].ap[:, cg * 512:(cg + 1) * 512], b=pj.ap[:, 0:512])
        P.dma("sp", Z3[:, 4096:4096 + D], norm_ffn_w.partition_broadcast(128), writes=[W2[1]])
        xn2T = Buf(Z1b[:, 0:KC * 1152].rearrange("p (k t) -> p k t", k=KC))
        fence([xnT, xn2T])
        for i in range(9):
            norm_tile(None, 128, xn2T, i * 128, x_in=x_new[i], wbuf=W2[1], wap=Z3[:, 4096:4096 + D])

        P.checkpoint(9)
        hT = Buf(Z2b[:, 0:11 * 1152].rearrange("p (f t) -> p f t", f=11))
        WG = [Buf(Z3b[:, i * 2048:(i + 1) * 2048].rearrange("p (k n) -> p k n", k=KC)) for i in range(2)]
        WUp = [Buf(Z3b[:, 4096 + i * 2048:4096 + (i + 1) * 2048].rearrange("p (k n) -> p k n", k=KC)) for i in range(2)]
        WD = [Buf(Z3b[:, 8192 + i * 2816:8192 + (i + 1) * 2816].rearrange("p (f n) -> p f n", f=11)) for i in range(2)]
        fence([W2[0], W2[1], yT, hT] + WG + WUp + WD + silt)
        tgs = [(0, 512), (512, 512), (1024, 128)]
        fr = [0]
        for qd in range(4):
            for fl in range(11):
                fc = qd * 11 + fl
                wgb = WG[fr[0] % 2]
                wub = WUp[fr[0] % 2]
                fr[0] += 1
                P.dma("pool", wgb.ap, w_gate_v[:, :, fc * 128:(fc + 1) * 128], writes=[wgb])
                P.dma("pool", wub.ap, w_up_v[:, :, fc * 128:(fc + 1) * 128], writes=[wub])
                for ti, (t0, tn) in enumerate(tgs):
                    pg = PJ[ti % 2]
                    pu = PS[ti % 2]
                    for kc in range(KC):
                        mm(pg, pg.ap[:, 0:tn], wgb, wgb.ap[:, kc, :], xn2T, xn2T.ap[:, kc, t0:t0 + tn], kc == 0, kc == KC - 1)
                    for kc in range(KC):
                        mm(pu, pu.ap[:, 0:tn], wub, wub.ap[:, kc, :], xn2T, xn2T.ap[:, kc, t0:t0 + tn], kc == 0, kc == KC - 1)
                    st_ = silt[ti % 2]
                    act(st_, pg, AF.Silu, o=st_.ap[:, 0:tn], i=pg.ap[:, 0:tn])
                    tt("dve", hT, st_, pu, ALU.mult, o=hT.ap[:, fl, t0:t0 + tn], a=st_.ap[:, 0:tn], b=pu.ap[:, 0:tn])
            for cg in range(8):
                W = WD[cg % 2]
                P.dma("pool", W.ap, w_down_v[:, qd * 11:(qd + 1) * 11, cg * 256:(cg + 1) * 256], writes=[W])
                for i in range(9):
                    pc = PC[i % 2]
                    for fl in range(11):
                        mm(pc, pc.ap[:, 0:256], hT, hT.ap[:, fl, i * 128:(i + 1) * 128], W, W.ap[:, fl, :], fl == 0, fl == 10)
                    tt("dve", x_new[i], x_new[i], pc, ALU.add, o=x_new[i].ap[:, cg * 256:(cg + 1) * 256],
                       a=x_new[i].ap[:, cg * 256:(cg + 1) * 256], b=pc.ap[:, 0:256])

        P.checkpoint(10)
        wbE = Buf(Z1[:, 0:D])
        fence([xn2T, wbE])
        P.dma("sp", wbE.ap, norm_final_w.partition_broadcast(128), writes=[wbE])
        for i in range(9):
            ssb = SS[i % 2]
            xb = x_new[i]
            act(sq, xb, AF.Square, accum=(ssb, ssb.ap[:, 0:1]))
            act(ssb, ssb, AF.Ln, bias=EPS, scale=1.0 / D, o=ssb.ap[:, 1:2], i=ssb.ap[:, 0:1])
            act(ssb, ssb, AF.Exp, scale=-0.5, o=ssb.ap[:, 2:3], i=ssb.ap[:, 1:2])
            stt(xb, xb, ssb.ap[:, 2:3], wbE, ALU.mult, ALU.mult, extra_r=[ssb])
            dst = y_own[i * 128:(i + 1) * 128, :] if i < 8 else y_smp
            P.dma("sp", dst, xb.ap, reads=[xb])

        sem_es = ExitStack()
        with sem_es:
            sems = {}
            for e in ENGS:
                sems[e] = sem_es.enter_context(nc.semaphore("s_" + e))
            for q, n in P.ndma.items():
                for i in range(n):
                    sems[(q, i)] = sem_es.enter_context(nc.semaphore("d_%s%d" % (q, i)))
            with nc.Block() as block:
                @block.tensor
                def _(e):
                    P.replay("pe", e, sems)

                @block.scalar
                def _(e):
                    P.replay("act", e, sems)

                @block.vector
                def _(e):
                    P.replay("dve", e, sems)

                @block.gpsimd
                def _(e):
                    P.replay("pool", e, sems)
                    P.final_waits("pool", e, sems)

                @block.sync
                def _(e):
                    P.replay("sp", e, sems)
                    P.final_waits("sp", e, sems)
    return nc


_NC = None


def _get_nc():
    global _NC
    if _NC is None:
        _NC = build_nc()
    return _NC


def kernel(x_prompt, x_sample, state_pool, state_mlstm_C, state_mlstm_n, state_mlstm_m, meta_tokens, norm_mix_w, w_in,
           b_igate, b_fgate, w_pool, pool_scale, mlstm_norm_w, w_out, norm_ffn_w, w_gate, w_up, w_down, norm_final_w):
    f = lambda a: np.ascontiguousarray(np.asarray(a, dtype=np.float32))
    xp, xs = f(x_prompt), f(x_sample)
    meta = f(meta_tokens)
    shared = {
        "norm_mix_w": f(norm_mix_w)[0], "w_in": f(w_in)[0], "b_igate": f(b_igate)[0], "b_fgate": f(b_fgate)[0],
        "w_pool": f(w_pool)[0], "pool_scale": f(pool_scale)[0], "mlstm_norm_w": f(mlstm_norm_w)[0], "w_out": f(w_out)[0],
        "norm_ffn_w": f(norm_ffn_w)[0], "w_gate": f(w_gate)[0], "w_up": f(w_up)[0], "w_down": f(w_down)[0],
        "norm_final_w": f(norm_final_w),
    }
    sp, sC, sn, sm = f(state_pool)[0], f(state_mlstm_C)[0], f(state_mlstm_n)[0], f(state_mlstm_m)[0]
    in_maps = []
    for c in range(8):
        s, h = c // 2, c % 2
        m = dict(shared)
        m["x_meta"] = meta
        m["x_pre"] = xp[s, 0:1024] if h == 1 else np.zeros((1024, D), np.float32)
        m["x_own"] = np.ascontiguousarray(xp[s, 1024 * h:1024 * (h + 1)])
        m["x_halo"] = meta if h == 0 else np.ascontiguousarray(xp[s, 1008:1024])
        m["x_smp"] = np.ascontiguousarray(xs[16 * c:16 * (c + 1)].reshape(128, D))
        m["flag"] = np.full((128, 1), float(h), np.float32)
        m["st_pool"] = np.ascontiguousarray(sp[16 * c:16 * (c + 1)])
        m["st_C"] = np.ascontiguousarray(sC[16 * c:16 * (c + 1)])
        m["st_n"] = np.ascontiguousarray(sn[16 * c:16 * (c + 1)].reshape(64, HD))
        m["st_m"] = np.ascontiguousarray(sm[16 * c:16 * (c + 1)])
        in_maps.append(m)
    nc = _get_nc()
    res = run_bass_kernel_spmd(nc, in_maps, core_ids=list(range(8))).results
    y_prompt = np.stack([np.concatenate([res[2 * s]["y_own"], res[2 * s + 1]["y_own"]], axis=0) for s in range(4)])
    y_sample = np.concatenate([r["y_smp"] for r in res], axis=0).reshape(128, 8, D)
    pool_pp = np.stack([res[2 * s + 1]["pool_p"] for s in range(4)])[None]
    C_pp = np.stack([res[2 * s + 1]["C_p"] for s in range(4)])[None]
    n_pp = np.stack([res[2 * s + 1]["n_p"] for s in range(4)])[None]
    m_pp = np.stack([res[2 * s + 1]["m_p"].reshape(NH) for s in range(4)])[None]
    pool_ss = np.concatenate([r["pool_s"] for r in res], axis=0)[None]
    C_ss = np.concatenate([r["C_s"] for r in res], axis=0)[None]
    n_ss = np.concatenate([r["n_s"].reshape(16, NH, HD) for r in res], axis=0)[None]
    m_ss = np.concatenate([r["m_s"] for r in res], axis=0)[None]
    outs = (y_prompt, y_sample, pool_pp, C_pp, n_pp, m_pp, pool_ss, C_ss, n_ss, m_ss)
    return tuple(np.ascontiguousarray(o, dtype=np.float32) for o in outs)
```

```python
import math
import numpy as np
import concourse.bass as bass
import concourse.mybir as mybir
from concourse.bass_utils import run_bass_kernel_spmd

F32 = mybir.dt.float32
BF16 = mybir.dt.bfloat16
AF = mybir.ActivationFunctionType
ALU = mybir.AluOpType
AX = mybir.AxisListType

D = 2048
KC = 16
NH = 4
HD = 256
HDE = 257
DFF = 5632
FCH = 44
EPS = 1e-6
LN16 = math.log(16.0)
BIG = 1.0e30
ENGS = ("pe", "act", "dve", "pool", "sp")
STOP = None


class Buf:
    __slots__ = ("ap", "w", "r", "excl")

    def __init__(self, ap, excl=False):
        self.ap = ap
        self.w = None
        self.r = []
        self.excl = excl


class Prog:
    def __init__(self):
        self.streams = {e: [] for e in ENGS}
        self.cnt = {e: 0 for e in ENGS}
        self.seen = {e: {} for e in ENGS}
        self.ndma = {"sp": 20, "pool": 12}
        self.dma_cnt = {q: [0] * n for q, n in self.ndma.items()}
        self.dma_rr = {q: 0 for q in self.ndma}
        self.dead = False
        self.stop = STOP

    def checkpoint(self, k):
        if self.stop is not None and k >= self.stop - 1e-9:
            self.dead = True

    def _waits(self, eng, deps):
        out = []
        for d in deps:
            if d is None:
                continue
            k, c = d
            if self.seen[eng].get(k, 0) >= c:
                continue
            self.seen[eng][k] = c
            out.append((k, c))
        return out

    def _deps(self, eng, reads, writes):
        deps = []
        for b in reads:
            if b.w is not None and not (eng == "pe" and b.w[0] == "pe"):
                deps.append(b.w)
            if b.excl:
                for t in b.r:
                    if t is not None and t[0] != eng:
                        deps.append(t)
        for b in writes:
            for t in [b.w] + b.r:
                if t is not None and not (eng == "pe" and t[0] == "pe"):
                    deps.append(t)
        return deps

    def op(self, eng, fn, reads=(), writes=(), signal=True):
        if self.dead:
            return None
        waits = self._waits(eng, self._deps(eng, reads, writes))
        if signal:
            self.cnt[eng] += 1
            tok = (eng, self.cnt[eng])
        else:
            tok = (eng, self.cnt[eng] + 1)
        self.streams[eng].append((waits, fn, "inc" if signal else None))
        for b in reads:
            b.r.append(tok)
        for b in writes:
            b.w = tok
            b.r = []
        return tok

    def dma(self, q, out_ap, in_ap, reads=(), writes=(), slow=False):
        if self.dead:
            return None
        i = self.dma_rr[q]
        self.dma_rr[q] = (i + 1) % self.ndma[q]
        key = (q, i)
        deps = self._deps("dmaq", reads, writes)
        prev = self.dma_cnt[q][i]
        if prev:
            deps.append((key, prev))
        waits = self._waits(q, deps)
        self.dma_cnt[q][i] = prev + 16
        tok = (key, prev + 16)
        if slow:
            fn = lambda e, o=out_ap, a=in_ap: e.dma_start(out=o, in_=a, allow_slow_non_contiguous=True)
        else:
            fn = lambda e, o=out_ap, a=in_ap: e.dma_start(out=o, in_=a)
        self.streams[q].append((waits, fn, key))
        for b in reads:
            b.r.append(tok)
        for b in writes:
            b.w = tok
            b.r = []
        return tok

    def replay(self, eng, e, sems):
        for waits, fn, sig in self.streams[eng]:
            for k, c in waits:
                e.wait_ge(sems[k], c)
            ins = fn(e)
            if sig == "inc":
                ins.then_inc(sems[eng], 1)
            elif sig is not None:
                ins.then_inc(sems[sig], 16)

    def final_waits(self, eng, e, sems):
        for i, c in enumerate(self.dma_cnt.get(eng, [])):
            if c:
                e.wait_ge(sems[(eng, i)], c)


def build_nc():
    nc = bass.Bass("TRN2", target_bir_lowering=False)
    P = Prog()

    def din(name, shape, dt=F32):
        return nc.dram_tensor(name, list(shape), dt, kind="ExternalInput").ap()

    def dout(name, shape, dt=F32):
        return nc.dram_tensor(name, list(shape), dt, kind="ExternalOutput").ap()

    x_meta = din("x_meta", [16, D])
    x_pre = din("x_pre", [1024, D])
    x_own = din("x_own", [1024, D])
    x_halo = din("x_halo", [16, D])
    x_smp = din("x_smp", [128, D])
    flag_d = din("flag", [128, 1])
    st_pool = din("st_pool", [16, 15, 1024])
    st_C = din("st_C", [16, NH, HD, HD])
    st_n = din("st_n", [64, HD])
    st_m = din("st_m", [16, NH])
    norm_mix_w = din("norm_mix_w", [D])
    w_in = din("w_in", [D, 5128])
    b_ig = din("b_igate", [NH])
    b_fg = din("b_fgate", [NH])
    w_pool = din("w_pool", [4, 256, 256])
    pool_scale = din("pool_scale", [1024])
    mnorm_w = din("mlstm_norm_w", [1024])
    w_out = din("w_out", [D, D])
    norm_ffn_w = din("norm_ffn_w", [D])
    w_gate = din("w_gate", [D, DFF])
    w_up = din("w_up", [D, DFF])
    w_down = din("w_down", [DFF, D])
    norm_final_w = din("norm_final_w", [D])

    y_own = dout("y_own", [1024, D])
    y_smp = dout("y_smp", [128, D])
    pool_p = dout("pool_p", [15, 1024])
    C_p = dout("C_p", [NH, HD, HD])
    n_p = dout("n_p", [NH, HD])
    m_p = dout("m_p", [1, NH])
    pool_s = dout("pool_s", [16, 15, 1024])
    C_s = dout("C_s", [16, NH, HD, HD])
    n_s = dout("n_s", [128, 128])
    m_s = dout("m_s", [16, NH])

    w_in_v = w_in.rearrange("(kc p) n -> p kc n", p=128)
    w_out_v = w_out.rearrange("(kc p) n -> p kc n", p=128)
    w_gate_v = w_gate.rearrange("(kc p) n -> p kc n", p=128)
    w_up_v = w_up.rearrange("(kc p) n -> p kc n", p=128)
    w_down_v = w_down.rearrange("(fc p) n -> p fc n", p=128)

    from contextlib import ExitStack
    es = ExitStack()

    def sb(name, shape, dt):
        return es.enter_context(nc.sbuf_tensor(name, list(shape), dt))

    def ps(name, shape, dt):
        return es.enter_context(nc.psum_tensor(name, list(shape), dt))

    with es:
        Z1 = sb("z1", [128, KC * 1168 // 2], F32)
        Z2 = sb("z2", [128, KC * 1152 // 2], F32)
        Z3 = sb("z3", [128, 8192], F32)
        Z4 = sb("z4", [128, 9 * D], F32)
        Z1b = Z1[:, :].bitcast(BF16)
        Z2b = Z2[:, :].bitcast(BF16)
        Z3b = Z3[:, :].bitcast(BF16)
        Z4BUFS = []
        tl = [9600]

        def zt(nfl, dt=F32, shape=None):
            a0 = tl[0]
            tl[0] += nfl
            assert tl[0] <= 9 * D, tl[0]
            ap = Z4[:, a0:a0 + nfl]
            if dt == BF16:
                ap = ap.bitcast(BF16)
            if shape is not None:
                ap = ap.rearrange(shape[0], **shape[1])
            bb = Buf(ap)
            Z4BUFS.append(bb)
            return bb

        ones32 = Buf(sb("ones32", [128, 128], F32)[:, :])
        ident32 = Buf(sb("ident32", [128, 128], F32)[:, :])
        identb = Buf(sb("identb", [128, 128], BF16)[:, :])
        tri32 = Buf(sb("tri32", [128, 128], F32)[:, :])
        tril32 = Buf(sb("tril32", [128, 128], F32)[:, :])
        maskneg = Buf(sb("maskneg", [128, 128], F32)[:, :])
        maskposb = Buf(sb("maskposb", [128, 128], BF16)[:, :])
        triB32 = Buf(sb("triB32", [128, 128], F32)[:, :])
        trilB32 = Buf(sb("trilB32", [128, 128], F32)[:, :])
        masknegB = Buf(sb("masknegB", [128, 128], F32)[:, :])
        maskposbB = Buf(sb("maskposbB", [128, 128], BF16)[:, :])
        SLm = Buf(sb("SLm", [128, 128], F32)[:, :])
        sel32 = Buf(sb("sel32", [16, 128], F32)[:, :])
        selA = Buf(sb("selA", [16, 128], F32)[:, :])
        sellastT = Buf(sb("sellastT", [16, 128], F32)[:, :])
        blockmask = Buf(sb("blockmask", [128, 16], F32)[:, :])
        blockA = Buf(sb("blockA", [128, 16], F32)[:, :])
        sellast = Buf(sb("sellast", [128, 16], F32)[:, :])
        bias8 = Buf(sb("bias8", [128, 8], F32)[:, :])
        gcol = Buf(sb("gcol", [128, 8], F32)[:, :])
        pscol = Buf(sb("pscol", [128, 8], F32)[:, :])
        flagc = Buf(sb("flagc", [128, 1], F32)[:, :])
        fsc = Buf(sb("fsc", [128, 1], F32)[:, :])
        negBc = Buf(sb("negBc", [128, 4], F32)[:, :])
        Mc = Buf(sb("Mc", [128, 4], F32)[:, :])
        negBm = Buf(sb("negBm", [128, 4], F32)[:, :])
        Mm = Buf(sb("Mm", [128, 4], F32)[:, :])
        C32t = sb("C32", [128, NH, 2, HDE], F32)
        C32 = [Buf(C32t[:, h, :, :]) for h in range(NH)]
        Cbft = sb("Cbf", [128, NH, 2, HDE], BF16)
        Cbf = [Buf(Cbft[:, h, :, :]) for h in range(NH)]
        sst = sb("ss", [128, 2, 8], F32)
        SS = [Buf(sst[:, i, :]) for i in range(2)]
        xnb1 = Buf(sb("xnb", [128, D], BF16)[:, :])
        xnb = [xnb1, xnb1]
        sq = xnb1
        wpl = Buf(sb("wpl", [128, 4, 2, 256], BF16)[:, :, :, :])

        selrepA = zt(1024, BF16, ("p (b t) -> p b t", dict(b=16)))
        wg8 = zt(64, BF16, ("p (k n) -> p k n", dict(k=KC)))
        gwsb = zt(10 * 20 * 4)
        GWl = [Buf(gwsb.ap[:, j * 80:(j + 1) * 80].rearrange("p (k h) -> p k h", k=20)) for j in range(10)]
        Z4BUFS.extend(GWl)
        GW = GWl[0:9] + GWl[0:10]
        dg = zt(512, F32, ("p (h s) -> p h s", dict(h=4)))
        tmpm = zt(512, F32, ("p (h s) -> p h s", dict(h=4)))
        dgM = [zt(128) for i in range(2)]
        DT = [zt(128) for i in range(2)]
        interb = [zt(128) for i in range(2)]
        STb = [zt(64, BF16) for i in range(2)]
        qkT = [zt(256, BF16, ("p (a t) -> p a t", dict(a=4))) for i in range(2)]
        qtl = [zt(128, BF16, ("p (a t) -> p a t", dict(a=2))) for i in range(2)]
        kwb = [zt(128, BF16) for i in range(2)]
        colsb = zt(24)
        COLS = [Buf(colsb.ap[:, i * 12:(i + 1) * 12]) for i in range(2)]
        Z4BUFS.extend(COLS)
        ytm = [zt(128, BF16) for i in range(2)]
        sqs = zt(128, BF16)
        dsel = zt(64, F32, ("p (b h) -> p b h", dict(b=16)))
        decs = zt(64, F32, ("p (b h) -> p b h", dict(b=16)))
        nT = zt(128, F32, ("p (b h c) -> p b h c", dict(b=16, h=4)))
        nS = zt(128, F32, ("p (b h c) -> p b h c", dict(b=16, h=4)))
        nrow = zt(128)
        nin = zt(256)
        msm = zt(4)
        msout = zt(4)
        mpos = zt(4)
        mvec = zt(4)
        qt32 = zt(256, F32, ("p (a t) -> p a t", dict(a=2)))
        qm = [zt(512, F32, ("p (b t) -> p b t", dict(b=4))) for i in range(2)]
        C32s3 = zt(2 * 2 * HDE, F32, ("p (b c v) -> p b c v", dict(b=2, c=2)))
        sgt = zt(256)

        o4 = [0]

        def z4(nfl):
            a = o4[0]
            o4[0] += nfl
            assert o4[0] <= 9600, o4[0]
            return Z4[:, a:a + nfl]
        xt = [Buf(z4(D)) for _ in range(2)]
        o4[0] = 0
        utm = Buf(z4(1024))
        uT = Buf(z4(8 * 143).rearrange("p (c t) -> p c t", c=8))
        uTs = Buf(z4(8 * 16 * 23).rearrange("p (c b t) -> p c b t", c=8, b=16))
        sw = [Buf(z4(1472)) for _ in range(2)]
        dTb = Buf(z4(512).bitcast(BF16).rearrange("p (c t) -> p c t", c=8))
        sptm = Buf(z4(1024))
        SET_PL = [utm, uT, uTs, sw[0], sw[1], dTb, sptm]
        o4[0] = 0
        q_tm = Buf(z4(9 * 128).bitcast(BF16).rearrange("p (i f) -> p i f", i=9))
        k_tm = Buf(z4(9 * 128).bitcast(BF16).rearrange("p (i f) -> p i f", i=9))
        sigo = Buf(z4(9 * 128).bitcast(BF16).rearrange("p (i f) -> p i f", i=9))
        v_ext = Buf(z4(9 * 130).bitcast(BF16).rearrange("p (i f) -> p i f", i=9))
        Vmask = Buf(z4(16 * 130).bitcast(BF16).rearrange("p (b f) -> p b f", b=16))
        C32s = [Buf(z4(2 * 2 * HDE).rearrange("p (b c v) -> p b c v", b=2, c=2)) for _ in range(2)]
        C32s.append(C32s3)
        SET_HD = [q_tm, k_tm, sigo, v_ext, Vmask, C32s[0], C32s[1]]
        Z4BUFS.extend(xt + SET_PL + SET_HD)
        x_new = [Buf(Z4[:, i * D:(i + 1) * D]) for i in range(9)]
        C32mt = Z2[:, 2400:2400 + NH * 2 * HDE].rearrange("p (h c v) -> p h c v", h=NH, c=2)
        C32m = [Buf(C32mt[:, h, :, :]) for h in range(NH)]
        wb = Buf(Z2[:, 4600:4600 + D])
        silt = [Buf(Z2[:, 6400 + i * 512:6400 + (i + 1) * 512]) for i in range(2)]
        W2 = [Buf(Z3b[:, i * 8192:(i + 1) * 8192].rearrange("p (k n) -> p k n", k=KC)) for i in range(2)]

        def fence(bufs):
            P.op("dve", lambda e: e.memset(fsc.ap, 0.0), writes=[fsc] + list(bufs))

        PJ = [Buf(ps("pj%d" % i, [128, 512], F32)[:, :], excl=True) for i in range(2)]
        PT = [Buf(ps("pt%d" % i, [128, 1024], BF16)[:, :], excl=True) for i in range(2)]
        PS = [Buf(ps("ps%d" % i, [128, 512], F32)[:, :], excl=True) for i in range(2)]
        PC = [Buf(ps("pc%d" % i, [128, 512], F32)[:, :], excl=True) for i in range(2)]

        def act(out_b, in_b, func, bias=0.0, scale=1.0, accum=None, extra_r=(), o=None, i=None):
            oa = out_b.ap if o is None else o
            ia = in_b.ap if i is None else i
            kw = {}
            if accum is not None:
                kw["accum_out"] = accum[1]
            w = [out_b] + ([accum[0]] if accum is not None else [])
            return P.op("act", lambda e: e.activation(out=oa, in_=ia, func=func, bias=bias, scale=scale, **kw),
                        reads=[in_b] + list(extra_r), writes=w)

        def tt(eng, out_b, a_b, b_b, op, o=None, a=None, b=None):
            oa = out_b.ap if o is None else o
            aa = a_b.ap if a is None else a
            ba = b_b.ap if b is None else b
            return P.op(eng, lambda e: e.tensor_tensor(out=oa, in0=aa, in1=ba, op=op), reads=[a_b, b_b], writes=[out_b])

        def ts(eng, out_b, a_b, s1, s2, op0, op1=None, o=None, a=None, extra_r=()):
            oa = out_b.ap if o is None else o
            aa = a_b.ap if a is None else a
            if op1 is None:
                f = lambda e: e.tensor_scalar(out=oa, in0=aa, scalar1=s1, scalar2=None, op0=op0)
            else:
                f = lambda e: e.tensor_scalar(out=oa, in0=aa, scalar1=s1, scalar2=s2, op0=op0, op1=op1)
            return P.op(eng, f, reads=[a_b] + list(extra_r), writes=[out_b])

        def stt(out_b, a_b, sc, b_b, op0, op1, o=None, a=None, b=None, extra_r=()):
            oa = out_b.ap if o is None else o
            aa = a_b.ap if a is None else a
            ba = b_b.ap if b is None else b
            return P.op("dve", lambda e: e.scalar_tensor_tensor(out=oa, in0=aa, scalar=sc, in1=ba, op0=op0, op1=op1),
                        reads=[a_b, b_b] + list(extra_r), writes=[out_b])

        def red(out_b, in_b, o, i):
            return P.op("dve", lambda e: e.tensor_reduce(out=o, in_=i, axis=AX.X, op=ALU.max), reads=[in_b], writes=[out_b])

        def cp(eng, out_b, in_b, o=None, i=None):
            oa = out_b.ap if o is None else o
            ia = in_b.ap if i is None else i
            if eng == "act":
                return P.op("act", lambda e: e.copy(out=oa, in_=ia), reads=[in_b], writes=[out_b])
            return P.op(eng, lambda e: e.tensor_copy(out=oa, in_=ia), reads=[in_b], writes=[out_b])

        def mm(out_b, o, l_b, l, r_b, r, start, stop, signal=None):
            if signal is None:
                signal = stop
            return P.op("pe", lambda e: e.matmul(out=o, lhsT=l, rhs=r, start=start, stop=stop),
                        reads=[l_b, r_b], writes=[out_b], signal=signal)

        def tr(out_b, o, in_b, i, id_b, idap, signal):
            return P.op("pe", lambda e: e.transpose(out=o, in_=i, identity=idap), reads=[in_b, id_b], writes=[out_b], signal=signal)

        def memset(eng, b, val, ap=None):
            a = b.ap if ap is None else ap
            return P.op(eng, lambda e: e.memset(a, val), writes=[b])

        def asel(out_b, in_b, pattern, cmp, base, cm, o=None, i=None):
            oa = out_b.ap if o is None else o
            ia = in_b.ap if i is None else i
            return P.op("pool", lambda e: e.affine_select(out=oa, in_=ia, pattern=pattern, compare_op=cmp, fill=0.0,
                                                          base=base, channel_multiplier=cm), reads=[in_b], writes=[out_b])

        memset("pool", ones32, 1.0)
        asel(ident32, ones32, [[1, 128]], ALU.is_equal, 0, -1)
        asel(tri32, ones32, [[1, 128]], ALU.is_ge, 0, -1)
        asel(tril32, ones32, [[-1, 128]], ALU.is_ge, 0, 1)
        cp("pool", identb, ident32)
        ts("pool", maskneg, tril32, BIG, -BIG, ALU.mult, ALU.add)
        ts("pool", maskposb, tri32, -BIG, BIG, ALU.mult, ALU.add)
        asel(selA, ones32, [[1, 128]], ALU.is_ge, 0, -8, i=ones32.ap[0:16, :])
        asel(sel32, selA, [[-1, 128]], ALU.is_ge, 7, 8)
        asel(sellastT, ones32, [[1, 128]], ALU.is_equal, -7, -8, i=ones32.ap[0:16, :])
        asel(blockA, ones32, [[-8, 16]], ALU.is_ge, 0, 1, i=ones32.ap[:, 0:16])
        asel(blockmask, blockA, [[8, 16]], ALU.is_ge, 7, -1)
        asel(sellast, ones32, [[-8, 16]], ALU.is_equal, -7, 1, i=ones32.ap[:, 0:16])
        memset("pool", selrepA, 1.0)
        asel(selrepA, selrepA, [[-8, 16], [1, 128]], ALU.is_ge, 0, 0)
        asel(selrepA, selrepA, [[8, 16], [-1, 128]], ALU.is_ge, 7, 0)
        SELREP = selrepA
        mm(PS[0], PS[0].ap[:, 0:128], sel32, sel32.ap, sel32, sel32.ap, True, True)
        tt("dve", triB32, tri32, PS[0], ALU.mult, b=PS[0].ap[:, 0:128])
        tt("dve", trilB32, tril32, PS[0], ALU.mult, b=PS[0].ap[:, 0:128])
        ts("dve", masknegB, trilB32, BIG, -BIG, ALU.mult, ALU.add)
        ts("dve", maskposbB, triB32, -BIG, BIG, ALU.mult, ALU.add)
        mm(PS[1], PS[1].ap[:, 0:128], sellastT, sellastT.ap, sel32, sel32.ap, True, True)
        cp("dve", SLm, PS[1], i=PS[1].ap[:, 0:128])

        P.dma("sp", bias8.ap[:, 0:4], b_ig.partition_broadcast(128), writes=[bias8])
        P.dma("sp", bias8.ap[:, 4:8], b_fg.partition_broadcast(128), writes=[bias8])
        P.dma("sp", gcol.ap, mnorm_w.rearrange("(c p) -> p c", p=128), writes=[gcol], slow=True)
        P.dma("sp", pscol.ap, pool_scale.rearrange("(c p) -> p c", p=128), writes=[pscol], slow=True)
        P.dma("sp", flagc.ap, flag_d, writes=[flagc])
        P.dma("sp", wb.ap, norm_mix_w.partition_broadcast(128), writes=[wb])
        P.dma("pool", wg8.ap, w_in_v[:, :, 5120:5128], writes=[wg8])
        P.dma("pool", wpl.ap, w_pool.rearrange("g (c p) e -> p g c e", p=128), writes=[wpl])
        for h in range(NH):
            memset("dve", C32[h], 0.0)
            memset("dve", Cbf[h], 0.0)
        memset("dve", negBc, 0.0)
        memset("dve", Mc, 0.0)

        P.checkpoint(1)
        rr = [0]

        def norm_tile(src_ap, L, dstT, tok0, x_in=None, wbuf=None, wap=None):
            i = rr[0] % 2
            rr[0] += 1
            if x_in is None:
                xb = xt[i]
                P.dma("sp", xb.ap[:L, :], src_ap, writes=[xb])
            else:
                xb = x_in
            ssb = SS[i]
            act(sq, xb, AF.Square, accum=(ssb, ssb.ap[:L, 0:1]), o=sq.ap[:L, :], i=xb.ap[:L, :])
            act(ssb, ssb, AF.Ln, bias=EPS, scale=1.0 / D, o=ssb.ap[:L, 1:2], i=ssb.ap[:L, 0:1])
            act(ssb, ssb, AF.Exp, scale=-0.5, o=ssb.ap[:L, 2:3], i=ssb.ap[:L, 1:2])
            if wbuf is None:
                wbuf, wap = wb, wb.ap
            stt(xnb[i], xb, ssb.ap[:L, 2:3], wbuf, ALU.mult, ALU.mult, o=xnb[i].ap[:L, :], a=xb.ap[:L, :], b=wap[:L, :],
                extra_r=[ssb])
            for half in range(2):
                pt = PT[half]
                for j in range(8):
                    kc = half * 8 + j
                    tr(pt, pt.ap[:, j * 128:j * 128 + L], xnb[i], xnb[i].ap[:L, kc * 128:(kc + 1) * 128],
                       identb, identb.ap[:L, :L], signal=(j == 7))
                src = pt.ap.rearrange("p (j t) -> p j t", j=8)[:, :, 0:L]
                cp("act" if half == 0 else "dve", dstT, pt, o=dstT.ap[:, half * 8:half * 8 + 8, tok0:tok0 + L], i=src)
            return xb

        pjr = [0]

        def proj(xT, tok0, L, Wb, Wap, ncols):
            pj = PJ[pjr[0] % 2]
            pjr[0] += 1
            for kc in range(KC):
                mm(pj, pj.ap[:L, 0:ncols], xT, xT.ap[:, kc, tok0:tok0 + L], Wb, Wap[:, kc, 0:ncols], kc == 0, kc == KC - 1)
            return pj

        K_IG, K_Z, K_E, K_SP, K_NEGB, K_A, K_M, K_NEGM, K_EM, K_MPREV, K_MEND, K_W, K_DEC, K_T1, K_CMX, K_T2, K_AADJ, K_CME, K_INTER, K_EM2 = range(20)

        def gate_prep(xT, tok0, L, g, mode):
            G = GW[g]

            def k(kind, rows=L):
                return G.ap[:rows, kind, :]
            pj = PC[1]
            for kc in range(KC):
                mm(pj, pj.ap[:L, 0:8], xT, xT.ap[:, kc, tok0:tok0 + L], wg8, wg8.ap[:, kc, 0:8], kc == 0, kc == KC - 1)
            tt("dve", G, pj, bias8, ALU.add, o=G.ap[:L, 0:2, :].rearrange("p a h -> p (a h)"), a=pj.ap[:L, 0:8], b=bias8.ap[:L, :])
            act(G, G, AF.Exp, scale=-1.0, o=k(K_E), i=k(K_Z))
            act(G, G, AF.Ln, bias=1.0, o=k(K_SP), i=k(K_E))
            TRI = tri32 if mode == "p" else triB32
            MN = maskneg if mode == "p" else masknegB
            p0 = PS[0]
            mm(p0, p0.ap[:L, 0:4], TRI, TRI.ap[:L, :L], G, k(K_SP), True, True)
            if mode == "p":
                mm(p0, p0.ap[:, 4:8], ones32, ones32.ap[:L, :], G, k(K_SP), True, True)
                tt("dve", G, p0, negBc, ALU.add, o=k(K_NEGB), a=p0.ap[:L, 0:4], b=negBc.ap[:L, :])
            else:
                cp("dve", G, p0, o=k(K_NEGB), i=p0.ap[:L, 0:4])
                mm(p0, p0.ap[:, 8:12], sel32, sel32.ap, msm, msm.ap[0:16, :], True, True)
                cp("dve", G, p0, o=k(K_MPREV, 128), i=p0.ap[:, 8:12])
            tt("dve", G, G, G, ALU.add, o=k(K_A), a=k(K_IG), b=k(K_NEGB))
            ts("dve", G, G, -LN16, None, ALU.add, o=k(K_AADJ), a=k(K_A))
            tt("dve", dg, ident32, G, ALU.mult, o=dg.ap[:L, :, :L],
               a=ident32.ap[:L, :L].unsqueeze(1).to_broadcast([L, 4, L]),
               b=k(K_A).unsqueeze(2).to_broadcast([L, 4, L]))
            p1 = PS[1]
            for h in range(NH):
                mm(p1, p1.ap[:, h * 128:h * 128 + L], ones32, ones32.ap[:L, :], dg, dg.ap[:L, h, :L], True, True, signal=(h == NH - 1))
            Ab = p1.ap.rearrange("p (h s) -> p h s", h=4)
            tt("dve", tmpm, p1, MN, ALU.add, o=tmpm.ap[:L, :, :L], a=Ab[:L, :, :L],
               b=MN.ap[:L, :L].unsqueeze(1).to_broadcast([L, 4, L]))
            red(G, tmpm, k(K_CMX), tmpm.ap[:L, :, :L])
            if mode == "p":
                cp("dve", G, Mc, o=k(K_MPREV, 128), i=Mc.ap)
                tt("dve", G, G, Mc, ALU.max, o=k(K_M), a=k(K_CMX), b=Mc.ap[:L, :])
                red(G, p1, k(K_CME, 128), Ab[:, :, :L])
                tt("dve", G, G, Mc, ALU.max, o=k(K_MEND, 128), a=k(K_CME, 128), b=Mc.ap)
            else:
                tt("dve", G, G, G, ALU.max, o=k(K_M), a=k(K_CMX), b=k(K_MPREV))
                mm(p0, p0.ap[:, 12:16], SLm, SLm.ap, G, k(K_M), True, True)
                cp("dve", G, p0, o=k(K_MEND, 128), i=p0.ap[:, 12:16])
            tt("dve", G, G, G, ALU.subtract, o=k(K_NEGM), a=k(K_NEGB), b=k(K_M))
            tt("dve", G, G, G, ALU.subtract, o=k(K_INTER), a=k(K_MPREV), b=k(K_M))
            act(G, G, AF.Exp, o=k(K_INTER), i=k(K_INTER))
            act(G, G, AF.Exp, o=k(K_EM), i=k(K_NEGM))
            act(G, G, AF.Exp, scale=2.0, o=k(K_EM2), i=k(K_NEGM))
            tt("dve", G, G, G, ALU.subtract, o=k(K_T1), a=k(K_AADJ), b=k(K_MEND))
            act(G, G, AF.Exp, o=k(K_W), i=k(K_T1))
            tt("dve", G, G, G, ALU.subtract, o=k(K_T2, 128), a=k(K_MPREV, 128), b=k(K_MEND, 128))
            act(G, G, AF.Exp, o=k(K_DEC, 128), i=k(K_T2, 128))
            if mode == "p":
                cp("dve", Mc, G, i=k(K_MEND, 128))
                tt("dve", negBc, negBc, p0, ALU.add, b=p0.ap[:, 4:8])
            else:
                tt("dve", dsel, sellast, G, ALU.mult, a=sellast.ap.unsqueeze(2).to_broadcast([128, 16, 4]),
                   b=k(K_T2, 128).unsqueeze(1).to_broadcast([128, 16, 4]))
                mm(p1, p1.ap[:, 0:64], ones32, ones32.ap, dsel, dsel.ap.rearrange("p b h -> p (b h)"), True, True)
                act(decs, p1, AF.Exp, o=decs.ap.rearrange("p b h -> p (b h)"), i=p1.ap[:, 0:64])
                ts("dve", mpos, G, -1.0, None, ALU.mult, a=k(K_NEGM))
                mm(p0, p0.ap[0:16, 16:20], sellast, sellast.ap, mpos, mpos.ap, True, True)
                cp("dve", msout, p0, o=msout.ap[0:16, :], i=p0.ap[0:16, 16:20])
                P.dma("sp", m_s, msout.ap[0:16, :], reads=[msout])

        cr = [0]

        def state_update(h, g, L, ktm_ap, vext_ap, k_b, v_b):
            G = GW[g]
            i = cr[0] % 2
            kw = kwb[i]
            ts("dve", kw, k_b, G.ap[:L, K_W, h:h + 1], None, ALU.mult, o=kw.ap[:L, :], a=ktm_ap, extra_r=[G])
            for dc in range(2):
                mm(PC[dc], PC[dc].ap[:, 0:HDE], kw, kw.ap[:L, dc * 128:(dc + 1) * 128], v_b, vext_ap, True, True)
            for dc in range(2):
                stt(C32[h], C32[h], G.ap[:, K_DEC, h:h + 1], PC[dc], ALU.mult, ALU.add,
                    o=C32[h].ap[:, dc, :], a=C32[h].ap[:, dc, :], b=PC[dc].ap[:, 0:HDE], extra_r=[G])
            cp("act", Cbf[h], C32[h])

        xnTp = Buf(Z1b[:, 0:KC * 1040].rearrange("p (k t) -> p k t", k=KC))
        kvp = Buf(Z2b[:, 0:9 * 516].rearrange("p (i f) -> p i f", i=9))
        pre_tiles = [(x_meta, 16, 1024)] + [(x_pre[i * 128:(i + 1) * 128, :], 128, i * 128) for i in range(8)]
        for (src, L, tok0) in pre_tiles:
            norm_tile(src, L, xnTp, tok0)
        P.checkpoint(2)
        def gatesA():
            for gi, (src, L, tok0) in enumerate(pre_tiles):
                gate_prep(xnTp, tok0, L, gi, "p")
                if gi == 0:
                    cp("dve", negBm, negBc)
                    cp("dve", Mm, Mc)
                yield

        kvp_t = [[Buf(kvp.ap[:, gi, :]) for gi in range(9)] for _ in range(1)][0]

        def loadA(h):
            W = W2[h % 2]
            P.dma("pool", W.ap[:, :, 0:256], w_in_v[:, :, 2048 + h * 256:2048 + (h + 1) * 256], writes=[W])
            P.dma("pool", W.ap[:, :, 256:512], w_in_v[:, :, 3072 + h * 256:3072 + (h + 1) * 256], writes=[W])

        def projA(h, dst):
            W = W2[h % 2]
            for gi, (src, L, tok0) in enumerate(pre_tiles):
                pj = PJ[gi % 2]
                for kc in range(KC):
                    mm(pj, pj.ap[:L, 0:512], xnTp, xnTp.ap[:, kc, tok0:tok0 + L], W, W.ap[:, kc, :], kc == 0, kc == KC - 1)
                    if kc % 4 == 3:
                        yield
                cp("act", dst[gi], pj, o=dst[gi].ap[:L, 0:512], i=pj.ap[:L, 0:512])
                yield

        def updA(h, srcb):
            for gi, (src, L, tok0) in enumerate(pre_tiles):
                G = GW[gi]
                kw = kwb[gi % 2]
                kb = srcb[gi]
                ts("dve", kw, kb, G.ap[:L, K_W, h:h + 1], None, ALU.mult, o=kw.ap[:L, :], a=kb.ap[:L, 0:256], extra_r=[G])
                yield
                for dc in range(2):
                    mm(PC[dc], PC[dc].ap[:, 0:HDE], kw, kw.ap[:L, dc * 128:(dc + 1) * 128], kb, kb.ap[:L, 256:513], True, True)
                yield
                for dc in range(2):
                    stt(C32[h], C32[h], G.ap[:, K_DEC, h:h + 1], PC[dc], ALU.mult, ALU.add,
                        o=C32[h].ap[:, dc, :], a=C32[h].ap[:, dc, :], b=PC[dc].ap[:, 0:HDE], extra_r=[G])
                yield
                if gi == 0:
                    cp("dve", C32m[h], C32[h])

        kvp2 = Buf(Z2b[:, 13312:13312 + 9 * 516].rearrange("p (i f) -> p i f", i=9))
        memset("dve", kvp, 1.0)
        memset("dve", kvp2, 1.0)
        kvA = [[Buf(kvp.ap[:, gi, :]) for gi in range(9)], [Buf(kvp2.ap[:, gi, :]) for gi in range(9)]]
        for lst, par_b in ((kvA[0], kvp), (kvA[1], kvp2)):
            for b_ in lst:
                b_.w = par_b.w
        def chain(*gs):
            for g_ in gs:
                yield from g_

        def run(gens):
            gens = list(gens)
            while gens:
                for g_ in list(gens):
                    try:
                        next(g_)
                    except StopIteration:
                        gens.remove(g_)

        loadA(0)
        loadA(1)
        run([gatesA(), chain(projA(0, kvA[0]), projA(1, kvA[1]))])
        loadA(2)
        loadA(3)
        run([updA(0, kvA[0])])
        run([updA(1, kvA[1]), projA(2, kvA[0])])
        run([updA(2, kvA[0]), projA(3, kvA[1])])
        run([updA(3, kvA[1])])
        for h in range(NH):
            tt("dve", C32[h], C32[h], C32m[h], ALU.subtract)
            stt(C32[h], C32[h], flagc.ap[:, 0:1], C32m[h], ALU.mult, ALU.add, extra_r=[flagc])
            cp("act", Cbf[h], C32[h])
        for (cur, sav) in ((negBc, negBm), (Mc, Mm)):
            tt("dve", cur, cur, sav, ALU.subtract)
            stt(cur, cur, flagc.ap[:, 0:1], sav, ALU.mult, ALU.add, extra_r=[flagc])

        P.checkpoint(4)
        xnT = Buf(Z1b[:, 0:KC * 1168].rearrange("p (k t) -> p k t", k=KC))
        fence([xnTp, xnT])
        yT = Buf(Z2b[:, 0:KC * 1152].rearrange("p (k t) -> p k t", k=KC))
        main_tiles = [(x_own[i * 128:(i + 1) * 128, :], 128, i * 128) for i in range(8)] + [(x_smp, 128, 1024)]
        for (src, L, tok0) in main_tiles + [(x_halo, 16, 1152)]:
            norm_tile(src, L, xnT, tok0)
        P.dma("sp", msm.ap[0:16, :], st_m, writes=[msm])
        P.dma("sp", nin.ap[0:64, :], st_n, writes=[nin])
        for dc in range(2):
            tr(PS[0], PS[0].ap[:, dc * 64:(dc + 1) * 64], nin, nin.ap[0:64, dc * 128:(dc + 1) * 128], ident32, ident32.ap[0:64, 0:64], signal=(dc == 1))
        cp("dve", nT, PS[0], o=nT.ap.rearrange("p b h c -> p c (b h)"), i=PS[0].ap[:, 0:128].rearrange("p (c x) -> p c x", c=2))
        for gi, (src, L, tok0) in enumerate(main_tiles):
            gate_prep(xnT, tok0, L, 9 + gi, "p" if gi < 8 else "s")
        tt("dve", mvec, Mc, negBc, ALU.subtract)
        P.dma("sp", m_p, mvec.ap[0:1, :], reads=[mvec])

        P.checkpoint(5)
        WU = W2
        fence([kvp, kvp2, wb, yT] + C32m + xt + SET_PL + kvA[0] + kvA[1])
        for half in range(2):
            P.dma("pool", WU[half].ap, w_in_v[:, :, half * 512:(half + 1) * 512], writes=[WU[half]])
        sp_rows = st_pool.rearrange("b j c -> (b j) c")
        for blk, (r0, nr) in enumerate(((0, 128), (128, 112))):
            P.dma("sp", sptm.ap[0:nr, :], sp_rows[r0:r0 + nr, :], writes=[sptm])
            for cc in range(8):
                p = PS[cc % 2]
                tr(p, p.ap[:, 0:nr], sptm, sptm.ap[0:nr, cc * 128:(cc + 1) * 128], ident32, ident32.ap[0:nr, 0:nr], signal=True)
                eng = "dve" if cc % 2 else "act"
                if blk == 0:
                    cp(eng, uTs, p, o=uTs.ap[:, cc, 0:8, 0:15], i=p.ap[:, 0:120].rearrange("p (b j) -> p b j", b=8))
                    cp(eng, uTs, p, o=uTs.ap[:, cc, 8, 0:8], i=p.ap[:, 120:128])
                else:
                    cp(eng, uTs, p, o=uTs.ap[:, cc, 8, 8:15], i=p.ap[:, 0:7])
                    cp(eng, uTs, p, o=uTs.ap[:, cc, 9:16, 0:15], i=p.ap[:, 7:112].rearrange("p (b j) -> p b j", b=7))

        def pool_group(U, shp, g, o_ap):
            nd = len(shp)
            T = shp[-1]

            def v(b, t0, t1):
                if nd == 1:
                    return b[:, :, t0:t1]
                return b[:, :, :, t0:t1]
            if nd == 1:
                sv = [sw[i].ap[:, 0:2 * T].rearrange("p (c t) -> p c t", c=2) for i in range(2)]
            else:
                sv = [sw[i].ap[:, 0:2 * shp[0] * T].rearrange("p (c b t) -> p c b t", c=2, b=shp[0]) for i in range(2)]
            cur_b, cur = U, (U.ap[:, 2 * g:2 * g + 2, :] if nd == 1 else U.ap[:, 2 * g:2 * g + 2, :, :])
            for k in range(g + 1):
                sh = 1 << k
                lo = 2 * sh - 1
                nb, nv = sw[k % 2], sv[k % 2]
                tt("dve", nb, cur_b, cur_b, ALU.add, o=v(nv, lo, T), a=v(cur, lo, T), b=v(cur, lo - sh, T - sh))
                cur_b, cur = nb, nv
            wdw = 2 << g
            uv = U.ap[:, 2 * g:2 * g + 2, :] if nd == 1 else U.ap[:, 2 * g:2 * g + 2, :, :]
            stt(dTb, cur_b, 1.0 / wdw, U, ALU.mult, ALU.subtract, o=o_ap, a=v(cur, 15, T), b=v(uv, 15, T))

        def pool_tile(tok0, L, ytok0, hist_mode):
            for half in range(2):
                pj = proj(xnT, tok0, L, WU[half], WU[half].ap, 512)
                cp("act" if half == 0 else "dve", utm, pj, o=utm.ap[:L, half * 512:(half + 1) * 512], i=pj.ap[:L, :])
            if hist_mode == "smp":
                for b in range(16):
                    P.dma("sp", pool_s[b, 7:15, :], utm.ap[b * 8:(b + 1) * 8, :], reads=[utm])
                    P.dma("sp", pool_s[b, 0:7, :], st_pool[b, 8:15, :])
            if hist_mode == "own" and tok0 == 7 * 128:
                P.dma("sp", pool_p, utm.ap[113:128, :], reads=[utm])
            for cc in range(8):
                p = PS[cc % 2]
                tr(p, p.ap[:, 0:L], utm, utm.ap[:L, cc * 128:(cc + 1) * 128], ident32, ident32.ap[:L, :L], signal=True)
                if hist_mode == "halo":
                    cp("dve" if cc % 2 else "act", uT, p, o=uT.ap[:, cc, 0:15], i=p.ap[:, 1:16])
                elif hist_mode == "own":
                    cp("dve" if cc % 2 else "act", uT, p, o=uT.ap[:, cc, 15:143], i=p.ap[:, 0:128])
                else:
                    cp("dve" if cc % 2 else "act", uTs, p, o=uTs.ap[:, cc, :, 15:23], i=p.ap[:, 0:128].rearrange("p (b t) -> p b t", b=16))
            if hist_mode == "halo":
                return
            if hist_mode == "own":
                U, shp, hcol = uT, (143,), 15
            else:
                U, shp, hcol = uTs, (16, 23), 15
            for g in range(4):
                if hist_mode == "own":
                    o_ap = dTb.ap[:, 2 * g:2 * g + 2, :]
                else:
                    o_ap = dTb.ap[:, 2 * g:2 * g + 2, :].rearrange("p c (b t) -> p c b t", b=16)
                pool_group(U, shp, g, o_ap)
            for g in range(4):
                for ec in range(2):
                    p = PC[ec]
                    for c in range(2):
                        mm(p, p.ap[:, 0:128], wpl, wpl.ap[:, g, c, ec * 128:(ec + 1) * 128], dTb, dTb.ap[:, 2 * g + c, :], c == 0, c == 1)
                    ch = 2 * g + ec
                    ts("dve", yT, p, pscol.ap[:, ch:ch + 1], None, ALU.mult, o=yT.ap[:, ch, ytok0:ytok0 + 128], a=p.ap[:, 0:128], extra_r=[pscol])
            if hist_mode == "own":
                cp("dve", sw[0], uT, o=sw[0].ap[:, 0:120].rearrange("p (c t) -> p c t", c=8), i=uT.ap[:, :, 128:143])
                cp("dve", uT, sw[0], o=uT.ap[:, :, 0:15], i=sw[0].ap[:, 0:120].rearrange("p (c t) -> p c t", c=8))

        pool_tile(1152, 16, 0, "halo")
        for i in range(8):
            pool_tile(i * 128, 128, i * 128, "own")
        pool_tile(1024, 128, 1024, "smp")

        P.checkpoint(6)
        q_t = [Buf(q_tm.ap[:, i, :]) for i in range(9)]
        k_t = [Buf(k_tm.ap[:, i, :]) for i in range(9)]
        s_t = [Buf(sigo.ap[:, i, :]) for i in range(9)]
        v_t = [Buf(v_ext.ap[:, i, :]) for i in range(9)]
        Z4BUFS.extend(q_t + k_t + s_t + v_t)

        def front(h, ti, g, mode):
            G = GW[g]
            par = ti % 2
            MP = maskposb if mode == "p" else maskposbB
            pt = PT[par]
            for dc in range(2):
                tr(pt, pt.ap[:, dc * 128:(dc + 1) * 128], k_t[ti], k_t[ti].ap[:, dc * 128:(dc + 1) * 128], identb, identb.ap, signal=False)
            for dc in range(2):
                tr(pt, pt.ap[:, (2 + dc) * 128:(3 + dc) * 128], q_t[ti], q_t[ti].ap[:, dc * 128:(dc + 1) * 128], identb, identb.ap, signal=(dc == 1))
            yield
            cp("act", qkT[par], pt, o=qkT[par].ap.rearrange("p a t -> p (a t)"), i=pt.ap[:, 0:512])
            ts("dve", dgM[par], ident32, G.ap[:, K_M, h:h + 1], None, ALU.mult, extra_r=[G])
            yield
            p = PS[par]
            mm(p, p.ap[:, 0:128], ones32, ones32.ap, dgM[par], dgM[par].ap, True, True)
            mm(p, p.ap[:, 128:256], ones32, ones32.ap, dgM[par], dgM[par].ap, True, False, signal=False)
            mm(p, p.ap[:, 128:256], identb, identb.ap, MP, MP.ap, False, True)
            yield
            act(DT[par], p, AF.Exp, bias=G.ap[:, K_AADJ, h:h + 1], scale=-1.0, i=p.ap[:, 128:256], extra_r=[G])
            yield
            if mode == "p":
                act(interb[par], p, AF.Exp, bias=G.ap[:, K_MPREV, h:h + 1], scale=-1.0, i=p.ap[:, 0:128], extra_r=[G])
            else:
                ts("dve", dgM[par], ident32, G.ap[:, K_INTER, h:h + 1], None, ALU.mult, extra_r=[G])
                mm(p, p.ap[:, 384:512], ones32, ones32.ap, dgM[par], dgM[par].ap, True, True)
                cp("act", interb[par], p, i=p.ap[:, 384:512])
            yield
            for dc in range(2):
                mm(p, p.ap[:, 256:384], qkT[par], qkT[par].ap[:, dc, :], qkT[par], qkT[par].ap[:, 2 + dc, :], dc == 0, dc == 1)
            yield
            tt("dve", STb[par], p, DT[par], ALU.mult, a=p.ap[:, 256:384])
            yield
            ts("dve", kwb[par], k_t[ti], G.ap[:, K_W, h:h + 1], None, ALU.mult, extra_r=[G])
            yield
            if mode == "p":
                tt("dve", qtl[par], qkT[par], interb[par], ALU.mult, a=qkT[par].ap[:, 2:4, :],
                   b=interb[par].ap.unsqueeze(1).to_broadcast([128, 2, 128]))
            else:
                tt("dve", qt32, qkT[par], interb[par], ALU.mult, a=qkT[par].ap[:, 2:4, :],
                   b=interb[par].ap.unsqueeze(1).to_broadcast([128, 2, 128]))
                yield
                tt("dve", Vmask, v_t[ti], blockmask, ALU.mult, o=Vmask.ap[:, :, 0:HDE],
                   a=v_t[ti].ap[:, 0:HDE].unsqueeze(1).to_broadcast([128, 16, HDE]),
                   b=blockmask.ap.unsqueeze(2).to_broadcast([128, 16, HDE]))
            yield

        def back(h, ti, g, mode):
            G = GW[g]
            par = ti % 2
            pn = PN[par]
            vb = v_t[ti]
            if mode == "p":
                mm(pn, pn.ap[:, 0:HDE], STb[par], STb[par].ap, vb, vb.ap[:, 0:HDE], True, False, signal=False)
                for dc in range(2):
                    mm(pn, pn.ap[:, 0:HDE], qtl[par], qtl[par].ap[:, dc, :], Cbf[h], Cbf[h].ap[:, dc, :], False, dc == 1)
                yield
                for dc in range(2):
                    mm(PC[dc], PC[dc].ap[:, 0:HDE], kwb[par], kwb[par].ap[:, dc * 128:(dc + 1) * 128], vb, vb.ap[:, 0:HDE], True, True)
                yield
                for dc in range(2):
                    stt(C32[h], C32[h], G.ap[:, K_DEC, h:h + 1], PC[dc], ALU.mult, ALU.add,
                        o=C32[h].ap[:, dc, :], a=C32[h].ap[:, dc, :], b=PC[dc].ap[:, 0:HDE], extra_r=[G])
                    yield
                cp("act", Cbf[h], C32[h])
                yield
            else:
                mm(pn, pn.ap[:, 0:HDE], STb[par], STb[par].ap, vb, vb.ap[:, 0:HDE], True, False, signal=False)
                def load_grp(gq):
                    c_ = C32s[gq % 3]
                    for bl_ in range(2):
                        P.dma("sp", c_.ap[:, bl_, :, 0:HD], st_C[gq * 2 + bl_, h].rearrange("(c p) v -> p c v", p=128), writes=[c_])
                    cp("dve", c_, nT, o=c_.ap[:, :, :, HD], i=nT.ap[:, gq * 2:(gq + 1) * 2, h, :])
                load_grp(0)
                load_grp(1)
                for grp in range(8):
                    cs_ = C32s[grp % 3]
                    if grp + 2 < 8:
                        load_grp(grp + 2)
                    qmb = qm[grp % 2]
                    for dc in range(2):
                        tt("dve", qmb, qt32, SELREP, ALU.mult, o=qmb.ap[:, dc * 2:dc * 2 + 2, :],
                           a=qt32.ap[:, dc, :].unsqueeze(1).to_broadcast([128, 2, 128]), b=SELREP.ap[:, grp * 2:(grp + 1) * 2, :])
                    yield
                    for bl in range(2):
                        for dc in range(2):
                            last = (grp == 7 and bl == 1 and dc == 1)
                            mm(pn, pn.ap[:, 0:HDE], qmb, qmb.ap[:, dc * 2 + bl, :], cs_, cs_.ap[:, bl, dc, :], False, last,
                               signal=(last or (bl == 1 and dc == 1)))
                    yield
                    for bl in range(2):
                        b = grp * 2 + bl
                        for dc in range(2):
                            pc = PC[dc]
                            mm(pc, pc.ap[:, 0:HDE], kwb[par], kwb[par].ap[:, dc * 128:(dc + 1) * 128], Vmask, Vmask.ap[:, b, 0:HDE], True, True)
                            stt(cs_, cs_, decs.ap[:, b, h:h + 1], pc, ALU.mult, ALU.add, o=cs_.ap[:, bl, dc, :], a=cs_.ap[:, bl, dc, :], b=pc.ap[:, 0:HDE], extra_r=[decs])
                            yield
                    cp("act", nS, cs_, o=nS.ap[:, grp * 2:(grp + 1) * 2, h, :], i=cs_.ap[:, :, :, HD])
                    for bl in range(2):
                        b = grp * 2 + bl
                        P.dma("sp", C_s[b, h].rearrange("(c p) v -> p c v", p=128), cs_.ap[:, bl, :, 0:HD], reads=[cs_])
                    yield
            cl = COLS[par]
            act(sqs, pn, AF.Square, accum=(cl, cl.ap[:, 0:1]), o=sqs.ap[:, 0:HD], i=pn.ap[:, 0:HD])
            P.op("dve", lambda e, o=cl.ap[:, 1:2], i=pn.ap[:, HD:HDE], s2=G.ap[:, K_EM2, h:h + 1]:
                 e.tensor_scalar(out=o, in0=i, scalar1=i, scalar2=s2, op0=ALU.mult, op1=ALU.max), reads=[pn, G], writes=[cl])
            yield
            stt(cl, cl, EPS * HD, cl, ALU.mult, ALU.add, o=cl.ap[:, 2:3], a=cl.ap[:, 1:2], b=cl.ap[:, 0:1])
            yield
            act(cl, cl, AF.Ln, scale=1.0 / HD, o=cl.ap[:, 5:6], i=cl.ap[:, 2:3])
            act(cl, cl, AF.Exp, scale=-0.5, o=cl.ap[:, 7:8], i=cl.ap[:, 5:6])
            yield
            yb = ytm[par]
            stt(yb, pn, cl.ap[:, 7:8], s_t[ti], ALU.mult, ALU.mult, o=yb.ap[:, 0:HD], a=pn.ap[:, 0:HD],
                b=s_t[ti].ap, extra_r=[cl])
            yield
            pt2 = PT[par]
            for c in range(2):
                tr(pt2, pt2.ap[:, 512 + c * 128:512 + (c + 1) * 128], yb, yb.ap[:, c * 128:(c + 1) * 128], identb, identb.ap, signal=(c == 1))
            yield
            for c in range(2):
                ch = 2 * h + c
                ts("dve", yT, pt2, gcol.ap[:, ch:ch + 1], None, ALU.mult, o=yT.ap[:, 8 + ch, ti * 128:(ti + 1) * 128],
                   a=pt2.ap[:, 512 + c * 128:512 + (c + 1) * 128], extra_r=[gcol])
                yield

        def load_head_w(h):
            for half in range(2):
                for j in range(2):
                    c0 = 1024 + (half * 2 + j) * 1024 + h * 256
                    P.dma("pool", W2[half].ap[:, :, j * 256:(j + 1) * 256], w_in_v[:, :, c0:c0 + 256], writes=[W2[half]])

        def projgen(h, ti):
            tok0 = main_tiles[ti][2]
            pj = PJ[0]
            for kc in range(KC):
                mm(pj, pj.ap[:, 0:512], xnT, xnT.ap[:, kc, tok0:tok0 + 128], W2[0], W2[0].ap[:, kc, :], kc == 0, kc == KC - 1)
                if kc % 2 == 1:
                    yield
            cp("act", q_t[ti], pj, i=pj.ap[:, 0:256])
            cp("dve", k_t[ti], pj, i=pj.ap[:, 256:512])
            yield
            pj = PJ[1]
            for kc in range(KC):
                mm(pj, pj.ap[:, 0:512], xnT, xnT.ap[:, kc, tok0:tok0 + 128], W2[1], W2[1].ap[:, kc, :], kc == 0, kc == KC - 1)
                if kc % 2 == 1:
                    yield
            cp("dve", v_t[ti], pj, o=v_t[ti].ap[:, 0:HD], i=pj.ap[:, 0:256])
            act(sgt, pj, AF.Exp, scale=-1.0, i=pj.ap[:, 256:512])
            yield
            act(sgt, sgt, AF.Ln, bias=1.0)
            yield
            act(s_t[ti], sgt, AF.Exp, scale=-1.0)
            yield

        def interleave(gens):
            gens = list(gens)
            while gens:
                for g_ in list(gens):
                    try:
                        next(g_)
                    except StopIteration:
                        gens.remove(g_)

        fence(SET_PL + SET_HD + q_t + k_t + s_t + v_t)
        memset("dve", v_ext, 1.0)
        for vt_ in v_t:
            vt_.w = v_ext.w
        memset("dve", Vmask, 0.0)
        PN = PS
        load_head_w(0)
        for ti in range(9):
            interleave([projgen(0, ti)])
        for h in range(NH):
            if h + 1 < NH:
                load_head_w(h + 1)
            for s_ in range(11):
                tasks = []
                if s_ <= 8:
                    tasks.append(front(h, s_, 9 + s_, "p" if s_ < 8 else "s"))
                if 1 <= s_ <= 9:
                    tasks.append(back(h, s_ - 1, 9 + s_ - 1, "p" if s_ - 1 < 8 else "s"))
                if h + 1 < NH and 2 <= s_ <= 10:
                    tasks.append(projgen(h + 1, s_ - 2))
                interleave(tasks)
        P.checkpoint(7)
        for h in range(NH):
            for dc in range(2):
                P.dma("sp", C_p[h, dc * 128:(dc + 1) * 128, :], C32[h].ap[:, dc, 0:HD], reads=[C32[h]])
                P.dma("sp", n_p[h:h + 1, dc * 128:(dc + 1) * 128].rearrange("o p -> p o"), C32[h].ap[:, dc, HD:HDE], reads=[C32[h]], slow=True)
        P.checkpoint(7.5)
        tr(PS[0], PS[0].ap[:, 0:128], nS, nS.ap.rearrange("p b h c -> p (b h c)"), ident32, ident32.ap, signal=True)
        cp("dve", nrow, PS[0], i=PS[0].ap[:, 0:128])
        P.dma("sp", n_s, nrow.ap, reads=[nrow])

        P.checkpoint(8)
        all_src = [x_own[i * 128:(i + 1) * 128, :] for i in range(8)] + [x_smp]
        z4_users = Z4BUFS
        for i in range(9):
            P.dma("sp", x_new[i].ap, all_src[i], reads=[], writes=[x_new[i]] + z4_users)
        WO = [W2[0], W2[1]]
        for cg in range(4):
            W = WO[cg % 2]
            P.dma("pool", W.ap, w_out_v[:, :, cg * 512:(cg + 1) * 512], writes=[W])
            for i in range(9):
                pj = proj(yT, i * 128, 128, W, W.ap, 512)
                tt("dve", x_new[i], x_new[i], pj, ALU.add, o=x_new[i].ap[:, cg * 512:(cg + 1) * 512],
                   a=x_new[i].ap[:, cg * 512:(cg + 1) * 512], b=pj.ap[:, 0:512])
        P.dma("sp", Z3[:, 4096:4096 + D], norm_ffn_w.partition_broadcast(128), writes=[W2[1]])
        xn2T = Buf(Z1b[:, 0:KC * 1152].rearrange("p (k t) -> p k t", k=KC))
        fence([xnT, xn2T])
        for i in range(9):
            norm_tile(None, 128, xn2T, i * 128, x_in=x_new[i], wbuf=W2[1], wap=Z3[:, 4096:4096 + D])

        P.checkpoint(9)
        hT = Buf(Z2b[:, 0:11 * 1152].rearrange("p (f t) -> p f t", f=11))
        WG = [Buf(Z3b[:, i * 2048:(i + 1) * 2048].rearrange("p (k n) -> p k n", k=KC)) for i in range(2)]
        WUp = [Buf(Z3b[:, 4096 + i * 2048:4096 + (i + 1) * 2048].rearrange("p (k n) -> p k n", k=KC)) for i in range(2)]
        WD = [Buf(Z3b[:, 8192 + i * 2816:8192 + (i + 1) * 2816].rearrange("p (f n) -> p f n", f=11)) for i in range(2)]
        fence([W2[0], W2[1], yT, hT] + WG + WUp + WD + silt)
        tgs = [(0, 512), (512, 512), (1024, 128)]
        fr = [0]
        for qd in range(4):
            for fl in range(11):
                fc = qd * 11 + fl
                wgb = WG[fr[0] % 2]
                wub = WUp[fr[0] % 2]
                fr[0] += 1
                P.dma("pool", wgb.ap, w_gate_v[:, :, fc * 128:(fc + 1) * 128], writes=[wgb])
                P.dma("pool", wub.ap, w_up_v[:, :, fc * 128:(fc + 1) * 128], writes=[wub])
                for ti, (t0, tn) in enumerate(tgs):
                    pg = PJ[ti % 2]
                    pu = PS[ti % 2]
                    for kc in range(KC):
                        mm(pg, pg.ap[:, 0:tn], wgb, wgb.ap[:, kc, :], xn2T, xn2T.ap[:, kc, t0:t0 + tn], kc == 0, kc == KC - 1)
                    for kc in range(KC):
                        mm(pu, pu.ap[:, 0:tn], wub, wub.ap[:, kc, :], xn2T, xn2T.ap[:, kc, t0:t0 + tn], kc == 0, kc == KC - 1)
                    st_ = silt[ti % 2]
                    act(st_, pg, AF.Silu, o=st_.ap[:, 0:tn], i=pg.ap[:, 0:tn])
                    tt("dve", hT, st_, pu, ALU.mult, o=hT.ap[:, fl, t0:t0 + tn], a=st_.ap[:, 0:tn], b=pu.ap[:, 0:tn])
            for cg in range(8):
                W = WD[cg % 2]
                P.dma("pool", W.ap, w_down_v[:, qd * 11:(qd + 1) * 11, cg * 256:(cg + 1) * 256], writes=[W])
                for i in range(9):
                    pc = PC[i % 2]
                    for fl in range(11):
                        mm(pc, pc.ap[:, 0:256], hT, hT.ap[:, fl, i * 128:(i + 1) * 128], W, W.ap[:, fl, :], fl == 0, fl == 10)
                    tt("dve", x_new[i], x_new[i], pc, ALU.add, o=x_new[i].ap[:, cg * 256:(cg + 1) * 256],
                       a=x_new[i].ap[:, cg * 256:(cg + 1) * 256], b=pc.ap[:, 0:256])

        P.checkpoint(10)
        wbE = Buf(Z1[:, 0:D])
        fence([xn2T, wbE])
        P.dma("sp", wbE.ap, norm_final_w.partition_broadcast(128), writes=[wbE])
        for i in range(9):
            ssb = SS[i % 2]
            xb = x_new[i]
            act(sq, xb, AF.Square, accum=(ssb, ssb.ap[:, 0:1]))
            act(ssb, ssb, AF.Ln, bias=EPS, scale=1.0 / D, o=ssb.ap[:, 1:2], i=ssb.ap[:, 0:1])
            act(ssb, ssb, AF.Exp, scale=-0.5, o=ssb.ap[:, 2:3], i=ssb.ap[:, 1:2])
            stt(xb, xb, ssb.ap[:, 2:3], wbE, ALU.mult, ALU.mult, extra_r=[ssb])
            dst = y_own[i * 128:(i + 1) * 128, :] if i < 8 else y_smp
            P.dma("sp", dst, xb.ap, reads=[xb])

        sem_es = ExitStack()
        with sem_es:
            sems = {}
            for e in ENGS:
                sems[e] = sem_es.enter_context(nc.semaphore("s_" + e))
            for q, n in P.ndma.items():
                for i in range(n):
                    sems[(q, i)] = sem_es.enter_context(nc.semaphore("d_%s%d" % (q, i)))
            with nc.Block() as block:
                @block.tensor
                def _(e):
                    P.replay("pe", e, sems)

                @block.scalar
                def _(e):
                    P.replay("act", e, sems)

                @block.vector
                def _(e):
                    P.replay("dve", e, sems)

                @block.gpsimd
                def _(e):
                    P.replay("pool", e, sems)
                    P.final_waits("pool", e, sems)

                @block.sync
                def _(e):
                    P.replay("sp", e, sems)
                    P.final_waits("sp", e, sems)
    return nc


_NC = None


def _get_nc():
    global _NC
    if _NC is None:
        _NC = build_nc()
    return _NC


def kernel(x_prompt, x_sample, state_pool, state_mlstm_C, state_mlstm_n, state_mlstm_m, meta_tokens, norm_mix_w, w_in,
           b_igate, b_fgate, w_pool, pool_scale, mlstm_norm_w, w_out, norm_ffn_w, w_gate, w_up, w_down, norm_final_w):
    f = lambda a: np.ascontiguousarray(np.asarray(a, dtype=np.float32))
    xp, xs = f(x_prompt), f(x_sample)
    meta = f(meta_tokens)
    shared = {
        "norm_mix_w": f(norm_mix_w)[0], "w_in": f(w_in)[0], "b_igate": f(b_igate)[0], "b_fgate": f(b_fgate)[0],
        "w_pool": f(w_pool)[0], "pool_scale": f(pool_scale)[0], "mlstm_norm_w": f(mlstm_norm_w)[0], "w_out": f(w_out)[0],
        "norm_ffn_w": f(norm_ffn_w)[0], "w_gate": f(w_gate)[0], "w_up": f(w_up)[0], "w_down": f(w_down)[0],
        "norm_final_w": f(norm_final_w),
    }
    sp, sC, sn, sm = f(state_pool)[0], f(state_mlstm_C)[0], f(state_mlstm_n)[0], f(state_mlstm_m)[0]
    in_maps = []
    for c in range(8):
        s, h = c // 2, c % 2
        m = dict(shared)
        m["x_meta"] = meta
        m["x_pre"] = xp[s, 0:1024] if h == 1 else np.zeros((1024, D), np.float32)
        m["x_own"] = np.ascontiguousarray(xp[s, 1024 * h:1024 * (h + 1)])
        m["x_halo"] = meta if h == 0 else np.ascontiguousarray(xp[s, 1008:1024])
        m["x_smp"] = np.ascontiguousarray(xs[16 * c:16 * (c + 1)].reshape(128, D))
        m["flag"] = np.full((128, 1), float(h), np.float32)
        m["st_pool"] = np.ascontiguousarray(sp[16 * c:16 * (c + 1)])
        m["st_C"] = np.ascontiguousarray(sC[16 * c:16 * (c + 1)])
        m["st_n"] = np.ascontiguousarray(sn[16 * c:16 * (c + 1)].reshape(64, HD))
        m["st_m"] = np.ascontiguousarray(sm[16 * c:16 * (c + 1)])
        in_maps.append(m)
    nc = _get_nc()
    res = run_bass_kernel_spmd(nc, in_maps, core_ids=list(range(8))).results
    y_prompt = np.stack([np.concatenate([res[2 * s]["y_own"], res[2 * s + 1]["y_own"]], axis=0) for s in range(4)])
    y_sample = np.concatenate([r["y_smp"] for r in res], axis=0).reshape(128, 8, D)
    pool_pp = np.stack([res[2 * s + 1]["pool_p"] for s in range(4)])[None]
    C_pp = np.stack([res[2 * s + 1]["C_p"] for s in range(4)])[None]
    n_pp = np.stack([res[2 * s + 1]["n_p"] for s in range(4)])[None]
    m_pp = np.stack([res[2 * s + 1]["m_p"].reshape(NH) for s in range(4)])[None]
    pool_ss = np.concatenate([r["pool_s"] for r in res], axis=0)[None]
    C_ss = np.concatenate([r["C_s"] for r in res], axis=0)[None]
    n_ss = np.concatenate([r["n_s"].reshape(16, NH, HD) for r in res], axis=0)[None]
    m_ss = np.concatenate([r["m_s"] for r in res], axis=0)[None]
    outs = (y_prompt, y_sample, pool_pp, C_pp, n_pp, m_pp, pool_ss, C_ss, n_ss, m_ss)
    return tuple(np.ascontiguousarray(o, dtype=np.float32) for o in outs)
```

```python
import math
import numpy as np
import concourse.bass as bass
import concourse.mybir as mybir
from concourse.bass_utils import run_bass_kernel_spmd

F32 = mybir.dt.float32
BF16 = mybir.dt.bfloat16
AF = mybir.ActivationFunctionType
ALU = mybir.AluOpType
AX = mybir.AxisListType

D = 2048
KC = 16
NH = 4
HD = 256
HDE = 257
DFF = 5632
FCH = 44
EPS = 1e-6
LN16 = math.log(16.0)
BIG = 1.0e30
ENGS = ("pe", "act", "dve", "pool", "sp")
STOP = None


class Buf:
    __slots__ = ("ap", "w", "r", "excl")

    def __init__(self, ap, excl=False):
        self.ap = ap
        self.w = None
        self.r = []
        self.excl = excl


class Prog:
    def __init__(self):
        self.streams = {e: [] for e in ENGS}
        self.cnt = {e: 0 for e in ENGS}
        self.seen = {e: {} for e in ENGS}
        self.ndma = {"sp": 20, "pool": 12}
        self.dma_cnt = {q: [0] * n for q, n in self.ndma.items()}
        self.dma_rr = {q: 0 for q in self.ndma}
        self.dead = False
        self.stop = STOP

    def checkpoint(self, k):
        if self.stop is not None and k >= self.stop - 1e-9:
            self.dead = True

    def _waits(self, eng, deps):
        out = []
        for d in deps:
            if d is None:
                continue
            k, c = d
            if self.seen[eng].get(k, 0) >= c:
                continue
            self.seen[eng][k] = c
            out.append((k, c))
        return out

    def _deps(self, eng, reads, writes):
        deps = []
        for b in reads:
            if b.w is not None and not (eng == "pe" and b.w[0] == "pe"):
                deps.append(b.w)
            if b.excl:
                for t in b.r:
                    if t is not None and t[0] != eng:
                        deps.append(t)
        for b in writes:
            for t in [b.w] + b.r:
                if t is not None and not (eng == "pe" and t[0] == "pe"):
                    deps.append(t)
        return deps

    def op(self, eng, fn, reads=(), writes=(), signal=True):
        if self.dead:
            return None
        waits = self._waits(eng, self._deps(eng, reads, writes))
        if signal:
            self.cnt[eng] += 1
            tok = (eng, self.cnt[eng])
        else:
            tok = (eng, self.cnt[eng] + 1)
        self.streams[eng].append((waits, fn, "inc" if signal else None))
        for b in reads:
            b.r.append(tok)
        for b in writes:
            b.w = tok
            b.r = []
        return tok

    def dma(self, q, out_ap, in_ap, reads=(), writes=(), slow=False):
        if self.dead:
            return None
        i = self.dma_rr[q]
        self.dma_rr[q] = (i + 1) % self.ndma[q]
        key = (q, i)
        deps = self._deps("dmaq", reads, writes)
        prev = self.dma_cnt[q][i]
        if prev:
            deps.append((key, prev))
        waits = self._waits(q, deps)
        self.dma_cnt[q][i] = prev + 16
        tok = (key, prev + 16)
        if slow:
            fn = lambda e, o=out_ap, a=in_ap: e.dma_start(out=o, in_=a, allow_slow_non_contiguous=True)
        else:
            fn = lambda e, o=out_ap, a=in_ap: e.dma_start(out=o, in_=a)
        self.streams[q].append((waits, fn, key))
        for b in reads:
            b.r.append(tok)
        for b in writes:
            b.w = tok
            b.r = []
        return tok

    def replay(self, eng, e, sems):
        for waits, fn, sig in self.streams[eng]:
            for k, c in waits:
                e.wait_ge(sems[k], c)
            ins = fn(e)
            if sig == "inc":
                ins.then_inc(sems[eng], 1)
            elif sig is not None:
                ins.then_inc(sems[sig], 16)

    def final_waits(self, eng, e, sems):
        for i, c in enumerate(self.dma_cnt.get(eng, [])):
            if c:
                e.wait_ge(sems[(eng, i)], c)


def build_nc():
    nc = bass.Bass("TRN2", target_bir_lowering=False)
    P = Prog()

    def din(name, shape, dt=F32):
        return nc.dram_tensor(name, list(shape), dt, kind="ExternalInput").ap()

    def dout(name, shape, dt=F32):
        return nc.dram_tensor(name, list(shape), dt, kind="ExternalOutput").ap()

    x_meta = din("x_meta", [16, D])
    x_pre = din("x_pre", [1024, D])
    x_own = din("x_own", [1024, D])
    x_halo = din("x_halo", [16, D])
    x_smp = din("x_smp", [128, D])
    flag_d = din("flag", [128, 1])
    st_pool = din("st_pool", [16, 15, 1024])
    st_C = din("st_C", [16, NH, HD, HD])
    st_n = din("st_n", [64, HD])
    st_m = din("st_m", [16, NH])
    norm_mix_w = din("norm_mix_w", [D])
    w_in = din("w_in", [D, 5128])
    b_ig = din("b_igate", [NH])
    b_fg = din("b_fgate", [NH])
    w_pool = din("w_pool", [4, 256, 256])
    pool_scale = din("pool_scale", [1024])
    mnorm_w = din("mlstm_norm_w", [1024])
    w_out = din("w_out", [D, D])
    norm_ffn_w = din("norm_ffn_w", [D])
    w_gate = din("w_gate", [D, DFF])
    w_up = din("w_up", [D, DFF])
    w_down = din("w_down", [DFF, D])
    norm_final_w = din("norm_final_w", [D])

    y_own = dout("y_own", [1024, D])
    y_smp = dout("y_smp", [128, D])
    pool_p = dout("pool_p", [15, 1024])
    C_p = dout("C_p", [NH, HD, HD])
    n_p = dout("n_p", [NH, HD])
    m_p = dout("m_p", [1, NH])
    pool_s = dout("pool_s", [16, 15, 1024])
    C_s = dout("C_s", [16, NH, HD, HD])
    n_s = dout("n_s", [128, 128])
    m_s = dout("m_s", [16, NH])

    w_in_v = w_in.rearrange("(kc p) n -> p kc n", p=128)
    w_out_v = w_out.rearrange("(kc p) n -> p kc n", p=128)
    w_gate_v = w_gate.rearrange("(kc p) n -> p kc n", p=128)
    w_up_v = w_up.rearrange("(kc p) n -> p kc n", p=128)
    w_down_v = w_down.rearrange("(fc p) n -> p fc n", p=128)

    from contextlib import ExitStack
    es = ExitStack()

    def sb(name, shape, dt):
        return es.enter_context(nc.sbuf_tensor(name, list(shape), dt))

    def ps(name, shape, dt):
        return es.enter_context(nc.psum_tensor(name, list(shape), dt))

    with es:
        Z1 = sb("z1", [128, KC * 1168 // 2], F32)
        Z2 = sb("z2", [128, KC * 1152 // 2], F32)
        Z3 = sb("z3", [128, 8192], F32)
        Z4 = sb("z4", [128, 9 * D], F32)
        Z1b = Z1[:, :].bitcast(BF16)
        Z2b = Z2[:, :].bitcast(BF16)
        Z3b = Z3[:, :].bitcast(BF16)
        Z4BUFS = []
        tl = [9600]

        def zt(nfl, dt=F32, shape=None):
            a0 = tl[0]
            tl[0] += nfl
            assert tl[0] <= 9 * D, tl[0]
            ap = Z4[:, a0:a0 + nfl]
            if dt == BF16:
                ap = ap.bitcast(BF16)
            if shape is not None:
                ap = ap.rearrange(shape[0], **shape[1])
            bb = Buf(ap)
            Z4BUFS.append(bb)
            return bb

        ones32 = Buf(sb("ones32", [128, 128], F32)[:, :])
        ident32 = Buf(sb("ident32", [128, 128], F32)[:, :])
        identb = Buf(sb("identb", [128, 128], BF16)[:, :])
        tri32 = Buf(sb("tri32", [128, 128], F32)[:, :])
        tril32 = Buf(sb("tril32", [128, 128], F32)[:, :])
        maskneg = Buf(sb("maskneg", [128, 128], F32)[:, :])
        maskposb = Buf(sb("maskposb", [128, 128], BF16)[:, :])
        triB32 = Buf(sb("triB32", [128, 128], F32)[:, :])
        trilB32 = Buf(sb("trilB32", [128, 128], F32)[:, :])
        masknegB = Buf(sb("masknegB", [128, 128], F32)[:, :])
        maskposbB = Buf(sb("maskposbB", [128, 128], BF16)[:, :])
        SLm = Buf(sb("SLm", [128, 128], F32)[:, :])
        sel32 = Buf(sb("sel32", [16, 128], F32)[:, :])
        selA = Buf(sb("selA", [16, 128], F32)[:, :])
        sellastT = Buf(sb("sellastT", [16, 128], F32)[:, :])
        blockmask = Buf(sb("blockmask", [128, 16], F32)[:, :])
        blockA = Buf(sb("blockA", [128, 16], F32)[:, :])
        sellast = Buf(sb("sellast", [128, 16], F32)[:, :])
        bias8 = Buf(sb("bias8", [128, 8], F32)[:, :])
        gcol = Buf(sb("gcol", [128, 8], F32)[:, :])
        pscol = Buf(sb("pscol", [128, 8], F32)[:, :])
        flagc = Buf(sb("flagc", [128, 1], F32)[:, :])
        fsc = Buf(sb("fsc", [128, 1], F32)[:, :])
        negBc = Buf(sb("negBc", [128, 4], F32)[:, :])
        Mc = Buf(sb("Mc", [128, 4], F32)[:, :])
        negBm = Buf(sb("negBm", [128, 4], F32)[:, :])
        Mm = Buf(sb("Mm", [128, 4], F32)[:, :])
        C32t = sb("C32", [128, NH, 2, HDE], F32)
        C32 = [Buf(C32t[:, h, :, :]) for h in range(NH)]
        Cbft = sb("Cbf", [128, NH, 2, HDE], BF16)
        Cbf = [Buf(Cbft[:, h, :, :]) for h in range(NH)]
        sst = sb("ss", [128, 2, 8], F32)
        SS = [Buf(sst[:, i, :]) for i in range(2)]
        xnb1 = Buf(sb("xnb", [128, D], BF16)[:, :])
        xnb = [xnb1, xnb1]
        sq = xnb1
        wpl = Buf(sb("wpl", [128, 4, 2, 256], BF16)[:, :, :, :])

        selrepA = zt(1024, BF16, ("p (b t) -> p b t", dict(b=16)))
        wg8 = zt(64, BF16, ("p (k n) -> p k n", dict(k=KC)))
        gwsb = zt(10 * 20 * 4)
        GWl = [Buf(gwsb.ap[:, j * 80:(j + 1) * 80].rearrange("p (k h) -> p k h", k=20)) for j in range(10)]
        Z4BUFS.extend(GWl)
        GW = GWl[0:9] + GWl[0:10]
        dg = zt(512, F32, ("p (h s) -> p h s", dict(h=4)))
        tmpm = zt(512, F32, ("p (h s) -> p h s", dict(h=4)))
        dgM = [zt(128) for i in range(2)]
        DT = [zt(128) for i in range(2)]
        interb = [zt(128) for i in range(2)]
        STb = [zt(64, BF16) for i in range(2)]
        qkT = [zt(256, BF16, ("p (a t) -> p a t", dict(a=4))) for i in range(2)]
        qtl = [zt(128, BF16, ("p (a t) -> p a t", dict(a=2))) for i in range(2)]
        kwb = [zt(128, BF16) for i in range(2)]
        colsb = zt(24)
        COLS = [Buf(colsb.ap[:, i * 12:(i + 1) * 12]) for i in range(2)]
        Z4BUFS.extend(COLS)
        ytm = [zt(128, BF16) for i in range(2)]
        sqs = zt(128, BF16)
        dsel = zt(64, F32, ("p (b h) -> p b h", dict(b=16)))
        decs = zt(64, F32, ("p (b h) -> p b h", dict(b=16)))
        nT = zt(128, F32, ("p (b h c) -> p b h c", dict(b=16, h=4)))
        nS = zt(128, F32, ("p (b h c) -> p b h c", dict(b=16, h=4)))
        nrow = zt(128)
        nin = zt(256)
        msm = zt(4)
        msout = zt(4)
        mpos = zt(4)
        mvec = zt(4)
        qt32 = zt(256, F32, ("p (a t) -> p a t", dict(a=2)))
        qm = [zt(256, BF16, ("p (b t) -> p b t", dict(b=4))) for i in range(2)]
        C16 = zt(514, BF16, ("p (b c v) -> p b c v", dict(b=2, c=2)))
        C32s3 = zt(2 * 2 * HDE, F32, ("p (b c v) -> p b c v", dict(b=2, c=2)))
        sgt = zt(256)

        o4 = [0]

        def z4(nfl):
            a = o4[0]
            o4[0] += nfl
            assert o4[0] <= 9600, o4[0]
            return Z4[:, a:a + nfl]
        xt = [Buf(z4(D)) for _ in range(2)]
        o4[0] = 0
        utm = Buf(z4(1024))
        uT = Buf(z4(8 * 143).rearrange("p (c t) -> p c t", c=8))
        uTs = Buf(z4(8 * 16 * 23).rearrange("p (c b t) -> p c b t", c=8, b=16))
        sw = [Buf(z4(1472)) for _ in range(2)]
        dTb = Buf(z4(512).bitcast(BF16).rearrange("p (c t) -> p c t", c=8))
        sptm = Buf(z4(1024))
        SET_PL = [utm, uT, uTs, sw[0], sw[1], dTb, sptm]
        o4[0] = 0
        q_tm = Buf(z4(9 * 128).bitcast(BF16).rearrange("p (i f) -> p i f", i=9))
        k_tm = Buf(z4(9 * 128).bitcast(BF16).rearrange("p (i f) -> p i f", i=9))
        sigo = Buf(z4(9 * 128).bitcast(BF16).rearrange("p (i f) -> p i f", i=9))
        v_ext = Buf(z4(9 * 130).bitcast(BF16).rearrange("p (i f) -> p i f", i=9))
        Vmask = Buf(z4(16 * 130).bitcast(BF16).rearrange("p (b f) -> p b f", b=16))
        C32s = [Buf(z4(2 * 2 * HDE).rearrange("p (b c v) -> p b c v", b=2, c=2)) for _ in range(2)]
        C32s.append(C32s3)
        SET_HD = [q_tm, k_tm, sigo, v_ext, Vmask, C32s[0], C32s[1]]
        Z4BUFS.extend(xt + SET_PL + SET_HD)
        x_new = [Buf(Z4[:, i * D:(i + 1) * D]) for i in range(9)]
        C32mt = Z2[:, 2400:2400 + NH * 2 * HDE].rearrange("p (h c v) -> p h c v", h=NH, c=2)
        C32m = [Buf(C32mt[:, h, :, :]) for h in range(NH)]
        wb = Buf(Z2[:, 4600:4600 + D])
        silt = [Buf(Z2[:, 6400 + i * 512:6400 + (i + 1) * 512]) for i in range(2)]
        W2 = [Buf(Z3b[:, i * 8192:(i + 1) * 8192].rearrange("p (k n) -> p k n", k=KC)) for i in range(2)]

        def fence(bufs):
            P.op("dve", lambda e: e.memset(fsc.ap, 0.0), writes=[fsc] + list(bufs))

        PJ = [Buf(ps("pj%d" % i, [128, 512], F32)[:, :], excl=True) for i in range(2)]
        PT = [Buf(ps("pt%d" % i, [128, 1024], BF16)[:, :], excl=True) for i in range(2)]
        PS = [Buf(ps("ps%d" % i, [128, 512], F32)[:, :], excl=True) for i in range(2)]
        PC = [Buf(ps("pc%d" % i, [128, 512], F32)[:, :], excl=True) for i in range(2)]

        def act(out_b, in_b, func, bias=0.0, scale=1.0, accum=None, extra_r=(), o=None, i=None):
            oa = out_b.ap if o is None else o
            ia = in_b.ap if i is None else i
            kw = {}
            if accum is not None:
                kw["accum_out"] = accum[1]
            w = [out_b] + ([accum[0]] if accum is not None else [])
            return P.op("act", lambda e: e.activation(out=oa, in_=ia, func=func, bias=bias, scale=scale, **kw),
                        reads=[in_b] + list(extra_r), writes=w)

        def tt(eng, out_b, a_b, b_b, op, o=None, a=None, b=None):
            oa = out_b.ap if o is None else o
            aa = a_b.ap if a is None else a
            ba = b_b.ap if b is None else b
            return P.op(eng, lambda e: e.tensor_tensor(out=oa, in0=aa, in1=ba, op=op), reads=[a_b, b_b], writes=[out_b])

        def ts(eng, out_b, a_b, s1, s2, op0, op1=None, o=None, a=None, extra_r=()):
            oa = out_b.ap if o is None else o
            aa = a_b.ap if a is None else a
            if op1 is None:
                f = lambda e: e.tensor_scalar(out=oa, in0=aa, scalar1=s1, scalar2=None, op0=op0)
            else:
                f = lambda e: e.tensor_scalar(out=oa, in0=aa, scalar1=s1, scalar2=s2, op0=op0, op1=op1)
            return P.op(eng, f, reads=[a_b] + list(extra_r), writes=[out_b])

        def stt(out_b, a_b, sc, b_b, op0, op1, o=None, a=None, b=None, extra_r=()):
            oa = out_b.ap if o is None else o
            aa = a_b.ap if a is None else a
            ba = b_b.ap if b is None else b
            return P.op("dve", lambda e: e.scalar_tensor_tensor(out=oa, in0=aa, scalar=sc, in1=ba, op0=op0, op1=op1),
                        reads=[a_b, b_b] + list(extra_r), writes=[out_b])

        def red(out_b, in_b, o, i):
            return P.op("dve", lambda e: e.tensor_reduce(out=o, in_=i, axis=AX.X, op=ALU.max), reads=[in_b], writes=[out_b])

        def cp(eng, out_b, in_b, o=None, i=None):
            oa = out_b.ap if o is None else o
            ia = in_b.ap if i is None else i
            if eng == "act":
                return P.op("act", lambda e: e.copy(out=oa, in_=ia), reads=[in_b], writes=[out_b])
            return P.op(eng, lambda e: e.tensor_copy(out=oa, in_=ia), reads=[in_b], writes=[out_b])

        def mm(out_b, o, l_b, l, r_b, r, start, stop, signal=None):
            if signal is None:
                signal = stop
            return P.op("pe", lambda e: e.matmul(out=o, lhsT=l, rhs=r, start=start, stop=stop),
                        reads=[l_b, r_b], writes=[out_b], signal=signal)

        def tr(out_b, o, in_b, i, id_b, idap, signal):
            return P.op("pe", lambda e: e.transpose(out=o, in_=i, identity=idap), reads=[in_b, id_b], writes=[out_b], signal=signal)

        def memset(eng, b, val, ap=None):
            a = b.ap if ap is None else ap
            return P.op(eng, lambda e: e.memset(a, val), writes=[b])

        def asel(out_b, in_b, pattern, cmp, base, cm, o=None, i=None):
            oa = out_b.ap if o is None else o
            ia = in_b.ap if i is None else i
            return P.op("pool", lambda e: e.affine_select(out=oa, in_=ia, pattern=pattern, compare_op=cmp, fill=0.0,
                                                          base=base, channel_multiplier=cm), reads=[in_b], writes=[out_b])

        memset("pool", ones32, 1.0)
        asel(ident32, ones32, [[1, 128]], ALU.is_equal, 0, -1)
        asel(tri32, ones32, [[1, 128]], ALU.is_ge, 0, -1)
        asel(tril32, ones32, [[-1, 128]], ALU.is_ge, 0, 1)
        cp("pool", identb, ident32)
        ts("pool", maskneg, tril32, BIG, -BIG, ALU.mult, ALU.add)
        ts("pool", maskposb, tri32, -BIG, BIG, ALU.mult, ALU.add)
        asel(selA, ones32, [[1, 128]], ALU.is_ge, 0, -8, i=ones32.ap[0:16, :])
        asel(sel32, selA, [[-1, 128]], ALU.is_ge, 7, 8)
        asel(sellastT, ones32, [[1, 128]], ALU.is_equal, -7, -8, i=ones32.ap[0:16, :])
        asel(blockA, ones32, [[-8, 16]], ALU.is_ge, 0, 1, i=ones32.ap[:, 0:16])
        asel(blockmask, blockA, [[8, 16]], ALU.is_ge, 7, -1)
        asel(sellast, ones32, [[-8, 16]], ALU.is_equal, -7, 1, i=ones32.ap[:, 0:16])
        memset("pool", selrepA, 1.0)
        asel(selrepA, selrepA, [[-8, 16], [1, 128]], ALU.is_ge, 0, 0)
        asel(selrepA, selrepA, [[8, 16], [-1, 128]], ALU.is_ge, 7, 0)
        SELREP = selrepA
        mm(PS[0], PS[0].ap[:, 0:128], sel32, sel32.ap, sel32, sel32.ap, True, True)
        tt("dve", triB32, tri32, PS[0], ALU.mult, b=PS[0].ap[:, 0:128])
        tt("dve", trilB32, tril32, PS[0], ALU.mult, b=PS[0].ap[:, 0:128])
        ts("dve", masknegB, trilB32, BIG, -BIG, ALU.mult, ALU.add)
        ts("dve", maskposbB, triB32, -BIG, BIG, ALU.mult, ALU.add)
        mm(PS[1], PS[1].ap[:, 0:128], sellastT, sellastT.ap, sel32, sel32.ap, True, True)
        cp("dve", SLm, PS[1], i=PS[1].ap[:, 0:128])

        P.dma("sp", bias8.ap[:, 0:4], b_ig.partition_broadcast(128), writes=[bias8])
        P.dma("sp", bias8.ap[:, 4:8], b_fg.partition_broadcast(128), writes=[bias8])
        P.dma("sp", gcol.ap, mnorm_w.rearrange("(c p) -> p c", p=128), writes=[gcol], slow=True)
        P.dma("sp", pscol.ap, pool_scale.rearrange("(c p) -> p c", p=128), writes=[pscol], slow=True)
        P.dma("sp", flagc.ap, flag_d, writes=[flagc])
        P.dma("sp", wb.ap, norm_mix_w.partition_broadcast(128), writes=[wb])
        P.dma("pool", wg8.ap, w_in_v[:, :, 5120:5128], writes=[wg8])
        P.dma("pool", wpl.ap, w_pool.rearrange("g (c p) e -> p g c e", p=128), writes=[wpl])
        for h in range(NH):
            memset("dve", C32[h], 0.0)
            memset("dve", Cbf[h], 0.0)
        memset("dve", negBc, 0.0)
        memset("dve", Mc, 0.0)

        P.checkpoint(1)
        rr = [0]

        def norm_tile(src_ap, L, dstT, tok0, x_in=None, wbuf=None, wap=None):
            i = rr[0] % 2
            rr[0] += 1
            if x_in is None:
                xb = xt[i]
                P.dma("sp", xb.ap[:L, :], src_ap, writes=[xb])
            else:
                xb = x_in
            ssb = SS[i]
            act(sq, xb, AF.Square, accum=(ssb, ssb.ap[:L, 0:1]), o=sq.ap[:L, :], i=xb.ap[:L, :])
            act(ssb, ssb, AF.Ln, bias=EPS, scale=1.0 / D, o=ssb.ap[:L, 1:2], i=ssb.ap[:L, 0:1])
            act(ssb, ssb, AF.Exp, scale=-0.5, o=ssb.ap[:L, 2:3], i=ssb.ap[:L, 1:2])
            if wbuf is None:
                wbuf, wap = wb, wb.ap
            stt(xnb[i], xb, ssb.ap[:L, 2:3], wbuf, ALU.mult, ALU.mult, o=xnb[i].ap[:L, :], a=xb.ap[:L, :], b=wap[:L, :],
                extra_r=[ssb])
            for half in range(2):
                pt = PT[half]
                for j in range(8):
                    kc = half * 8 + j
                    tr(pt, pt.ap[:, j * 128:j * 128 + L], xnb[i], xnb[i].ap[:L, kc * 128:(kc + 1) * 128],
                       identb, identb.ap[:L, :L], signal=(j == 7))
                src = pt.ap.rearrange("p (j t) -> p j t", j=8)[:, :, 0:L]
                cp("act" if half == 0 else "dve", dstT, pt, o=dstT.ap[:, half * 8:half * 8 + 8, tok0:tok0 + L], i=src)
            return xb

        pjr = [0]

        def proj(xT, tok0, L, Wb, Wap, ncols):
            pj = PJ[pjr[0] % 2]
            pjr[0] += 1
            for kc in range(KC):
                mm(pj, pj.ap[:L, 0:ncols], xT, xT.ap[:, kc, tok0:tok0 + L], Wb, Wap[:, kc, 0:ncols], kc == 0, kc == KC - 1)
            return pj

        K_IG, K_Z, K_E, K_SP, K_NEGB, K_A, K_M, K_NEGM, K_EM, K_MPREV, K_MEND, K_W, K_DEC, K_T1, K_CMX, K_T2, K_AADJ, K_CME, K_INTER, K_EM2 = range(20)

        def gate_prep(xT, tok0, L, g, mode):
            G = GW[g]

            def k(kind, rows=L):
                return G.ap[:rows, kind, :]
            pj = PC[1]
            for kc in range(KC):
                mm(pj, pj.ap[:L, 0:8], xT, xT.ap[:, kc, tok0:tok0 + L], wg8, wg8.ap[:, kc, 0:8], kc == 0, kc == KC - 1)
            tt("dve", G, pj, bias8, ALU.add, o=G.ap[:L, 0:2, :].rearrange("p a h -> p (a h)"), a=pj.ap[:L, 0:8], b=bias8.ap[:L, :])
            act(G, G, AF.Exp, scale=-1.0, o=k(K_E), i=k(K_Z))
            act(G, G, AF.Ln, bias=1.0, o=k(K_SP), i=k(K_E))
            TRI = tri32 if mode == "p" else triB32
            MN = maskneg if mode == "p" else masknegB
            p0 = PS[0]
            mm(p0, p0.ap[:L, 0:4], TRI, TRI.ap[:L, :L], G, k(K_SP), True, True)
            if mode == "p":
                mm(p0, p0.ap[:, 4:8], ones32, ones32.ap[:L, :], G, k(K_SP), True, True)
                tt("dve", G, p0, negBc, ALU.add, o=k(K_NEGB), a=p0.ap[:L, 0:4], b=negBc.ap[:L, :])
            else:
                cp("dve", G, p0, o=k(K_NEGB), i=p0.ap[:L, 0:4])
                mm(p0, p0.ap[:, 8:12], sel32, sel32.ap, msm, msm.ap[0:16, :], True, True)
                cp("dve", G, p0, o=k(K_MPREV, 128), i=p0.ap[:, 8:12])
            tt("dve", G, G, G, ALU.add, o=k(K_A), a=k(K_IG), b=k(K_NEGB))
            ts("dve", G, G, -LN16, None, ALU.add, o=k(K_AADJ), a=k(K_A))
            tt("dve", dg, ident32, G, ALU.mult, o=dg.ap[:L, :, :L],
               a=ident32.ap[:L, :L].unsqueeze(1).to_broadcast([L, 4, L]),
               b=k(K_A).unsqueeze(2).to_broadcast([L, 4, L]))
            p1 = PS[1]
            for h in range(NH):
                mm(p1, p1.ap[:, h * 128:h * 128 + L], ones32, ones32.ap[:L, :], dg, dg.ap[:L, h, :L], True, True, signal=(h == NH - 1))
            Ab = p1.ap.rearrange("p (h s) -> p h s", h=4)
            tt("dve", tmpm, p1, MN, ALU.add, o=tmpm.ap[:L, :, :L], a=Ab[:L, :, :L],
               b=MN.ap[:L, :L].unsqueeze(1).to_broadcast([L, 4, L]))
            red(G, tmpm, k(K_CMX), tmpm.ap[:L, :, :L])
            if mode == "p":
                cp("dve", G, Mc, o=k(K_MPREV, 128), i=Mc.ap)
                tt("dve", G, G, Mc, ALU.max, o=k(K_M), a=k(K_CMX), b=Mc.ap[:L, :])
                red(G, p1, k(K_CME, 128), Ab[:, :, :L])
                tt("dve", G, G, Mc, ALU.max, o=k(K_MEND, 128), a=k(K_CME, 128), b=Mc.ap)
            else:
                tt("dve", G, G, G, ALU.max, o=k(K_M), a=k(K_CMX), b=k(K_MPREV))
                mm(p0, p0.ap[:, 12:16], SLm, SLm.ap, G, k(K_M), True, True)
                cp("dve", G, p0, o=k(K_MEND, 128), i=p0.ap[:, 12:16])
            tt("dve", G, G, G, ALU.subtract, o=k(K_NEGM), a=k(K_NEGB), b=k(K_M))
            tt("dve", G, G, G, ALU.subtract, o=k(K_INTER), a=k(K_MPREV), b=k(K_M))
            act(G, G, AF.Exp, o=k(K_INTER), i=k(K_INTER))
            act(G, G, AF.Exp, o=k(K_EM), i=k(K_NEGM))
            act(G, G, AF.Exp, scale=2.0, o=k(K_EM2), i=k(K_NEGM))
            tt("dve", G, G, G, ALU.subtract, o=k(K_T1), a=k(K_AADJ), b=k(K_MEND))
            act(G, G, AF.Exp, o=k(K_W), i=k(K_T1))
            tt("dve", G, G, G, ALU.subtract, o=k(K_T2, 128), a=k(K_MPREV, 128), b=k(K_MEND, 128))
            act(G, G, AF.Exp, o=k(K_DEC, 128), i=k(K_T2, 128))
            if mode == "p":
                cp("dve", Mc, G, i=k(K_MEND, 128))
                tt("dve", negBc, negBc, p0, ALU.add, b=p0.ap[:, 4:8])
            else:
                tt("dve", dsel, sellast, G, ALU.mult, a=sellast.ap.unsqueeze(2).to_broadcast([128, 16, 4]),
                   b=k(K_T2, 128).unsqueeze(1).to_broadcast([128, 16, 4]))
                mm(p1, p1.ap[:, 0:64], ones32, ones32.ap, dsel, dsel.ap.rearrange("p b h -> p (b h)"), True, True)
                act(decs, p1, AF.Exp, o=decs.ap.rearrange("p b h -> p (b h)"), i=p1.ap[:, 0:64])
                ts("dve", mpos, G, -1.0, None, ALU.mult, a=k(K_NEGM))
                mm(p0, p0.ap[0:16, 16:20], sellast, sellast.ap, mpos, mpos.ap, True, True)
                cp("dve", msout, p0, o=msout.ap[0:16, :], i=p0.ap[0:16, 16:20])
                P.dma("sp", m_s, msout.ap[0:16, :], reads=[msout])

        cr = [0]

        def state_update(h, g, L, ktm_ap, vext_ap, k_b, v_b):
            G = GW[g]
            i = cr[0] % 2
            kw = kwb[i]
            ts("dve", kw, k_b, G.ap[:L, K_W, h:h + 1], None, ALU.mult, o=kw.ap[:L, :], a=ktm_ap, extra_r=[G])
            for dc in range(2):
                mm(PC[dc], PC[dc].ap[:, 0:HDE], kw, kw.ap[:L, dc * 128:(dc + 1) * 128], v_b, vext_ap, True, True)
            for dc in range(2):
                stt(C32[h], C32[h], G.ap[:, K_DEC, h:h + 1], PC[dc], ALU.mult, ALU.add,
                    o=C32[h].ap[:, dc, :], a=C32[h].ap[:, dc, :], b=PC[dc].ap[:, 0:HDE], extra_r=[G])
            cp("act", Cbf[h], C32[h])

        xnTp = Buf(Z1b[:, 0:KC * 1040].rearrange("p (k t) -> p k t", k=KC))
        kvp = Buf(Z2b[:, 0:9 * 516].rearrange("p (i f) -> p i f", i=9))
        pre_tiles = [(x_meta, 16, 1024)] + [(x_pre[i * 128:(i + 1) * 128, :], 128, i * 128) for i in range(8)]
        for (src, L, tok0) in pre_tiles:
            norm_tile(src, L, xnTp, tok0)
        P.checkpoint(2)
        def gatesA():
            for gi, (src, L, tok0) in enumerate(pre_tiles):
                gate_prep(xnTp, tok0, L, gi, "p")
                if gi == 0:
                    cp("dve", negBm, negBc)
                    cp("dve", Mm, Mc)
                yield

        kvp_t = [[Buf(kvp.ap[:, gi, :]) for gi in range(9)] for _ in range(1)][0]

        def loadA(h):
            W = W2[h % 2]
            P.dma("pool", W.ap[:, :, 0:256], w_in_v[:, :, 2048 + h * 256:2048 + (h + 1) * 256], writes=[W])
            P.dma("pool", W.ap[:, :, 256:512], w_in_v[:, :, 3072 + h * 256:3072 + (h + 1) * 256], writes=[W])

        def projA(h, dst):
            W = W2[h % 2]
            for gi, (src, L, tok0) in enumerate(pre_tiles):
                pj = PJ[gi % 2]
                for kc in range(KC):
                    mm(pj, pj.ap[:L, 0:512], xnTp, xnTp.ap[:, kc, tok0:tok0 + L], W, W.ap[:, kc, :], kc == 0, kc == KC - 1)
                    if kc % 4 == 3:
                        yield
                cp("act", dst[gi], pj, o=dst[gi].ap[:L, 0:512], i=pj.ap[:L, 0:512])
                yield

        def updA(h, srcb):
            for gi, (src, L, tok0) in enumerate(pre_tiles):
                G = GW[gi]
                kw = kwb[gi % 2]
                kb = srcb[gi]
                ts("dve", kw, kb, G.ap[:L, K_W, h:h + 1], None, ALU.mult, o=kw.ap[:L, :], a=kb.ap[:L, 0:256], extra_r=[G])
                yield
                for dc in range(2):
                    mm(PC[dc], PC[dc].ap[:, 0:HDE], kw, kw.ap[:L, dc * 128:(dc + 1) * 128], kb, kb.ap[:L, 256:513], True, True)
                yield
                for dc in range(2):
                    stt(C32[h], C32[h], G.ap[:, K_DEC, h:h + 1], PC[dc], ALU.mult, ALU.add,
                        o=C32[h].ap[:, dc, :], a=C32[h].ap[:, dc, :], b=PC[dc].ap[:, 0:HDE], extra_r=[G])
                yield
                if gi == 0:
                    cp("dve", C32m[h], C32[h])

        kvp2 = Buf(Z2b[:, 13312:13312 + 9 * 516].rearrange("p (i f) -> p i f", i=9))
        memset("dve", kvp, 1.0)
        memset("dve", kvp2, 1.0)
        kvA = [[Buf(kvp.ap[:, gi, :]) for gi in range(9)], [Buf(kvp2.ap[:, gi, :]) for gi in range(9)]]
        for lst, par_b in ((kvA[0], kvp), (kvA[1], kvp2)):
            for b_ in lst:
                b_.w = par_b.w
        def chain(*gs):
            for g_ in gs:
                yield from g_

        def run(gens):
            gens = list(gens)
            while gens:
                for g_ in list(gens):
                    try:
                        next(g_)
                    except StopIteration:
                        gens.remove(g_)

        loadA(0)
        loadA(1)
        run([gatesA(), chain(projA(0, kvA[0]), projA(1, kvA[1]))])
        loadA(2)
        loadA(3)
        run([updA(0, kvA[0])])
        run([updA(1, kvA[1]), projA(2, kvA[0])])
        run([updA(2, kvA[0]), projA(3, kvA[1])])
        run([updA(3, kvA[1])])
        for h in range(NH):
            tt("dve", C32[h], C32[h], C32m[h], ALU.subtract)
            stt(C32[h], C32[h], flagc.ap[:, 0:1], C32m[h], ALU.mult, ALU.add, extra_r=[flagc])
            cp("act", Cbf[h], C32[h])
        for (cur, sav) in ((negBc, negBm), (Mc, Mm)):
            tt("dve", cur, cur, sav, ALU.subtract)
            stt(cur, cur, flagc.ap[:, 0:1], sav, ALU.mult, ALU.add, extra_r=[flagc])

        P.checkpoint(4)
        xnT = Buf(Z1b[:, 0:KC * 1168].rearrange("p (k t) -> p k t", k=KC))
        fence([xnTp, xnT])
        yT = Buf(Z2b[:, 0:KC * 1152].rearrange("p (k t) -> p k t", k=KC))
        main_tiles = [(x_own[i * 128:(i + 1) * 128, :], 128, i * 128) for i in range(8)] + [(x_smp, 128, 1024)]
        for (src, L, tok0) in main_tiles + [(x_halo, 16, 1152)]:
            norm_tile(src, L, xnT, tok0)
        P.dma("sp", msm.ap[0:16, :], st_m, writes=[msm])
        P.dma("sp", nin.ap[0:64, :], st_n, writes=[nin])
        for dc in range(2):
            tr(PS[0], PS[0].ap[:, dc * 64:(dc + 1) * 64], nin, nin.ap[0:64, dc * 128:(dc + 1) * 128], ident32, ident32.ap[0:64, 0:64], signal=(dc == 1))
        cp("dve", nT, PS[0], o=nT.ap.rearrange("p b h c -> p c (b h)"), i=PS[0].ap[:, 0:128].rearrange("p (c x) -> p c x", c=2))
        for gi, (src, L, tok0) in enumerate(main_tiles):
            gate_prep(xnT, tok0, L, 9 + gi, "p" if gi < 8 else "s")
        tt("dve", mvec, Mc, negBc, ALU.subtract)
        P.dma("sp", m_p, mvec.ap[0:1, :], reads=[mvec])

        P.checkpoint(5)
        WU = W2
        fence([kvp, kvp2, wb, yT] + C32m + xt + SET_PL + kvA[0] + kvA[1])
        for half in range(2):
            P.dma("pool", WU[half].ap, w_in_v[:, :, half * 512:(half + 1) * 512], writes=[WU[half]])
        sp_rows = st_pool.rearrange("b j c -> (b j) c")
        for blk, (r0, nr) in enumerate(((0, 128), (128, 112))):
            P.dma("sp", sptm.ap[0:nr, :], sp_rows[r0:r0 + nr, :], writes=[sptm])
            for cc in range(8):
                p = PS[cc % 2]
                tr(p, p.ap[:, 0:nr], sptm, sptm.ap[0:nr, cc * 128:(cc + 1) * 128], ident32, ident32.ap[0:nr, 0:nr], signal=True)
                eng = "dve" if cc % 2 else "act"
                if blk == 0:
                    cp(eng, uTs, p, o=uTs.ap[:, cc, 0:8, 0:15], i=p.ap[:, 0:120].rearrange("p (b j) -> p b j", b=8))
                    cp(eng, uTs, p, o=uTs.ap[:, cc, 8, 0:8], i=p.ap[:, 120:128])
                else:
                    cp(eng, uTs, p, o=uTs.ap[:, cc, 8, 8:15], i=p.ap[:, 0:7])
                    cp(eng, uTs, p, o=uTs.ap[:, cc, 9:16, 0:15], i=p.ap[:, 7:112].rearrange("p (b j) -> p b j", b=7))

        def pool_group(U, shp, g, o_ap):
            nd = len(shp)
            T = shp[-1]

            def v(b, t0, t1):
                if nd == 1:
                    return b[:, :, t0:t1]
                return b[:, :, :, t0:t1]
            if nd == 1:
                sv = [sw[i].ap[:, 0:2 * T].rearrange("p (c t) -> p c t", c=2) for i in range(2)]
            else:
                sv = [sw[i].ap[:, 0:2 * shp[0] * T].rearrange("p (c b t) -> p c b t", c=2, b=shp[0]) for i in range(2)]
            cur_b, cur = U, (U.ap[:, 2 * g:2 * g + 2, :] if nd == 1 else U.ap[:, 2 * g:2 * g + 2, :, :])
            for k in range(g + 1):
                sh = 1 << k
                lo = 2 * sh - 1
                nb, nv = sw[k % 2], sv[k % 2]
                tt("dve", nb, cur_b, cur_b, ALU.add, o=v(nv, lo, T), a=v(cur, lo, T), b=v(cur, lo - sh, T - sh))
                cur_b, cur = nb, nv
            wdw = 2 << g
            uv = U.ap[:, 2 * g:2 * g + 2, :] if nd == 1 else U.ap[:, 2 * g:2 * g + 2, :, :]
            stt(dTb, cur_b, 1.0 / wdw, U, ALU.mult, ALU.subtract, o=o_ap, a=v(cur, 15, T), b=v(uv, 15, T))

        def pool_tile(tok0, L, ytok0, hist_mode):
            for half in range(2):
                pj = proj(xnT, tok0, L, WU[half], WU[half].ap, 512)
                cp("act" if half == 0 else "dve", utm, pj, o=utm.ap[:L, half * 512:(half + 1) * 512], i=pj.ap[:L, :])
            if hist_mode == "smp":
                for b in range(16):
                    P.dma("sp", pool_s[b, 7:15, :], utm.ap[b * 8:(b + 1) * 8, :], reads=[utm])
                    P.dma("sp", pool_s[b, 0:7, :], st_pool[b, 8:15, :])
            if hist_mode == "own" and tok0 == 7 * 128:
                P.dma("sp", pool_p, utm.ap[113:128, :], reads=[utm])
            for cc in range(8):
                p = PS[cc % 2]
                tr(p, p.ap[:, 0:L], utm, utm.ap[:L, cc * 128:(cc + 1) * 128], ident32, ident32.ap[:L, :L], signal=True)
                if hist_mode == "halo":
                    cp("dve" if cc % 2 else "act", uT, p, o=uT.ap[:, cc, 0:15], i=p.ap[:, 1:16])
                elif hist_mode == "own":
                    cp("dve" if cc % 2 else "act", uT, p, o=uT.ap[:, cc, 15:143], i=p.ap[:, 0:128])
                else:
                    cp("dve" if cc % 2 else "act", uTs, p, o=uTs.ap[:, cc, :, 15:23], i=p.ap[:, 0:128].rearrange("p (b t) -> p b t", b=16))
            if hist_mode == "halo":
                return
            if hist_mode == "own":
                U, shp, hcol = uT, (143,), 15
            else:
                U, shp, hcol = uTs, (16, 23), 15
            for g in range(4):
                if hist_mode == "own":
                    o_ap = dTb.ap[:, 2 * g:2 * g + 2, :]
                else:
                    o_ap = dTb.ap[:, 2 * g:2 * g + 2, :].rearrange("p c (b t) -> p c b t", b=16)
                pool_group(U, shp, g, o_ap)
            for g in range(4):
                for ec in range(2):
                    p = PC[ec]
                    for c in range(2):
                        mm(p, p.ap[:, 0:128], wpl, wpl.ap[:, g, c, ec * 128:(ec + 1) * 128], dTb, dTb.ap[:, 2 * g + c, :], c == 0, c == 1)
                    ch = 2 * g + ec
                    ts("dve", yT, p, pscol.ap[:, ch:ch + 1], None, ALU.mult, o=yT.ap[:, ch, ytok0:ytok0 + 128], a=p.ap[:, 0:128], extra_r=[pscol])
            if hist_mode == "own":
                cp("dve", sw[0], uT, o=sw[0].ap[:, 0:120].rearrange("p (c t) -> p c t", c=8), i=uT.ap[:, :, 128:143])
                cp("dve", uT, sw[0], o=uT.ap[:, :, 0:15], i=sw[0].ap[:, 0:120].rearrange("p (c t) -> p c t", c=8))

        pool_tile(1152, 16, 0, "halo")
        for i in range(8):
            pool_tile(i * 128, 128, i * 128, "own")
        pool_tile(1024, 128, 1024, "smp")

        P.checkpoint(6)
        q_t = [Buf(q_tm.ap[:, i, :]) for i in range(9)]
        k_t = [Buf(k_tm.ap[:, i, :]) for i in range(9)]
        s_t = [Buf(sigo.ap[:, i, :]) for i in range(9)]
        v_t = [Buf(v_ext.ap[:, i, :]) for i in range(9)]
        Z4BUFS.extend(q_t + k_t + s_t + v_t)

        def front(h, ti, g, mode):
            G = GW[g]
            par = ti % 2
            MP = maskposb if mode == "p" else maskposbB
            pt = PT[par]
            for dc in range(2):
                tr(pt, pt.ap[:, dc * 128:(dc + 1) * 128], k_t[ti], k_t[ti].ap[:, dc * 128:(dc + 1) * 128], identb, identb.ap, signal=False)
            for dc in range(2):
                tr(pt, pt.ap[:, (2 + dc) * 128:(3 + dc) * 128], q_t[ti], q_t[ti].ap[:, dc * 128:(dc + 1) * 128], identb, identb.ap, signal=(dc == 1))
            yield
            cp("act", qkT[par], pt, o=qkT[par].ap.rearrange("p a t -> p (a t)"), i=pt.ap[:, 0:512])
            ts("dve", dgM[par], ident32, G.ap[:, K_M, h:h + 1], None, ALU.mult, extra_r=[G])
            yield
            p = PS[par]
            mm(p, p.ap[:, 0:128], ones32, ones32.ap, dgM[par], dgM[par].ap, True, True)
            mm(p, p.ap[:, 128:256], ones32, ones32.ap, dgM[par], dgM[par].ap, True, False, signal=False)
            mm(p, p.ap[:, 128:256], identb, identb.ap, MP, MP.ap, False, True)
            yield
            act(DT[par], p, AF.Exp, bias=G.ap[:, K_AADJ, h:h + 1], scale=-1.0, i=p.ap[:, 128:256], extra_r=[G])
            yield
            if mode == "p":
                act(interb[par], p, AF.Exp, bias=G.ap[:, K_MPREV, h:h + 1], scale=-1.0, i=p.ap[:, 0:128], extra_r=[G])
            else:
                ts("dve", dgM[par], ident32, G.ap[:, K_INTER, h:h + 1], None, ALU.mult, extra_r=[G])
                mm(p, p.ap[:, 384:512], ones32, ones32.ap, dgM[par], dgM[par].ap, True, True)
                cp("act", interb[par], p, i=p.ap[:, 384:512])
            yield
            for dc in range(2):
                mm(p, p.ap[:, 256:384], qkT[par], qkT[par].ap[:, dc, :], qkT[par], qkT[par].ap[:, 2 + dc, :], dc == 0, dc == 1)
            yield
            tt("dve", STb[par], p, DT[par], ALU.mult, a=p.ap[:, 256:384])
            yield
            ts("dve", kwb[par], k_t[ti], G.ap[:, K_W, h:h + 1], None, ALU.mult, extra_r=[G])
            yield
            if mode == "p":
                tt("dve", qtl[par], qkT[par], interb[par], ALU.mult, a=qkT[par].ap[:, 2:4, :],
                   b=interb[par].ap.unsqueeze(1).to_broadcast([128, 2, 128]))
            else:
                tt("dve", qt32, qkT[par], interb[par], ALU.mult, a=qkT[par].ap[:, 2:4, :],
                   b=interb[par].ap.unsqueeze(1).to_broadcast([128, 2, 128]))
                yield
                tt("dve", Vmask, v_t[ti], blockmask, ALU.mult, o=Vmask.ap[:, :, 0:HDE],
                   a=v_t[ti].ap[:, 0:HDE].unsqueeze(1).to_broadcast([128, 16, HDE]),
                   b=blockmask.ap.unsqueeze(2).to_broadcast([128, 16, HDE]))
            yield

        def back(h, ti, g, mode):
            G = GW[g]
            par = ti % 2
            pn = PN[par]
            vb = v_t[ti]
            if mode == "p":
                mm(pn, pn.ap[:, 0:HDE], STb[par], STb[par].ap, vb, vb.ap[:, 0:HDE], True, False, signal=False)
                for dc in range(2):
                    mm(pn, pn.ap[:, 0:HDE], qtl[par], qtl[par].ap[:, dc, :], Cbf[h], Cbf[h].ap[:, dc, :], False, dc == 1)
                yield
                for dc in range(2):
                    mm(PC[dc], PC[dc].ap[:, 0:HDE], kwb[par], kwb[par].ap[:, dc * 128:(dc + 1) * 128], vb, vb.ap[:, 0:HDE], True, True)
                yield
                for dc in range(2):
                    stt(C32[h], C32[h], G.ap[:, K_DEC, h:h + 1], PC[dc], ALU.mult, ALU.add,
                        o=C32[h].ap[:, dc, :], a=C32[h].ap[:, dc, :], b=PC[dc].ap[:, 0:HDE], extra_r=[G])
                    yield
                cp("act", Cbf[h], C32[h])
                yield
            else:
                mm(pn, pn.ap[:, 0:HDE], STb[par], STb[par].ap, vb, vb.ap[:, 0:HDE], True, False, signal=False)
                def load_grp(gq):
                    c_ = C32s[gq % 3]
                    for bl_ in range(2):
                        P.dma("sp", c_.ap[:, bl_, :, 0:HD], st_C[gq * 2 + bl_, h].rearrange("(c p) v -> p c v", p=128), writes=[c_])
                load_grp(0)
                load_grp(1)
                for grp in range(8):
                    cs_ = C32s[grp % 3]
                    if grp + 2 < 8:
                        load_grp(grp + 2)
                    cp("dve", cs_, nT, o=cs_.ap[:, :, :, HD], i=nT.ap[:, grp * 2:(grp + 1) * 2, h, :])
                    cp("act", C16, cs_)
                    qmb = qm[grp % 2]
                    for dc in range(2):
                        tt("dve", qmb, qt32, SELREP, ALU.mult, o=qmb.ap[:, dc * 2:dc * 2 + 2, :],
                           a=qt32.ap[:, dc, :].unsqueeze(1).to_broadcast([128, 2, 128]), b=SELREP.ap[:, grp * 2:(grp + 1) * 2, :])
                    yield
                    for bl in range(2):
                        for dc in range(2):
                            last = (grp == 7 and bl == 1 and dc == 1)
                            mm(pn, pn.ap[:, 0:HDE], qmb, qmb.ap[:, dc * 2 + bl, :], C16, C16.ap[:, bl, dc, :], False, last,
                               signal=(last or (bl == 1 and dc == 1)))
                    yield
                    for bl in range(2):
                        b = grp * 2 + bl
                        for dc in range(2):
                            pc = PC[dc]
                            mm(pc, pc.ap[:, 0:HDE], kwb[par], kwb[par].ap[:, dc * 128:(dc + 1) * 128], Vmask, Vmask.ap[:, b, 0:HDE], True, True)
                            stt(cs_, cs_, decs.ap[:, b, h:h + 1], pc, ALU.mult, ALU.add, o=cs_.ap[:, bl, dc, :], a=cs_.ap[:, bl, dc, :], b=pc.ap[:, 0:HDE], extra_r=[decs])
                            yield
                    cp("act", nS, cs_, o=nS.ap[:, grp * 2:(grp + 1) * 2, h, :], i=cs_.ap[:, :, :, HD])
                    for bl in range(2):
                        b = grp * 2 + bl
                        P.dma("sp", C_s[b, h].rearrange("(c p) v -> p c v", p=128), cs_.ap[:, bl, :, 0:HD], reads=[cs_])
                    yield
            cl = COLS[par]
            act(sqs, pn, AF.Square, accum=(cl, cl.ap[:, 0:1]), o=sqs.ap[:, 0:HD], i=pn.ap[:, 0:HD])
            P.op("dve", lambda e, o=cl.ap[:, 1:2], i=pn.ap[:, HD:HDE], s2=G.ap[:, K_EM2, h:h + 1]:
                 e.tensor_scalar(out=o, in0=i, scalar1=i, scalar2=s2, op0=ALU.mult, op1=ALU.max), reads=[pn, G], writes=[cl])
            yield
            stt(cl, cl, EPS * HD, cl, ALU.mult, ALU.add, o=cl.ap[:, 2:3], a=cl.ap[:, 1:2], b=cl.ap[:, 0:1])
            yield
            act(cl, cl, AF.Ln, scale=1.0 / HD, o=cl.ap[:, 5:6], i=cl.ap[:, 2:3])
            act(cl, cl, AF.Exp, scale=-0.5, o=cl.ap[:, 7:8], i=cl.ap[:, 5:6])
            yield
            yb = ytm[par]
            stt(yb, pn, cl.ap[:, 7:8], s_t[ti], ALU.mult, ALU.mult, o=yb.ap[:, 0:HD], a=pn.ap[:, 0:HD],
                b=s_t[ti].ap, extra_r=[cl])
            yield
            pt2 = PT[par]
            for c in range(2):
                tr(pt2, pt2.ap[:, 512 + c * 128:512 + (c + 1) * 128], yb, yb.ap[:, c * 128:(c + 1) * 128], identb, identb.ap, signal=(c == 1))
            yield
            for c in range(2):
                ch = 2 * h + c
                ts("dve", yT, pt2, gcol.ap[:, ch:ch + 1], None, ALU.mult, o=yT.ap[:, 8 + ch, ti * 128:(ti + 1) * 128],
                   a=pt2.ap[:, 512 + c * 128:512 + (c + 1) * 128], extra_r=[gcol])
                yield

        def load_head_w(h):
            for half in range(2):
                for j in range(2):
                    c0 = 1024 + (half * 2 + j) * 1024 + h * 256
                    P.dma("pool", W2[half].ap[:, :, j * 256:(j + 1) * 256], w_in_v[:, :, c0:c0 + 256], writes=[W2[half]])

        def projgen(h, ti):
            tok0 = main_tiles[ti][2]
            pj = PJ[0]
            for kc in range(KC):
                mm(pj, pj.ap[:, 0:512], xnT, xnT.ap[:, kc, tok0:tok0 + 128], W2[0], W2[0].ap[:, kc, :], kc == 0, kc == KC - 1)
                if kc % 2 == 1:
                    yield
            cp("act", q_t[ti], pj, i=pj.ap[:, 0:256])
            cp("dve", k_t[ti], pj, i=pj.ap[:, 256:512])
            yield
            pj = PJ[1]
            for kc in range(KC):
                mm(pj, pj.ap[:, 0:512], xnT, xnT.ap[:, kc, tok0:tok0 + 128], W2[1], W2[1].ap[:, kc, :], kc == 0, kc == KC - 1)
                if kc % 2 == 1:
                    yield
            cp("dve", v_t[ti], pj, o=v_t[ti].ap[:, 0:HD], i=pj.ap[:, 0:256])
            act(sgt, pj, AF.Exp, scale=-1.0, i=pj.ap[:, 256:512])
            yield
            act(sgt, sgt, AF.Ln, bias=1.0)
            yield
            act(s_t[ti], sgt, AF.Exp, scale=-1.0)
            yield

        def interleave(gens):
            gens = list(gens)
            while gens:
                for g_ in list(gens):
                    try:
                        next(g_)
                    except StopIteration:
                        gens.remove(g_)

        fence(SET_PL + SET_HD + q_t + k_t + s_t + v_t)
        memset("dve", v_ext, 1.0)
        for vt_ in v_t:
            vt_.w = v_ext.w
        memset("dve", Vmask, 0.0)
        PN = PS
        load_head_w(0)
        for ti in range(9):
            interleave([projgen(0, ti)])
        for h in range(NH):
            if h + 1 < NH:
                load_head_w(h + 1)
            for s_ in range(11):
                tasks = []
                if s_ <= 8:
                    tasks.append(front(h, s_, 9 + s_, "p" if s_ < 8 else "s"))
                if 1 <= s_ <= 9:
                    tasks.append(back(h, s_ - 1, 9 + s_ - 1, "p" if s_ - 1 < 8 else "s"))
                if h + 1 < NH and 2 <= s_ <= 10:
                    tasks.append(projgen(h + 1, s_ - 2))
                interleave(tasks)
        P.checkpoint(7)
        for h in range(NH):
            for dc in range(2):
                P.dma("sp", C_p[h, dc * 128:(dc + 1) * 128, :], C32[h].ap[:, dc, 0:HD], reads=[C32[h]])
                P.dma("sp", n_p[h:h + 1, dc * 128:(dc + 1) * 128].rearrange("o p -> p o"), C32[h].ap[:, dc, HD:HDE], reads=[C32[h]], slow=True)
        P.checkpoint(7.5)
        tr(PS[0], PS[0].ap[:, 0:128], nS, nS.ap.rearrange("p b h c -> p (b h c)"), ident32, ident32.ap, signal=True)
        cp("dve", nrow, PS[0], i=PS[0].ap[:, 0:128])
        P.dma("sp", n_s, nrow.ap, reads=[nrow])

        P.checkpoint(8)
        all_src = [x_own[i * 128:(i + 1) * 128, :] for i in range(8)] + [x_smp]
        z4_users = Z4BUFS
        for i in range(9):
            P.dma("sp", x_new[i].ap, all_src[i], reads=[], writes=[x_new[i]] + z4_users)
        WO = [W2[0], W2[1]]
        for cg in range(4):
            W = WO[cg % 2]
            P.dma("pool", W.ap, w_out_v[:, :, cg * 512:(cg + 1) * 512], writes=[W])
            for i in range(9):
                pj = proj(yT, i * 128, 128, W, W.ap, 512)
                tt("dve", x_new[i], x_new[i], pj, ALU.add, o=x_new[i].ap[:, cg * 512:(cg + 1) * 512],
                   a=x_new[i].ap[:, cg * 512:(cg + 1) * 512], b=pj.ap[:, 0:512])
        P.dma("sp", Z3[:, 4096:4096 + D], norm_ffn_w.partition_broadcast(128), writes=[W2[1]])
        xn2T = Buf(Z1b[:, 0:KC * 1152].rearrange("p (k t) -> p k t", k=KC))
        fence([xnT, xn2T])
        for i in range(9):
            norm_tile(None, 128, xn2T, i * 128, x_in=x_new[i], wbuf=W2[1], wap=Z3[:, 4096:4096 + D])

        P.checkpoint(9)
        hT = Buf(Z2b[:, 0:11 * 1152].rearrange("p (f t) -> p f t", f=11))
        WG = [Buf(Z3b[:, i * 2048:(i + 1) * 2048].rearrange("p (k n) -> p k n", k=KC)) for i in range(2)]
        WUp = [Buf(Z3b[:, 4096 + i * 2048:4096 + (i + 1) * 2048].rearrange("p (k n) -> p k n", k=KC)) for i in range(2)]
        WD = [Buf(Z3b[:, 8192 + i * 2816:8192 + (i + 1) * 2816].rearrange("p (f n) -> p f n", f=11)) for i in range(2)]
        fence([W2[0], W2[1], yT, hT] + WG + WUp + WD + silt)
        tgs = [(0, 512), (512, 512), (1024, 128)]
        fr = [0]
        for qd in range(4):
            for fl in range(11):
                fc = qd * 11 + fl
                wgb = WG[fr[0] % 2]
                wub = WUp[fr[0] % 2]
                fr[0] += 1
                P.dma("pool", wgb.ap, w_gate_v[:, :, fc * 128:(fc + 1) * 128], writes=[wgb])
                P.dma("pool", wub.ap, w_up_v[:, :, fc * 128:(fc + 1) * 128], writes=[wub])
                for ti, (t0, tn) in enumerate(tgs):
                    pg = PJ[ti % 2]
                    pu = PS[ti % 2]
                    for kc in range(KC):
                        mm(pg, pg.ap[:, 0:tn], wgb, wgb.ap[:, kc, :], xn2T, xn2T.ap[:, kc, t0:t0 + tn], kc == 0, kc == KC - 1)
                    for kc in range(KC):
                        mm(pu, pu.ap[:, 0:tn], wub, wub.ap[:, kc, :], xn2T, xn2T.ap[:, kc, t0:t0 + tn], kc == 0, kc == KC - 1)
                    st_ = silt[ti % 2]
                    act(st_, pg, AF.Silu, o=st_.ap[:, 0:tn], i=pg.ap[:, 0:tn])
                    tt("dve", hT, st_, pu, ALU.mult, o=hT.ap[:, fl, t0:t0 + tn], a=st_.ap[:, 0:tn], b=pu.ap[:, 0:tn])
            for cg in range(8):
                W = WD[cg % 2]
                P.dma("pool", W.ap, w_down_v[:, qd * 11:(qd + 1) * 11, cg * 256:(cg + 1) * 256], writes=[W])
                for i in range(9):
                    pc = PC[i % 2]
                    for fl in range(11):
                        mm(pc, pc.ap[:, 0:256], hT, hT.ap[:, fl, i * 128:(i + 1) * 128], W, W.ap[:, fl, :], fl == 0, fl == 10)
                    tt("dve", x_new[i], x_new[i], pc, ALU.add, o=x_new[i].ap[:, cg * 256:(cg + 1) * 256],
                       a=x_new[i].ap[:, cg * 256:(cg + 1) * 256], b=pc.ap[:, 0:256])

        P.checkpoint(10)
        wbE = Buf(Z1[:, 0:D])
        fence([xn2T, wbE])
        P.dma("sp", wbE.ap, norm_final_w.partition_broadcast(128), writes=[wbE])
        for i in range(9):
            ssb = SS[i % 2]
            xb = x_new[i]
            act(sq, xb, AF.Square, accum=(ssb, ssb.ap[:, 0:1]))
            act(ssb, ssb, AF.Ln, bias=EPS, scale=1.0 / D, o=ssb.ap[:, 1:2], i=ssb.ap[:, 0:1])
            act(ssb, ssb, AF.Exp, scale=-0.5, o=ssb.ap[:, 2:3], i=ssb.ap[:, 1:2])
            stt(xb, xb, ssb.ap[:, 2:3], wbE, ALU.mult, ALU.mult, extra_r=[ssb])
            dst = y_own[i * 128:(i + 1) * 128, :] if i < 8 else y_smp
            P.dma("sp", dst, xb.ap, reads=[xb])

        sem_es = ExitStack()
        with sem_es:
            sems = {}
            for e in ENGS:
                sems[e] = sem_es.enter_context(nc.semaphore("s_" + e))
            for q, n in P.ndma.items():
                for i in range(n):
                    sems[(q, i)] = sem_es.enter_context(nc.semaphore("d_%s%d" % (q, i)))
            with nc.Block() as block:
                @block.tensor
                def _(e):
                    P.replay("pe", e, sems)

                @block.scalar
                def _(e):
                    P.replay("act", e, sems)

                @block.vector
                def _(e):
                    P.replay("dve", e, sems)

                @block.gpsimd
                def _(e):
                    P.replay("pool", e, sems)
                    P.final_waits("pool", e, sems)

                @block.sync
                def _(e):
                    P.replay("sp", e, sems)
                    P.final_waits("sp", e, sems)
    return nc


_NC = None


def _get_nc():
    global _NC
    if _NC is None:
        _NC = build_nc()
    return _NC


def kernel(x_prompt, x_sample, state_pool, state_mlstm_C, state_mlstm_n, state_mlstm_m, meta_tokens, norm_mix_w, w_in,
           b_igate, b_fgate, w_pool, pool_scale, mlstm_norm_w, w_out, norm_ffn_w, w_gate, w_up, w_down, norm_final_w):
    f = lambda a: np.ascontiguousarray(np.asarray(a, dtype=np.float32))
    xp, xs = f(x_prompt), f(x_sample)
    meta = f(meta_tokens)
    shared = {
        "norm_mix_w": f(norm_mix_w)[0], "w_in": f(w_in)[0], "b_igate": f(b_igate)[0], "b_fgate": f(b_fgate)[0],
        "w_pool": f(w_pool)[0], "pool_scale": f(pool_scale)[0], "mlstm_norm_w": f(mlstm_norm_w)[0], "w_out": f(w_out)[0],
        "norm_ffn_w": f(norm_ffn_w)[0], "w_gate": f(w_gate)[0], "w_up": f(w_up)[0], "w_down": f(w_down)[0],
        "norm_final_w": f(norm_final_w),
    }
    sp, sC, sn, sm = f(state_pool)[0], f(state_mlstm_C)[0], f(state_mlstm_n)[0], f(state_mlstm_m)[0]
    in_maps = []
    for c in range(8):
        s, h = c // 2, c % 2
        m = dict(shared)
        m["x_meta"] = meta
        m["x_pre"] = xp[s, 0:1024] if h == 1 else np.zeros((1024, D), np.float32)
        m["x_own"] = np.ascontiguousarray(xp[s, 1024 * h:1024 * (h + 1)])
        m["x_halo"] = meta if h == 0 else np.ascontiguousarray(xp[s, 1008:1024])
        m["x_smp"] = np.ascontiguousarray(xs[16 * c:16 * (c + 1)].reshape(128, D))
        m["flag"] = np.full((128, 1), float(h), np.float32)
        m["st_pool"] = np.ascontiguousarray(sp[16 * c:16 * (c + 1)])
        m["st_C"] = np.ascontiguousarray(sC[16 * c:16 * (c + 1)])
        m["st_n"] = np.ascontiguousarray(sn[16 * c:16 * (c + 1)].reshape(64, HD))
        m["st_m"] = np.ascontiguousarray(sm[16 * c:16 * (c + 1)])
        in_maps.append(m)
    nc = _get_nc()
    res = run_bass_kernel_spmd(nc, in_maps, core_ids=list(range(8))).results
    y_prompt = np.stack([np.concatenate([res[2 * s]["y_own"], res[2 * s + 1]["y_own"]], axis=0) for s in range(4)])
    y_sample = np.concatenate([r["y_smp"] for r in res], axis=0).reshape(128, 8, D)
    pool_pp = np.stack([res[2 * s + 1]["pool_p"] for s in range(4)])[None]
    C_pp = np.stack([res[2 * s + 1]["C_p"] for s in range(4)])[None]
    n_pp = np.stack([res[2 * s + 1]["n_p"] for s in range(4)])[None]
    m_pp = np.stack([res[2 * s + 1]["m_p"].reshape(NH) for s in range(4)])[None]
    pool_ss = np.concatenate([r["pool_s"] for r in res], axis=0)[None]
    C_ss = np.concatenate([r["C_s"] for r in res], axis=0)[None]
    n_ss = np.concatenate([r["n_s"].reshape(16, NH, HD) for r in res], axis=0)[None]
    m_ss = np.concatenate([r["m_s"] for r in res], axis=0)[None]
    outs = (y_prompt, y_sample, pool_pp, C_pp, n_pp, m_pp, pool_ss, C_ss, n_ss, m_ss)
    return tuple(np.ascontiguousarray(o, dtype=np.float32) for o in outs)
```

```python
import math
import numpy as np
import concourse.bass as bass
import concourse.mybir as mybir
from concourse.bass_utils import run_bass_kernel_spmd

F32 = mybir.dt.float32
BF16 = mybir.dt.bfloat16
AF = mybir.ActivationFunctionType
ALU = mybir.AluOpType
AX = mybir.AxisListType

D = 2048
KC = 16
NH = 4
HD = 256
HDE = 257
DFF = 5632
FCH = 44
EPS = 1e-6
LN16 = math.log(16.0)
BIG = 1.0e30
ENGS = ("pe", "act", "dve", "pool", "sp")
STOP = None


class Buf:
    __slots__ = ("ap", "w", "r", "excl")

    def __init__(self, ap, excl=False):
        self.ap = ap
        self.w = None
        self.r = []
        self.excl = excl


class Prog:
    def __init__(self):
        self.streams = {e: [] for e in ENGS}
        self.cnt = {e: 0 for e in ENGS}
        self.seen = {e: {} for e in ENGS}
        self.ndma = {"sp": 20, "pool": 12}
        self.dma_cnt = {q: [0] * n for q, n in self.ndma.items()}
        self.dma_rr = {q: 0 for q in self.ndma}
        self.dead = False
        self._q = None
        self.stop = STOP

    def checkpoint(self, k):
        if self.stop is not None and k >= self.stop - 1e-9:
            self.dead = True

    def _waits(self, eng, deps):
        out = []
        for d in deps:
            if d is None:
                continue
            k, c = d
            if self.seen[eng].get(k, 0) >= c:
                continue
            self.seen[eng][k] = c
            out.append((k, c))
        return out

    def _deps(self, eng, reads, writes):
        deps = []
        for b in reads:
            if b.w is not None and not (eng == "pe" and b.w[0] == "pe"):
                deps.append(b.w)
            if b.excl:
                for t in b.r:
                    if t is not None and t[0] != eng:
                        deps.append(t)
        for b in writes:
            for t in [b.w] + b.r:
                if t is not None and not (eng == "pe" and t[0] == "pe"):
                    deps.append(t)
        return deps

    def defer(self, f):
        self._q = []
        f()
        q, self._q = self._q, None

        def g():
            for kind, a in q:
                if kind == "op":
                    self.op(*a)
                else:
                    self.dma(*a)
                yield
        return g()

    def op(self, eng, fn, reads=(), writes=(), signal=True):
        if self.dead:
            return None
        if self._q is not None:
            self._q.append(("op", (eng, fn, list(reads), list(writes), signal)))
            return None
        waits = self._waits(eng, self._deps(eng, reads, writes))
        if signal:
            self.cnt[eng] += 1
            tok = (eng, self.cnt[eng])
        else:
            tok = (eng, self.cnt[eng] + 1)
        self.streams[eng].append((waits, fn, "inc" if signal else None))
        for b in reads:
            b.r.append(tok)
        for b in writes:
            b.w = tok
            b.r = []
        return tok

    def dma(self, q, out_ap, in_ap, reads=(), writes=(), slow=False):
        if self.dead:
            return None
        if self._q is not None:
            self._q.append(("dma", (q, out_ap, in_ap, list(reads), list(writes), slow)))
            return None
        i = self.dma_rr[q]
        self.dma_rr[q] = (i + 1) % self.ndma[q]
        key = (q, i)
        deps = self._deps("dmaq", reads, writes)
        prev = self.dma_cnt[q][i]
        if prev:
            deps.append((key, prev))
        waits = self._waits(q, deps)
        self.dma_cnt[q][i] = prev + 16
        tok = (key, prev + 16)
        if slow:
            fn = lambda e, o=out_ap, a=in_ap: e.dma_start(out=o, in_=a, allow_slow_non_contiguous=True)
        else:
            fn = lambda e, o=out_ap, a=in_ap: e.dma_start(out=o, in_=a)
        self.streams[q].append((waits, fn, key))
        for b in reads:
            b.r.append(tok)
        for b in writes:
            b.w = tok
            b.r = []
        return tok

    def replay(self, eng, e, sems):
        for waits, fn, sig in self.streams[eng]:
            for k, c in waits:
                e.wait_ge(sems[k], c)
            ins = fn(e)
            if sig == "inc":
                ins.then_inc(sems[eng], 1)
            elif sig is not None:
                ins.then_inc(sems[sig], 16)

    def final_waits(self, eng, e, sems):
        for i, c in enumerate(self.dma_cnt.get(eng, [])):
            if c:
                e.wait_ge(sems[(eng, i)], c)


def build_nc():
    nc = bass.Bass("TRN2", target_bir_lowering=False)
    P = Prog()

    def din(name, shape, dt=F32):
        return nc.dram_tensor(name, list(shape), dt, kind="ExternalInput").ap()

    def dout(name, shape, dt=F32):
        return nc.dram_tensor(name, list(shape), dt, kind="ExternalOutput").ap()

    x_meta = din("x_meta", [16, D])
    x_pre = din("x_pre", [1024, D])
    x_own = din("x_own", [1024, D])
    x_halo = din("x_halo", [16, D])
    x_smp = din("x_smp", [128, D])
    flag_d = din("flag", [128, 1])
    st_pool = din("st_pool", [16, 15, 1024])
    st_C = din("st_C", [16, NH, HD, HD])
    st_n = din("st_n", [64, HD])
    st_m = din("st_m", [16, NH])
    norm_mix_w = din("norm_mix_w", [D])
    w_in = din("w_in", [D, 5128])
    b_ig = din("b_igate", [NH])
    b_fg = din("b_fgate", [NH])
    w_pool = din("w_pool", [4, 256, 256])
    pool_scale = din("pool_scale", [1024])
    mnorm_w = din("mlstm_norm_w", [1024])
    w_out = din("w_out", [D, D])
    norm_ffn_w = din("norm_ffn_w", [D])
    w_gate = din("w_gate", [D, DFF])
    w_up = din("w_up", [D, DFF])
    w_down = din("w_down", [DFF, D])
    norm_final_w = din("norm_final_w", [D])

    y_own = dout("y_own", [1024, D])
    y_smp = dout("y_smp", [128, D])
    pool_p = dout("pool_p", [15, 1024])
    C_p = dout("C_p", [NH, HD, HD])
    n_p = dout("n_p", [NH, HD])
    m_p = dout("m_p", [1, NH])
    pool_s = dout("pool_s", [16, 15, 1024])
    C_s = dout("C_s", [16, NH, HD, HD])
    n_s = dout("n_s", [128, 128])
    m_s = dout("m_s", [16, NH])

    w_in_v = w_in.rearrange("(kc p) n -> p kc n", p=128)
    w_out_v = w_out.rearrange("(kc p) n -> p kc n", p=128)
    w_gate_v = w_gate.rearrange("(kc p) n -> p kc n", p=128)
    w_up_v = w_up.rearrange("(kc p) n -> p kc n", p=128)
    w_down_v = w_down.rearrange("(fc p) n -> p fc n", p=128)

    from contextlib import ExitStack
    es = ExitStack()

    def sb(name, shape, dt):
        return es.enter_context(nc.sbuf_tensor(name, list(shape), dt))

    def ps(name, shape, dt):
        return es.enter_context(nc.psum_tensor(name, list(shape), dt))

    with es:
        Z1 = sb("z1", [128, KC * 1168 // 2], F32)
        Z2 = sb("z2", [128, KC * 1152 // 2], F32)
        Z3 = sb("z3", [128, 8192], F32)
        Z4 = sb("z4", [128, 9 * D], F32)
        Z1b = Z1[:, :].bitcast(BF16)
        Z2b = Z2[:, :].bitcast(BF16)
        Z3b = Z3[:, :].bitcast(BF16)
        Z4BUFS = []
        tl = [9600]

        def zt(nfl, dt=F32, shape=None):
            a0 = tl[0]
            tl[0] += nfl
            assert tl[0] <= 9 * D, tl[0]
            ap = Z4[:, a0:a0 + nfl]
            if dt == BF16:
                ap = ap.bitcast(BF16)
            if shape is not None:
                ap = ap.rearrange(shape[0], **shape[1])
            bb = Buf(ap)
            Z4BUFS.append(bb)
            return bb

        ones32 = Buf(sb("ones32", [128, 128], F32)[:, :])
        ident32 = Buf(sb("ident32", [128, 128], F32)[:, :])
        identb = Buf(sb("identb", [128, 128], BF16)[:, :])
        tri32 = Buf(sb("tri32", [128, 128], F32)[:, :])
        tril32 = Buf(sb("tril32", [128, 128], F32)[:, :])
        maskneg = Buf(sb("maskneg", [128, 128], F32)[:, :])
        maskposb = Buf(sb("maskposb", [128, 128], BF16)[:, :])
        triB32 = Buf(sb("triB32", [128, 128], F32)[:, :])
        trilB32 = Buf(sb("trilB32", [128, 128], F32)[:, :])
        masknegB = Buf(sb("masknegB", [128, 128], F32)[:, :])
        maskposbB = Buf(sb("maskposbB", [128, 128], BF16)[:, :])
        SLm = Buf(sb("SLm", [128, 128], F32)[:, :])
        sel32 = Buf(sb("sel32", [16, 128], F32)[:, :])
        selA = Buf(sb("selA", [16, 128], F32)[:, :])
        sellastT = Buf(sb("sellastT", [16, 128], F32)[:, :])
        blockmask = Buf(sb("blockmask", [128, 16], F32)[:, :])
        blockA = Buf(sb("blockA", [128, 16], F32)[:, :])
        sellast = Buf(sb("sellast", [128, 16], F32)[:, :])
        bias8 = Buf(sb("bias8", [128, 8], F32)[:, :])
        gcol = Buf(sb("gcol", [128, 8], F32)[:, :])
        pscol = Buf(sb("pscol", [128, 8], F32)[:, :])
        flagc = Buf(sb("flagc", [128, 1], F32)[:, :])
        fsc = Buf(sb("fsc", [128, 1], F32)[:, :])
        negBc = Buf(sb("negBc", [128, 4], F32)[:, :])
        Mc = Buf(sb("Mc", [128, 4], F32)[:, :])
        negBm = Buf(sb("negBm", [128, 4], F32)[:, :])
        Mm = Buf(sb("Mm", [128, 4], F32)[:, :])
        C32t = sb("C32", [128, NH, 2, HDE], F32)
        C32 = [Buf(C32t[:, h, :, :]) for h in range(NH)]
        Cbft = sb("Cbf", [128, NH, 2, HDE], BF16)
        Cbf = [Buf(Cbft[:, h, :, :]) for h in range(NH)]
        sst = sb("ss", [128, 2, 8], F32)
        SS = [Buf(sst[:, i, :]) for i in range(2)]
        xnb1 = Buf(sb("xnb", [128, D], BF16)[:, :])
        xnb = [xnb1, xnb1]
        sq = xnb1
        wpl = Buf(sb("wpl", [128, 4, 2, 256], BF16)[:, :, :, :])

        selrepA = zt(1024, BF16, ("p (b t) -> p b t", dict(b=16)))
        wg8 = zt(64, BF16, ("p (k n) -> p k n", dict(k=KC)))
        gwsb = zt(10 * 20 * 4)
        GWl = [Buf(gwsb.ap[:, j * 80:(j + 1) * 80].rearrange("p (k h) -> p k h", k=20)) for j in range(10)]
        Z4BUFS.extend(GWl)
        GW = GWl[0:9] + GWl[0:10]
        dg = zt(512, F32, ("p (h s) -> p h s", dict(h=4)))
        tmpm = zt(512, F32, ("p (h s) -> p h s", dict(h=4)))
        dgM = [zt(128) for i in range(2)]
        DT = [zt(128) for i in range(2)]
        interb = [zt(128) for i in range(2)]
        STb = [zt(64, BF16) for i in range(2)]
        qkT = [zt(256, BF16, ("p (a t) -> p a t", dict(a=4))) for i in range(2)]
        qtl = [zt(128, BF16, ("p (a t) -> p a t", dict(a=2))) for i in range(2)]
        kwb = [zt(128, BF16) for i in range(2)]
        colsb = zt(24)
        COLS = [Buf(colsb.ap[:, i * 12:(i + 1) * 12]) for i in range(2)]
        Z4BUFS.extend(COLS)
        ytm = [zt(128, BF16) for i in range(2)]
        sqs = zt(128, BF16)
        dsel = zt(64, F32, ("p (b h) -> p b h", dict(b=16)))
        decs = zt(64, F32, ("p (b h) -> p b h", dict(b=16)))
        nT = zt(128, F32, ("p (b h c) -> p b h c", dict(b=16, h=4)))
        nS = zt(128, F32, ("p (b h c) -> p b h c", dict(b=16, h=4)))
        nrow = zt(128)
        nin = zt(256)
        msm = zt(4)
        msout = zt(4)
        mpos = zt(4)
        mvec = zt(4)
        qt32 = zt(256, F32, ("p (a t) -> p a t", dict(a=2)))
        qm = [zt(256, BF16, ("p (b t) -> p b t", dict(b=4))) for i in range(2)]
        C16 = zt(514, BF16, ("p (b c v) -> p b c v", dict(b=2, c=2)))
        C32s3 = zt(2 * 2 * HDE, F32, ("p (b c v) -> p b c v", dict(b=2, c=2)))
        sgt = zt(256)

        o4 = [0]

        def z4(nfl):
            a = o4[0]
            o4[0] += nfl
            assert o4[0] <= 9600, o4[0]
            return Z4[:, a:a + nfl]
        xt = [Buf(z4(D)) for _ in range(2)]
        o4[0] = 0
        utm = Buf(z4(1024))
        uT = Buf(z4(8 * 143).rearrange("p (c t) -> p c t", c=8))
        uTs = Buf(z4(8 * 16 * 23).rearrange("p (c b t) -> p c b t", c=8, b=16))
        sw = [Buf(z4(1472)) for _ in range(2)]
        dTb = Buf(z4(512).bitcast(BF16).rearrange("p (c t) -> p c t", c=8))
        sptm = Buf(z4(1024))
        SET_PL = [utm, uT, uTs, sw[0], sw[1], dTb, sptm]
        o4[0] = 0
        q_tm = Buf(z4(9 * 128).bitcast(BF16).rearrange("p (i f) -> p i f", i=9))
        k_tm = Buf(z4(9 * 128).bitcast(BF16).rearrange("p (i f) -> p i f", i=9))
        sigo = Buf(z4(9 * 128).bitcast(BF16).rearrange("p (i f) -> p i f", i=9))
        v_ext = Buf(z4(9 * 130).bitcast(BF16).rearrange("p (i f) -> p i f", i=9))
        Vmask = Buf(z4(16 * 130).bitcast(BF16).rearrange("p (b f) -> p b f", b=16))
        C32s = [Buf(z4(2 * 2 * HDE).rearrange("p (b c v) -> p b c v", b=2, c=2)) for _ in range(2)]
        C32s.append(C32s3)
        SET_HD = [q_tm, k_tm, sigo, v_ext, Vmask, C32s[0], C32s[1]]
        Z4BUFS.extend(xt + SET_PL + SET_HD)
        x_new = [Buf(Z4[:, i * D:(i + 1) * D]) for i in range(9)]
        C32mt = Z2[:, 2400:2400 + NH * 2 * HDE].rearrange("p (h c v) -> p h c v", h=NH, c=2)
        C32m = [Buf(C32mt[:, h, :, :]) for h in range(NH)]
        wb = Buf(Z2[:, 4600:4600 + D])
        silt = [Buf(Z2[:, 6400 + i * 512:6400 + (i + 1) * 512]) for i in range(2)]
        W2 = [Buf(Z3b[:, i * 8192:(i + 1) * 8192].rearrange("p (k n) -> p k n", k=KC)) for i in range(2)]

        def fence(bufs):
            P.op("dve", lambda e: e.memset(fsc.ap, 0.0), writes=[fsc] + list(bufs))

        PJ = [Buf(ps("pj%d" % i, [128, 512], F32)[:, :], excl=True) for i in range(2)]
        PT = [Buf(ps("pt%d" % i, [128, 1024], BF16)[:, :], excl=True) for i in range(2)]
        PS = [Buf(ps("ps%d" % i, [128, 512], F32)[:, :], excl=True) for i in range(2)]
        PC = [Buf(ps("pc%d" % i, [128, 512], F32)[:, :], excl=True) for i in range(2)]

        def act(out_b, in_b, func, bias=0.0, scale=1.0, accum=None, extra_r=(), o=None, i=None):
            oa = out_b.ap if o is None else o
            ia = in_b.ap if i is None else i
            kw = {}
            if accum is not None:
                kw["accum_out"] = accum[1]
            w = [out_b] + ([accum[0]] if accum is not None else [])
            return P.op("act", lambda e: e.activation(out=oa, in_=ia, func=func, bias=bias, scale=scale, **kw),
                        reads=[in_b] + list(extra_r), writes=w)

        def tt(eng, out_b, a_b, b_b, op, o=None, a=None, b=None):
            oa = out_b.ap if o is None else o
            aa = a_b.ap if a is None else a
            ba = b_b.ap if b is None else b
            return P.op(eng, lambda e: e.tensor_tensor(out=oa, in0=aa, in1=ba, op=op), reads=[a_b, b_b], writes=[out_b])

        def ts(eng, out_b, a_b, s1, s2, op0, op1=None, o=None, a=None, extra_r=()):
            oa = out_b.ap if o is None else o
            aa = a_b.ap if a is None else a
            if op1 is None:
                f = lambda e: e.tensor_scalar(out=oa, in0=aa, scalar1=s1, scalar2=None, op0=op0)
            else:
                f = lambda e: e.tensor_scalar(out=oa, in0=aa, scalar1=s1, scalar2=s2, op0=op0, op1=op1)
            return P.op(eng, f, reads=[a_b] + list(extra_r), writes=[out_b])

        def stt(out_b, a_b, sc, b_b, op0, op1, o=None, a=None, b=None, extra_r=()):
            oa = out_b.ap if o is None else o
            aa = a_b.ap if a is None else a
            ba = b_b.ap if b is None else b
            return P.op("dve", lambda e: e.scalar_tensor_tensor(out=oa, in0=aa, scalar=sc, in1=ba, op0=op0, op1=op1),
                        reads=[a_b, b_b] + list(extra_r), writes=[out_b])

        def red(out_b, in_b, o, i):
            return P.op("dve", lambda e: e.tensor_reduce(out=o, in_=i, axis=AX.X, op=ALU.max), reads=[in_b], writes=[out_b])

        def cp(eng, out_b, in_b, o=None, i=None):
            oa = out_b.ap if o is None else o
            ia = in_b.ap if i is None else i
            if eng == "act":
                return P.op("act", lambda e: e.copy(out=oa, in_=ia), reads=[in_b], writes=[out_b])
            return P.op(eng, lambda e: e.tensor_copy(out=oa, in_=ia), reads=[in_b], writes=[out_b])

        def mm(out_b, o, l_b, l, r_b, r, start, stop, signal=None):
            if signal is None:
                signal = stop
            return P.op("pe", lambda e: e.matmul(out=o, lhsT=l, rhs=r, start=start, stop=stop),
                        reads=[l_b, r_b], writes=[out_b], signal=signal)

        def tr(out_b, o, in_b, i, id_b, idap, signal):
            return P.op("pe", lambda e: e.transpose(out=o, in_=i, identity=idap), reads=[in_b, id_b], writes=[out_b], signal=signal)

        def memset(eng, b, val, ap=None):
            a = b.ap if ap is None else ap
            return P.op(eng, lambda e: e.memset(a, val), writes=[b])

        def asel(out_b, in_b, pattern, cmp, base, cm, o=None, i=None):
            oa = out_b.ap if o is None else o
            ia = in_b.ap if i is None else i
            return P.op("pool", lambda e: e.affine_select(out=oa, in_=ia, pattern=pattern, compare_op=cmp, fill=0.0,
                                                          base=base, channel_multiplier=cm), reads=[in_b], writes=[out_b])

        memset("pool", ones32, 1.0)
        asel(ident32, ones32, [[1, 128]], ALU.is_equal, 0, -1)
        asel(tri32, ones32, [[1, 128]], ALU.is_ge, 0, -1)
        asel(tril32, ones32, [[-1, 128]], ALU.is_ge, 0, 1)
        cp("pool", identb, ident32)
        ts("pool", maskneg, tril32, BIG, -BIG, ALU.mult, ALU.add)
        ts("pool", maskposb, tri32, -BIG, BIG, ALU.mult, ALU.add)
        asel(selA, ones32, [[1, 128]], ALU.is_ge, 0, -8, i=ones32.ap[0:16, :])
        asel(sel32, selA, [[-1, 128]], ALU.is_ge, 7, 8)
        asel(sellastT, ones32, [[1, 128]], ALU.is_equal, -7, -8, i=ones32.ap[0:16, :])
        asel(blockA, ones32, [[-8, 16]], ALU.is_ge, 0, 1, i=ones32.ap[:, 0:16])
        asel(blockmask, blockA, [[8, 16]], ALU.is_ge, 7, -1)
        asel(sellast, ones32, [[-8, 16]], ALU.is_equal, -7, 1, i=ones32.ap[:, 0:16])
        memset("pool", selrepA, 1.0)
        asel(selrepA, selrepA, [[-8, 16], [1, 128]], ALU.is_ge, 0, 0)
        asel(selrepA, selrepA, [[8, 16], [-1, 128]], ALU.is_ge, 7, 0)
        SELREP = selrepA
        mm(PS[0], PS[0].ap[:, 0:128], sel32, sel32.ap, sel32, sel32.ap, True, True)
        tt("dve", triB32, tri32, PS[0], ALU.mult, b=PS[0].ap[:, 0:128])
        tt("dve", trilB32, tril32, PS[0], ALU.mult, b=PS[0].ap[:, 0:128])
        ts("dve", masknegB, trilB32, BIG, -BIG, ALU.mult, ALU.add)
        ts("dve", maskposbB, triB32, -BIG, BIG, ALU.mult, ALU.add)
        mm(PS[1], PS[1].ap[:, 0:128], sellastT, sellastT.ap, sel32, sel32.ap, True, True)
        cp("dve", SLm, PS[1], i=PS[1].ap[:, 0:128])

        P.dma("sp", bias8.ap[:, 0:4], b_ig.partition_broadcast(128), writes=[bias8])
        P.dma("sp", bias8.ap[:, 4:8], b_fg.partition_broadcast(128), writes=[bias8])
        P.dma("sp", gcol.ap, mnorm_w.rearrange("(c p) -> p c", p=128), writes=[gcol], slow=True)
        P.dma("sp", pscol.ap, pool_scale.rearrange("(c p) -> p c", p=128), writes=[pscol], slow=True)
        P.dma("sp", flagc.ap, flag_d, writes=[flagc])
        P.dma("sp", wb.ap, norm_mix_w.partition_broadcast(128), writes=[wb])
        P.dma("pool", wg8.ap, w_in_v[:, :, 5120:5128], writes=[wg8])
        P.dma("pool", wpl.ap, w_pool.rearrange("g (c p) e -> p g c e", p=128), writes=[wpl])
        for h in range(NH):
            memset("dve", C32[h], 0.0)
            memset("dve", Cbf[h], 0.0)
        memset("dve", negBc, 0.0)
        memset("dve", Mc, 0.0)

        P.checkpoint(1)
        rr = [0]

        def norm_tile(src_ap, L, dstT, tok0, x_in=None, wbuf=None, wap=None):
            i = rr[0] % 2
            rr[0] += 1
            if x_in is None:
                xb = xt[i]
                P.dma("sp", xb.ap[:L, :], src_ap, writes=[xb])
            else:
                xb = x_in
            ssb = SS[i]
            act(sq, xb, AF.Square, accum=(ssb, ssb.ap[:L, 0:1]), o=sq.ap[:L, :], i=xb.ap[:L, :])
            act(ssb, ssb, AF.Ln, bias=EPS, scale=1.0 / D, o=ssb.ap[:L, 1:2], i=ssb.ap[:L, 0:1])
            act(ssb, ssb, AF.Exp, scale=-0.5, o=ssb.ap[:L, 2:3], i=ssb.ap[:L, 1:2])
            if wbuf is None:
                wbuf, wap = wb, wb.ap
            stt(xnb[i], xb, ssb.ap[:L, 2:3], wbuf, ALU.mult, ALU.mult, o=xnb[i].ap[:L, :], a=xb.ap[:L, :], b=wap[:L, :],
                extra_r=[ssb])
            for half in range(2):
                pt = PT[half]
                for j in range(8):
                    kc = half * 8 + j
                    tr(pt, pt.ap[:, j * 128:j * 128 + L], xnb[i], xnb[i].ap[:L, kc * 128:(kc + 1) * 128],
                       identb, identb.ap[:L, :L], signal=(j == 7))
                src = pt.ap.rearrange("p (j t) -> p j t", j=8)[:, :, 0:L]
                cp("act" if half == 0 else "dve", dstT, pt, o=dstT.ap[:, half * 8:half * 8 + 8, tok0:tok0 + L], i=src)
            return xb

        pjr = [0]

        def proj(xT, tok0, L, Wb, Wap, ncols):
            pj = PJ[pjr[0] % 2]
            pjr[0] += 1
            for kc in range(KC):
                mm(pj, pj.ap[:L, 0:ncols], xT, xT.ap[:, kc, tok0:tok0 + L], Wb, Wap[:, kc, 0:ncols], kc == 0, kc == KC - 1)
            return pj

        K_IG, K_Z, K_E, K_SP, K_NEGB, K_A, K_M, K_NEGM, K_EM, K_MPREV, K_MEND, K_W, K_DEC, K_T1, K_CMX, K_T2, K_AADJ, K_CME, K_INTER, K_EM2 = range(20)

        def gate_prep(xT, tok0, L, g, mode):
            G = GW[g]

            def k(kind, rows=L):
                return G.ap[:rows, kind, :]
            pj = PC[1]
            for kc in range(KC):
                mm(pj, pj.ap[:L, 0:8], xT, xT.ap[:, kc, tok0:tok0 + L], wg8, wg8.ap[:, kc, 0:8], kc == 0, kc == KC - 1)
            tt("dve", G, pj, bias8, ALU.add, o=G.ap[:L, 0:2, :].rearrange("p a h -> p (a h)"), a=pj.ap[:L, 0:8], b=bias8.ap[:L, :])
            act(G, G, AF.Exp, scale=-1.0, o=k(K_E), i=k(K_Z))
            act(G, G, AF.Ln, bias=1.0, o=k(K_SP), i=k(K_E))
            TRI = tri32 if mode == "p" else triB32
            MN = maskneg if mode == "p" else masknegB
            p0 = PS[0]
            mm(p0, p0.ap[:L, 0:4], TRI, TRI.ap[:L, :L], G, k(K_SP), True, True)
            if mode == "p":
                mm(p0, p0.ap[:, 4:8], ones32, ones32.ap[:L, :], G, k(K_SP), True, True)
                tt("dve", G, p0, negBc, ALU.add, o=k(K_NEGB), a=p0.ap[:L, 0:4], b=negBc.ap[:L, :])
            else:
                cp("dve", G, p0, o=k(K_NEGB), i=p0.ap[:L, 0:4])
                mm(p0, p0.ap[:, 8:12], sel32, sel32.ap, msm, msm.ap[0:16, :], True, True)
                cp("dve", G, p0, o=k(K_MPREV, 128), i=p0.ap[:, 8:12])
            tt("dve", G, G, G, ALU.add, o=k(K_A), a=k(K_IG), b=k(K_NEGB))
            ts("dve", G, G, -LN16, None, ALU.add, o=k(K_AADJ), a=k(K_A))
            tt("dve", dg, ident32, G, ALU.mult, o=dg.ap[:L, :, :L],
               a=ident32.ap[:L, :L].unsqueeze(1).to_broadcast([L, 4, L]),
               b=k(K_A).unsqueeze(2).to_broadcast([L, 4, L]))
            p1 = PS[1]
            for h in range(NH):
                mm(p1, p1.ap[:, h * 128:h * 128 + L], ones32, ones32.ap[:L, :], dg, dg.ap[:L, h, :L], True, True, signal=(h == NH - 1))
            Ab = p1.ap.rearrange("p (h s) -> p h s", h=4)
            tt("dve", tmpm, p1, MN, ALU.add, o=tmpm.ap[:L, :, :L], a=Ab[:L, :, :L],
               b=MN.ap[:L, :L].unsqueeze(1).to_broadcast([L, 4, L]))
            red(G, tmpm, k(K_CMX), tmpm.ap[:L, :, :L])
            if mode == "p":
                cp("dve", G, Mc, o=k(K_MPREV, 128), i=Mc.ap)
                tt("dve", G, G, Mc, ALU.max, o=k(K_M), a=k(K_CMX), b=Mc.ap[:L, :])
                red(G, p1, k(K_CME, 128), Ab[:, :, :L])
                tt("dve", G, G, Mc, ALU.max, o=k(K_MEND, 128), a=k(K_CME, 128), b=Mc.ap)
            else:
                tt("dve", G, G, G, ALU.max, o=k(K_M), a=k(K_CMX), b=k(K_MPREV))
                mm(p0, p0.ap[:, 12:16], SLm, SLm.ap, G, k(K_M), True, True)
                cp("dve", G, p0, o=k(K_MEND, 128), i=p0.ap[:, 12:16])
            tt("dve", G, G, G, ALU.subtract, o=k(K_NEGM), a=k(K_NEGB), b=k(K_M))
            tt("dve", G, G, G, ALU.subtract, o=k(K_INTER), a=k(K_MPREV), b=k(K_M))
            act(G, G, AF.Exp, o=k(K_INTER), i=k(K_INTER))
            act(G, G, AF.Exp, o=k(K_EM), i=k(K_NEGM))
            act(G, G, AF.Exp, scale=2.0, o=k(K_EM2), i=k(K_NEGM))
            tt("dve", G, G, G, ALU.subtract, o=k(K_T1), a=k(K_AADJ), b=k(K_MEND))
            act(G, G, AF.Exp, o=k(K_W), i=k(K_T1))
            tt("dve", G, G, G, ALU.subtract, o=k(K_T2, 128), a=k(K_MPREV, 128), b=k(K_MEND, 128))
            act(G, G, AF.Exp, o=k(K_DEC, 128), i=k(K_T2, 128))
            if mode == "p":
                cp("dve", Mc, G, i=k(K_MEND, 128))
                tt("dve", negBc, negBc, p0, ALU.add, b=p0.ap[:, 4:8])
            else:
                tt("dve", dsel, sellast, G, ALU.mult, a=sellast.ap.unsqueeze(2).to_broadcast([128, 16, 4]),
                   b=k(K_T2, 128).unsqueeze(1).to_broadcast([128, 16, 4]))
                mm(p1, p1.ap[:, 0:64], ones32, ones32.ap, dsel, dsel.ap.rearrange("p b h -> p (b h)"), True, True)
                act(decs, p1, AF.Exp, o=decs.ap.rearrange("p b h -> p (b h)"), i=p1.ap[:, 0:64])
                ts("dve", mpos, G, -1.0, None, ALU.mult, a=k(K_NEGM))
                mm(p0, p0.ap[0:16, 16:20], sellast, sellast.ap, mpos, mpos.ap, True, True)
                cp("dve", msout, p0, o=msout.ap[0:16, :], i=p0.ap[0:16, 16:20])
                P.dma("sp", m_s, msout.ap[0:16, :], reads=[msout])

        cr = [0]

        def state_update(h, g, L, ktm_ap, vext_ap, k_b, v_b):
            G = GW[g]
            i = cr[0] % 2
            kw = kwb[i]
            ts("dve", kw, k_b, G.ap[:L, K_W, h:h + 1], None, ALU.mult, o=kw.ap[:L, :], a=ktm_ap, extra_r=[G])
            for dc in range(2):
                mm(PC[dc], PC[dc].ap[:, 0:HDE], kw, kw.ap[:L, dc * 128:(dc + 1) * 128], v_b, vext_ap, True, True)
            for dc in range(2):
                stt(C32[h], C32[h], G.ap[:, K_DEC, h:h + 1], PC[dc], ALU.mult, ALU.add,
                    o=C32[h].ap[:, dc, :], a=C32[h].ap[:, dc, :], b=PC[dc].ap[:, 0:HDE], extra_r=[G])
            cp("act", Cbf[h], C32[h])

        xnTp = Buf(Z1b[:, 0:KC * 1040].rearrange("p (k t) -> p k t", k=KC))
        kvp = Buf(Z2b[:, 0:9 * 516].rearrange("p (i f) -> p i f", i=9))
        pre_tiles = [(x_meta, 16, 1024)] + [(x_pre[i * 128:(i + 1) * 128, :], 128, i * 128) for i in range(8)]
        for (src, L, tok0) in pre_tiles:
            norm_tile(src, L, xnTp, tok0)
        P.checkpoint(2)
        def gatesA_plain():
            for gi, (src, L, tok0) in enumerate(pre_tiles):
                gate_prep(xnTp, tok0, L, gi, "p")
                if gi == 0:
                    cp("dve", negBm, negBc)
                    cp("dve", Mm, Mc)

        def gatesA():
            return P.defer(gatesA_plain)

        kvp_t = [[Buf(kvp.ap[:, gi, :]) for gi in range(9)] for _ in range(1)][0]

        def loadA(h):
            W = W2[h % 2]
            P.dma("pool", W.ap[:, :, 0:256], w_in_v[:, :, 2048 + h * 256:2048 + (h + 1) * 256], writes=[W])
            P.dma("pool", W.ap[:, :, 256:512], w_in_v[:, :, 3072 + h * 256:3072 + (h + 1) * 256], writes=[W])

        def projA(h, dst):
            W = W2[h % 2]
            for gi, (src, L, tok0) in enumerate(pre_tiles):
                pj = PJ[gi % 2]
                for kc in range(KC):
                    mm(pj, pj.ap[:L, 0:512], xnTp, xnTp.ap[:, kc, tok0:tok0 + L], W, W.ap[:, kc, :], kc == 0, kc == KC - 1)
                    if kc % 4 == 3:
                        yield
                cp("act", dst[gi], pj, o=dst[gi].ap[:L, 0:512], i=pj.ap[:L, 0:512])
                yield

        def updA(h, srcb):
            for gi, (src, L, tok0) in enumerate(pre_tiles):
                G = GW[gi]
                kw = kwb[gi % 2]
                kb = srcb[gi]
                ts("dve", kw, kb, G.ap[:L, K_W, h:h + 1], None, ALU.mult, o=kw.ap[:L, :], a=kb.ap[:L, 0:256], extra_r=[G])
                yield
                for dc in range(2):
                    mm(PC[dc], PC[dc].ap[:, 0:HDE], kw, kw.ap[:L, dc * 128:(dc + 1) * 128], kb, kb.ap[:L, 256:513], True, True)
                yield
                for dc in range(2):
                    stt(C32[h], C32[h], G.ap[:, K_DEC, h:h + 1], PC[dc], ALU.mult, ALU.add,
                        o=C32[h].ap[:, dc, :], a=C32[h].ap[:, dc, :], b=PC[dc].ap[:, 0:HDE], extra_r=[G])
                yield
                if gi == 0:
                    cp("dve", C32m[h], C32[h])

        kvp2 = Buf(Z2b[:, 13312:13312 + 9 * 516].rearrange("p (i f) -> p i f", i=9))
        memset("dve", kvp, 1.0)
        memset("dve", kvp2, 1.0)
        kvA = [[Buf(kvp.ap[:, gi, :]) for gi in range(9)], [Buf(kvp2.ap[:, gi, :]) for gi in range(9)]]
        for lst, par_b in ((kvA[0], kvp), (kvA[1], kvp2)):
            for b_ in lst:
                b_.w = par_b.w
        def chain(*gs):
            for g_ in gs:
                yield from g_

        def run(gens):
            gens = list(gens)
            while gens:
                for g_ in list(gens):
                    try:
                        next(g_)
                    except StopIteration:
                        gens.remove(g_)

        loadA(0)
        loadA(1)
        run([gatesA(), chain(projA(0, kvA[0]), projA(1, kvA[1]))])
        loadA(2)
        loadA(3)
        run([updA(0, kvA[0])])
        run([updA(1, kvA[1]), projA(2, kvA[0])])
        run([updA(2, kvA[0]), projA(3, kvA[1])])
        run([updA(3, kvA[1])])
        for h in range(NH):
            tt("dve", C32[h], C32[h], C32m[h], ALU.subtract)
            stt(C32[h], C32[h], flagc.ap[:, 0:1], C32m[h], ALU.mult, ALU.add, extra_r=[flagc])
            cp("act", Cbf[h], C32[h])
        for (cur, sav) in ((negBc, negBm), (Mc, Mm)):
            tt("dve", cur, cur, sav, ALU.subtract)
            stt(cur, cur, flagc.ap[:, 0:1], sav, ALU.mult, ALU.add, extra_r=[flagc])

        P.checkpoint(4)
        xnT = Buf(Z1b[:, 0:KC * 1168].rearrange("p (k t) -> p k t", k=KC))
        fence([xnTp, xnT])
        yT = Buf(Z2b[:, 0:KC * 1152].rearrange("p (k t) -> p k t", k=KC))
        main_tiles = [(x_own[i * 128:(i + 1) * 128, :], 128, i * 128) for i in range(8)] + [(x_smp, 128, 1024)]
        for (src, L, tok0) in main_tiles + [(x_halo, 16, 1152)]:
            norm_tile(src, L, xnT, tok0)
        P.dma("sp", msm.ap[0:16, :], st_m, writes=[msm])
        P.dma("sp", nin.ap[0:64, :], st_n, writes=[nin])
        for dc in range(2):
            tr(PS[0], PS[0].ap[:, dc * 64:(dc + 1) * 64], nin, nin.ap[0:64, dc * 128:(dc + 1) * 128], ident32, ident32.ap[0:64, 0:64], signal=(dc == 1))
        cp("dve", nT, PS[0], o=nT.ap.rearrange("p b h c -> p c (b h)"), i=PS[0].ap[:, 0:128].rearrange("p (c x) -> p c x", c=2))
        def gates_main():
            for gi, (src, L, tok0) in enumerate(main_tiles):
                gate_prep(xnT, tok0, L, 9 + gi, "p" if gi < 8 else "s")
            tt("dve", mvec, Mc, negBc, ALU.subtract)
            P.dma("sp", m_p, mvec.ap[0:1, :], reads=[mvec])

        P.checkpoint(5)
        WU = W2
        fence([kvp, kvp2, wb, yT] + C32m + xt + SET_PL + kvA[0] + kvA[1])
        for half in range(2):
            P.dma("pool", WU[half].ap, w_in_v[:, :, half * 512:(half + 1) * 512], writes=[WU[half]])
        sp_rows = st_pool.rearrange("b j c -> (b j) c")
        for blk, (r0, nr) in enumerate(((0, 128), (128, 112))):
            P.dma("sp", sptm.ap[0:nr, :], sp_rows[r0:r0 + nr, :], writes=[sptm])
            for cc in range(8):
                p = PS[cc % 2]
                tr(p, p.ap[:, 0:nr], sptm, sptm.ap[0:nr, cc * 128:(cc + 1) * 128], ident32, ident32.ap[0:nr, 0:nr], signal=True)
                eng = "dve" if cc % 2 else "act"
                if blk == 0:
                    cp(eng, uTs, p, o=uTs.ap[:, cc, 0:8, 0:15], i=p.ap[:, 0:120].rearrange("p (b j) -> p b j", b=8))
                    cp(eng, uTs, p, o=uTs.ap[:, cc, 8, 0:8], i=p.ap[:, 120:128])
                else:
                    cp(eng, uTs, p, o=uTs.ap[:, cc, 8, 8:15], i=p.ap[:, 0:7])
                    cp(eng, uTs, p, o=uTs.ap[:, cc, 9:16, 0:15], i=p.ap[:, 7:112].rearrange("p (b j) -> p b j", b=7))

        def pool_group(U, shp, g, o_ap):
            nd = len(shp)
            T = shp[-1]

            def v(b, t0, t1):
                if nd == 1:
                    return b[:, :, t0:t1]
                return b[:, :, :, t0:t1]
            if nd == 1:
                sv = [sw[i].ap[:, 0:2 * T].rearrange("p (c t) -> p c t", c=2) for i in range(2)]
            else:
                sv = [sw[i].ap[:, 0:2 * shp[0] * T].rearrange("p (c b t) -> p c b t", c=2, b=shp[0]) for i in range(2)]
            cur_b, cur = U, (U.ap[:, 2 * g:2 * g + 2, :] if nd == 1 else U.ap[:, 2 * g:2 * g + 2, :, :])
            for k in range(g + 1):
                sh = 1 << k
                lo = 2 * sh - 1
                nb, nv = sw[k % 2], sv[k % 2]
                tt("dve", nb, cur_b, cur_b, ALU.add, o=v(nv, lo, T), a=v(cur, lo, T), b=v(cur, lo - sh, T - sh))
                cur_b, cur = nb, nv
            wdw = 2 << g
            uv = U.ap[:, 2 * g:2 * g + 2, :] if nd == 1 else U.ap[:, 2 * g:2 * g + 2, :, :]
            stt(dTb, cur_b, 1.0 / wdw, U, ALU.mult, ALU.subtract, o=o_ap, a=v(cur, 15, T), b=v(uv, 15, T))

        def pool_tile(tok0, L, ytok0, hist_mode):
            for half in range(2):
                pj = proj(xnT, tok0, L, WU[half], WU[half].ap, 512)
                cp("act" if half == 0 else "dve", utm, pj, o=utm.ap[:L, half * 512:(half + 1) * 512], i=pj.ap[:L, :])
            if hist_mode == "smp":
                for b in range(16):
                    P.dma("sp", pool_s[b, 7:15, :], utm.ap[b * 8:(b + 1) * 8, :], reads=[utm])
                    P.dma("sp", pool_s[b, 0:7, :], st_pool[b, 8:15, :])
            if hist_mode == "own" and tok0 == 7 * 128:
                P.dma("sp", pool_p, utm.ap[113:128, :], reads=[utm])
            for cc in range(8):
                p = PT[cc % 2]
                pa = p.ap.bitcast(F32)
                tr(p, pa[:, 0:L], utm, utm.ap[:L, cc * 128:(cc + 1) * 128], ident32, ident32.ap[:L, :L], signal=True)
                if hist_mode == "halo":
                    cp("dve" if cc % 2 else "act", uT, p, o=uT.ap[:, cc, 0:15], i=pa[:, 1:16])
                elif hist_mode == "own":
                    cp("dve" if cc % 2 else "act", uT, p, o=uT.ap[:, cc, 15:143], i=pa[:, 0:128])
                else:
                    cp("dve" if cc % 2 else "act", uTs, p, o=uTs.ap[:, cc, :, 15:23], i=pa[:, 0:128].rearrange("p (b t) -> p b t", b=16))
            if hist_mode == "halo":
                return
            if hist_mode == "own":
                U, shp, hcol = uT, (143,), 15
            else:
                U, shp, hcol = uTs, (16, 23), 15
            for g in range(4):
                if hist_mode == "own":
                    o_ap = dTb.ap[:, 2 * g:2 * g + 2, :]
                else:
                    o_ap = dTb.ap[:, 2 * g:2 * g + 2, :].rearrange("p c (b t) -> p c b t", b=16)
                pool_group(U, shp, g, o_ap)
            for g in range(4):
                for ec in range(2):
                    p = PC[0]
                    for c in range(2):
                        mm(p, p.ap[:, ec * 128:(ec + 1) * 128], wpl, wpl.ap[:, g, c, ec * 128:(ec + 1) * 128], dTb, dTb.ap[:, 2 * g + c, :], c == 0, c == 1)
                    ch = 2 * g + ec
                    ts("dve", yT, p, pscol.ap[:, ch:ch + 1], None, ALU.mult, o=yT.ap[:, ch, ytok0:ytok0 + 128], a=p.ap[:, ec * 128:(ec + 1) * 128], extra_r=[pscol])
            if hist_mode == "own":
                cp("dve", sw[0], uT, o=sw[0].ap[:, 0:120].rearrange("p (c t) -> p c t", c=8), i=uT.ap[:, :, 128:143])
                cp("dve", uT, sw[0], o=uT.ap[:, :, 0:15], i=sw[0].ap[:, 0:120].rearrange("p (c t) -> p c t", c=8))

        def pool_all():
            pool_tile(1152, 16, 0, "halo")
            for i in range(8):
                pool_tile(i * 128, 128, i * 128, "own")
            pool_tile(1024, 128, 1024, "smp")
        run([P.defer(gates_main), P.defer(pool_all)])

        P.checkpoint(6)
        q_t = [Buf(q_tm.ap[:, i, :]) for i in range(9)]
        k_t = [Buf(k_tm.ap[:, i, :]) for i in range(9)]
        s_t = [Buf(sigo.ap[:, i, :]) for i in range(9)]
        v_t = [Buf(v_ext.ap[:, i, :]) for i in range(9)]
        Z4BUFS.extend(q_t + k_t + s_t + v_t)

        def front(h, ti, g, mode):
            G = GW[g]
            par = ti % 2
            MP = maskposb if mode == "p" else maskposbB
            pt = PT[par]
            for dc in range(2):
                tr(pt, pt.ap[:, dc * 128:(dc + 1) * 128], k_t[ti], k_t[ti].ap[:, dc * 128:(dc + 1) * 128], identb, identb.ap, signal=False)
            for dc in range(2):
                tr(pt, pt.ap[:, (2 + dc) * 128:(3 + dc) * 128], q_t[ti], q_t[ti].ap[:, dc * 128:(dc + 1) * 128], identb, identb.ap, signal=(dc == 1))
            yield
            cp("act", qkT[par], pt, o=qkT[par].ap.rearrange("p a t -> p (a t)"), i=pt.ap[:, 0:512])
            ts("dve", dgM[par], ident32, G.ap[:, K_M, h:h + 1], None, ALU.mult, extra_r=[G])
            yield
            p = PS[par]
            mm(p, p.ap[:, 0:128], ones32, ones32.ap, dgM[par], dgM[par].ap, True, True)
            mm(p, p.ap[:, 128:256], ones32, ones32.ap, dgM[par], dgM[par].ap, True, False, signal=False)
            mm(p, p.ap[:, 128:256], identb, identb.ap, MP, MP.ap, False, True)
            yield
            act(DT[par], p, AF.Exp, bias=G.ap[:, K_AADJ, h:h + 1], scale=-1.0, i=p.ap[:, 128:256], extra_r=[G])
            yield
            if mode == "p":
                act(interb[par], p, AF.Exp, bias=G.ap[:, K_MPREV, h:h + 1], scale=-1.0, i=p.ap[:, 0:128], extra_r=[G])
            else:
                ts("dve", dgM[par], ident32, G.ap[:, K_INTER, h:h + 1], None, ALU.mult, extra_r=[G])
                mm(p, p.ap[:, 384:512], ones32, ones32.ap, dgM[par], dgM[par].ap, True, True)
                cp("act", interb[par], p, i=p.ap[:, 384:512])
            yield
            for dc in range(2):
                mm(p, p.ap[:, 256:384], qkT[par], qkT[par].ap[:, dc, :], qkT[par], qkT[par].ap[:, 2 + dc, :], dc == 0, dc == 1)
            yield
            tt("dve", STb[par], p, DT[par], ALU.mult, a=p.ap[:, 256:384])
            yield
            ts("dve", kwb[par], k_t[ti], G.ap[:, K_W, h:h + 1], None, ALU.mult, extra_r=[G])
            yield
            if mode == "p":
                tt("dve", qtl[par], qkT[par], interb[par], ALU.mult, a=qkT[par].ap[:, 2:4, :],
                   b=interb[par].ap.unsqueeze(1).to_broadcast([128, 2, 128]))
            else:
                tt("dve", qt32, qkT[par], interb[par], ALU.mult, a=qkT[par].ap[:, 2:4, :],
                   b=interb[par].ap.unsqueeze(1).to_broadcast([128, 2, 128]))
                yield
                tt("dve", Vmask, v_t[ti], blockmask, ALU.mult, o=Vmask.ap[:, :, 0:HDE],
                   a=v_t[ti].ap[:, 0:HDE].unsqueeze(1).to_broadcast([128, 16, HDE]),
                   b=blockmask.ap.unsqueeze(2).to_broadcast([128, 16, HDE]))
            yield

        def back(h, ti, g, mode):
            G = GW[g]
            par = ti % 2
            pn = PN[par]
            vb = v_t[ti]
            if mode == "p":
                mm(pn, pn.ap[:, 0:HDE], STb[par], STb[par].ap, vb, vb.ap[:, 0:HDE], True, False, signal=False)
                for dc in range(2):
                    mm(pn, pn.ap[:, 0:HDE], qtl[par], qtl[par].ap[:, dc, :], Cbf[h], Cbf[h].ap[:, dc, :], False, dc == 1)
                yield
                for dc in range(2):
                    mm(PC[dc], PC[dc].ap[:, 0:HDE], kwb[par], kwb[par].ap[:, dc * 128:(dc + 1) * 128], vb, vb.ap[:, 0:HDE], True, True)
                yield
                for dc in range(2):
                    stt(C32[h], C32[h], G.ap[:, K_DEC, h:h + 1], PC[dc], ALU.mult, ALU.add,
                        o=C32[h].ap[:, dc, :], a=C32[h].ap[:, dc, :], b=PC[dc].ap[:, 0:HDE], extra_r=[G])
                    yield
                cp("act", Cbf[h], C32[h])
                yield
            else:
                mm(pn, pn.ap[:, 0:HDE], STb[par], STb[par].ap, vb, vb.ap[:, 0:HDE], True, False, signal=False)
                def load_grp(gq):
                    c_ = C32s[gq % 3]
                    for bl_ in range(2):
                        P.dma("sp", c_.ap[:, bl_, :, 0:HD], st_C[gq * 2 + bl_, h].rearrange("(c p) v -> p c v", p=128), writes=[c_])
                load_grp(0)
                load_grp(1)

                def prep_grp(gq):
                    c_ = C32s[gq % 3]
                    cp("dve", c_, nT, o=c_.ap[:, :, :, HD], i=nT.ap[:, gq * 2:(gq + 1) * 2, h, :])
                    cp("act", C16, c_)
                    qb_ = qm[gq % 2]
                    for dc_ in range(2):
                        tt("dve", qb_, qt32, SELREP, ALU.mult, o=qb_.ap[:, dc_ * 2:dc_ * 2 + 2, :],
                           a=qt32.ap[:, dc_, :].unsqueeze(1).to_broadcast([128, 2, 128]), b=SELREP.ap[:, gq * 2:(gq + 1) * 2, :])

                def inter_grp(gq):
                    c_ = C32s[gq % 3]
                    qb_ = qm[gq % 2]
                    for bl_ in range(2):
                        for dc_ in range(2):
                            last = (gq == 7 and bl_ == 1 and dc_ == 1)
                            mm(pn, pn.ap[:, 0:HDE], qb_, qb_.ap[:, dc_ * 2 + bl_, :], C16, C16.ap[:, bl_, dc_, :], False, last,
                               signal=(last or (bl_ == 1 and dc_ == 1)))
                prep_grp(0)
                yield
                inter_grp(0)
                yield
                for grp in range(8):
                    cs_ = C32s[grp % 3]
                    if grp + 2 < 8:
                        load_grp(grp + 2)
                    if grp + 1 < 8:
                        prep_grp(grp + 1)
                        yield
                        inter_grp(grp + 1)
                        yield
                    for bl in range(2):
                        b = grp * 2 + bl
                        for dc in range(2):
                            pc = PC[dc]
                            mm(pc, pc.ap[:, 0:HDE], kwb[par], kwb[par].ap[:, dc * 128:(dc + 1) * 128], Vmask, Vmask.ap[:, b, 0:HDE], True, True)
                            stt(cs_, cs_, decs.ap[:, b, h:h + 1], pc, ALU.mult, ALU.add, o=cs_.ap[:, bl, dc, :], a=cs_.ap[:, bl, dc, :], b=pc.ap[:, 0:HDE], extra_r=[decs])
                            yield
                    cp("act", nS, cs_, o=nS.ap[:, grp * 2:(grp + 1) * 2, h, :], i=cs_.ap[:, :, :, HD])
                    for bl in range(2):
                        b = grp * 2 + bl
                        P.dma("sp", C_s[b, h].rearrange("(c p) v -> p c v", p=128), cs_.ap[:, bl, :, 0:HD], reads=[cs_])
                    yield
            cl = COLS[par]
            act(sqs, pn, AF.Square, accum=(cl, cl.ap[:, 0:1]), o=sqs.ap[:, 0:HD], i=pn.ap[:, 0:HD])
            P.op("dve", lambda e, o=cl.ap[:, 1:2], i=pn.ap[:, HD:HDE], s2=G.ap[:, K_EM2, h:h + 1]:
                 e.tensor_scalar(out=o, in0=i, scalar1=i, scalar2=s2, op0=ALU.mult, op1=ALU.max), reads=[pn, G], writes=[cl])
            yield
            stt(cl, cl, EPS * HD, cl, ALU.mult, ALU.add, o=cl.ap[:, 2:3], a=cl.ap[:, 1:2], b=cl.ap[:, 0:1])
            yield
            act(cl, cl, AF.Ln, scale=1.0 / HD, o=cl.ap[:, 5:6], i=cl.ap[:, 2:3])
            act(cl, cl, AF.Exp, scale=-0.5, o=cl.ap[:, 7:8], i=cl.ap[:, 5:6])
            yield
            yb = ytm[par]
            stt(yb, pn, cl.ap[:, 7:8], s_t[ti], ALU.mult, ALU.mult, o=yb.ap[:, 0:HD], a=pn.ap[:, 0:HD],
                b=s_t[ti].ap, extra_r=[cl])
            yield
            pt2 = PT[par]
            for c in range(2):
                tr(pt2, pt2.ap[:, 512 + c * 128:512 + (c + 1) * 128], yb, yb.ap[:, c * 128:(c + 1) * 128], identb, identb.ap, signal=(c == 1))
            yield
            for c in range(2):
                ch = 2 * h + c
                ts("dve", yT, pt2, gcol.ap[:, ch:ch + 1], None, ALU.mult, o=yT.ap[:, 8 + ch, ti * 128:(ti + 1) * 128],
                   a=pt2.ap[:, 512 + c * 128:512 + (c + 1) * 128], extra_r=[gcol])
                yield

        def load_head_w(h):
            for half in range(2):
                for j in range(2):
                    c0 = 1024 + (half * 2 + j) * 1024 + h * 256
                    P.dma("pool", W2[half].ap[:, :, j * 256:(j + 1) * 256], w_in_v[:, :, c0:c0 + 256], writes=[W2[half]])

        def projgen(h, ti):
            tok0 = main_tiles[ti][2]
            pj = PJ[0]
            for kc in range(KC):
                mm(pj, pj.ap[:, 0:512], xnT, xnT.ap[:, kc, tok0:tok0 + 128], W2[0], W2[0].ap[:, kc, :], kc == 0, kc == KC - 1)
                if kc % 2 == 1:
                    yield
            cp("act", q_t[ti], pj, i=pj.ap[:, 0:256])
            cp("dve", k_t[ti], pj, i=pj.ap[:, 256:512])
            yield
            pj = PJ[1]
            for kc in range(KC):
                mm(pj, pj.ap[:, 0:512], xnT, xnT.ap[:, kc, tok0:tok0 + 128], W2[1], W2[1].ap[:, kc, :], kc == 0, kc == KC - 1)
                if kc % 2 == 1:
                    yield
            cp("dve", v_t[ti], pj, o=v_t[ti].ap[:, 0:HD], i=pj.ap[:, 0:256])
            act(sgt, pj, AF.Exp, scale=-1.0, i=pj.ap[:, 256:512])
            yield
            act(sgt, sgt, AF.Ln, bias=1.0)
            yield
            act(s_t[ti], sgt, AF.Exp, scale=-1.0)
            yield

        def interleave(gens):
            gens = list(gens)
            while gens:
                for g_ in list(gens):
                    try:
                        next(g_)
                    except StopIteration:
                        gens.remove(g_)

        fence(SET_PL + SET_HD + q_t + k_t + s_t + v_t)
        memset("dve", v_ext, 1.0)
        for vt_ in v_t:
            vt_.w = v_ext.w
        memset("dve", Vmask, 0.0)
        PN = PS
        load_head_w(0)
        for ti in range(9):
            interleave([projgen(0, ti)])
        for h in range(NH):
            if h + 1 < NH:
                load_head_w(h + 1)
            for s_ in range(11):
                tasks = []
                if s_ <= 8:
                    tasks.append(front(h, s_, 9 + s_, "p" if s_ < 8 else "s"))
                if 1 <= s_ <= 9:
                    tasks.append(back(h, s_ - 1, 9 + s_ - 1, "p" if s_ - 1 < 8 else "s"))
                if h + 1 < NH and 2 <= s_ <= 10:
                    tasks.append(projgen(h + 1, s_ - 2))
                interleave(tasks)
        P.checkpoint(7)
        for h in range(NH):
            for dc in range(2):
                P.dma("sp", C_p[h, dc * 128:(dc + 1) * 128, :], C32[h].ap[:, dc, 0:HD], reads=[C32[h]])
                P.dma("sp", n_p[h:h + 1, dc * 128:(dc + 1) * 128].rearrange("o p -> p o"), C32[h].ap[:, dc, HD:HDE], reads=[C32[h]], slow=True)
        P.checkpoint(7.5)
        tr(PS[0], PS[0].ap[:, 0:128], nS, nS.ap.rearrange("p b h c -> p (b h c)"), ident32, ident32.ap, signal=True)
        cp("dve", nrow, PS[0], i=PS[0].ap[:, 0:128])
        P.dma("sp", n_s, nrow.ap, reads=[nrow])

        P.checkpoint(8)
        all_src = [x_own[i * 128:(i + 1) * 128, :] for i in range(8)] + [x_smp]
        z4_users = Z4BUFS
        for i in range(9):
            P.dma("sp", x_new[i].ap, all_src[i], reads=[], writes=[x_new[i]] + z4_users)
        WO = [W2[0], W2[1]]
        for cg in range(4):
            W = WO[cg % 2]
            P.dma("pool", W.ap, w_out_v[:, :, cg * 512:(cg + 1) * 512], writes=[W])
            for i in range(9):
                pj = proj(yT, i * 128, 128, W, W.ap, 512)
                tt("dve", x_new[i], x_new[i], pj, ALU.add, o=x_new[i].ap[:, cg * 512:(cg + 1) * 512],
                   a=x_new[i].ap[:, cg * 512:(cg + 1) * 512], b=pj.ap[:, 0:512])
        P.dma("sp", Z3[:, 4096:4096 + D], norm_ffn_w.partition_broadcast(128), writes=[W2[1]])
        xn2T = Buf(Z1b[:, 0:KC * 1152].rearrange("p (k t) -> p k t", k=KC))
        fence([xnT, xn2T])
        for i in range(9):
            norm_tile(None, 128, xn2T, i * 128, x_in=x_new[i], wbuf=W2[1], wap=Z3[:, 4096:4096 + D])

        P.checkpoint(9)
        hT = Buf(Z2b[:, 0:11 * 1152].rearrange("p (f t) -> p f t", f=11))
        WG = [Buf(Z3b[:, i * 2048:(i + 1) * 2048].rearrange("p (k n) -> p k n", k=KC)) for i in range(2)]
        WUp = [Buf(Z3b[:, 4096 + i * 2048:4096 + (i + 1) * 2048].rearrange("p (k n) -> p k n", k=KC)) for i in range(2)]
        WD = [Buf(Z3b[:, 8192 + i * 2816:8192 + (i + 1) * 2816].rearrange("p (f n) -> p f n", f=11)) for i in range(2)]
        fence([W2[0], W2[1], yT, hT] + WG + WUp + WD + silt)
        tgs = [(0, 512), (512, 512), (1024, 128)]
        fr = [0]
        for qd in range(4):
            for fl in range(11):
                fc = qd * 11 + fl
                wgb = WG[fr[0] % 2]
                wub = WUp[fr[0] % 2]
                fr[0] += 1
                P.dma("pool", wgb.ap, w_gate_v[:, :, fc * 128:(fc + 1) * 128], writes=[wgb])
                P.dma("pool", wub.ap, w_up_v[:, :, fc * 128:(fc + 1) * 128], writes=[wub])
                for ti, (t0, tn) in enumerate(tgs):
                    pg = PJ[ti % 2]
                    pu = PS[ti % 2]
                    for kc in range(KC):
                        mm(pg, pg.ap[:, 0:tn], wgb, wgb.ap[:, kc, :], xn2T, xn2T.ap[:, kc, t0:t0 + tn], kc == 0, kc == KC - 1)
                    for kc in range(KC):
                        mm(pu, pu.ap[:, 0:tn], wub, wub.ap[:, kc, :], xn2T, xn2T.ap[:, kc, t0:t0 + tn], kc == 0, kc == KC - 1)
                    st_ = silt[ti % 2]
                    act(st_, pg, AF.Silu, o=st_.ap[:, 0:tn], i=pg.ap[:, 0:tn])
                    tt("dve", hT, st_, pu, ALU.mult, o=hT.ap[:, fl, t0:t0 + tn], a=st_.ap[:, 0:tn], b=pu.ap[:, 0:tn])
            for cg in range(8):
                W = WD[cg % 2]
                P.dma("pool", W.ap, w_down_v[:, qd * 11:(qd + 1) * 11, cg * 256:(cg + 1) * 256], writes=[W])
                for i in range(9):
                    pc = PC[i % 2]
                    for fl in range(11):
                        mm(pc, pc.ap[:, 0:256], hT, hT.ap[:, fl, i * 128:(i + 1) * 128], W, W.ap[:, fl, :], fl == 0, fl == 10)
                    tt("dve", x_new[i], x_new[i], pc, ALU.add, o=x_new[i].ap[:, cg * 256:(cg + 1) * 256],
                       a=x_new[i].ap[:, cg * 256:(cg + 1) * 256], b=pc.ap[:, 0:256])

        P.checkpoint(10)
        wbE = Buf(Z1[:, 0:D])
        fence([xn2T, wbE])
        P.dma("sp", wbE.ap, norm_final_w.partition_broadcast(128), writes=[wbE])
        for i in range(9):
            ssb = SS[i % 2]
            xb = x_new[i]
            act(sq, xb, AF.Square, accum=(ssb, ssb.ap[:, 0:1]))
            act(ssb, ssb, AF.Ln, bias=EPS, scale=1.0 / D, o=ssb.ap[:, 1:2], i=ssb.ap[:, 0:1])
            act(ssb, ssb, AF.Exp, scale=-0.5, o=ssb.ap[:, 2:3], i=ssb.ap[:, 1:2])
            stt(xb, xb, ssb.ap[:, 2:3], wbE, ALU.mult, ALU.mult, extra_r=[ssb])
            dst = y_own[i * 128:(i + 1) * 128, :] if i < 8 else y_smp
            P.dma("sp", dst, xb.ap, reads=[xb])

        sem_es = ExitStack()
        with sem_es:
            sems = {}
            for e in ENGS:
                sems[e] = sem_es.enter_context(nc.semaphore("s_" + e))
            for q, n in P.ndma.items():
                for i in range(n):
                    sems[(q, i)] = sem_es.enter_context(nc.semaphore("d_%s%d" % (q, i)))
            with nc.Block() as block:
                @block.tensor
                def _(e):
                    P.replay("pe", e, sems)

                @block.scalar
                def _(e):
                    P.replay("act", e, sems)

                @block.vector
                def _(e):
                    P.replay("dve", e, sems)

                @block.gpsimd
                def _(e):
                    P.replay("pool", e, sems)
                    P.final_waits("pool", e, sems)

                @block.sync
                def _(e):
                    P.replay("sp", e, sems)
                    P.final_waits("sp", e, sems)
    return nc


_NC = None


def _get_nc():
    global _NC
    if _NC is None:
        _NC = build_nc()
    return _NC


def kernel(x_prompt, x_sample, state_pool, state_mlstm_C, state_mlstm_n, state_mlstm_m, meta_tokens, norm_mix_w, w_in,
           b_igate, b_fgate, w_pool, pool_scale, mlstm_norm_w, w_out, norm_ffn_w, w_gate, w_up, w_down, norm_final_w):
    f = lambda a: np.ascontiguousarray(np.asarray(a, dtype=np.float32))
    xp, xs = f(x_prompt), f(x_sample)
    meta = f(meta_tokens)
    shared = {
        "norm_mix_w": f(norm_mix_w)[0], "w_in": f(w_in)[0], "b_igate": f(b_igate)[0], "b_fgate": f(b_fgate)[0],
        "w_pool": f(w_pool)[0], "pool_scale": f(pool_scale)[0], "mlstm_norm_w": f(mlstm_norm_w)[0], "w_out": f(w_out)[0],
        "norm_ffn_w": f(norm_ffn_w)[0], "w_gate": f(w_gate)[0], "w_up": f(w_up)[0], "w_down": f(w_down)[0],
        "norm_final_w": f(norm_final_w),
    }
    sp, sC, sn, sm = f(state_pool)[0], f(state_mlstm_C)[0], f(state_mlstm_n)[0], f(state_mlstm_m)[0]
    in_maps = []
    for c in range(8):
        s, h = c // 2, c % 2
        m = dict(shared)
        m["x_meta"] = meta
        m["x_pre"] = xp[s, 0:1024] if h == 1 else np.zeros((1024, D), np.float32)
        m["x_own"] = np.ascontiguousarray(xp[s, 1024 * h:1024 * (h + 1)])
        m["x_halo"] = meta if h == 0 else np.ascontiguousarray(xp[s, 1008:1024])
        m["x_smp"] = np.ascontiguousarray(xs[16 * c:16 * (c + 1)].reshape(128, D))
        m["flag"] = np.full((128, 1), float(h), np.float32)
        m["st_pool"] = np.ascontiguousarray(sp[16 * c:16 * (c + 1)])
        m["st_C"] = np.ascontiguousarray(sC[16 * c:16 * (c + 1)])
        m["st_n"] = np.ascontiguousarray(sn[16 * c:16 * (c + 1)].reshape(64, HD))
        m["st_m"] = np.ascontiguousarray(sm[16 * c:16 * (c + 1)])
        in_maps.append(m)
    nc = _get_nc()
    res = run_bass_kernel_spmd(nc, in_maps, core_ids=list(range(8))).results
    y_prompt = np.stack([np.concatenate([res[2 * s]["y_own"], res[2 * s + 1]["y_own"]], axis=0) for s in range(4)])
    y_sample = np.concatenate([r["y_smp"] for r in res], axis=0).reshape(128, 8, D)
    pool_pp = np.stack([res[2 * s + 1]["pool_p"] for s in range(4)])[None]
    C_pp = np.stack([res[2 * s + 1]["C_p"] for s in range(4)])[None]
    n_pp = np.stack([res[2 * s + 1]["n_p"] for s in range(4)])[None]
    m_pp = np.stack([res[2 * s + 1]["m_p"].reshape(NH) for s in range(4)])[None]
    pool_ss = np.concatenate([r["pool_s"] for r in res], axis=0)[None]
    C_ss = np.concatenate([r["C_s"] for r in res], axis=0)[None]
    n_ss = np.concatenate([r["n_s"].reshape(16, NH, HD) for r in res], axis=0)[None]
    m_ss = np.concatenate([r["m_s"] for r in res], axis=0)[None]
    outs = (y_prompt, y_sample, pool_pp, C_pp, n_pp, m_pp, pool_ss, C_ss, n_ss, m_ss)
    return tuple(np.ascontiguousarray(o, dtype=np.float32) for o in outs)
```

```python
import math
import numpy as np
import concourse.bass as bass
import concourse.mybir as mybir
from concourse.bass_utils import run_bass_kernel_spmd

F32 = mybir.dt.float32
BF16 = mybir.dt.bfloat16
AF = mybir.ActivationFunctionType
ALU = mybir.AluOpType
AX = mybir.AxisListType

D = 2048
KC = 16
NH = 4
HD = 256
HDE = 257
DFF = 5632
FCH = 44
EPS = 1e-6
LN16 = math.log(16.0)
BIG = 1.0e30
ENGS = ("pe", "act", "dve", "pool", "sp")
STOP = None


class Buf:
    __slots__ = ("ap", "w", "r", "excl")

    def __init__(self, ap, excl=False):
        self.ap = ap
        self.w = None
        self.r = []
        self.excl = excl


class Prog:
    def __init__(self):
        self.streams = {e: [] for e in ENGS}
        self.cnt = {e: 0 for e in ENGS}
        self.seen = {e: {} for e in ENGS}
        self.ndma = {"sp": 20, "pool": 12}
        self.dma_cnt = {q: [0] * n for q, n in self.ndma.items()}
        self.dma_rr = {q: 0 for q in self.ndma}
        self.dead = False
        self._q = None
        self.stop = STOP

    def checkpoint(self, k):
        if self.stop is not None and k >= self.stop - 1e-9:
            self.dead = True

    def _waits(self, eng, deps):
        out = []
        for d in deps:
            if d is None:
                continue
            k, c = d
            if self.seen[eng].get(k, 0) >= c:
                continue
            self.seen[eng][k] = c
            out.append((k, c))
        return out

    def _deps(self, eng, reads, writes):
        deps = []
        for b in reads:
            if b.w is not None and not (eng == "pe" and b.w[0] == "pe"):
                deps.append(b.w)
            if b.excl:
                for t in b.r:
                    if t is not None and t[0] != eng:
                        deps.append(t)
        for b in writes:
            for t in [b.w] + b.r:
                if t is not None and not (eng == "pe" and t[0] == "pe"):
                    deps.append(t)
        return deps

    def defer(self, f):
        self._q = []
        f()
        q, self._q = self._q, None

        def g():
            for kind, a in q:
                if kind == "op":
                    self.op(*a)
                else:
                    self.dma(*a)
                yield
        return g()

    def op(self, eng, fn, reads=(), writes=(), signal=True):
        if self.dead:
            return None
        if self._q is not None:
            self._q.append(("op", (eng, fn, list(reads), list(writes), signal)))
            return None
        waits = self._waits(eng, self._deps(eng, reads, writes))
        if signal:
            self.cnt[eng] += 1
            tok = (eng, self.cnt[eng])
        else:
            tok = (eng, self.cnt[eng] + 1)
        self.streams[eng].append((waits, fn, "inc" if signal else None))
        for b in reads:
            b.r.append(tok)
        for b in writes:
            b.w = tok
            b.r = []
        return tok

    def dma(self, q, out_ap, in_ap, reads=(), writes=(), slow=False):
        if self.dead:
            return None
        if self._q is not None:
            self._q.append(("dma", (q, out_ap, in_ap, list(reads), list(writes), slow)))
            return None
        i = self.dma_rr[q]
        self.dma_rr[q] = (i + 1) % self.ndma[q]
        key = (q, i)
        deps = self._deps("dmaq", reads, writes)
        prev = self.dma_cnt[q][i]
        if prev:
            deps.append((key, prev))
        waits = self._waits(q, deps)
        self.dma_cnt[q][i] = prev + 16
        tok = (key, prev + 16)
        if slow:
            fn = lambda e, o=out_ap, a=in_ap: e.dma_start(out=o, in_=a, allow_slow_non_contiguous=True)
        else:
            fn = lambda e, o=out_ap, a=in_ap: e.dma_start(out=o, in_=a)
        self.streams[q].append((waits, fn, key))
        for b in reads:
            b.r.append(tok)
        for b in writes:
            b.w = tok
            b.r = []
        return tok

    def replay(self, eng, e, sems):
        for waits, fn, sig in self.streams[eng]:
            for k, c in waits:
                e.wait_ge(sems[k], c)
            ins = fn(e)
            if sig == "inc":
                ins.then_inc(sems[eng], 1)
            elif sig is not None:
                ins.then_inc(sems[sig], 16)

    def final_waits(self, eng, e, sems):
        for i, c in enumerate(self.dma_cnt.get(eng, [])):
            if c:
                e.wait_ge(sems[(eng, i)], c)


def build_nc():
    nc = bass.Bass("TRN2", target_bir_lowering=False)
    P = Prog()

    def din(name, shape, dt=F32):
        return nc.dram_tensor(name, list(shape), dt, kind="ExternalInput").ap()

    def dout(name, shape, dt=F32):
        return nc.dram_tensor(name, list(shape), dt, kind="ExternalOutput").ap()

    x_meta = din("x_meta", [16, D])
    x_pre = din("x_pre", [1024, D])
    x_own = din("x_own", [1024, D])
    x_halo = din("x_halo", [16, D])
    x_smp = din("x_smp", [128, D])
    flag_d = din("flag", [128, 1])
    st_pool = din("st_pool", [16, 15, 1024])
    st_C = din("st_C", [16, NH, HD, HD])
    st_n = din("st_n", [64, HD])
    st_m = din("st_m", [16, NH])
    norm_mix_w = din("norm_mix_w", [D])
    w_in = din("w_in", [D, 5128])
    b_ig = din("b_igate", [NH])
    b_fg = din("b_fgate", [NH])
    w_pool = din("w_pool", [4, 256, 256])
    pool_scale = din("pool_scale", [1024])
    mnorm_w = din("mlstm_norm_w", [1024])
    w_out = din("w_out", [D, D])
    norm_ffn_w = din("norm_ffn_w", [D])
    w_gate = din("w_gate", [D, DFF])
    w_up = din("w_up", [D, DFF])
    w_down = din("w_down", [DFF, D])
    norm_final_w = din("norm_final_w", [D])

    y_own = dout("y_own", [1024, D])
    y_smp = dout("y_smp", [128, D])
    pool_p = dout("pool_p", [15, 1024])
    C_p = dout("C_p", [NH, HD, HD])
    n_p = dout("n_p", [NH, HD])
    m_p = dout("m_p", [1, NH])
    pool_s = dout("pool_s", [16, 15, 1024])
    C_s = dout("C_s", [16, NH, HD, HD])
    n_s = dout("n_s", [128, 128])
    m_s = dout("m_s", [16, NH])

    w_in_v = w_in.rearrange("(kc p) n -> p kc n", p=128)
    w_out_v = w_out.rearrange("(kc p) n -> p kc n", p=128)
    w_gate_v = w_gate.rearrange("(kc p) n -> p kc n", p=128)
    w_up_v = w_up.rearrange("(kc p) n -> p kc n", p=128)
    w_down_v = w_down.rearrange("(fc p) n -> p fc n", p=128)

    from contextlib import ExitStack
    es = ExitStack()

    def sb(name, shape, dt):
        return es.enter_context(nc.sbuf_tensor(name, list(shape), dt))

    def ps(name, shape, dt):
        return es.enter_context(nc.psum_tensor(name, list(shape), dt))

    with es:
        Z1 = sb("z1", [128, KC * 1168 // 2], F32)
        Z2 = sb("z2", [128, KC * 1152 // 2], F32)
        Z3 = sb("z3", [128, 8192], F32)
        Z4 = sb("z4", [128, 9 * D], F32)
        Z1b = Z1[:, :].bitcast(BF16)
        Z2b = Z2[:, :].bitcast(BF16)
        Z3b = Z3[:, :].bitcast(BF16)
        Z4BUFS = []
        tl = [9600]

        def zt(nfl, dt=F32, shape=None):
            a0 = tl[0]
            tl[0] += nfl
            assert tl[0] <= 9 * D, tl[0]
            ap = Z4[:, a0:a0 + nfl]
            if dt == BF16:
                ap = ap.bitcast(BF16)
            if shape is not None:
                ap = ap.rearrange(shape[0], **shape[1])
            bb = Buf(ap)
            Z4BUFS.append(bb)
            return bb

        ones32 = Buf(sb("ones32", [128, 128], F32)[:, :])
        ident32 = Buf(sb("ident32", [128, 128], F32)[:, :])
        identb = Buf(sb("identb", [128, 128], BF16)[:, :])
        tri32 = Buf(sb("tri32", [128, 128], F32)[:, :])
        tril32 = Buf(sb("tril32", [128, 128], F32)[:, :])
        maskneg = Buf(sb("maskneg", [128, 128], F32)[:, :])
        maskposb = Buf(sb("maskposb", [128, 128], BF16)[:, :])
        triB32 = Buf(sb("triB32", [128, 128], F32)[:, :])
        trilB32 = Buf(sb("trilB32", [128, 128], F32)[:, :])
        masknegB = Buf(sb("masknegB", [128, 128], F32)[:, :])
        maskposbB = Buf(sb("maskposbB", [128, 128], BF16)[:, :])
        SLm = Buf(sb("SLm", [128, 128], F32)[:, :])
        sel32 = Buf(sb("sel32", [16, 128], F32)[:, :])
        selA = Buf(sb("selA", [16, 128], F32)[:, :])
        sellastT = Buf(sb("sellastT", [16, 128], F32)[:, :])
        blockmask = Buf(sb("blockmask", [128, 16], F32)[:, :])
        blockmaskb = Buf(sb("blockmaskb", [128, 16], BF16)[:, :])
        blockA = Buf(sb("blockA", [128, 16], F32)[:, :])
        sellast = Buf(sb("sellast", [128, 16], F32)[:, :])
        bias8 = Buf(sb("bias8", [128, 8], F32)[:, :])
        gcol = Buf(sb("gcol", [128, 8], F32)[:, :])
        pscol = Buf(sb("pscol", [128, 8], F32)[:, :])
        flagc = Buf(sb("flagc", [128, 1], F32)[:, :])
        fsc = Buf(sb("fsc", [128, 1], F32)[:, :])
        negBc = Buf(sb("negBc", [128, 4], F32)[:, :])
        Mc = Buf(sb("Mc", [128, 4], F32)[:, :])
        negBm = Buf(sb("negBm", [128, 4], F32)[:, :])
        Mm = Buf(sb("Mm", [128, 4], F32)[:, :])
        C32t = sb("C32", [128, NH, 2, HDE], F32)
        C32 = [Buf(C32t[:, h, :, :]) for h in range(NH)]
        Cbft = sb("Cbf", [128, NH, 2, HDE], BF16)
        Cbf = [Buf(Cbft[:, h, :, :]) for h in range(NH)]
        sst = sb("ss", [128, 2, 8], F32)
        SS = [Buf(sst[:, i, :]) for i in range(2)]
        xnb1 = Buf(sb("xnb", [128, D], BF16)[:, :])
        xnb = [xnb1, xnb1]
        sq = xnb1
        wpl = Buf(sb("wpl", [128, 4, 2, 256], BF16)[:, :, :, :])

        selrepA = zt(1024, BF16, ("p (b t) -> p b t", dict(b=16)))
        wg8 = zt(64, BF16, ("p (k n) -> p k n", dict(k=KC)))
        gwsb = zt(10 * 20 * 4)
        GWl = [Buf(gwsb.ap[:, j * 80:(j + 1) * 80].rearrange("p (k h) -> p k h", k=20)) for j in range(10)]
        Z4BUFS.extend(GWl)
        GW = GWl[0:9] + GWl[0:10]
        dg = zt(512, F32, ("p (h s) -> p h s", dict(h=4)))
        tmpm = zt(512, F32, ("p (h s) -> p h s", dict(h=4)))
        dgM = [zt(128) for i in range(2)]
        DT = [zt(128) for i in range(2)]
        interb = [zt(128) for i in range(2)]
        STb = [zt(64, BF16) for i in range(2)]
        qkT = [zt(256, BF16, ("p (a t) -> p a t", dict(a=4))) for i in range(2)]
        qtl = [zt(128, BF16, ("p (a t) -> p a t", dict(a=2))) for i in range(2)]
        kwb = [zt(128, BF16) for i in range(2)]
        colsb = zt(24)
        COLS = [Buf(colsb.ap[:, i * 12:(i + 1) * 12]) for i in range(2)]
        Z4BUFS.extend(COLS)
        ytm = [zt(128, BF16) for i in range(2)]
        sqs = zt(128, BF16)
        dsel = zt(64, F32, ("p (b h) -> p b h", dict(b=16)))
        decs = zt(64, F32, ("p (b h) -> p b h", dict(b=16)))
        nT = zt(128, F32, ("p (b h c) -> p b h c", dict(b=16, h=4)))
        nS = zt(128, F32, ("p (b h c) -> p b h c", dict(b=16, h=4)))
        nrow = zt(128)
        nin = zt(256)
        msm = zt(4)
        msout = zt(4)
        mpos = zt(4)
        mvec = zt(4)
        qt32 = zt(256, F32, ("p (a t) -> p a t", dict(a=2)))
        qm = [zt(256, BF16, ("p (b t) -> p b t", dict(b=4))) for i in range(2)]
        C16 = zt(514, BF16, ("p (b c v) -> p b c v", dict(b=2, c=2)))
        C32s3 = zt(2 * 2 * HDE, F32, ("p (b c v) -> p b c v", dict(b=2, c=2)))
        sgt = zt(256)

        o4 = [0]

        def z4(nfl):
            a = o4[0]
            o4[0] += nfl
            assert o4[0] <= 9600, o4[0]
            return Z4[:, a:a + nfl]
        xt = [Buf(z4(D)) for _ in range(2)]
        o4[0] = 0
        utm = Buf(z4(1024))
        uT = Buf(z4(8 * 143).rearrange("p (c t) -> p c t", c=8))
        uTs = Buf(z4(8 * 16 * 23).rearrange("p (c b t) -> p c b t", c=8, b=16))
        sw = [Buf(z4(1472)) for _ in range(2)]
        dTb = Buf(z4(512).bitcast(BF16).rearrange("p (c t) -> p c t", c=8))
        sptm = Buf(z4(1024))
        SET_PL = [utm, uT, uTs, sw[0], sw[1], dTb, sptm]
        o4[0] = 0
        q_tm = Buf(z4(9 * 128).bitcast(BF16).rearrange("p (i f) -> p i f", i=9))
        k_tm = Buf(z4(9 * 128).bitcast(BF16).rearrange("p (i f) -> p i f", i=9))
        sigo = Buf(z4(9 * 128).bitcast(BF16).rearrange("p (i f) -> p i f", i=9))
        v_ext = Buf(z4(9 * 130).bitcast(BF16).rearrange("p (i f) -> p i f", i=9))
        Vmask = Buf(z4(16 * 130).bitcast(BF16).rearrange("p (b f) -> p b f", b=16))
        C32s = [Buf(z4(2 * 2 * HDE).rearrange("p (b c v) -> p b c v", b=2, c=2)) for _ in range(2)]
        C32s.append(C32s3)
        SET_HD = [q_tm, k_tm, sigo, v_ext, Vmask, C32s[0], C32s[1]]
        Z4BUFS.extend(xt + SET_PL + SET_HD)
        x_new = [Buf(Z4[:, i * D:(i + 1) * D]) for i in range(9)]
        C32mt = Z2[:, 2400:2400 + NH * 2 * HDE].rearrange("p (h c v) -> p h c v", h=NH, c=2)
        C32m = [Buf(C32mt[:, h, :, :]) for h in range(NH)]
        wb = Buf(Z2[:, 4600:4600 + D])
        silt = [Buf(Z2[:, 6400 + i * 512:6400 + (i + 1) * 512]) for i in range(2)]
        W2 = [Buf(Z3b[:, i * 8192:(i + 1) * 8192].rearrange("p (k n) -> p k n", k=KC)) for i in range(2)]

        def fence(bufs):
            P.op("dve", lambda e: e.memset(fsc.ap, 0.0), writes=[fsc] + list(bufs))

        PJ = [Buf(ps("pj%d" % i, [128, 512], F32)[:, :], excl=True) for i in range(2)]
        PT = [Buf(ps("pt%d" % i, [128, 1024], BF16)[:, :], excl=True) for i in range(2)]
        PS = [Buf(ps("ps%d" % i, [128, 512], F32)[:, :], excl=True) for i in range(2)]
        PC = [Buf(ps("pc%d" % i, [128, 512], F32)[:, :], excl=True) for i in range(2)]

        def act(out_b, in_b, func, bias=0.0, scale=1.0, accum=None, extra_r=(), o=None, i=None):
            oa = out_b.ap if o is None else o
            ia = in_b.ap if i is None else i
            kw = {}
            if accum is not None:
                kw["accum_out"] = accum[1]
            w = [out_b] + ([accum[0]] if accum is not None else [])
            return P.op("act", lambda e: e.activation(out=oa, in_=ia, func=func, bias=bias, scale=scale, **kw),
                        reads=[in_b] + list(extra_r), writes=w)

        def tt(eng, out_b, a_b, b_b, op, o=None, a=None, b=None):
            oa = out_b.ap if o is None else o
            aa = a_b.ap if a is None else a
            ba = b_b.ap if b is None else b
            return P.op(eng, lambda e: e.tensor_tensor(out=oa, in0=aa, in1=ba, op=op), reads=[a_b, b_b], writes=[out_b])

        def ts(eng, out_b, a_b, s1, s2, op0, op1=None, o=None, a=None, extra_r=()):
            oa = out_b.ap if o is None else o
            aa = a_b.ap if a is None else a
            if op1 is None:
                f = lambda e: e.tensor_scalar(out=oa, in0=aa, scalar1=s1, scalar2=None, op0=op0)
            else:
                f = lambda e: e.tensor_scalar(out=oa, in0=aa, scalar1=s1, scalar2=s2, op0=op0, op1=op1)
            return P.op(eng, f, reads=[a_b] + list(extra_r), writes=[out_b])

        def stt(out_b, a_b, sc, b_b, op0, op1, o=None, a=None, b=None, extra_r=()):
            oa = out_b.ap if o is None else o
            aa = a_b.ap if a is None else a
            ba = b_b.ap if b is None else b
            return P.op("dve", lambda e: e.scalar_tensor_tensor(out=oa, in0=aa, scalar=sc, in1=ba, op0=op0, op1=op1),
                        reads=[a_b, b_b] + list(extra_r), writes=[out_b])

        def red(out_b, in_b, o, i):
            return P.op("dve", lambda e: e.tensor_reduce(out=o, in_=i, axis=AX.X, op=ALU.max), reads=[in_b], writes=[out_b])

        def cp(eng, out_b, in_b, o=None, i=None):
            oa = out_b.ap if o is None else o
            ia = in_b.ap if i is None else i
            if eng == "act":
                return P.op("act", lambda e: e.copy(out=oa, in_=ia), reads=[in_b], writes=[out_b])
            return P.op(eng, lambda e: e.tensor_copy(out=oa, in_=ia), reads=[in_b], writes=[out_b])

        def mm(out_b, o, l_b, l, r_b, r, start, stop, signal=None):
            if signal is None:
                signal = stop
            return P.op("pe", lambda e: e.matmul(out=o, lhsT=l, rhs=r, start=start, stop=stop),
                        reads=[l_b, r_b], writes=[out_b], signal=signal)

        def tr(out_b, o, in_b, i, id_b, idap, signal):
            return P.op("pe", lambda e: e.transpose(out=o, in_=i, identity=idap), reads=[in_b, id_b], writes=[out_b], signal=signal)

        def memset(eng, b, val, ap=None):
            a = b.ap if ap is None else ap
            return P.op(eng, lambda e: e.memset(a, val), writes=[b])

        def asel(out_b, in_b, pattern, cmp, base, cm, o=None, i=None):
            oa = out_b.ap if o is None else o
            ia = in_b.ap if i is None else i
            return P.op("pool", lambda e: e.affine_select(out=oa, in_=ia, pattern=pattern, compare_op=cmp, fill=0.0,
                                                          base=base, channel_multiplier=cm), reads=[in_b], writes=[out_b])

        memset("pool", ones32, 1.0)
        asel(ident32, ones32, [[1, 128]], ALU.is_equal, 0, -1)
        asel(tri32, ones32, [[1, 128]], ALU.is_ge, 0, -1)
        asel(tril32, ones32, [[-1, 128]], ALU.is_ge, 0, 1)
        cp("pool", identb, ident32)
        ts("pool", maskneg, tril32, BIG, -BIG, ALU.mult, ALU.add)
        ts("pool", maskposb, tri32, -BIG, BIG, ALU.mult, ALU.add)
        asel(selA, ones32, [[1, 128]], ALU.is_ge, 0, -8, i=ones32.ap[0:16, :])
        asel(sel32, selA, [[-1, 128]], ALU.is_ge, 7, 8)
        asel(sellastT, ones32, [[1, 128]], ALU.is_equal, -7, -8, i=ones32.ap[0:16, :])
        asel(blockA, ones32, [[-8, 16]], ALU.is_ge, 0, 1, i=ones32.ap[:, 0:16])
        asel(blockmask, blockA, [[8, 16]], ALU.is_ge, 7, -1)
        cp("pool", blockmaskb, blockmask)
        asel(sellast, ones32, [[-8, 16]], ALU.is_equal, -7, 1, i=ones32.ap[:, 0:16])
        memset("pool", selrepA, 1.0)
        asel(selrepA, selrepA, [[-8, 16], [1, 128]], ALU.is_ge, 0, 0)
        asel(selrepA, selrepA, [[8, 16], [-1, 128]], ALU.is_ge, 7, 0)
        SELREP = selrepA
        mm(PS[0], PS[0].ap[:, 0:128], sel32, sel32.ap, sel32, sel32.ap, True, True)
        tt("dve", triB32, tri32, PS[0], ALU.mult, b=PS[0].ap[:, 0:128])
        tt("dve", trilB32, tril32, PS[0], ALU.mult, b=PS[0].ap[:, 0:128])
        ts("dve", masknegB, trilB32, BIG, -BIG, ALU.mult, ALU.add)
        ts("dve", maskposbB, triB32, -BIG, BIG, ALU.mult, ALU.add)
        mm(PS[1], PS[1].ap[:, 0:128], sellastT, sellastT.ap, sel32, sel32.ap, True, True)
        cp("dve", SLm, PS[1], i=PS[1].ap[:, 0:128])

        P.dma("sp", bias8.ap[:, 0:4], b_ig.partition_broadcast(128), writes=[bias8])
        P.dma("sp", bias8.ap[:, 4:8], b_fg.partition_broadcast(128), writes=[bias8])
        P.dma("sp", gcol.ap, mnorm_w.rearrange("(c p) -> p c", p=128), writes=[gcol], slow=True)
        P.dma("sp", pscol.ap, pool_scale.rearrange("(c p) -> p c", p=128), writes=[pscol], slow=True)
        P.dma("sp", flagc.ap, flag_d, writes=[flagc])
        P.dma("sp", wb.ap, norm_mix_w.partition_broadcast(128), writes=[wb])
        P.dma("pool", wg8.ap, w_in_v[:, :, 5120:5128], writes=[wg8])
        P.dma("pool", wpl.ap, w_pool.rearrange("g (c p) e -> p g c e", p=128), writes=[wpl])
        for h in range(NH):
            memset("dve", C32[h], 0.0)
            memset("dve", Cbf[h], 0.0)
        memset("dve", negBc, 0.0)
        memset("dve", Mc, 0.0)

        P.checkpoint(1)
        rr = [0]

        def norm_tile(src_ap, L, dstT, tok0, x_in=None, wbuf=None, wap=None, xnbs=None, sqb=None):
            i = rr[0] % 2
            rr[0] += 1
            if x_in is None:
                xb = xt[i]
                P.dma("sp", xb.ap[:L, :], src_ap, writes=[xb])
            else:
                xb = x_in
            ssb = SS[i]
            xn_ = xnb[i] if xnbs is None else xnbs[i]
            sq_ = sq if sqb is None else sqb
            act(sq_, xb, AF.Square, accum=(ssb, ssb.ap[:L, 0:1]), o=sq_.ap[:L, :], i=xb.ap[:L, :])
            act(ssb, ssb, AF.Ln, bias=EPS, scale=1.0 / D, o=ssb.ap[:L, 1:2], i=ssb.ap[:L, 0:1])
            act(ssb, ssb, AF.Exp, scale=-0.5, o=ssb.ap[:L, 2:3], i=ssb.ap[:L, 1:2])
            if wbuf is None:
                wbuf, wap = wb, wb.ap
            stt(xn_, xb, ssb.ap[:L, 2:3], wbuf, ALU.mult, ALU.mult, o=xn_.ap[:L, :], a=xb.ap[:L, :], b=wap[:L, :],
                extra_r=[ssb])
            for half in range(2):
                pt = PT[half]
                for j in range(8):
                    kc = half * 8 + j
                    tr(pt, pt.ap[:, j * 128:j * 128 + L], xn_, xn_.ap[:L, kc * 128:(kc + 1) * 128],
                       identb, identb.ap[:L, :L], signal=(j == 7))
                src = pt.ap.rearrange("p (j t) -> p j t", j=8)[:, :, 0:L]
                cp("act" if half == 0 else "dve", dstT, pt, o=dstT.ap[:, half * 8:half * 8 + 8, tok0:tok0 + L], i=src)
            return xb

        pjr = [0]

        def proj(xT, tok0, L, Wb, Wap, ncols):
            pj = PJ[pjr[0] % 2]
            pjr[0] += 1
            for kc in range(KC):
                mm(pj, pj.ap[:L, 0:ncols], xT, xT.ap[:, kc, tok0:tok0 + L], Wb, Wap[:, kc, 0:ncols], kc == 0, kc == KC - 1)
            return pj

        K_IG, K_Z, K_E, K_SP, K_NEGB, K_A, K_M, K_NEGM, K_EM, K_MPREV, K_MEND, K_W, K_DEC, K_T1, K_CMX, K_T2, K_AADJ, K_CME, K_INTER, K_EM2 = range(20)

        def gate_prep(xT, tok0, L, g, mode):
            G = GW[g]

            def k(kind, rows=L):
                return G.ap[:rows, kind, :]
            pj = PC[1]
            for kc in range(KC):
                mm(pj, pj.ap[:L, 0:8], xT, xT.ap[:, kc, tok0:tok0 + L], wg8, wg8.ap[:, kc, 0:8], kc == 0, kc == KC - 1)
            tt("dve", G, pj, bias8, ALU.add, o=G.ap[:L, 0:2, :].rearrange("p a h -> p (a h)"), a=pj.ap[:L, 0:8], b=bias8.ap[:L, :])
            act(G, G, AF.Exp, scale=-1.0, o=k(K_E), i=k(K_Z))
            act(G, G, AF.Ln, bias=1.0, o=k(K_SP), i=k(K_E))
            TRI = tri32 if mode == "p" else triB32
            MN = maskneg if mode == "p" else masknegB
            p0 = PS[0]
            mm(p0, p0.ap[:L, 0:4], TRI, TRI.ap[:L, :L], G, k(K_SP), True, True)
            if mode == "p":
                mm(p0, p0.ap[:, 4:8], ones32, ones32.ap[:L, :], G, k(K_SP), True, True)
                tt("dve", G, p0, negBc, ALU.add, o=k(K_NEGB), a=p0.ap[:L, 0:4], b=negBc.ap[:L, :])
            else:
                cp("dve", G, p0, o=k(K_NEGB), i=p0.ap[:L, 0:4])
                mm(p0, p0.ap[:, 8:12], sel32, sel32.ap, msm, msm.ap[0:16, :], True, True)
                cp("dve", G, p0, o=k(K_MPREV, 128), i=p0.ap[:, 8:12])
            tt("dve", G, G, G, ALU.add, o=k(K_A), a=k(K_IG), b=k(K_NEGB))
            ts("dve", G, G, -LN16, None, ALU.add, o=k(K_AADJ), a=k(K_A))
            tt("dve", dg, ident32, G, ALU.mult, o=dg.ap[:L, :, :L],
               a=ident32.ap[:L, :L].unsqueeze(1).to_broadcast([L, 4, L]),
               b=k(K_A).unsqueeze(2).to_broadcast([L, 4, L]))
            p1 = PS[1]
            for h in range(NH):
                mm(p1, p1.ap[:, h * 128:h * 128 + L], ones32, ones32.ap[:L, :], dg, dg.ap[:L, h, :L], True, True, signal=(h == NH - 1))
            Ab = p1.ap.rearrange("p (h s) -> p h s", h=4)
            tt("dve", tmpm, p1, MN, ALU.add, o=tmpm.ap[:L, :, :L], a=Ab[:L, :, :L],
               b=MN.ap[:L, :L].unsqueeze(1).to_broadcast([L, 4, L]))
            red(G, tmpm, k(K_CMX), tmpm.ap[:L, :, :L])
            if mode == "p":
                cp("dve", G, Mc, o=k(K_MPREV, 128), i=Mc.ap)
                tt("dve", G, G, Mc, ALU.max, o=k(K_M), a=k(K_CMX), b=Mc.ap[:L, :])
                red(G, p1, k(K_CME, 128), Ab[:, :, :L])
                tt("dve", G, G, Mc, ALU.max, o=k(K_MEND, 128), a=k(K_CME, 128), b=Mc.ap)
            else:
                tt("dve", G, G, G, ALU.max, o=k(K_M), a=k(K_CMX), b=k(K_MPREV))
                mm(p0, p0.ap[:, 12:16], SLm, SLm.ap, G, k(K_M), True, True)
                cp("dve", G, p0, o=k(K_MEND, 128), i=p0.ap[:, 12:16])
            tt("dve", G, G, G, ALU.subtract, o=k(K_NEGM), a=k(K_NEGB), b=k(K_M))
            tt("dve", G, G, G, ALU.subtract, o=k(K_INTER), a=k(K_MPREV), b=k(K_M))
            act(G, G, AF.Exp, o=k(K_INTER), i=k(K_INTER))
            act(G, G, AF.Exp, o=k(K_EM), i=k(K_NEGM))
            act(G, G, AF.Exp, scale=2.0, o=k(K_EM2), i=k(K_NEGM))
            tt("dve", G, G, G, ALU.subtract, o=k(K_T1), a=k(K_AADJ), b=k(K_MEND))
            act(G, G, AF.Exp, o=k(K_W), i=k(K_T1))
            tt("dve", G, G, G, ALU.subtract, o=k(K_T2, 128), a=k(K_MPREV, 128), b=k(K_MEND, 128))
            act(G, G, AF.Exp, o=k(K_DEC, 128), i=k(K_T2, 128))
            if mode == "p":
                cp("dve", Mc, G, i=k(K_MEND, 128))
                tt("dve", negBc, negBc, p0, ALU.add, b=p0.ap[:, 4:8])
            else:
                tt("dve", dsel, sellast, G, ALU.mult, a=sellast.ap.unsqueeze(2).to_broadcast([128, 16, 4]),
                   b=k(K_T2, 128).unsqueeze(1).to_broadcast([128, 16, 4]))
                mm(p1, p1.ap[:, 0:64], ones32, ones32.ap, dsel, dsel.ap.rearrange("p b h -> p (b h)"), True, True)
                act(decs, p1, AF.Exp, o=decs.ap.rearrange("p b h -> p (b h)"), i=p1.ap[:, 0:64])
                ts("dve", mpos, G, -1.0, None, ALU.mult, a=k(K_NEGM))
                mm(p0, p0.ap[0:16, 16:20], sellast, sellast.ap, mpos, mpos.ap, True, True)
                cp("dve", msout, p0, o=msout.ap[0:16, :], i=p0.ap[0:16, 16:20])
                P.dma("sp", m_s, msout.ap[0:16, :], reads=[msout])

        cr = [0]

        def state_update(h, g, L, ktm_ap, vext_ap, k_b, v_b):
            G = GW[g]
            i = cr[0] % 2
            kw = kwb[i]
            ts("dve", kw, k_b, G.ap[:L, K_W, h:h + 1], None, ALU.mult, o=kw.ap[:L, :], a=ktm_ap, extra_r=[G])
            for dc in range(2):
                mm(PC[dc], PC[dc].ap[:, 0:HDE], kw, kw.ap[:L, dc * 128:(dc + 1) * 128], v_b, vext_ap, True, True)
            for dc in range(2):
                stt(C32[h], C32[h], G.ap[:, K_DEC, h:h + 1], PC[dc], ALU.mult, ALU.add,
                    o=C32[h].ap[:, dc, :], a=C32[h].ap[:, dc, :], b=PC[dc].ap[:, 0:HDE], extra_r=[G])
            cp("act", Cbf[h], C32[h])

        xnTp = Buf(Z1b[:, 0:KC * 1040].rearrange("p (k t) -> p k t", k=KC))
        kvp = Buf(Z2b[:, 0:9 * 516].rearrange("p (i f) -> p i f", i=9))
        pre_tiles = [(x_meta, 16, 1024)] + [(x_pre[i * 128:(i + 1) * 128, :], 128, i * 128) for i in range(8)]
        for (src, L, tok0) in pre_tiles:
            norm_tile(src, L, xnTp, tok0)
        P.checkpoint(2)
        def gatesA_plain():
            for gi, (src, L, tok0) in enumerate(pre_tiles):
                gate_prep(xnTp, tok0, L, gi, "p")
                if gi == 0:
                    cp("dve", negBm, negBc)
                    cp("dve", Mm, Mc)

        def gatesA():
            return P.defer(gatesA_plain)

        kvp_t = [[Buf(kvp.ap[:, gi, :]) for gi in range(9)] for _ in range(1)][0]

        def loadA(h):
            W = W2[h % 2]
            P.dma("pool", W.ap[:, :, 0:256], w_in_v[:, :, 2048 + h * 256:2048 + (h + 1) * 256], writes=[W])
            P.dma("pool", W.ap[:, :, 256:512], w_in_v[:, :, 3072 + h * 256:3072 + (h + 1) * 256], writes=[W])

        def projA(h, dst):
            W = W2[h % 2]
            for gi, (src, L, tok0) in enumerate(pre_tiles):
                pj = PJ[gi % 2]
                for kc in range(KC):
                    mm(pj, pj.ap[:L, 0:512], xnTp, xnTp.ap[:, kc, tok0:tok0 + L], W, W.ap[:, kc, :], kc == 0, kc == KC - 1)
                    if kc % 4 == 3:
                        yield
                cp("act", dst[gi], pj, o=dst[gi].ap[:L, 0:512], i=pj.ap[:L, 0:512])
                yield

        def updA(h, srcb):
            for gi, (src, L, tok0) in enumerate(pre_tiles):
                G = GW[gi]
                kw = kwb[gi % 2]
                kb = srcb[gi]
                ts("dve", kw, kb, G.ap[:L, K_W, h:h + 1], None, ALU.mult, o=kw.ap[:L, :], a=kb.ap[:L, 0:256], extra_r=[G])
                yield
                for dc in range(2):
                    mm(PC[dc], PC[dc].ap[:, 0:HDE], kw, kw.ap[:L, dc * 128:(dc + 1) * 128], kb, kb.ap[:L, 256:513], True, True)
                yield
                for dc in range(2):
                    stt(C32[h], C32[h], G.ap[:, K_DEC, h:h + 1], PC[dc], ALU.mult, ALU.add,
                        o=C32[h].ap[:, dc, :], a=C32[h].ap[:, dc, :], b=PC[dc].ap[:, 0:HDE], extra_r=[G])
                yield
                if gi == 0:
                    cp("dve", C32m[h], C32[h])

        kvp2 = Buf(Z2b[:, 13312:13312 + 9 * 516].rearrange("p (i f) -> p i f", i=9))
        memset("dve", kvp, 1.0)
        memset("dve", kvp2, 1.0)
        kvA = [[Buf(kvp.ap[:, gi, :]) for gi in range(9)], [Buf(kvp2.ap[:, gi, :]) for gi in range(9)]]
        for lst, par_b in ((kvA[0], kvp), (kvA[1], kvp2)):
            for b_ in lst:
                b_.w = par_b.w
        def chain(*gs):
            for g_ in gs:
                yield from g_

        def run(gens):
            gens = list(gens)
            while gens:
                for g_ in list(gens):
                    try:
                        next(g_)
                    except StopIteration:
                        gens.remove(g_)

        loadA(0)
        loadA(1)
        run([gatesA(), chain(projA(0, kvA[0]), projA(1, kvA[1]))])
        loadA(2)
        loadA(3)
        run([updA(0, kvA[0])])
        run([updA(1, kvA[1]), projA(2, kvA[0])])
        run([updA(2, kvA[0]), projA(3, kvA[1])])
        run([updA(3, kvA[1])])
        for h in range(NH):
            tt("dve", C32[h], C32[h], C32m[h], ALU.subtract)
            stt(C32[h], C32[h], flagc.ap[:, 0:1], C32m[h], ALU.mult, ALU.add, extra_r=[flagc])
            cp("act", Cbf[h], C32[h])
        for (cur, sav) in ((negBc, negBm), (Mc, Mm)):
            tt("dve", cur, cur, sav, ALU.subtract)
            stt(cur, cur, flagc.ap[:, 0:1], sav, ALU.mult, ALU.add, extra_r=[flagc])

        P.checkpoint(4)
        xnT = Buf(Z1b[:, 0:KC * 1168].rearrange("p (k t) -> p k t", k=KC))
        fence([xnTp, xnT])
        yT = Buf(Z2b[:, 0:KC * 1152].rearrange("p (k t) -> p k t", k=KC))
        main_tiles = [(x_own[i * 128:(i + 1) * 128, :], 128, i * 128) for i in range(8)] + [(x_smp, 128, 1024)]
        for (src, L, tok0) in main_tiles + [(x_halo, 16, 1152)]:
            norm_tile(src, L, xnT, tok0)
        P.dma("sp", msm.ap[0:16, :], st_m, writes=[msm])
        P.dma("sp", nin.ap[0:64, :], st_n, writes=[nin])
        for dc in range(2):
            tr(PS[0], PS[0].ap[:, dc * 64:(dc + 1) * 64], nin, nin.ap[0:64, dc * 128:(dc + 1) * 128], ident32, ident32.ap[0:64, 0:64], signal=(dc == 1))
        cp("dve", nT, PS[0], o=nT.ap.rearrange("p b h c -> p c (b h)"), i=PS[0].ap[:, 0:128].rearrange("p (c x) -> p c x", c=2))
        def gates_main():
            for gi, (src, L, tok0) in enumerate(main_tiles):
                gate_prep(xnT, tok0, L, 9 + gi, "p" if gi < 8 else "s")
            tt("dve", mvec, Mc, negBc, ALU.subtract)
            P.dma("sp", m_p, mvec.ap[0:1, :], reads=[mvec])

        P.checkpoint(5)
        WU = W2
        fence([kvp, kvp2, wb, yT] + C32m + xt + SET_PL + kvA[0] + kvA[1])
        for half in range(2):
            P.dma("pool", WU[half].ap, w_in_v[:, :, half * 512:(half + 1) * 512], writes=[WU[half]])
        sp_rows = st_pool.rearrange("b j c -> (b j) c")
        for blk, (r0, nr) in enumerate(((0, 128), (128, 112))):
            P.dma("sp", sptm.ap[0:nr, :], sp_rows[r0:r0 + nr, :], writes=[sptm])
            for cc in range(8):
                p = PS[cc % 2]
                tr(p, p.ap[:, 0:nr], sptm, sptm.ap[0:nr, cc * 128:(cc + 1) * 128], ident32, ident32.ap[0:nr, 0:nr], signal=True)
                eng = "dve" if cc % 2 else "act"
                if blk == 0:
                    cp(eng, uTs, p, o=uTs.ap[:, cc, 0:8, 0:15], i=p.ap[:, 0:120].rearrange("p (b j) -> p b j", b=8))
                    cp(eng, uTs, p, o=uTs.ap[:, cc, 8, 0:8], i=p.ap[:, 120:128])
                else:
                    cp(eng, uTs, p, o=uTs.ap[:, cc, 8, 8:15], i=p.ap[:, 0:7])
                    cp(eng, uTs, p, o=uTs.ap[:, cc, 9:16, 0:15], i=p.ap[:, 7:112].rearrange("p (b j) -> p b j", b=7))

        def pool_group(U, shp, g, o_ap):
            nd = len(shp)
            T = shp[-1]

            def v(b, t0, t1):
                if nd == 1:
                    return b[:, :, t0:t1]
                return b[:, :, :, t0:t1]
            if nd == 1:
                sv = [sw[i].ap[:, 0:2 * T].rearrange("p (c t) -> p c t", c=2) for i in range(2)]
            else:
                sv = [sw[i].ap[:, 0:2 * shp[0] * T].rearrange("p (c b t) -> p c b t", c=2, b=shp[0]) for i in range(2)]
            cur_b, cur = U, (U.ap[:, 2 * g:2 * g + 2, :] if nd == 1 else U.ap[:, 2 * g:2 * g + 2, :, :])
            for k in range(g + 1):
                sh = 1 << k
                lo = 2 * sh - 1
                nb, nv = sw[k % 2], sv[k % 2]
                tt("dve", nb, cur_b, cur_b, ALU.add, o=v(nv, lo, T), a=v(cur, lo, T), b=v(cur, lo - sh, T - sh))
                cur_b, cur = nb, nv
            wdw = 2 << g
            uv = U.ap[:, 2 * g:2 * g + 2, :] if nd == 1 else U.ap[:, 2 * g:2 * g + 2, :, :]
            stt(dTb, cur_b, 1.0 / wdw, U, ALU.mult, ALU.subtract, o=o_ap, a=v(cur, 15, T), b=v(uv, 15, T))

        def pool_tile(tok0, L, ytok0, hist_mode):
            for half in range(2):
                pj = proj(xnT, tok0, L, WU[half], WU[half].ap, 512)
                cp("act" if half == 0 else "dve", utm, pj, o=utm.ap[:L, half * 512:(half + 1) * 512], i=pj.ap[:L, :])
            if hist_mode == "smp":
                for b in range(16):
                    P.dma("sp", pool_s[b, 7:15, :], utm.ap[b * 8:(b + 1) * 8, :], reads=[utm])
                    P.dma("sp", pool_s[b, 0:7, :], st_pool[b, 8:15, :])
            if hist_mode == "own" and tok0 == 7 * 128:
                P.dma("sp", pool_p, utm.ap[113:128, :], reads=[utm])
            for cc in range(8):
                p = PT[cc % 2]
                pa = p.ap.bitcast(F32)
                tr(p, pa[:, 0:L], utm, utm.ap[:L, cc * 128:(cc + 1) * 128], ident32, ident32.ap[:L, :L], signal=True)
                if hist_mode == "halo":
                    cp("dve" if cc % 2 else "act", uT, p, o=uT.ap[:, cc, 0:15], i=pa[:, 1:16])
                elif hist_mode == "own":
                    cp("dve" if cc % 2 else "act", uT, p, o=uT.ap[:, cc, 15:143], i=pa[:, 0:128])
                else:
                    cp("dve" if cc % 2 else "act", uTs, p, o=uTs.ap[:, cc, :, 15:23], i=pa[:, 0:128].rearrange("p (b t) -> p b t", b=16))
            if hist_mode == "halo":
                return
            if hist_mode == "own":
                U, shp, hcol = uT, (143,), 15
            else:
                U, shp, hcol = uTs, (16, 23), 15
            for g in range(4):
                if hist_mode == "own":
                    o_ap = dTb.ap[:, 2 * g:2 * g + 2, :]
                else:
                    o_ap = dTb.ap[:, 2 * g:2 * g + 2, :].rearrange("p c (b t) -> p c b t", b=16)
                pool_group(U, shp, g, o_ap)
            for g in range(4):
                for ec in range(2):
                    p = PC[0]
                    for c in range(2):
                        mm(p, p.ap[:, ec * 128:(ec + 1) * 128], wpl, wpl.ap[:, g, c, ec * 128:(ec + 1) * 128], dTb, dTb.ap[:, 2 * g + c, :], c == 0, c == 1)
                    ch = 2 * g + ec
                    ts("dve", yT, p, pscol.ap[:, ch:ch + 1], None, ALU.mult, o=yT.ap[:, ch, ytok0:ytok0 + 128], a=p.ap[:, ec * 128:(ec + 1) * 128], extra_r=[pscol])
            if hist_mode == "own":
                cp("dve", sw[0], uT, o=sw[0].ap[:, 0:120].rearrange("p (c t) -> p c t", c=8), i=uT.ap[:, :, 128:143])
                cp("dve", uT, sw[0], o=uT.ap[:, :, 0:15], i=sw[0].ap[:, 0:120].rearrange("p (c t) -> p c t", c=8))

        def pool_all():
            pool_tile(1152, 16, 0, "halo")
            for i in range(8):
                pool_tile(i * 128, 128, i * 128, "own")
            pool_tile(1024, 128, 1024, "smp")
        run([P.defer(gates_main), P.defer(pool_all)])

        P.checkpoint(6)
        q_t = [Buf(q_tm.ap[:, i, :]) for i in range(9)]
        k_t = [Buf(k_tm.ap[:, i, :]) for i in range(9)]
        s_t = [Buf(sigo.ap[:, i, :]) for i in range(9)]
        v_t = [Buf(v_ext.ap[:, i, :]) for i in range(9)]
        Z4BUFS.extend(q_t + k_t + s_t + v_t)

        def front(h, ti, g, mode):
            G = GW[g]
            par = ti % 2
            MP = maskposb if mode == "p" else maskposbB
            pt = PT[par]
            for dc in range(2):
                tr(pt, pt.ap[:, dc * 128:(dc + 1) * 128], k_t[ti], k_t[ti].ap[:, dc * 128:(dc + 1) * 128], identb, identb.ap, signal=False)
            for dc in range(2):
                tr(pt, pt.ap[:, (2 + dc) * 128:(3 + dc) * 128], q_t[ti], q_t[ti].ap[:, dc * 128:(dc + 1) * 128], identb, identb.ap, signal=(dc == 1))
            yield
            cp("act", qkT[par], pt, o=qkT[par].ap.rearrange("p a t -> p (a t)"), i=pt.ap[:, 0:512])
            ts("dve", dgM[par], ident32, G.ap[:, K_M, h:h + 1], None, ALU.mult, extra_r=[G])
            yield
            p = PS[par]
            mm(p, p.ap[:, 0:128], ones32, ones32.ap, dgM[par], dgM[par].ap, True, True)
            mm(p, p.ap[:, 128:256], ones32, ones32.ap, dgM[par], dgM[par].ap, True, False, signal=False)
            mm(p, p.ap[:, 128:256], identb, identb.ap, MP, MP.ap, False, True)
            yield
            act(DT[par], p, AF.Exp, bias=G.ap[:, K_AADJ, h:h + 1], scale=-1.0, i=p.ap[:, 128:256], extra_r=[G])
            yield
            if mode == "p":
                act(interb[par], p, AF.Exp, bias=G.ap[:, K_MPREV, h:h + 1], scale=-1.0, i=p.ap[:, 0:128], extra_r=[G])
            else:
                ts("dve", dgM[par], ident32, G.ap[:, K_INTER, h:h + 1], None, ALU.mult, extra_r=[G])
                mm(p, p.ap[:, 384:512], ones32, ones32.ap, dgM[par], dgM[par].ap, True, True)
                cp("act", interb[par], p, i=p.ap[:, 384:512])
            yield
            for dc in range(2):
                mm(p, p.ap[:, 256:384], qkT[par], qkT[par].ap[:, dc, :], qkT[par], qkT[par].ap[:, 2 + dc, :], dc == 0, dc == 1)
            yield
            tt("dve", STb[par], p, DT[par], ALU.mult, a=p.ap[:, 256:384])
            yield
            ts("dve", kwb[par], k_t[ti], G.ap[:, K_W, h:h + 1], None, ALU.mult, extra_r=[G])
            yield
            if mode == "p":
                tt("dve", qtl[par], qkT[par], interb[par], ALU.mult, a=qkT[par].ap[:, 2:4, :],
                   b=interb[par].ap.unsqueeze(1).to_broadcast([128, 2, 128]))
            else:
                tt("dve", qt32, qkT[par], interb[par], ALU.mult, a=qkT[par].ap[:, 2:4, :],
                   b=interb[par].ap.unsqueeze(1).to_broadcast([128, 2, 128]))
                yield
                tt("dve", Vmask, v_t[ti], blockmask, ALU.mult, o=Vmask.ap[:, :, 0:HDE],
                   a=v_t[ti].ap[:, 0:HDE].unsqueeze(1).to_broadcast([128, 16, HDE]),
                   b=blockmask.ap.unsqueeze(2).to_broadcast([128, 16, HDE]))
            yield

        def back(h, ti, g, mode):
            G = GW[g]
            par = ti % 2
            pn = PN[par]
            vb = v_t[ti]
            if mode == "p":
                mm(pn, pn.ap[:, 0:HDE], STb[par], STb[par].ap, vb, vb.ap[:, 0:HDE], True, False, signal=False)
                for dc in range(2):
                    mm(pn, pn.ap[:, 0:HDE], qtl[par], qtl[par].ap[:, dc, :], Cbf[h], Cbf[h].ap[:, dc, :], False, dc == 1)
                yield
                for dc in range(2):
                    mm(PC[dc], PC[dc].ap[:, 0:HDE], kwb[par], kwb[par].ap[:, dc * 128:(dc + 1) * 128], vb, vb.ap[:, 0:HDE], True, True)
                yield
                for dc in range(2):
                    stt(C32[h], C32[h], G.ap[:, K_DEC, h:h + 1], PC[dc], ALU.mult, ALU.add,
                        o=C32[h].ap[:, dc, :], a=C32[h].ap[:, dc, :], b=PC[dc].ap[:, 0:HDE], extra_r=[G])
                    yield
                cp("act", Cbf[h], C32[h])
                yield
            else:
                mm(pn, pn.ap[:, 0:HDE], STb[par], STb[par].ap, vb, vb.ap[:, 0:HDE], True, False, signal=False)
                def load_grp(gq):
                    c_ = C32s[gq % 3]
                    for bl_ in range(2):
                        P.dma("sp", c_.ap[:, bl_, :, 0:HD], st_C[gq * 2 + bl_, h].rearrange("(c p) v -> p c v", p=128), writes=[c_])
                load_grp(0)
                load_grp(1)

                def prep_grp(gq):
                    c_ = C32s[gq % 3]
                    cp("dve", c_, nT, o=c_.ap[:, :, :, HD], i=nT.ap[:, gq * 2:(gq + 1) * 2, h, :])
                    cp("act", C16, c_)
                    qb_ = qm[gq % 2]
                    for dc_ in range(2):
                        tt("dve", qb_, qt32, SELREP, ALU.mult, o=qb_.ap[:, dc_ * 2:dc_ * 2 + 2, :],
                           a=qt32.ap[:, dc_, :].unsqueeze(1).to_broadcast([128, 2, 128]), b=SELREP.ap[:, gq * 2:(gq + 1) * 2, :])

                def inter_grp(gq):
                    c_ = C32s[gq % 3]
                    qb_ = qm[gq % 2]
                    for bl_ in range(2):
                        for dc_ in range(2):
                            last = (gq == 7 and bl_ == 1 and dc_ == 1)
                            mm(pn, pn.ap[:, 0:HDE], qb_, qb_.ap[:, dc_ * 2 + bl_, :], C16, C16.ap[:, bl_, dc_, :], False, last,
                               signal=(last or (bl_ == 1 and dc_ == 1)))
                pq = PS[1]
                for dc_ in range(2):
                    mm(pq, pq.ap[:, dc_ * 16:(dc_ + 1) * 16], kwb[par], kwb[par].ap[:, dc_ * 128:(dc_ + 1) * 128], blockmaskb, blockmaskb.ap, True, True)
                for dc_ in range(2):
                    tt("dve", nS, nT, decs, ALU.mult, o=nS.ap[:, :, h, dc_], a=nT.ap[:, :, h, dc_], b=decs.ap[:, :, h])
                    tt("dve", nS, nS, pq, ALU.add, o=nS.ap[:, :, h, dc_], a=nS.ap[:, :, h, dc_], b=pq.ap[:, dc_ * 16:(dc_ + 1) * 16])
                yield
                prep_grp(0)
                yield
                inter_grp(0)
                yield
                for grp in range(8):
                    cs_ = C32s[grp % 3]
                    if grp + 2 < 8:
                        load_grp(grp + 2)
                    if grp + 1 < 8:
                        prep_grp(grp + 1)
                        yield
                        inter_grp(grp + 1)
                        yield
                    for bl in range(2):
                        b = grp * 2 + bl
                        pc = PC[bl]
                        for dc in range(2):
                            mm(pc, pc.ap[:, dc * HD:(dc + 1) * HD], kwb[par], kwb[par].ap[:, dc * 128:(dc + 1) * 128], Vmask, Vmask.ap[:, b, 0:HD], True, True)
                        stt(cs_, cs_, decs.ap[:, b, h:h + 1], pc, ALU.mult, ALU.add, o=cs_.ap[:, bl, :, 0:HD], a=cs_.ap[:, bl, :, 0:HD],
                            b=pc.ap[:, 0:2 * HD].rearrange("p (c v) -> p c v", c=2), extra_r=[decs])
                        yield
                    for bl in range(2):
                        b = grp * 2 + bl
                        P.dma("sp", C_s[b, h].rearrange("(c p) v -> p c v", p=128), cs_.ap[:, bl, :, 0:HD], reads=[cs_])
                    yield
            cl = COLS[par]
            act(sqs, pn, AF.Square, accum=(cl, cl.ap[:, 0:1]), o=sqs.ap[:, 0:HD], i=pn.ap[:, 0:HD])
            P.op("dve", lambda e, o=cl.ap[:, 1:2], i=pn.ap[:, HD:HDE], s2=G.ap[:, K_EM2, h:h + 1]:
                 e.tensor_scalar(out=o, in0=i, scalar1=i, scalar2=s2, op0=ALU.mult, op1=ALU.max), reads=[pn, G], writes=[cl])
            yield
            stt(cl, cl, EPS * HD, cl, ALU.mult, ALU.add, o=cl.ap[:, 2:3], a=cl.ap[:, 1:2], b=cl.ap[:, 0:1])
            yield
            act(cl, cl, AF.Ln, scale=1.0 / HD, o=cl.ap[:, 5:6], i=cl.ap[:, 2:3])
            act(cl, cl, AF.Exp, scale=-0.5, o=cl.ap[:, 7:8], i=cl.ap[:, 5:6])
            yield
            yb = ytm[par]
            stt(yb, pn, cl.ap[:, 7:8], s_t[ti], ALU.mult, ALU.mult, o=yb.ap[:, 0:HD], a=pn.ap[:, 0:HD],
                b=s_t[ti].ap, extra_r=[cl])
            yield
            pt2 = PT[par]
            for c in range(2):
                tr(pt2, pt2.ap[:, 512 + c * 128:512 + (c + 1) * 128], yb, yb.ap[:, c * 128:(c + 1) * 128], identb, identb.ap, signal=(c == 1))
            yield
            for c in range(2):
                ch = 2 * h + c
                ts("dve", yT, pt2, gcol.ap[:, ch:ch + 1], None, ALU.mult, o=yT.ap[:, 8 + ch, ti * 128:(ti + 1) * 128],
                   a=pt2.ap[:, 512 + c * 128:512 + (c + 1) * 128], extra_r=[gcol])
                yield

        def load_head_w(h):
            for half in range(2):
                for j in range(2):
                    c0 = 1024 + (half * 2 + j) * 1024 + h * 256
                    P.dma("pool", W2[half].ap[:, :, j * 256:(j + 1) * 256], w_in_v[:, :, c0:c0 + 256], writes=[W2[half]])

        def projgen(h, ti):
            tok0 = main_tiles[ti][2]
            pj = PJ[0]
            for kc in range(KC):
                mm(pj, pj.ap[:, 0:512], xnT, xnT.ap[:, kc, tok0:tok0 + 128], W2[0], W2[0].ap[:, kc, :], kc == 0, kc == KC - 1)
                if kc % 2 == 1:
                    yield
            cp("act", q_t[ti], pj, i=pj.ap[:, 0:256])
            cp("dve", k_t[ti], pj, i=pj.ap[:, 256:512])
            yield
            pj = PJ[1]
            for kc in range(KC):
                mm(pj, pj.ap[:, 0:512], xnT, xnT.ap[:, kc, tok0:tok0 + 128], W2[1], W2[1].ap[:, kc, :], kc == 0, kc == KC - 1)
                if kc % 2 == 1:
                    yield
            cp("dve", v_t[ti], pj, o=v_t[ti].ap[:, 0:HD], i=pj.ap[:, 0:256])
            act(sgt, pj, AF.Exp, scale=-1.0, i=pj.ap[:, 256:512])
            yield
            act(sgt, sgt, AF.Ln, bias=1.0)
            yield
            act(s_t[ti], sgt, AF.Exp, scale=-1.0)
            yield

        def interleave(gens):
            gens = list(gens)
            while gens:
                for g_ in list(gens):
                    try:
                        next(g_)
                    except StopIteration:
                        gens.remove(g_)

        fence(SET_PL + SET_HD + q_t + k_t + s_t + v_t)
        memset("dve", v_ext, 1.0)
        for vt_ in v_t:
            vt_.w = v_ext.w
        memset("dve", Vmask, 0.0)
        PN = PS
        load_head_w(0)
        for ti in range(9):
            interleave([projgen(0, ti)])
        for h in range(NH):
            if h + 1 < NH:
                load_head_w(h + 1)
            for s_ in range(11):
                tasks = []
                if s_ == 0 and h > 0:
                    continue
                if s_ <= 8:
                    tasks.append(front(h, s_, 9 + s_, "p" if s_ < 8 else "s"))
                if 1 <= s_ <= 9:
                    tasks.append(back(h, s_ - 1, 9 + s_ - 1, "p" if s_ - 1 < 8 else "s"))
                if h + 1 < NH and 2 <= s_ <= 10:
                    tasks.append(projgen(h + 1, s_ - 2))
                if s_ == 10 and h + 1 < NH:
                    tasks.append(front(h + 1, 0, 9, "p"))
                interleave(tasks)
        P.checkpoint(7)
        for h in range(NH):
            for dc in range(2):
                P.dma("sp", C_p[h, dc * 128:(dc + 1) * 128, :], C32[h].ap[:, dc, 0:HD], reads=[C32[h]])
                P.dma("sp", n_p[h:h + 1, dc * 128:(dc + 1) * 128].rearrange("o p -> p o"), C32[h].ap[:, dc, HD:HDE], reads=[C32[h]], slow=True)
        P.checkpoint(7.5)
        tr(PS[0], PS[0].ap[:, 0:128], nS, nS.ap.rearrange("p b h c -> p (b h c)"), ident32, ident32.ap, signal=True)
        cp("dve", nrow, PS[0], i=PS[0].ap[:, 0:128])
        P.dma("sp", n_s, nrow.ap, reads=[nrow])

        P.checkpoint(8)
        all_src = [x_own[i * 128:(i + 1) * 128, :] for i in range(8)] + [x_smp]
        z4_users = Z4BUFS
        for i in range(9):
            P.dma("sp", x_new[i].ap, all_src[i], reads=[], writes=[x_new[i]] + z4_users)
        WO = [W2[0], W2[1]]
        for cg in range(4):
            W = WO[cg % 2]
            P.dma("pool", W.ap, w_out_v[:, :, cg * 512:(cg + 1) * 512], writes=[W])
            for i in range(9):
                pj = proj(yT, i * 128, 128, W, W.ap, 512)
                tt("dve", x_new[i], x_new[i], pj, ALU.add, o=x_new[i].ap[:, cg * 512:(cg + 1) * 512],
                   a=x_new[i].ap[:, cg * 512:(cg + 1) * 512], b=pj.ap[:, 0:512])
        P.dma("sp", Z3[:, 4096:4096 + D], norm_ffn_w.partition_broadcast(128), writes=[W2[1]])
        xn2T = Buf(Z1b[:, 0:KC * 1152].rearrange("p (k t) -> p k t", k=KC))
        fence([xnT, xn2T])
        xnbC = [Buf(Z2b[:, 0:2048]), Buf(Z2b[:, 2048:4096])]
        sqC = Buf(Z2b[:, 4096:6144])
        fence([yT, sqC] + xnbC)
        for i in range(9):
            norm_tile(None, 128, xn2T, i * 128, x_in=x_new[i], wbuf=W2[1], wap=Z3[:, 4096:4096 + D], xnbs=xnbC, sqb=sqC)

        P.checkpoint(9)
        hT = Buf(Z2b[:, 0:11 * 1152].rearrange("p (f t) -> p f t", f=11))
        WG = [Buf(Z3b[:, i * 2048:(i + 1) * 2048].rearrange("p (k n) -> p k n", k=KC)) for i in range(2)]
        WUp = [Buf(Z3b[:, 4096 + i * 2048:4096 + (i + 1) * 2048].rearrange("p (k n) -> p k n", k=KC)) for i in range(2)]
        WD = [Buf(Z3b[:, 8192 + i * 2816:8192 + (i + 1) * 2816].rearrange("p (f n) -> p f n", f=11)) for i in range(2)]
        fence([W2[0], W2[1], yT, hT, sqC] + xnbC + WG + WUp + WD + silt)
        tgs = [(0, 512), (512, 512), (1024, 128)]
        fr = [0]
        for qd in range(4):
            for fl in range(11):
                fc = qd * 11 + fl
                wgb = WG[fr[0] % 2]
                wub = WUp[fr[0] % 2]
                fr[0] += 1
                P.dma("pool", wgb.ap, w_gate_v[:, :, fc * 128:(fc + 1) * 128], writes=[wgb])
                P.dma("pool", wub.ap, w_up_v[:, :, fc * 128:(fc + 1) * 128], writes=[wub])
                for ti, (t0, tn) in enumerate(tgs):
                    pg = PJ[ti % 2]
                    pu = PS[ti % 2]
                    for kc in range(KC):
                        mm(pg, pg.ap[:, 0:tn], wgb, wgb.ap[:, kc, :], xn2T, xn2T.ap[:, kc, t0:t0 + tn], kc == 0, kc == KC - 1)
                    for kc in range(KC):
                        mm(pu, pu.ap[:, 0:tn], wub, wub.ap[:, kc, :], xn2T, xn2T.ap[:, kc, t0:t0 + tn], kc == 0, kc == KC - 1)
                    st_ = silt[ti % 2]
                    act(st_, pg, AF.Silu, o=st_.ap[:, 0:tn], i=pg.ap[:, 0:tn])
                    tt("dve", hT, st_, pu, ALU.mult, o=hT.ap[:, fl, t0:t0 + tn], a=st_.ap[:, 0:tn], b=pu.ap[:, 0:tn])
            for cg in range(8):
                W = WD[cg % 2]
                P.dma("pool", W.ap, w_down_v[:, qd * 11:(qd + 1) * 11, cg * 256:(cg + 1) * 256], writes=[W])
                for i in range(9):
                    pc = PC[i % 2]
                    for fl in range(11):
                        mm(pc, pc.ap[:, 0:256], hT, hT.ap[:, fl, i * 128:(i + 1) * 128], W, W.ap[:, fl, :], fl == 0, fl == 10)
                    tt("dve", x_new[i], x_new[i], pc, ALU.add, o=x_new[i].ap[:, cg * 256:(cg + 1) * 256],
                       a=x_new[i].ap[:, cg * 256:(cg + 1) * 256], b=pc.ap[:, 0:256])

        P.checkpoint(10)
        wbE = Buf(Z1[:, 0:D])
        fence([xn2T, wbE])
        P.dma("sp", wbE.ap, norm_final_w.partition_broadcast(128), writes=[wbE])
        for i in range(9):
            ssb = SS[i % 2]
            xb = x_new[i]
            act(sq, xb, AF.Square, accum=(ssb, ssb.ap[:, 0:1]))
            act(ssb, ssb, AF.Ln, bias=EPS, scale=1.0 / D, o=ssb.ap[:, 1:2], i=ssb.ap[:, 0:1])
            act(ssb, ssb, AF.Exp, scale=-0.5, o=ssb.ap[:, 2:3], i=ssb.ap[:, 1:2])
            stt(xb, xb, ssb.ap[:, 2:3], wbE, ALU.mult, ALU.mult, extra_r=[ssb])
            dst = y_own[i * 128:(i + 1) * 128, :] if i < 8 else y_smp
            P.dma("sp", dst, xb.ap, reads=[xb])

        sem_es = ExitStack()
        with sem_es:
            sems = {}
            for e in ENGS:
                sems[e] = sem_es.enter_context(nc.semaphore("s_" + e))
            for q, n in P.ndma.items():
                for i in range(n):
                    sems[(q, i)] = sem_es.enter_context(nc.semaphore("d_%s%d" % (q, i)))
            with nc.Block() as block:
                @block.tensor
                def _(e):
                    P.replay("pe", e, sems)

                @block.scalar
                def _(e):
                    P.replay("act", e, sems)

                @block.vector
                def _(e):
                    P.replay("dve", e, sems)

                @block.gpsimd
                def _(e):
                    P.replay("pool", e, sems)
                    P.final_waits("pool", e, sems)

                @block.sync
                def _(e):
                    P.replay("sp", e, sems)
                    P.final_waits("sp", e, sems)
    return nc


_NC = None


def _get_nc():
    global _NC
    if _NC is None:
        _NC = build_nc()
    return _NC


def kernel(x_prompt, x_sample, state_pool, state_mlstm_C, state_mlstm_n, state_mlstm_m, meta_tokens, norm_mix_w, w_in,
           b_igate, b_fgate, w_pool, pool_scale, mlstm_norm_w, w_out, norm_ffn_w, w_gate, w_up, w_down, norm_final_w):
    f = lambda a: np.ascontiguousarray(np.asarray(a, dtype=np.float32))
    xp, xs = f(x_prompt), f(x_sample)
    meta = f(meta_tokens)
    shared = {
        "norm_mix_w": f(norm_mix_w)[0], "w_in": f(w_in)[0], "b_igate": f(b_igate)[0], "b_fgate": f(b_fgate)[0],
        "w_pool": f(w_pool)[0], "pool_scale": f(pool_scale)[0], "mlstm_norm_w": f(mlstm_norm_w)[0], "w_out": f(w_out)[0],
        "norm_ffn_w": f(norm_ffn_w)[0], "w_gate": f(w_gate)[0], "w_up": f(w_up)[0], "w_down": f(w_down)[0],
        "norm_final_w": f(norm_final_w),
    }
    sp, sC, sn, sm = f(state_pool)[0], f(state_mlstm_C)[0], f(state_mlstm_n)[0], f(state_mlstm_m)[0]
    in_maps = []
    for c in range(8):
        s, h = c // 2, c % 2
        m = dict(shared)
        m["x_meta"] = meta
        m["x_pre"] = xp[s, 0:1024] if h == 1 else np.zeros((1024, D), np.float32)
        m["x_own"] = np.ascontiguousarray(xp[s, 1024 * h:1024 * (h + 1)])
        m["x_halo"] = meta if h == 0 else np.ascontiguousarray(xp[s, 1008:1024])
        m["x_smp"] = np.ascontiguousarray(xs[16 * c:16 * (c + 1)].reshape(128, D))
        m["flag"] = np.full((128, 1), float(h), np.float32)
        m["st_pool"] = np.ascontiguousarray(sp[16 * c:16 * (c + 1)])
        m["st_C"] = np.ascontiguousarray(sC[16 * c:16 * (c + 1)])
        m["st_n"] = np.ascontiguousarray(sn[16 * c:16 * (c + 1)].reshape(64, HD))
        m["st_m"] = np.ascontiguousarray(sm[16 * c:16 * (c + 1)])
        in_maps.append(m)
    nc = _get_nc()
    res = run_bass_kernel_spmd(nc, in_maps, core_ids=list(range(8))).results
    y_prompt = np.stack([np.concatenate([res[2 * s]["y_own"], res[2 * s + 1]["y_own"]], axis=0) for s in range(4)])
    y_sample = np.concatenate([r["y_smp"] for r in res], axis=0).reshape(128, 8, D)
    pool_pp = np.stack([res[2 * s + 1]["pool_p"] for s in range(4)])[None]
    C_pp = np.stack([res[2 * s + 1]["C_p"] for s in range(4)])[None]
    n_pp = np.stack([res[2 * s + 1]["n_p"] for s in range(4)])[None]
    m_pp = np.stack([res[2 * s + 1]["m_p"].reshape(NH) for s in range(4)])[None]
    pool_ss = np.concatenate([r["pool_s"] for r in res], axis=0)[None]
    C_ss = np.concatenate([r["C_s"] for r in res], axis=0)[None]
    n_ss = np.concatenate([r["n_s"].reshape(16, NH, HD) for r in res], axis=0)[None]
    m_ss = np.concatenate([r["m_s"] for r in res], axis=0)[None]
    outs = (y_prompt, y_sample, pool_pp, C_pp, n_pp, m_pp, pool_ss, C_ss, n_ss, m_ss)
    return tuple(np.ascontiguousarray(o, dtype=np.float32) for o in outs)
```

```python
import math
import numpy as np
import concourse.bass as bass
import concourse.mybir as mybir
from concourse.bass_utils import run_bass_kernel_spmd

F32 = mybir.dt.float32
BF16 = mybir.dt.bfloat16
AF = mybir.ActivationFunctionType
ALU = mybir.AluOpType
AX = mybir.AxisListType

D = 2048
KC = 16
NH = 4
HD = 256
HDE = 257
DFF = 5632
FCH = 44
EPS = 1e-6
LN16 = math.log(16.0)
BIG = 1.0e30
ENGS = ("pe", "act", "dve", "pool", "sp")
STOP = None


class Buf:
    __slots__ = ("ap", "w", "r", "excl")

    def __init__(self, ap, excl=False):
        self.ap = ap
        self.w = None
        self.r = []
        self.excl = excl


class Prog:
    def __init__(self):
        self.streams = {e: [] for e in ENGS}
        self.cnt = {e: 0 for e in ENGS}
        self.seen = {e: {} for e in ENGS}
        self.ndma = {"sp": 20, "pool": 12}
        self.dma_cnt = {q: [0] * n for q, n in self.ndma.items()}
        self.dma_rr = {q: 0 for q in self.ndma}
        self.dead = False
        self._q = None
        self.stop = STOP

    def checkpoint(self, k):
        if self.stop is not None and k >= self.stop - 1e-9:
            self.dead = True

    def _waits(self, eng, deps):
        out = []
        for d in deps:
            if d is None:
                continue
            k, c = d
            if self.seen[eng].get(k, 0) >= c:
                continue
            self.seen[eng][k] = c
            out.append((k, c))
        return out

    def _deps(self, eng, reads, writes):
        deps = []
        for b in reads:
            if b.w is not None and not (eng == "pe" and b.w[0] == "pe"):
                deps.append(b.w)
            if b.excl:
                for t in b.r:
                    if t is not None and t[0] != eng:
                        deps.append(t)
        for b in writes:
            for t in [b.w] + b.r:
                if t is not None and not (eng == "pe" and t[0] == "pe"):
                    deps.append(t)
        return deps

    def defer(self, f):
        self._q = []
        f()
        q, self._q = self._q, None

        def g():
            for kind, a in q:
                if kind == "op":
                    self.op(*a)
                else:
                    self.dma(*a)
                yield
        return g()

    def op(self, eng, fn, reads=(), writes=(), signal=True):
        if self.dead:
            return None
        if self._q is not None:
            self._q.append(("op", (eng, fn, list(reads), list(writes), signal)))
            return None
        waits = self._waits(eng, self._deps(eng, reads, writes))
        if signal:
            self.cnt[eng] += 1
            tok = (eng, self.cnt[eng])
        else:
            tok = (eng, self.cnt[eng] + 1)
        self.streams[eng].append((waits, fn, "inc" if signal else None))
        for b in reads:
            b.r.append(tok)
        for b in writes:
            b.w = tok
            b.r = []
        return tok

    def dma(self, q, out_ap, in_ap, reads=(), writes=(), slow=False):
        if self.dead:
            return None
        if self._q is not None:
            self._q.append(("dma", (q, out_ap, in_ap, list(reads), list(writes), slow)))
            return None
        i = self.dma_rr[q]
        self.dma_rr[q] = (i + 1) % self.ndma[q]
        key = (q, i)
        deps = self._deps("dmaq", reads, writes)
        prev = self.dma_cnt[q][i]
        if prev:
            deps.append((key, prev))
        waits = self._waits(q, deps)
        self.dma_cnt[q][i] = prev + 16
        tok = (key, prev + 16)
        if slow:
            fn = lambda e, o=out_ap, a=in_ap: e.dma_start(out=o, in_=a, allow_slow_non_contiguous=True)
        else:
            fn = lambda e, o=out_ap, a=in_ap: e.dma_start(out=o, in_=a)
        self.streams[q].append((waits, fn, key))
        for b in reads:
            b.r.append(tok)
        for b in writes:
            b.w = tok
            b.r = []
        return tok

    def replay(self, eng, e, sems):
        for waits, fn, sig in self.streams[eng]:
            for k, c in waits:
                e.wait_ge(sems[k], c)
            ins = fn(e)
            if sig == "inc":
                ins.then_inc(sems[eng], 1)
            elif sig is not None:
                ins.then_inc(sems[sig], 16)

    def final_waits(self, eng, e, sems):
        for i, c in enumerate(self.dma_cnt.get(eng, [])):
            if c:
                e.wait_ge(sems[(eng, i)], c)


def build_nc():
    nc = bass.Bass("TRN2", target_bir_lowering=False)
    P = Prog()

    def din(name, shape, dt=F32):
        return nc.dram_tensor(name, list(shape), dt, kind="ExternalInput").ap()

    def dout(name, shape, dt=F32):
        return nc.dram_tensor(name, list(shape), dt, kind="ExternalOutput").ap()

    x_meta = din("x_meta", [16, D])
    x_pre = din("x_pre", [1024, D])
    x_own = din("x_own", [1024, D])
    x_halo = din("x_halo", [16, D])
    x_smp = din("x_smp", [128, D])
    flag_d = din("flag", [128, 1])
    st_pool = din("st_pool", [16, 15, 1024])
    st_C = din("st_C", [16, NH, HD, HD])
    st_n = din("st_n", [64, HD])
    st_m = din("st_m", [16, NH])
    norm_mix_w = din("norm_mix_w", [D])
    w_in = din("w_in", [D, 5128])
    b_ig = din("b_igate", [NH])
    b_fg = din("b_fgate", [NH])
    w_pool = din("w_pool", [4, 256, 256])
    pool_scale = din("pool_scale", [1024])
    mnorm_w = din("mlstm_norm_w", [1024])
    w_out = din("w_out", [D, D])
    norm_ffn_w = din("norm_ffn_w", [D])
    w_gate = din("w_gate", [D, DFF])
    w_up = din("w_up", [D, DFF])
    w_down = din("w_down", [DFF, D])
    norm_final_w = din("norm_final_w", [D])

    y_own = dout("y_own", [1024, D])
    y_smp = dout("y_smp", [128, D])
    pool_p = dout("pool_p", [15, 1024])
    C_p = dout("C_p", [NH, HD, HD])
    n_p = dout("n_p", [NH, HD])
    m_p = dout("m_p", [1, NH])
    pool_s = dout("pool_s", [16, 15, 1024])
    C_s = dout("C_s", [16, NH, HD, HD])
    n_s = dout("n_s", [128, 128])
    m_s = dout("m_s", [16, NH])

    w_in_v = w_in.rearrange("(kc p) n -> p kc n", p=128)
    w_out_v = w_out.rearrange("(kc p) n -> p kc n", p=128)
    w_gate_v = w_gate.rearrange("(kc p) n -> p kc n", p=128)
    w_up_v = w_up.rearrange("(kc p) n -> p kc n", p=128)
    w_down_v = w_down.rearrange("(fc p) n -> p fc n", p=128)

    from contextlib import ExitStack
    es = ExitStack()

    def sb(name, shape, dt):
        return es.enter_context(nc.sbuf_tensor(name, list(shape), dt))

    def ps(name, shape, dt):
        return es.enter_context(nc.psum_tensor(name, list(shape), dt))

    with es:
        Z1 = sb("z1", [128, KC * 1168 // 2], F32)
        Z2 = sb("z2", [128, KC * 1152 // 2], F32)
        Z3 = sb("z3", [128, 8192], F32)
        Z4 = sb("z4", [128, 9 * D], F32)
        Z1b = Z1[:, :].bitcast(BF16)
        Z2b = Z2[:, :].bitcast(BF16)
        Z3b = Z3[:, :].bitcast(BF16)
        Z4BUFS = []
        tl = [9600]

        def zt(nfl, dt=F32, shape=None):
            a0 = tl[0]
            tl[0] += nfl
            assert tl[0] <= 9 * D, tl[0]
            ap = Z4[:, a0:a0 + nfl]
            if dt == BF16:
                ap = ap.bitcast(BF16)
            if shape is not None:
                ap = ap.rearrange(shape[0], **shape[1])
            bb = Buf(ap)
            Z4BUFS.append(bb)
            return bb

        ones32 = Buf(sb("ones32", [128, 128], F32)[:, :])
        ident32 = Buf(sb("ident32", [128, 128], F32)[:, :])
        identb = Buf(sb("identb", [128, 128], BF16)[:, :])
        tri32 = Buf(sb("tri32", [128, 128], F32)[:, :])
        tril32 = Buf(sb("tril32", [128, 128], F32)[:, :])
        maskneg = Buf(sb("maskneg", [128, 128], F32)[:, :])
        maskposb = Buf(sb("maskposb", [128, 128], BF16)[:, :])
        triB32 = Buf(sb("triB32", [128, 128], F32)[:, :])
        trilB32 = Buf(sb("trilB32", [128, 128], F32)[:, :])
        masknegB = Buf(sb("masknegB", [128, 128], F32)[:, :])
        maskposbB = Buf(sb("maskposbB", [128, 128], BF16)[:, :])
        SLm = Buf(sb("SLm", [128, 128], F32)[:, :])
        sel32 = Buf(sb("sel32", [16, 128], F32)[:, :])
        selA = Buf(sb("selA", [16, 128], F32)[:, :])
        sellastT = Buf(sb("sellastT", [16, 128], F32)[:, :])
        blockmask = Buf(sb("blockmask", [128, 16], F32)[:, :])
        blockmaskb = Buf(sb("blockmaskb", [128, 16], BF16)[:, :])
        blockA = Buf(sb("blockA", [128, 16], F32)[:, :])
        sellast = Buf(sb("sellast", [128, 16], F32)[:, :])
        bias8 = Buf(sb("bias8", [128, 8], F32)[:, :])
        gcol = Buf(sb("gcol", [128, 8], F32)[:, :])
        pscol = Buf(sb("pscol", [128, 8], F32)[:, :])
        flagc = Buf(sb("flagc", [128, 1], F32)[:, :])
        fsc = Buf(sb("fsc", [128, 1], F32)[:, :])
        negBc = Buf(sb("negBc", [128, 4], F32)[:, :])
        Mc = Buf(sb("Mc", [128, 4], F32)[:, :])
        negBm = Buf(sb("negBm", [128, 4], F32)[:, :])
        Mm = Buf(sb("Mm", [128, 4], F32)[:, :])
        C32t = sb("C32", [128, NH, 2, HDE], F32)
        C32 = [Buf(C32t[:, h, :, :]) for h in range(NH)]
        Cbft = sb("Cbf", [128, NH, 2, HDE], BF16)
        Cbf = [Buf(Cbft[:, h, :, :]) for h in range(NH)]
        sst = sb("ss", [128, 2, 8], F32)
        SS = [Buf(sst[:, i, :]) for i in range(2)]
        xnb1 = Buf(sb("xnb", [128, D], BF16)[:, :])
        xnb = [xnb1, xnb1]
        sq = xnb1
        wpl = Buf(sb("wpl", [128, 4, 2, 256], BF16)[:, :, :, :])

        selrepA = zt(1024, BF16, ("p (b t) -> p b t", dict(b=16)))
        wg8 = zt(64, BF16, ("p (k n) -> p k n", dict(k=KC)))
        gwsb = zt(10 * 20 * 4)
        GWl = [Buf(gwsb.ap[:, j * 80:(j + 1) * 80].rearrange("p (k h) -> p k h", k=20)) for j in range(10)]
        Z4BUFS.extend(GWl)
        GW = GWl[0:9] + GWl[0:10]
        dg = zt(512, F32, ("p (h s) -> p h s", dict(h=4)))
        tmpm = zt(512, F32, ("p (h s) -> p h s", dict(h=4)))
        dgM = [zt(128) for i in range(2)]
        DT = [zt(128) for i in range(2)]
        interb = [zt(128) for i in range(2)]
        STb = [zt(64, BF16) for i in range(2)]
        qkT = [zt(256, BF16, ("p (a t) -> p a t", dict(a=4))) for i in range(2)]
        qtl = [zt(128, BF16, ("p (a t) -> p a t", dict(a=2))) for i in range(2)]
        kwb = [zt(128, BF16) for i in range(2)]
        colsb = zt(24)
        COLS = [Buf(colsb.ap[:, i * 12:(i + 1) * 12]) for i in range(2)]
        Z4BUFS.extend(COLS)
        ytm = [zt(128, BF16) for i in range(2)]
        sqs = zt(128, BF16)
        dsel = zt(64, F32, ("p (b h) -> p b h", dict(b=16)))
        decs = zt(64, F32, ("p (b h) -> p b h", dict(b=16)))
        nT = zt(128, F32, ("p (b h c) -> p b h c", dict(b=16, h=4)))
        nS = zt(128, F32, ("p (b h c) -> p b h c", dict(b=16, h=4)))
        nrow = zt(128)
        nin = zt(256)
        msm = zt(4)
        msout = zt(4)
        mpos = zt(4)
        mvec = zt(4)
        qt32 = zt(256, F32, ("p (a t) -> p a t", dict(a=2)))
        qm = [zt(256, BF16, ("p (b t) -> p b t", dict(b=4))) for i in range(2)]
        C16 = zt(514, BF16, ("p (b c v) -> p b c v", dict(b=2, c=2)))
        C32s3 = zt(2 * 2 * HDE, F32, ("p (b c v) -> p b c v", dict(b=2, c=2)))
        sgt = zt(256)

        o4 = [0]

        def z4(nfl):
            a = o4[0]
            o4[0] += nfl
            assert o4[0] <= 9600, o4[0]
            return Z4[:, a:a + nfl]
        xt = [Buf(z4(D)) for _ in range(2)]
        o4[0] = 0
        utm = Buf(z4(1024))
        uT = Buf(z4(8 * 143).rearrange("p (c t) -> p c t", c=8))
        uTs = Buf(z4(8 * 16 * 23).rearrange("p (c b t) -> p c b t", c=8, b=16))
        sw = [Buf(z4(1472)) for _ in range(2)]
        dTb = Buf(z4(512).bitcast(BF16).rearrange("p (c t) -> p c t", c=8))
        sptm = Buf(z4(1024))
        SET_PL = [utm, uT, uTs, sw[0], sw[1], dTb, sptm]
        o4[0] = 0
        q_tm = Buf(z4(9 * 128).bitcast(BF16).rearrange("p (i f) -> p i f", i=9))
        k_tm = Buf(z4(9 * 128).bitcast(BF16).rearrange("p (i f) -> p i f", i=9))
        sigo = Buf(z4(9 * 128).bitcast(BF16).rearrange("p (i f) -> p i f", i=9))
        v_ext = Buf(z4(9 * 130).bitcast(BF16).rearrange("p (i f) -> p i f", i=9))
        Vmask = Buf(z4(16 * 130).bitcast(BF16).rearrange("p (b f) -> p b f", b=16))
        C32s = [Buf(z4(2 * 2 * HDE).rearrange("p (b c v) -> p b c v", b=2, c=2)) for _ in range(2)]
        C32s.append(C32s3)
        SET_HD = [q_tm, k_tm, sigo, v_ext, Vmask, C32s[0], C32s[1]]
        Z4BUFS.extend(xt + SET_PL + SET_HD)
        x_new = [Buf(Z4[:, i * D:(i + 1) * D]) for i in range(9)]
        C32mt = Z2[:, 2400:2400 + NH * 2 * HDE].rearrange("p (h c v) -> p h c v", h=NH, c=2)
        C32m = [Buf(C32mt[:, h, :, :]) for h in range(NH)]
        wb = Buf(Z2[:, 4600:4600 + D])
        silt = [Buf(Z2[:, 6400 + i * 512:6400 + (i + 1) * 512]) for i in range(2)]
        W2 = [Buf(Z3b[:, i * 8192:(i + 1) * 8192].rearrange("p (k n) -> p k n", k=KC)) for i in range(2)]

        def fence(bufs):
            P.op("dve", lambda e: e.memset(fsc.ap, 0.0), writes=[fsc] + list(bufs))

        PJ = [Buf(ps("pj%d" % i, [128, 512], F32)[:, :], excl=True) for i in range(2)]
        PT = [Buf(ps("pt%d" % i, [128, 1024], BF16)[:, :], excl=True) for i in range(2)]
        PS = [Buf(ps("ps%d" % i, [128, 512], F32)[:, :], excl=True) for i in range(2)]
        PC = [Buf(ps("pc%d" % i, [128, 512], F32)[:, :], excl=True) for i in range(2)]

        def act(out_b, in_b, func, bias=0.0, scale=1.0, accum=None, extra_r=(), o=None, i=None):
            oa = out_b.ap if o is None else o
            ia = in_b.ap if i is None else i
            kw = {}
            if accum is not None:
                kw["accum_out"] = accum[1]
            w = [out_b] + ([accum[0]] if accum is not None else [])
            return P.op("act", lambda e: e.activation(out=oa, in_=ia, func=func, bias=bias, scale=scale, **kw),
                        reads=[in_b] + list(extra_r), writes=w)

        def tt(eng, out_b, a_b, b_b, op, o=None, a=None, b=None):
            oa = out_b.ap if o is None else o
            aa = a_b.ap if a is None else a
            ba = b_b.ap if b is None else b
            return P.op(eng, lambda e: e.tensor_tensor(out=oa, in0=aa, in1=ba, op=op), reads=[a_b, b_b], writes=[out_b])

        def ts(eng, out_b, a_b, s1, s2, op0, op1=None, o=None, a=None, extra_r=()):
            oa = out_b.ap if o is None else o
            aa = a_b.ap if a is None else a
            if op1 is None:
                f = lambda e: e.tensor_scalar(out=oa, in0=aa, scalar1=s1, scalar2=None, op0=op0)
            else:
                f = lambda e: e.tensor_scalar(out=oa, in0=aa, scalar1=s1, scalar2=s2, op0=op0, op1=op1)
            return P.op(eng, f, reads=[a_b] + list(extra_r), writes=[out_b])

        def stt(out_b, a_b, sc, b_b, op0, op1, o=None, a=None, b=None, extra_r=()):
            oa = out_b.ap if o is None else o
            aa = a_b.ap if a is None else a
            ba = b_b.ap if b is None else b
            return P.op("dve", lambda e: e.scalar_tensor_tensor(out=oa, in0=aa, scalar=sc, in1=ba, op0=op0, op1=op1),
                        reads=[a_b, b_b] + list(extra_r), writes=[out_b])

        def red(out_b, in_b, o, i):
            return P.op("dve", lambda e: e.tensor_reduce(out=o, in_=i, axis=AX.X, op=ALU.max), reads=[in_b], writes=[out_b])

        def cp(eng, out_b, in_b, o=None, i=None):
            oa = out_b.ap if o is None else o
            ia = in_b.ap if i is None else i
            if eng == "act":
                return P.op("act", lambda e: e.copy(out=oa, in_=ia), reads=[in_b], writes=[out_b])
            return P.op(eng, lambda e: e.tensor_copy(out=oa, in_=ia), reads=[in_b], writes=[out_b])

        def mm(out_b, o, l_b, l, r_b, r, start, stop, signal=None):
            if signal is None:
                signal = stop
            return P.op("pe", lambda e: e.matmul(out=o, lhsT=l, rhs=r, start=start, stop=stop),
                        reads=[l_b, r_b], writes=[out_b], signal=signal)

        def tr(out_b, o, in_b, i, id_b, idap, signal):
            return P.op("pe", lambda e: e.transpose(out=o, in_=i, identity=idap), reads=[in_b, id_b], writes=[out_b], signal=signal)

        def memset(eng, b, val, ap=None):
            a = b.ap if ap is None else ap
            return P.op(eng, lambda e: e.memset(a, val), writes=[b])

        def asel(out_b, in_b, pattern, cmp, base, cm, o=None, i=None):
            oa = out_b.ap if o is None else o
            ia = in_b.ap if i is None else i
            return P.op("pool", lambda e: e.affine_select(out=oa, in_=ia, pattern=pattern, compare_op=cmp, fill=0.0,
                                                          base=base, channel_multiplier=cm), reads=[in_b], writes=[out_b])

        memset("pool", ones32, 1.0)
        asel(ident32, ones32, [[1, 128]], ALU.is_equal, 0, -1)
        asel(tri32, ones32, [[1, 128]], ALU.is_ge, 0, -1)
        asel(tril32, ones32, [[-1, 128]], ALU.is_ge, 0, 1)
        cp("pool", identb, ident32)
        ts("pool", maskneg, tril32, BIG, -BIG, ALU.mult, ALU.add)
        ts("pool", maskposb, tri32, -BIG, BIG, ALU.mult, ALU.add)
        asel(selA, ones32, [[1, 128]], ALU.is_ge, 0, -8, i=ones32.ap[0:16, :])
        asel(sel32, selA, [[-1, 128]], ALU.is_ge, 7, 8)
        asel(sellastT, ones32, [[1, 128]], ALU.is_equal, -7, -8, i=ones32.ap[0:16, :])
        asel(blockA, ones32, [[-8, 16]], ALU.is_ge, 0, 1, i=ones32.ap[:, 0:16])
        asel(blockmask, blockA, [[8, 16]], ALU.is_ge, 7, -1)
        cp("pool", blockmaskb, blockmask)
        asel(sellast, ones32, [[-8, 16]], ALU.is_equal, -7, 1, i=ones32.ap[:, 0:16])
        memset("pool", selrepA, 1.0)
        asel(selrepA, selrepA, [[-8, 16], [1, 128]], ALU.is_ge, 0, 0)
        asel(selrepA, selrepA, [[8, 16], [-1, 128]], ALU.is_ge, 7, 0)
        SELREP = selrepA
        mm(PS[0], PS[0].ap[:, 0:128], sel32, sel32.ap, sel32, sel32.ap, True, True)
        tt("dve", triB32, tri32, PS[0], ALU.mult, b=PS[0].ap[:, 0:128])
        tt("dve", trilB32, tril32, PS[0], ALU.mult, b=PS[0].ap[:, 0:128])
        ts("dve", masknegB, trilB32, BIG, -BIG, ALU.mult, ALU.add)
        ts("dve", maskposbB, triB32, -BIG, BIG, ALU.mult, ALU.add)
        mm(PS[1], PS[1].ap[:, 0:128], sellastT, sellastT.ap, sel32, sel32.ap, True, True)
        cp("dve", SLm, PS[1], i=PS[1].ap[:, 0:128])

        P.dma("sp", bias8.ap[:, 0:4], b_ig.partition_broadcast(128), writes=[bias8])
        P.dma("sp", bias8.ap[:, 4:8], b_fg.partition_broadcast(128), writes=[bias8])
        P.dma("sp", gcol.ap, mnorm_w.rearrange("(c p) -> p c", p=128), writes=[gcol], slow=True)
        P.dma("sp", pscol.ap, pool_scale.rearrange("(c p) -> p c", p=128), writes=[pscol], slow=True)
        P.dma("sp", flagc.ap, flag_d, writes=[flagc])
        P.dma("sp", wb.ap, norm_mix_w.partition_broadcast(128), writes=[wb])
        P.dma("pool", wg8.ap, w_in_v[:, :, 5120:5128], writes=[wg8])
        P.dma("pool", wpl.ap, w_pool.rearrange("g (c p) e -> p g c e", p=128), writes=[wpl])
        for h in range(NH):
            memset("dve", C32[h], 0.0)
            memset("dve", Cbf[h], 0.0)
        memset("dve", negBc, 0.0)
        memset("dve", Mc, 0.0)

        P.checkpoint(1)
        rr = [0]

        def norm_tile(src_ap, L, dstT, tok0, x_in=None, wbuf=None, wap=None, xnbs=None, sqb=None):
            i = rr[0] % 2
            rr[0] += 1
            if x_in is None:
                xb = xt[i]
                P.dma("sp", xb.ap[:L, :], src_ap, writes=[xb])
            else:
                xb = x_in
            ssb = SS[i]
            xn_ = xnb[i] if xnbs is None else xnbs[i]
            sq_ = sq if sqb is None else sqb
            act(sq_, xb, AF.Square, accum=(ssb, ssb.ap[:L, 0:1]), o=sq_.ap[:L, :], i=xb.ap[:L, :])
            act(ssb, ssb, AF.Ln, bias=EPS, scale=1.0 / D, o=ssb.ap[:L, 1:2], i=ssb.ap[:L, 0:1])
            act(ssb, ssb, AF.Exp, scale=-0.5, o=ssb.ap[:L, 2:3], i=ssb.ap[:L, 1:2])
            if wbuf is None:
                wbuf, wap = wb, wb.ap
            stt(xn_, xb, ssb.ap[:L, 2:3], wbuf, ALU.mult, ALU.mult, o=xn_.ap[:L, :], a=xb.ap[:L, :], b=wap[:L, :],
                extra_r=[ssb])
            for half in range(2):
                pt = PT[i]
                for j in range(8):
                    kc = half * 8 + j
                    tr(pt, pt.ap[:, j * 128:j * 128 + L], xn_, xn_.ap[:L, kc * 128:(kc + 1) * 128],
                       identb, identb.ap[:L, :L], signal=(j == 7))
                src = pt.ap.rearrange("p (j t) -> p j t", j=8)[:, :, 0:L]
                cp("act" if half == 0 else "dve", dstT, pt, o=dstT.ap[:, half * 8:half * 8 + 8, tok0:tok0 + L], i=src)
            return xb

        def run_window(makers, width=2):
            pending = list(makers)
            active = []
            while pending or active:
                while pending and len(active) < width:
                    active.append(pending.pop(0)())
                for g_ in list(active):
                    try:
                        next(g_)
                    except StopIteration:
                        active.remove(g_)

        xnbAB = [xnb1, Buf(Z3b[:, 8192:8192 + D])]
        sqAB = Buf(Z3b[:, 8192 + D:8192 + 2 * D])

        pjr = [0]

        def proj(xT, tok0, L, Wb, Wap, ncols):
            pj = PJ[pjr[0] % 2]
            pjr[0] += 1
            for kc in range(KC):
                mm(pj, pj.ap[:L, 0:ncols], xT, xT.ap[:, kc, tok0:tok0 + L], Wb, Wap[:, kc, 0:ncols], kc == 0, kc == KC - 1)
            return pj

        K_IG, K_Z, K_E, K_SP, K_NEGB, K_A, K_M, K_NEGM, K_EM, K_MPREV, K_MEND, K_W, K_DEC, K_T1, K_CMX, K_T2, K_AADJ, K_CME, K_INTER, K_EM2 = range(20)

        def gate_prep(xT, tok0, L, g, mode):
            G = GW[g]

            def k(kind, rows=L):
                return G.ap[:rows, kind, :]
            pj = PC[1]
            for kc in range(KC):
                mm(pj, pj.ap[:L, 0:8], xT, xT.ap[:, kc, tok0:tok0 + L], wg8, wg8.ap[:, kc, 0:8], kc == 0, kc == KC - 1)
            tt("dve", G, pj, bias8, ALU.add, o=G.ap[:L, 0:2, :].rearrange("p a h -> p (a h)"), a=pj.ap[:L, 0:8], b=bias8.ap[:L, :])
            act(G, G, AF.Exp, scale=-1.0, o=k(K_E), i=k(K_Z))
            act(G, G, AF.Ln, bias=1.0, o=k(K_SP), i=k(K_E))
            TRI = tri32 if mode == "p" else triB32
            MN = maskneg if mode == "p" else masknegB
            p0 = PS[0]
            mm(p0, p0.ap[:L, 0:4], TRI, TRI.ap[:L, :L], G, k(K_SP), True, True)
            if mode == "p":
                mm(p0, p0.ap[:, 4:8], ones32, ones32.ap[:L, :], G, k(K_SP), True, True)
                tt("dve", G, p0, negBc, ALU.add, o=k(K_NEGB), a=p0.ap[:L, 0:4], b=negBc.ap[:L, :])
            else:
                cp("dve", G, p0, o=k(K_NEGB), i=p0.ap[:L, 0:4])
                mm(p0, p0.ap[:, 8:12], sel32, sel32.ap, msm, msm.ap[0:16, :], True, True)
                cp("dve", G, p0, o=k(K_MPREV, 128), i=p0.ap[:, 8:12])
            tt("dve", G, G, G, ALU.add, o=k(K_A), a=k(K_IG), b=k(K_NEGB))
            ts("dve", G, G, -LN16, None, ALU.add, o=k(K_AADJ), a=k(K_A))
            tt("dve", dg, ident32, G, ALU.mult, o=dg.ap[:L, :, :L],
               a=ident32.ap[:L, :L].unsqueeze(1).to_broadcast([L, 4, L]),
               b=k(K_A).unsqueeze(2).to_broadcast([L, 4, L]))
            p1 = PS[1]
            for h in range(NH):
                mm(p1, p1.ap[:, h * 128:h * 128 + L], ones32, ones32.ap[:L, :], dg, dg.ap[:L, h, :L], True, True, signal=(h == NH - 1))
            Ab = p1.ap.rearrange("p (h s) -> p h s", h=4)
            tt("dve", tmpm, p1, MN, ALU.add, o=tmpm.ap[:L, :, :L], a=Ab[:L, :, :L],
               b=MN.ap[:L, :L].unsqueeze(1).to_broadcast([L, 4, L]))
            red(G, tmpm, k(K_CMX), tmpm.ap[:L, :, :L])
            if mode == "p":
                cp("dve", G, Mc, o=k(K_MPREV, 128), i=Mc.ap)
                tt("dve", G, G, Mc, ALU.max, o=k(K_M), a=k(K_CMX), b=Mc.ap[:L, :])
                red(G, p1, k(K_CME, 128), Ab[:, :, :L])
                tt("dve", G, G, Mc, ALU.max, o=k(K_MEND, 128), a=k(K_CME, 128), b=Mc.ap)
            else:
                tt("dve", G, G, G, ALU.max, o=k(K_M), a=k(K_CMX), b=k(K_MPREV))
                mm(p0, p0.ap[:, 12:16], SLm, SLm.ap, G, k(K_M), True, True)
                cp("dve", G, p0, o=k(K_MEND, 128), i=p0.ap[:, 12:16])
            tt("dve", G, G, G, ALU.subtract, o=k(K_NEGM), a=k(K_NEGB), b=k(K_M))
            tt("dve", G, G, G, ALU.subtract, o=k(K_INTER), a=k(K_MPREV), b=k(K_M))
            act(G, G, AF.Exp, o=k(K_INTER), i=k(K_INTER))
            act(G, G, AF.Exp, o=k(K_EM), i=k(K_NEGM))
            act(G, G, AF.Exp, scale=2.0, o=k(K_EM2), i=k(K_NEGM))
            tt("dve", G, G, G, ALU.subtract, o=k(K_T1), a=k(K_AADJ), b=k(K_MEND))
            act(G, G, AF.Exp, o=k(K_W), i=k(K_T1))
            tt("dve", G, G, G, ALU.subtract, o=k(K_T2, 128), a=k(K_MPREV, 128), b=k(K_MEND, 128))
            act(G, G, AF.Exp, o=k(K_DEC, 128), i=k(K_T2, 128))
            if mode == "p":
                cp("dve", Mc, G, i=k(K_MEND, 128))
                tt("dve", negBc, negBc, p0, ALU.add, b=p0.ap[:, 4:8])
            else:
                tt("dve", dsel, sellast, G, ALU.mult, a=sellast.ap.unsqueeze(2).to_broadcast([128, 16, 4]),
                   b=k(K_T2, 128).unsqueeze(1).to_broadcast([128, 16, 4]))
                mm(p1, p1.ap[:, 0:64], ones32, ones32.ap, dsel, dsel.ap.rearrange("p b h -> p (b h)"), True, True)
                act(decs, p1, AF.Exp, o=decs.ap.rearrange("p b h -> p (b h)"), i=p1.ap[:, 0:64])
                ts("dve", mpos, G, -1.0, None, ALU.mult, a=k(K_NEGM))
                mm(p0, p0.ap[0:16, 16:20], sellast, sellast.ap, mpos, mpos.ap, True, True)
                cp("dve", msout, p0, o=msout.ap[0:16, :], i=p0.ap[0:16, 16:20])
                P.dma("sp", m_s, msout.ap[0:16, :], reads=[msout])

        cr = [0]

        def state_update(h, g, L, ktm_ap, vext_ap, k_b, v_b):
            G = GW[g]
            i = cr[0] % 2
            kw = kwb[i]
            ts("dve", kw, k_b, G.ap[:L, K_W, h:h + 1], None, ALU.mult, o=kw.ap[:L, :], a=ktm_ap, extra_r=[G])
            for dc in range(2):
                mm(PC[dc], PC[dc].ap[:, 0:HDE], kw, kw.ap[:L, dc * 128:(dc + 1) * 128], v_b, vext_ap, True, True)
            for dc in range(2):
                stt(C32[h], C32[h], G.ap[:, K_DEC, h:h + 1], PC[dc], ALU.mult, ALU.add,
                    o=C32[h].ap[:, dc, :], a=C32[h].ap[:, dc, :], b=PC[dc].ap[:, 0:HDE], extra_r=[G])
            cp("act", Cbf[h], C32[h])

        xnTp = Buf(Z1b[:, 0:KC * 1040].rearrange("p (k t) -> p k t", k=KC))
        kvp = Buf(Z2b[:, 0:9 * 516].rearrange("p (i f) -> p i f", i=9))
        pre_tiles = [(x_meta, 16, 1024)] + [(x_pre[i * 128:(i + 1) * 128, :], 128, i * 128) for i in range(8)]
        fence([W2[1], xnbAB[1], sqAB])
        run_window([(lambda a=src, b=L, c=tok0: P.defer(lambda: norm_tile(a, b, xnTp, c, xnbs=xnbAB, sqb=sqAB))) for (src, L, tok0) in pre_tiles])
        fence([W2[1], xnbAB[1], sqAB])
        P.checkpoint(2)
        def gatesA_plain():
            for gi, (src, L, tok0) in enumerate(pre_tiles):
                gate_prep(xnTp, tok0, L, gi, "p")
                if gi == 0:
                    cp("dve", negBm, negBc)
                    cp("dve", Mm, Mc)

        def gatesA():
            return P.defer(gatesA_plain)

        kvp_t = [[Buf(kvp.ap[:, gi, :]) for gi in range(9)] for _ in range(1)][0]

        def loadA(h):
            W = W2[h % 2]
            P.dma("pool", W.ap[:, :, 0:256], w_in_v[:, :, 2048 + h * 256:2048 + (h + 1) * 256], writes=[W])
            P.dma("pool", W.ap[:, :, 256:512], w_in_v[:, :, 3072 + h * 256:3072 + (h + 1) * 256], writes=[W])

        def projA(h, dst):
            W = W2[h % 2]
            for gi, (src, L, tok0) in enumerate(pre_tiles):
                pj = PJ[gi % 2]
                for kc in range(KC):
                    mm(pj, pj.ap[:L, 0:512], xnTp, xnTp.ap[:, kc, tok0:tok0 + L], W, W.ap[:, kc, :], kc == 0, kc == KC - 1)
                    if kc % 4 == 3:
                        yield
                cp("act", dst[gi], pj, o=dst[gi].ap[:L, 0:512], i=pj.ap[:L, 0:512])
                yield

        def updA(h, srcb):
            for gi, (src, L, tok0) in enumerate(pre_tiles):
                G = GW[gi]
                kw = kwb[gi % 2]
                kb = srcb[gi]
                ts("dve", kw, kb, G.ap[:L, K_W, h:h + 1], None, ALU.mult, o=kw.ap[:L, :], a=kb.ap[:L, 0:256], extra_r=[G])
                yield
                for dc in range(2):
                    mm(PC[dc], PC[dc].ap[:, 0:HDE], kw, kw.ap[:L, dc * 128:(dc + 1) * 128], kb, kb.ap[:L, 256:513], True, True)
                yield
                for dc in range(2):
                    stt(C32[h], C32[h], G.ap[:, K_DEC, h:h + 1], PC[dc], ALU.mult, ALU.add,
                        o=C32[h].ap[:, dc, :], a=C32[h].ap[:, dc, :], b=PC[dc].ap[:, 0:HDE], extra_r=[G])
                yield
                if gi == 0:
                    cp("dve", C32m[h], C32[h])

        kvp2 = Buf(Z2b[:, 13312:13312 + 9 * 516].rearrange("p (i f) -> p i f", i=9))
        memset("dve", kvp, 1.0)
        memset("dve", kvp2, 1.0)
        kvA = [[Buf(kvp.ap[:, gi, :]) for gi in range(9)], [Buf(kvp2.ap[:, gi, :]) for gi in range(9)]]
        for lst, par_b in ((kvA[0], kvp), (kvA[1], kvp2)):
            for b_ in lst:
                b_.w = par_b.w
        def chain(*gs):
            for g_ in gs:
                yield from g_

        def run(gens):
            gens = list(gens)
            while gens:
                for g_ in list(gens):
                    try:
                        next(g_)
                    except StopIteration:
                        gens.remove(g_)

        loadA(0)
        loadA(1)
        run([gatesA(), chain(projA(0, kvA[0]), projA(1, kvA[1]))])
        loadA(2)
        loadA(3)
        run([updA(0, kvA[0])])
        run([updA(1, kvA[1]), projA(2, kvA[0])])
        run([updA(2, kvA[0]), projA(3, kvA[1])])
        run([updA(3, kvA[1])])
        for h in range(NH):
            tt("dve", C32[h], C32[h], C32m[h], ALU.subtract)
            stt(C32[h], C32[h], flagc.ap[:, 0:1], C32m[h], ALU.mult, ALU.add, extra_r=[flagc])
            cp("act", Cbf[h], C32[h])
        for (cur, sav) in ((negBc, negBm), (Mc, Mm)):
            tt("dve", cur, cur, sav, ALU.subtract)
            stt(cur, cur, flagc.ap[:, 0:1], sav, ALU.mult, ALU.add, extra_r=[flagc])

        P.checkpoint(4)
        xnT = Buf(Z1b[:, 0:KC * 1168].rearrange("p (k t) -> p k t", k=KC))
        fence([xnTp, xnT])
        yT = Buf(Z2b[:, 0:KC * 1152].rearrange("p (k t) -> p k t", k=KC))
        main_tiles = [(x_own[i * 128:(i + 1) * 128, :], 128, i * 128) for i in range(8)] + [(x_smp, 128, 1024)]
        fence([W2[1], xnbAB[1], sqAB])
        run_window([(lambda a=src, b=L, c=tok0: P.defer(lambda: norm_tile(a, b, xnT, c, xnbs=xnbAB, sqb=sqAB))) for (src, L, tok0) in main_tiles + [(x_halo, 16, 1152)]])
        fence([W2[1], xnbAB[1], sqAB])
        P.dma("sp", msm.ap[0:16, :], st_m, writes=[msm])
        P.dma("sp", nin.ap[0:64, :], st_n, writes=[nin])
        for dc in range(2):
            tr(PS[0], PS[0].ap[:, dc * 64:(dc + 1) * 64], nin, nin.ap[0:64, dc * 128:(dc + 1) * 128], ident32, ident32.ap[0:64, 0:64], signal=(dc == 1))
        cp("dve", nT, PS[0], o=nT.ap.rearrange("p b h c -> p c (b h)"), i=PS[0].ap[:, 0:128].rearrange("p (c x) -> p c x", c=2))
        def gates_main():
            for gi, (src, L, tok0) in enumerate(main_tiles):
                gate_prep(xnT, tok0, L, 9 + gi, "p" if gi < 8 else "s")
            tt("dve", mvec, Mc, negBc, ALU.subtract)
            P.dma("sp", m_p, mvec.ap[0:1, :], reads=[mvec])

        P.checkpoint(5)
        WU = W2
        fence([kvp, kvp2, wb, yT] + C32m + xt + SET_PL + kvA[0] + kvA[1])
        for half in range(2):
            P.dma("pool", WU[half].ap, w_in_v[:, :, half * 512:(half + 1) * 512], writes=[WU[half]])
        sp_rows = st_pool.rearrange("b j c -> (b j) c")
        for blk, (r0, nr) in enumerate(((0, 128), (128, 112))):
            P.dma("sp", sptm.ap[0:nr, :], sp_rows[r0:r0 + nr, :], writes=[sptm])
            for cc in range(8):
                p = PS[cc % 2]
                tr(p, p.ap[:, 0:nr], sptm, sptm.ap[0:nr, cc * 128:(cc + 1) * 128], ident32, ident32.ap[0:nr, 0:nr], signal=True)
                eng = "dve" if cc % 2 else "act"
                if blk == 0:
                    cp(eng, uTs, p, o=uTs.ap[:, cc, 0:8, 0:15], i=p.ap[:, 0:120].rearrange("p (b j) -> p b j", b=8))
                    cp(eng, uTs, p, o=uTs.ap[:, cc, 8, 0:8], i=p.ap[:, 120:128])
                else:
                    cp(eng, uTs, p, o=uTs.ap[:, cc, 8, 8:15], i=p.ap[:, 0:7])
                    cp(eng, uTs, p, o=uTs.ap[:, cc, 9:16, 0:15], i=p.ap[:, 7:112].rearrange("p (b j) -> p b j", b=7))

        def pool_group(U, shp, g, o_ap):
            nd = len(shp)
            T = shp[-1]

            def v(b, t0, t1):
                if nd == 1:
                    return b[:, :, t0:t1]
                return b[:, :, :, t0:t1]
            if nd == 1:
                sv = [sw[i].ap[:, 0:2 * T].rearrange("p (c t) -> p c t", c=2) for i in range(2)]
            else:
                sv = [sw[i].ap[:, 0:2 * shp[0] * T].rearrange("p (c b t) -> p c b t", c=2, b=shp[0]) for i in range(2)]
            cur_b, cur = U, (U.ap[:, 2 * g:2 * g + 2, :] if nd == 1 else U.ap[:, 2 * g:2 * g + 2, :, :])
            for k in range(g + 1):
                sh = 1 << k
                lo = 2 * sh - 1
                nb, nv = sw[k % 2], sv[k % 2]
                tt("dve", nb, cur_b, cur_b, ALU.add, o=v(nv, lo, T), a=v(cur, lo, T), b=v(cur, lo - sh, T - sh))
                cur_b, cur = nb, nv
            wdw = 2 << g
            uv = U.ap[:, 2 * g:2 * g + 2, :] if nd == 1 else U.ap[:, 2 * g:2 * g + 2, :, :]
            stt(dTb, cur_b, 1.0 / wdw, U, ALU.mult, ALU.subtract, o=o_ap, a=v(cur, 15, T), b=v(uv, 15, T))

        def pool_tile(tok0, L, ytok0, hist_mode):
            for half in range(2):
                pj = proj(xnT, tok0, L, WU[half], WU[half].ap, 512)
                cp("act" if half == 0 else "dve", utm, pj, o=utm.ap[:L, half * 512:(half + 1) * 512], i=pj.ap[:L, :])
            if hist_mode == "smp":
                for b in range(16):
                    P.dma("sp", pool_s[b, 7:15, :], utm.ap[b * 8:(b + 1) * 8, :], reads=[utm])
                    P.dma("sp", pool_s[b, 0:7, :], st_pool[b, 8:15, :])
            if hist_mode == "own" and tok0 == 7 * 128:
                P.dma("sp", pool_p, utm.ap[113:128, :], reads=[utm])
            for cc in range(8):
                p = PT[cc % 2]
                pa = p.ap.bitcast(F32)
                tr(p, pa[:, 0:L], utm, utm.ap[:L, cc * 128:(cc + 1) * 128], ident32, ident32.ap[:L, :L], signal=True)
                if hist_mode == "halo":
                    cp("dve" if cc % 2 else "act", uT, p, o=uT.ap[:, cc, 0:15], i=pa[:, 1:16])
                elif hist_mode == "own":
                    cp("dve" if cc % 2 else "act", uT, p, o=uT.ap[:, cc, 15:143], i=pa[:, 0:128])
                else:
                    cp("dve" if cc % 2 else "act", uTs, p, o=uTs.ap[:, cc, :, 15:23], i=pa[:, 0:128].rearrange("p (b t) -> p b t", b=16))
            if hist_mode == "halo":
                return
            if hist_mode == "own":
                U, shp, hcol = uT, (143,), 15
            else:
                U, shp, hcol = uTs, (16, 23), 15
            for g in range(4):
                if hist_mode == "own":
                    o_ap = dTb.ap[:, 2 * g:2 * g + 2, :]
                else:
                    o_ap = dTb.ap[:, 2 * g:2 * g + 2, :].rearrange("p c (b t) -> p c b t", b=16)
                pool_group(U, shp, g, o_ap)
            for g in range(4):
                for ec in range(2):
                    p = PC[0]
                    for c in range(2):
                        mm(p, p.ap[:, ec * 128:(ec + 1) * 128], wpl, wpl.ap[:, g, c, ec * 128:(ec + 1) * 128], dTb, dTb.ap[:, 2 * g + c, :], c == 0, c == 1)
                    ch = 2 * g + ec
                    ts("dve", yT, p, pscol.ap[:, ch:ch + 1], None, ALU.mult, o=yT.ap[:, ch, ytok0:ytok0 + 128], a=p.ap[:, ec * 128:(ec + 1) * 128], extra_r=[pscol])
            if hist_mode == "own":
                cp("dve", sw[0], uT, o=sw[0].ap[:, 0:120].rearrange("p (c t) -> p c t", c=8), i=uT.ap[:, :, 128:143])
                cp("dve", uT, sw[0], o=uT.ap[:, :, 0:15], i=sw[0].ap[:, 0:120].rearrange("p (c t) -> p c t", c=8))

        def pool_all():
            pool_tile(1152, 16, 0, "halo")
            for i in range(8):
                pool_tile(i * 128, 128, i * 128, "own")
            pool_tile(1024, 128, 1024, "smp")
        run([P.defer(gates_main), P.defer(pool_all)])

        P.checkpoint(6)
        q_t = [Buf(q_tm.ap[:, i, :]) for i in range(9)]
        k_t = [Buf(k_tm.ap[:, i, :]) for i in range(9)]
        s_t = [Buf(sigo.ap[:, i, :]) for i in range(9)]
        v_t = [Buf(v_ext.ap[:, i, :]) for i in range(9)]
        Z4BUFS.extend(q_t + k_t + s_t + v_t)

        def front(h, ti, g, mode):
            G = GW[g]
            par = ti % 2
            MP = maskposb if mode == "p" else maskposbB
            pt = PT[par]
            for dc in range(2):
                tr(pt, pt.ap[:, dc * 128:(dc + 1) * 128], k_t[ti], k_t[ti].ap[:, dc * 128:(dc + 1) * 128], identb, identb.ap, signal=False)
            for dc in range(2):
                tr(pt, pt.ap[:, (2 + dc) * 128:(3 + dc) * 128], q_t[ti], q_t[ti].ap[:, dc * 128:(dc + 1) * 128], identb, identb.ap, signal=(dc == 1))
            yield
            cp("act", qkT[par], pt, o=qkT[par].ap.rearrange("p a t -> p (a t)"), i=pt.ap[:, 0:512])
            ts("dve", dgM[par], ident32, G.ap[:, K_M, h:h + 1], None, ALU.mult, extra_r=[G])
            yield
            p = PS[par]
            mm(p, p.ap[:, 0:128], ones32, ones32.ap, dgM[par], dgM[par].ap, True, True)
            mm(p, p.ap[:, 128:256], ones32, ones32.ap, dgM[par], dgM[par].ap, True, False, signal=False)
            mm(p, p.ap[:, 128:256], identb, identb.ap, MP, MP.ap, False, True)
            yield
            act(DT[par], p, AF.Exp, bias=G.ap[:, K_AADJ, h:h + 1], scale=-1.0, i=p.ap[:, 128:256], extra_r=[G])
            yield
            if mode == "p":
                act(interb[par], p, AF.Exp, bias=G.ap[:, K_MPREV, h:h + 1], scale=-1.0, i=p.ap[:, 0:128], extra_r=[G])
            else:
                ts("dve", dgM[par], ident32, G.ap[:, K_INTER, h:h + 1], None, ALU.mult, extra_r=[G])
                mm(p, p.ap[:, 384:512], ones32, ones32.ap, dgM[par], dgM[par].ap, True, True)
                cp("act", interb[par], p, i=p.ap[:, 384:512])
            yield
            for dc in range(2):
                mm(p, p.ap[:, 256:384], qkT[par], qkT[par].ap[:, dc, :], qkT[par], qkT[par].ap[:, 2 + dc, :], dc == 0, dc == 1)
            yield
            tt("dve", STb[par], p, DT[par], ALU.mult, a=p.ap[:, 256:384])
            yield
            ts("dve", kwb[par], k_t[ti], G.ap[:, K_W, h:h + 1], None, ALU.mult, extra_r=[G])
            yield
            if mode == "p":
                tt("dve", qtl[par], qkT[par], interb[par], ALU.mult, a=qkT[par].ap[:, 2:4, :],
                   b=interb[par].ap.unsqueeze(1).to_broadcast([128, 2, 128]))
            else:
                tt("dve", qt32, qkT[par], interb[par], ALU.mult, a=qkT[par].ap[:, 2:4, :],
                   b=interb[par].ap.unsqueeze(1).to_broadcast([128, 2, 128]))
                yield
                tt("dve", Vmask, v_t[ti], blockmask, ALU.mult, o=Vmask.ap[:, :, 0:HDE],
                   a=v_t[ti].ap[:, 0:HDE].unsqueeze(1).to_broadcast([128, 16, HDE]),
                   b=blockmask.ap.unsqueeze(2).to_broadcast([128, 16, HDE]))
            yield

        def back(h, ti, g, mode):
            G = GW[g]
            par = ti % 2
            pn = PN[par]
            vb = v_t[ti]
            if mode == "p":
                mm(pn, pn.ap[:, 0:HDE], STb[par], STb[par].ap, vb, vb.ap[:, 0:HDE], True, False, signal=False)
                for dc in range(2):
                    mm(pn, pn.ap[:, 0:HDE], qtl[par], qtl[par].ap[:, dc, :], Cbf[h], Cbf[h].ap[:, dc, :], False, dc == 1)
                yield
                for dc in range(2):
                    mm(PC[dc], PC[dc].ap[:, 0:HDE], kwb[par], kwb[par].ap[:, dc * 128:(dc + 1) * 128], vb, vb.ap[:, 0:HDE], True, True)
                yield
                for dc in range(2):
                    stt(C32[h], C32[h], G.ap[:, K_DEC, h:h + 1], PC[dc], ALU.mult, ALU.add,
                        o=C32[h].ap[:, dc, :], a=C32[h].ap[:, dc, :], b=PC[dc].ap[:, 0:HDE], extra_r=[G])
                    yield
                cp("act", Cbf[h], C32[h])
                yield
            else:
                mm(pn, pn.ap[:, 0:HDE], STb[par], STb[par].ap, vb, vb.ap[:, 0:HDE], True, False, signal=False)
                def load_grp(gq):
                    c_ = C32s[gq % 3]
                    for bl_ in range(2):
                        P.dma("sp", c_.ap[:, bl_, :, 0:HD], st_C[gq * 2 + bl_, h].rearrange("(c p) v -> p c v", p=128), writes=[c_])
                load_grp(0)
                load_grp(1)

                def prep_grp(gq):
                    c_ = C32s[gq % 3]
                    cp("dve", c_, nT, o=c_.ap[:, :, :, HD], i=nT.ap[:, gq * 2:(gq + 1) * 2, h, :])
                    cp("act", C16, c_)
                    qb_ = qm[gq % 2]
                    for dc_ in range(2):
                        tt("dve", qb_, qt32, SELREP, ALU.mult, o=qb_.ap[:, dc_ * 2:dc_ * 2 + 2, :],
                           a=qt32.ap[:, dc_, :].unsqueeze(1).to_broadcast([128, 2, 128]), b=SELREP.ap[:, gq * 2:(gq + 1) * 2, :])

                def inter_grp(gq):
                    c_ = C32s[gq % 3]
                    qb_ = qm[gq % 2]
                    for bl_ in range(2):
                        for dc_ in range(2):
                            last = (gq == 7 and bl_ == 1 and dc_ == 1)
                            mm(pn, pn.ap[:, 0:HDE], qb_, qb_.ap[:, dc_ * 2 + bl_, :], C16, C16.ap[:, bl_, dc_, :], False, last,
                               signal=(last or (bl_ == 1 and dc_ == 1)))
                pq = PS[1]
                for dc_ in range(2):
                    mm(pq, pq.ap[:, dc_ * 16:(dc_ + 1) * 16], kwb[par], kwb[par].ap[:, dc_ * 128:(dc_ + 1) * 128], blockmaskb, blockmaskb.ap, True, True)
                for dc_ in range(2):
                    tt("dve", nS, nT, decs, ALU.mult, o=nS.ap[:, :, h, dc_], a=nT.ap[:, :, h, dc_], b=decs.ap[:, :, h])
                    tt("dve", nS, nS, pq, ALU.add, o=nS.ap[:, :, h, dc_], a=nS.ap[:, :, h, dc_], b=pq.ap[:, dc_ * 16:(dc_ + 1) * 16])
                yield
                prep_grp(0)
                yield
                inter_grp(0)
                yield
                for grp in range(8):
                    cs_ = C32s[grp % 3]
                    if grp + 2 < 8:
                        load_grp(grp + 2)
                    if grp + 1 < 8:
                        prep_grp(grp + 1)
                        yield
                        inter_grp(grp + 1)
                        yield
                    for bl in range(2):
                        b = grp * 2 + bl
                        pc = PC[bl]
                        for dc in range(2):
                            mm(pc, pc.ap[:, dc * HD:(dc + 1) * HD], kwb[par], kwb[par].ap[:, dc * 128:(dc + 1) * 128], Vmask, Vmask.ap[:, b, 0:HD], True, True)
                        stt(cs_, cs_, decs.ap[:, b, h:h + 1], pc, ALU.mult, ALU.add, o=cs_.ap[:, bl, :, 0:HD], a=cs_.ap[:, bl, :, 0:HD],
                            b=pc.ap[:, 0:2 * HD].rearrange("p (c v) -> p c v", c=2), extra_r=[decs])
                        yield
                    for bl in range(2):
                        b = grp * 2 + bl
                        P.dma("sp", C_s[b, h].rearrange("(c p) v -> p c v", p=128), cs_.ap[:, bl, :, 0:HD], reads=[cs_])
                    yield
            cl = COLS[par]
            act(sqs, pn, AF.Square, accum=(cl, cl.ap[:, 0:1]), o=sqs.ap[:, 0:HD], i=pn.ap[:, 0:HD])
            P.op("dve", lambda e, o=cl.ap[:, 1:2], i=pn.ap[:, HD:HDE], s2=G.ap[:, K_EM2, h:h + 1]:
                 e.tensor_scalar(out=o, in0=i, scalar1=i, scalar2=s2, op0=ALU.mult, op1=ALU.max), reads=[pn, G], writes=[cl])
            yield
            stt(cl, cl, EPS * HD, cl, ALU.mult, ALU.add, o=cl.ap[:, 2:3], a=cl.ap[:, 1:2], b=cl.ap[:, 0:1])
            yield
            act(cl, cl, AF.Ln, scale=1.0 / HD, o=cl.ap[:, 5:6], i=cl.ap[:, 2:3])
            act(cl, cl, AF.Exp, scale=-0.5, o=cl.ap[:, 7:8], i=cl.ap[:, 5:6])
            yield
            yb = ytm[par]
            stt(yb, pn, cl.ap[:, 7:8], s_t[ti], ALU.mult, ALU.mult, o=yb.ap[:, 0:HD], a=pn.ap[:, 0:HD],
                b=s_t[ti].ap, extra_r=[cl])
            yield
            pt2 = PT[par]
            for c in range(2):
                tr(pt2, pt2.ap[:, 512 + c * 128:512 + (c + 1) * 128], yb, yb.ap[:, c * 128:(c + 1) * 128], identb, identb.ap, signal=(c == 1))
            yield
            for c in range(2):
                ch = 2 * h + c
                ts("dve", yT, pt2, gcol.ap[:, ch:ch + 1], None, ALU.mult, o=yT.ap[:, 8 + ch, ti * 128:(ti + 1) * 128],
                   a=pt2.ap[:, 512 + c * 128:512 + (c + 1) * 128], extra_r=[gcol])
                yield

        def load_head_w(h):
            for half in range(2):
                for j in range(2):
                    c0 = 1024 + (half * 2 + j) * 1024 + h * 256
                    P.dma("pool", W2[half].ap[:, :, j * 256:(j + 1) * 256], w_in_v[:, :, c0:c0 + 256], writes=[W2[half]])

        def projgen(h, ti):
            tok0 = main_tiles[ti][2]
            pj = PJ[0]
            for kc in range(KC):
                mm(pj, pj.ap[:, 0:512], xnT, xnT.ap[:, kc, tok0:tok0 + 128], W2[0], W2[0].ap[:, kc, :], kc == 0, kc == KC - 1)
                if kc % 2 == 1:
                    yield
            cp("act", q_t[ti], pj, i=pj.ap[:, 0:256])
            cp("dve", k_t[ti], pj, i=pj.ap[:, 256:512])
            yield
            pj = PJ[1]
            for kc in range(KC):
                mm(pj, pj.ap[:, 0:512], xnT, xnT.ap[:, kc, tok0:tok0 + 128], W2[1], W2[1].ap[:, kc, :], kc == 0, kc == KC - 1)
                if kc % 2 == 1:
                    yield
            cp("dve", v_t[ti], pj, o=v_t[ti].ap[:, 0:HD], i=pj.ap[:, 0:256])
            act(sgt, pj, AF.Exp, scale=-1.0, i=pj.ap[:, 256:512])
            yield
            act(sgt, sgt, AF.Ln, bias=1.0)
            yield
            act(s_t[ti], sgt, AF.Exp, scale=-1.0)
            yield

        def interleave(gens):
            gens = list(gens)
            while gens:
                for g_ in list(gens):
                    try:
                        next(g_)
                    except StopIteration:
                        gens.remove(g_)

        fence(SET_PL + SET_HD + q_t + k_t + s_t + v_t)
        memset("dve", v_ext, 1.0)
        for vt_ in v_t:
            vt_.w = v_ext.w
        memset("dve", Vmask, 0.0)
        PN = PS
        load_head_w(0)
        for ti in range(9):
            interleave([projgen(0, ti)])
        for h in range(NH):
            if h + 1 < NH:
                load_head_w(h + 1)
            else:
                for cg_ in range(2):
                    P.dma("pool", W2[cg_].ap, w_out_v[:, :, cg_ * 512:(cg_ + 1) * 512], writes=[W2[cg_]])
            for s_ in range(11):
                tasks = []
                if s_ == 0 and h > 0:
                    continue
                if s_ <= 8:
                    tasks.append(front(h, s_, 9 + s_, "p" if s_ < 8 else "s"))
                if 1 <= s_ <= 9:
                    tasks.append(back(h, s_ - 1, 9 + s_ - 1, "p" if s_ - 1 < 8 else "s"))
                if h + 1 < NH and 2 <= s_ <= 10:
                    tasks.append(projgen(h + 1, s_ - 2))
                if s_ == 10 and h + 1 < NH:
                    tasks.append(front(h + 1, 0, 9, "p"))
                interleave(tasks)
        P.checkpoint(7)
        for h in range(NH):
            for dc in range(2):
                P.dma("sp", C_p[h, dc * 128:(dc + 1) * 128, :], C32[h].ap[:, dc, 0:HD], reads=[C32[h]])
                P.dma("sp", n_p[h:h + 1, dc * 128:(dc + 1) * 128].rearrange("o p -> p o"), C32[h].ap[:, dc, HD:HDE], reads=[C32[h]], slow=True)
        P.checkpoint(7.5)
        tr(PS[0], PS[0].ap[:, 0:128], nS, nS.ap.rearrange("p b h c -> p (b h c)"), ident32, ident32.ap, signal=True)
        cp("dve", nrow, PS[0], i=PS[0].ap[:, 0:128])
        P.dma("sp", n_s, nrow.ap, reads=[nrow])

        P.checkpoint(8)
        all_src = [x_own[i * 128:(i + 1) * 128, :] for i in range(8)] + [x_smp]
        z4_users = Z4BUFS
        for i in range(9):
            P.dma("sp", x_new[i].ap, all_src[i], reads=[], writes=[x_new[i]] + z4_users)
        WO = [W2[0], W2[1]]
        for cg in range(4):
            W = WO[cg % 2]
            if cg >= 2:
                P.dma("pool", W.ap, w_out_v[:, :, cg * 512:(cg + 1) * 512], writes=[W])
            for i in range(9):
                pj = proj(yT, i * 128, 128, W, W.ap, 512)
                tt("dve", x_new[i], x_new[i], pj, ALU.add, o=x_new[i].ap[:, cg * 512:(cg + 1) * 512],
                   a=x_new[i].ap[:, cg * 512:(cg + 1) * 512], b=pj.ap[:, 0:512])
        P.dma("sp", Z3[:, 4096:4096 + D], norm_ffn_w.partition_broadcast(128), writes=[W2[1]])
        xn2T = Buf(Z1b[:, 0:KC * 1152].rearrange("p (k t) -> p k t", k=KC))
        fence([xnT, xn2T])
        xnbC = [Buf(Z2b[:, 0:2048]), Buf(Z2b[:, 2048:4096])]
        sqC = Buf(Z2b[:, 4096:6144])
        fence([yT, sqC] + xnbC)
        run_window([(lambda i=i: P.defer(lambda: norm_tile(None, 128, xn2T, i * 128, x_in=x_new[i], wbuf=W2[1], wap=Z3[:, 4096:4096 + D], xnbs=xnbC, sqb=sqC))) for i in range(9)])

        P.checkpoint(9)
        hT = Buf(Z2b[:, 0:11 * 1152].rearrange("p (f t) -> p f t", f=11))
        WG = [Buf(Z3b[:, i * 2048:(i + 1) * 2048].rearrange("p (k n) -> p k n", k=KC)) for i in range(2)]
        WUp = [Buf(Z3b[:, 4096 + i * 2048:4096 + (i + 1) * 2048].rearrange("p (k n) -> p k n", k=KC)) for i in range(2)]
        WD = [Buf(Z3b[:, 8192 + i * 2816:8192 + (i + 1) * 2816].rearrange("p (f n) -> p f n", f=11)) for i in range(2)]
        fence([W2[0], W2[1], yT, hT, sqC] + xnbC + WG + WUp + WD + silt)
        tgs = [(0, 512), (512, 512), (1024, 128)]
        fr = [0]
        for qd in range(4):
            for fl in range(11):
                fc = qd * 11 + fl
                wgb = WG[fr[0] % 2]
                wub = WUp[fr[0] % 2]
                fr[0] += 1
                P.dma("pool", wgb.ap, w_gate_v[:, :, fc * 128:(fc + 1) * 128], writes=[wgb])
                P.dma("pool", wub.ap, w_up_v[:, :, fc * 128:(fc + 1) * 128], writes=[wub])
                for ti, (t0, tn) in enumerate(tgs):
                    pg = PJ[ti % 2]
                    pu = PS[ti % 2]
                    for kc in range(KC):
                        mm(pg, pg.ap[:, 0:tn], wgb, wgb.ap[:, kc, :], xn2T, xn2T.ap[:, kc, t0:t0 + tn], kc == 0, kc == KC - 1)
                    for kc in range(KC):
                        mm(pu, pu.ap[:, 0:tn], wub, wub.ap[:, kc, :], xn2T, xn2T.ap[:, kc, t0:t0 + tn], kc == 0, kc == KC - 1)
                    st_ = silt[ti % 2]
                    act(st_, pg, AF.Silu, o=st_.ap[:, 0:tn], i=pg.ap[:, 0:tn])
                    tt("dve", hT, st_, pu, ALU.mult, o=hT.ap[:, fl, t0:t0 + tn], a=st_.ap[:, 0:tn], b=pu.ap[:, 0:tn])
            for cg in range(8):
                W = WD[cg % 2]
                P.dma("pool", W.ap, w_down_v[:, qd * 11:(qd + 1) * 11, cg * 256:(cg + 1) * 256], writes=[W])
                for i in range(9):
                    pc = PC[i % 2]
                    for fl in range(11):
                        mm(pc, pc.ap[:, 0:256], hT, hT.ap[:, fl, i * 128:(i + 1) * 128], W, W.ap[:, fl, :], fl == 0, fl == 10)
                    tt("dve", x_new[i], x_new[i], pc, ALU.add, o=x_new[i].ap[:, cg * 256:(cg + 1) * 256],
                       a=x_new[i].ap[:, cg * 256:(cg + 1) * 256], b=pc.ap[:, 0:256])

        P.checkpoint(10)
        wbE = Buf(Z1[:, 0:D])
        fence([xn2T, wbE])
        P.dma("sp", wbE.ap, norm_final_w.partition_broadcast(128), writes=[wbE])
        for i in range(9):
            ssb = SS[i % 2]
            xb = x_new[i]
            act(sq, xb, AF.Square, accum=(ssb, ssb.ap[:, 0:1]))
            act(ssb, ssb, AF.Ln, bias=EPS, scale=1.0 / D, o=ssb.ap[:, 1:2], i=ssb.ap[:, 0:1])
            act(ssb, ssb, AF.Exp, scale=-0.5, o=ssb.ap[:, 2:3], i=ssb.ap[:, 1:2])
            stt(xb, xb, ssb.ap[:, 2:3], wbE, ALU.mult, ALU.mult, extra_r=[ssb])
            dst = y_own[i * 128:(i + 1) * 128, :] if i < 8 else y_smp
            P.dma("sp", dst, xb.ap, reads=[xb])

        sem_es = ExitStack()
        with sem_es:
            sems = {}
            for e in ENGS:
                sems[e] = sem_es.enter_context(nc.semaphore("s_" + e))
            for q, n in P.ndma.items():
                for i in range(n):
                    sems[(q, i)] = sem_es.enter_context(nc.semaphore("d_%s%d" % (q, i)))
            with nc.Block() as block:
                @block.tensor
                def _(e):
                    P.replay("pe", e, sems)

                @block.scalar
                def _(e):
                    P.replay("act", e, sems)

                @block.vector
                def _(e):
                    P.replay("dve", e, sems)

                @block.gpsimd
                def _(e):
                    P.replay("pool", e, sems)
                    P.final_waits("pool", e, sems)

                @block.sync
                def _(e):
                    P.replay("sp", e, sems)
                    P.final_waits("sp", e, sems)
    return nc


_NC = None


def _get_nc():
    global _NC
    if _NC is None:
        _NC = build_nc()
    return _NC


def kernel(x_prompt, x_sample, state_pool, state_mlstm_C, state_mlstm_n, state_mlstm_m, meta_tokens, norm_mix_w, w_in,
           b_igate, b_fgate, w_pool, pool_scale, mlstm_norm_w, w_out, norm_ffn_w, w_gate, w_up, w_down, norm_final_w):
    f = lambda a: np.ascontiguousarray(np.asarray(a, dtype=np.float32))
    xp, xs = f(x_prompt), f(x_sample)
    meta = f(meta_tokens)
    shared = {
        "norm_mix_w": f(norm_mix_w)[0], "w_in": f(w_in)[0], "b_igate": f(b_igate)[0], "b_fgate": f(b_fgate)[0],
        "w_pool": f(w_pool)[0], "pool_scale": f(pool_scale)[0], "mlstm_norm_w": f(mlstm_norm_w)[0], "w_out": f(w_out)[0],
        "norm_ffn_w": f(norm_ffn_w)[0], "w_gate": f(w_gate)[0], "w_up": f(w_up)[0], "w_down": f(w_down)[0],
        "norm_final_w": f(norm_final_w),
    }
    sp, sC, sn, sm = f(state_pool)[0], f(state_mlstm_C)[0], f(state_mlstm_n)[0], f(state_mlstm_m)[0]
    in_maps = []
    for c in range(8):
        s, h = c // 2, c % 2
        m = dict(shared)
        m["x_meta"] = meta
        m["x_pre"] = xp[s, 0:1024] if h == 1 else np.zeros((1024, D), np.float32)
        m["x_own"] = np.ascontiguousarray(xp[s, 1024 * h:1024 * (h + 1)])
        m["x_halo"] = meta if h == 0 else np.ascontiguousarray(xp[s, 1008:1024])
        m["x_smp"] = np.ascontiguousarray(xs[16 * c:16 * (c + 1)].reshape(128, D))
        m["flag"] = np.full((128, 1), float(h), np.float32)
        m["st_pool"] = np.ascontiguousarray(sp[16 * c:16 * (c + 1)])
        m["st_C"] = np.ascontiguousarray(sC[16 * c:16 * (c + 1)])
        m["st_n"] = np.ascontiguousarray(sn[16 * c:16 * (c + 1)].reshape(64, HD))
        m["st_m"] = np.ascontiguousarray(sm[16 * c:16 * (c + 1)])
        in_maps.append(m)
    nc = _get_nc()
    res = run_bass_kernel_spmd(nc, in_maps, core_ids=list(range(8))).results
    y_prompt = np.stack([np.concatenate([res[2 * s]["y_own"], res[2 * s + 1]["y_own"]], axis=0) for s in range(4)])
    y_sample = np.concatenate([r["y_smp"] for r in res], axis=0).reshape(128, 8, D)
    pool_pp = np.stack([res[2 * s + 1]["pool_p"] for s in range(4)])[None]
    C_pp = np.stack([res[2 * s + 1]["C_p"] for s in range(4)])[None]
    n_pp = np.stack([res[2 * s + 1]["n_p"] for s in range(4)])[None]
    m_pp = np.stack([res[2 * s + 1]["m_p"].reshape(NH) for s in range(4)])[None]
    pool_ss = np.concatenate([r["pool_s"] for r in res], axis=0)[None]
    C_ss = np.concatenate([r["C_s"] for r in res], axis=0)[None]
    n_ss = np.concatenate([r["n_s"].reshape(16, NH, HD) for r in res], axis=0)[None]
    m_ss = np.concatenate([r["m_s"] for r in res], axis=0)[None]
    outs = (y_prompt, y_sample, pool_pp, C_pp, n_pp, m_pp, pool_ss, C_ss, n_ss, m_ss)
    return tuple(np.ascontiguousarray(o, dtype=np.float32) for o in outs)
```
